# Optimizing a Trainium2 kernel written in Bass

```python
import math
import jax, jax.numpy as jnp
from jax import lax
import numpy as np


D_MODEL = 1024
BATCH = 8
SEQ = 4096
DEPTH = 4

WIDTH_A = D_MODEL // 2
HEAD_DIM_A = 64
N_HEADS_A = WIDTH_A // HEAD_DIM_A
DILATED_BRANCHES = ((128, 1), (512, 4), (2048, 16))
ROPE_THETA = 10000.0

WIDTH_R = D_MODEL // 4
N_HEADS_R = 4
V_DIM_R = WIDTH_R // N_HEADS_R
QK_DIM_R = V_DIM_R // 2
RET_CHUNK = 128

WIDTH_G = D_MODEL // 4
N_HEADS_G = 4
V_DIM_G = WIDTH_G // N_HEADS_G
QK_DIM_G = V_DIM_G // 2
GLA_LOW_RANK = 16
GLA_TAU = 16.0
GLA_CHUNK = 64

MIX_WIDTH = WIDTH_A + WIDTH_R + WIDTH_G
PROJ_SIZES = (WIDTH_A, WIDTH_A, WIDTH_A,
              N_HEADS_R * QK_DIM_R, N_HEADS_R * QK_DIM_R, WIDTH_R, WIDTH_R,
              N_HEADS_G * QK_DIM_G, N_HEADS_G * QK_DIM_G, WIDTH_G, WIDTH_G, GLA_LOW_RANK)
PROJ_WIDTH = sum(PROJ_SIZES)

D_FF = 128 * ((8 * D_MODEL // 3 + 127) // 128)
CONV_WIDTH = 3

DEEPNORM_ALPHA = (2 * DEPTH) ** 0.25
DEEPNORM_BETA = (8 * DEPTH) ** -0.25
LN_EPS = 1e-5
HEAD_NORM_EPS = 1e-6

kernel_name = "hymba_style_dilated_retnet_gla_convglu"


def layer_norm(x, g, b):
    xf = x.astype(jnp.float32)
    mu = xf.mean(-1, keepdims=True)
    var = jnp.square(xf - mu).mean(-1, keepdims=True)
    y = (xf - mu) * lax.rsqrt(var + LN_EPS)
    return (y * g.astype(jnp.float32) + b.astype(jnp.float32)).astype(x.dtype)


def head_norm(t):
    t = t.astype(jnp.float32)
    mu = t.mean(-1, keepdims=True)
    var = jnp.square(t - mu).mean(-1, keepdims=True)
    return (t - mu) * lax.rsqrt(var + HEAD_NORM_EPS)


def split_heads(t, n_heads):
    b, s, _ = t.shape
    return t.reshape(b, s, n_heads, -1).transpose(0, 2, 1, 3)


def merge_heads(t):
    b, h, s, d = t.shape
    return t.transpose(0, 2, 1, 3).reshape(b, s, h * d)


def rope_tables(seq, dim):
    inv = 1.0 / (ROPE_THETA ** (jnp.arange(0, dim, 2, dtype=jnp.float32) / dim))
    ang = jnp.arange(seq, dtype=jnp.float32)[:, None] * inv[None, :]
    return jnp.cos(ang), jnp.sin(ang)


def apply_rope(t, cos, sin):
    c = cos.astype(t.dtype)
    s = sin.astype(t.dtype)
    t1, t2 = jnp.split(t, 2, axis=-1)
    return jnp.concatenate([t1 * c - t2 * s, t1 * s + t2 * c], axis=-1)


def dilated_branch(q, k, v, window, dilation):
    b, h, s, d = q.shape
    blk = window // dilation
    unit = blk * dilation
    s_pad = -(-s // unit) * unit
    nb = s_pad // unit

    def to_blocks(t):
        t = jnp.pad(t, ((0, 0), (0, 0), (0, s_pad - s), (0, 0)))
        t = t.reshape(b, h, nb, blk, dilation, d)
        return t.transpose(0, 1, 4, 2, 3, 5)

    def with_prev(t):
        prev = jnp.pad(t, ((0, 0), (0, 0), (0, 0), (1, 0), (0, 0), (0, 0)))[:, :, :, :-1]
        return jnp.concatenate([prev, t], axis=4)

    def from_blocks(t):
        t = jnp.moveaxis(t, 2, 4)
        return t.reshape(b, h, s_pad, *t.shape[5:])[:, :, :s]

    qb = to_blocks(q)
    kk = with_prev(to_blocks(k))
    vv = with_prev(to_blocks(v)).astype(jnp.float32)
    scores = jnp.einsum('bhrnqd,bhrnkd->bhrnqk', qb, kk).astype(jnp.float32) * (d ** -0.5)
    qi = jnp.arange(blk)[:, None]
    kj = jnp.arange(2 * blk)[None, :]
    band = (kj >= qi) & (kj <= qi + blk)
    has_prev = (jnp.arange(nb) > 0)[:, None, None] | (kj >= blk)[None]
    mask = band[None] & has_prev
    scores = jnp.where(mask, scores, -jnp.inf)
    m = scores.max(-1)
    p = jnp.exp(scores - m[..., None])
    l = p.sum(-1)
    o = jnp.einsum('bhrnqk,bhrnkd->bhrnqd', p, vv) / l[..., None]
    return from_blocks(o), from_blocks(m), from_blocks(l)


def dilated_attention(q, k, v):
    outs, maxes, denoms = zip(*[dilated_branch(q, k, v, w, r) for (w, r) in DILATED_BRANCHES])
    m_all = jnp.stack(maxes)
    l_all = jnp.stack(denoms)
    o_all = jnp.stack(outs)
    wts = l_all * jnp.exp(m_all - m_all.max(0))
    wts = wts / wts.sum(0)
    return jnp.einsum('gbhs,gbhsd->bhsd', wts, o_all)


def retention(q, k, v):
    b, h, s, dk = q.shape
    dv = v.shape[-1]
    c = RET_CHUNK
    n = s // c
    lg = jnp.log(1.0 - jnp.power(2.0, -5.0 - jnp.arange(h, dtype=jnp.float32)))
    idx = jnp.arange(c, dtype=jnp.float32)
    dist = idx[:, None] - idx[None, :]
    decay_intra = jnp.where(dist >= 0, jnp.exp(lg[:, None, None] * jnp.maximum(dist, 0.0)), 0.0)
    q_dec = jnp.exp(lg[:, None] * (idx + 1.0))
    k_dec = jnp.exp(lg[:, None] * (c - 1.0 - idx))
    chunk_dec = jnp.exp(lg * c)
    qc = q.reshape(b, h, n, c, dk)
    kc = k.reshape(b, h, n, c, dk)
    vc = v.reshape(b, h, n, c, dv)
    scores = jnp.einsum('bhncd,bhnsd->bhncs', qc, kc) * decay_intra[:, None]
    intra = jnp.einsum('bhncs,bhnse->bhnce', scores, vc)
    kv = jnp.einsum('bhnsd,bhnse->nbhde', kc * k_dec[:, None, :, None], vc)

    def step(state, kv_n):
        return chunk_dec[:, None, None] * state + kv_n, state

    _, prev = lax.scan(step, jnp.zeros((b, h, dk, dv), jnp.float32), kv)
    inter = jnp.einsum('bhncd,nbhde->bhnce', qc * q_dec[:, None, :, None], prev)
    return (intra + inter).reshape(b, h, s, dv)


def gated_linear_attention(q, k, v, log_alpha):
    b, h, s, dk = q.shape
    dv = v.shape[-1]
    c = GLA_CHUNK
    n = s // c
    qc = q.reshape(b, h, n, c, dk)
    kc = k.reshape(b, h, n, c, dk)
    vc = v.reshape(b, h, n, c, dv)
    cum = jnp.cumsum(log_alpha.reshape(b, h, n, c, dk), axis=3)
    q_t = qc * jnp.exp(cum)
    k_t = kc * jnp.exp(-cum)
    causal = jnp.tril(jnp.ones((c, c), dtype=bool))
    att = jnp.where(causal, jnp.einsum('bhncd,bhnsd->bhncs', q_t, k_t), 0.0)
    intra = jnp.einsum('bhncs,bhnse->bhnce', att, vc)
    last = cum[:, :, :, -1:]
    kv = jnp.einsum('bhnsd,bhnse->nbhde', kc * jnp.exp(last - cum), vc)
    chunk_dec = jnp.moveaxis(jnp.exp(last[:, :, :, 0]), 2, 0)

    def step(state, xs):
        dec, kv_n = xs
        return dec[..., None] * state + kv_n, state

    _, prev = lax.scan(step, jnp.zeros((b, h, dk, dv), jnp.float32), (chunk_dec, kv))
    inter = jnp.einsum('bhncd,nbhde->bhnce', q_t, prev)
    return (intra + inter).reshape(b, h, s, dv)


def hybrid_mixer(x, w_in, w_alpha, b_alpha, mix_scale, w_out, cos_a, sin_a, cos_r, sin_r):
    f32 = jnp.float32
    proj = x @ w_in
    split_points = np.cumsum(PROJ_SIZES)[:-1].tolist()
    qa, ka, va, qr, kr, vr, gr, qg, kg, vg, rg, ag = jnp.split(proj, split_points, axis=-1)

    qa = apply_rope(split_heads(qa, N_HEADS_A), cos_a, sin_a)
    ka = apply_rope(split_heads(ka, N_HEADS_A), cos_a, sin_a)
    va = split_heads(va, N_HEADS_A)
    ya = merge_heads(head_norm(dilated_attention(qa, ka, va)))

    qr = apply_rope(split_heads(qr, N_HEADS_R), cos_r, sin_r).astype(f32)
    kr = apply_rope(split_heads(kr, N_HEADS_R), cos_r, sin_r).astype(f32) * (QK_DIM_R ** -0.5)
    vr = split_heads(vr, N_HEADS_R).astype(f32)
    yr = merge_heads(head_norm(retention(qr, kr, vr))) * jax.nn.silu(gr.astype(f32))

    log_alpha = jax.nn.log_sigmoid((ag @ w_alpha + b_alpha).astype(f32)) / GLA_TAU
    qg = split_heads(qg, N_HEADS_G).astype(f32) * (QK_DIM_G ** -0.5)
    kg = split_heads(kg, N_HEADS_G).astype(f32)
    vg = split_heads(vg, N_HEADS_G).astype(f32)
    og = gated_linear_attention(qg, kg, vg, split_heads(log_alpha, N_HEADS_G))
    yg = merge_heads(head_norm(og)) * jax.nn.silu(rg.astype(f32))

    y = jnp.concatenate([ya, yr, yg], axis=-1) * mix_scale.astype(f32)
    return y.astype(x.dtype) @ w_out


def causal_dwconv(u, w, bias):
    ch = u.shape[-1]
    y = lax.conv_general_dilated(u, w[:, None, :].astype(u.dtype), window_strides=(1,),
                                 padding=((CONV_WIDTH - 1, 0),),
                                 dimension_numbers=('NWC', 'WIO', 'NWC'),
                                 feature_group_count=ch)
    return y + bias.astype(u.dtype)


def conv_glu_ffn(x, w_up, conv_w, conv_b, w_down):
    u = causal_dwconv(x @ w_up, conv_w, conv_b)
    gate, val = jnp.split(u, 2, axis=-1)
    return (jax.nn.silu(gate) * val) @ w_down


def setup_inputs(seed: int = 0) -> dict:
    key = jax.random.key(seed)
    ks = jax.random.split(key, 15)
    f32 = jnp.float32

    def nrm(k, shape, scale):
        return jax.random.normal(k, shape, f32) * scale

    hg = N_HEADS_G * QK_DIM_G
    return {
        "x": nrm(ks[0], (BATCH, SEQ, D_MODEL), 1.0),
        "w_in": nrm(ks[1], (DEPTH, D_MODEL, PROJ_WIDTH), D_MODEL ** -0.5),
        "w_alpha": nrm(ks[2], (DEPTH, GLA_LOW_RANK, hg), GLA_LOW_RANK ** -0.5),
        "b_alpha": nrm(ks[3], (DEPTH, hg), 0.1),
        "mix_scale": 1.0 + nrm(ks[4], (DEPTH, MIX_WIDTH), 0.02),
        "w_out": nrm(ks[5], (DEPTH, MIX_WIDTH, D_MODEL), MIX_WIDTH ** -0.5 * DEEPNORM_BETA),
        "ln1_g": 1.0 + nrm(ks[6], (DEPTH, D_MODEL), 0.02),
        "ln1_b": nrm(ks[7], (DEPTH, D_MODEL), 0.02),
        "w_up": nrm(ks[8], (DEPTH, D_MODEL, 2 * D_FF), D_MODEL ** -0.5),
        "conv_w": nrm(ks[9], (DEPTH, CONV_WIDTH, 2 * D_FF), CONV_WIDTH ** -0.5),
        "conv_b": nrm(ks[10], (DEPTH, 2 * D_FF), 0.02),
        "w_down": nrm(ks[11], (DEPTH, D_FF, D_MODEL), D_FF ** -0.5 * DEEPNORM_BETA),
        "ln2_g": 1.0 + nrm(ks[12], (DEPTH, D_MODEL), 0.02),
        "ln2_b": nrm(ks[13], (DEPTH, D_MODEL), 0.02),
    }


def reference(x, w_in, w_alpha, b_alpha, mix_scale, w_out, ln1_g, ln1_b,
              w_up, conv_w, conv_b, w_down, ln2_g, ln2_b):
    seq = x.shape[1]
    cos_a, sin_a = rope_tables(seq, HEAD_DIM_A)
    cos_r, sin_r = rope_tables(seq, QK_DIM_R)
    for l in range(DEPTH):
        mix = hybrid_mixer(x, w_in[l], w_alpha[l], b_alpha[l], mix_scale[l], w_out[l],
                           cos_a, sin_a, cos_r, sin_r)
        x = layer_norm(DEEPNORM_ALPHA * x + mix, ln1_g[l], ln1_b[l])
        ffn = conv_glu_ffn(x, w_up[l], conv_w[l], conv_b[l], w_down[l])
        x = layer_norm(DEEPNORM_ALPHA * x + ffn, ln2_g[l], ln2_b[l])
    return x
```

```python
import numpy as np
from contextlib import ExitStack
import concourse.bass as bass
import concourse.mybir as mybir
from concourse.bass_utils import run_bass_kernel_spmd

F32 = mybir.dt.float32
BF16 = mybir.dt.bfloat16
AF = mybir.ActivationFunctionType
ALU = mybir.AluOpType

S = 4096
D = 1024
DEPTH = 4
DFF = 2816
PW = 3088
ALPHA = float((2 * DEPTH) ** 0.25)
LN_EPS = 1e-5
HN_EPS = 1e-6
NEG = -30000.0
NFM = 16
SC_A = 64 ** -0.5
SC_L = 32 ** -0.5

C_MASKB, C_IDENT, C_LMASK, C_A1, C_A2, C_B1, C_ONES1024, C_HMR, C_HMG, C_DECR, C_ER, C_EINVR, C_KDR, C_ONES = (
    0, 256, 384, 512, 576, 640, 768, 896, 900, 904, 908, 1420, 1932, 2444)
C_LMASK4 = 2572
NCST = 3084

P_MSA, P_MSRG, P_LN1G, P_LN1B, P_LN2G, P_LN2B, P_CW, P_CB, P_BA = 0, 8, 12, 20, 28, 36, 44, 176, 220
NPL = 221


def make_consts():
    c = np.zeros((128, NCST), np.float32)
    j = np.arange(128)[:, None]
    i = np.arange(128)[None, :]
    c[:, C_MASKB:C_MASKB + 128] = np.where(j <= i, 0.0, NEG)
    c[:, C_MASKB + 128:C_MASKB + 256] = np.where(j >= i, 0.0, NEG)
    c[:, C_IDENT:C_IDENT + 128] = np.eye(128, dtype=np.float32)
    c[:, C_LMASK:C_LMASK + 128] = (j <= i).astype(np.float32)
    for h in range(4):
        c[:, C_LMASK4 + h * 128:C_LMASK4 + (h + 1) * 128] = (j <= i).astype(np.float32)
    c[0:64, C_A1:C_A1 + 64] = 1.0 / 64
    c[0:64, C_A2:C_A2 + 64] = 1.0 / 64
    c[64, C_A2:C_A2 + 64] = HN_EPS
    for h in range(2):
        c[h * 64:(h + 1) * 64, C_B1 + h * 64:C_B1 + (h + 1) * 64] = 1.0 / 64
    c[:, C_ONES1024:C_ONES1024 + 128] = 1.0 / 1024
    p = np.arange(128)
    headR = (p % 64) // 16
    headG = p // 32
    for h in range(4):
        c[:, C_HMR + h] = (headR == h) * SC_L
        c[:, C_HMG + h] = (headG == h) * SC_L
    lg = np.log(1.0 - np.power(2.0, -5.0 - np.arange(4, dtype=np.float64)))
    lgp = lg[headR][:, None]
    idx = (np.arange(512) % 128)[None, :].astype(np.float64)
    c[:, C_DECR:C_DECR + 4] = np.exp(lgp * 128.0)
    c[:, C_ER:C_ER + 512] = np.exp(lgp * (idx + 1.0))
    c[:, C_EINVR:C_EINVR + 512] = np.exp(-lgp * (idx + 1.0))
    c[:, C_KDR:C_KDR + 512] = np.exp(lgp * (127.0 - idx))
    c[:, C_ONES:C_ONES + 128] = 1.0
    rope = np.zeros((4, 128, S), np.float32)
    pos = np.arange(S, dtype=np.float32)[None, :]
    invA = (1.0 / (10000.0 ** (np.arange(0, 64, 2, dtype=np.float32) / 64))).astype(np.float32)
    invR = (1.0 / (10000.0 ** (np.arange(0, 32, 2, dtype=np.float32) / 32))).astype(np.float32)
    angA = (pos * invA[p % 32][:, None]).astype(np.float32)
    angR = (pos * invR[p % 16][:, None]).astype(np.float32)
    rope[0], rope[1] = np.cos(angA), np.sin(angA)
    rope[2], rope[3] = np.cos(angR), np.sin(angR)
    return c, rope


def win_perm():
    qA, kA, vA, qR, kR, vR, gR, qG, kG, vG, rG, aG = 0, 512, 1024, 1536, 1664, 1792, 2048, 2304, 2432, 2560, 2816, 3072
    cols = []
    for base in (qA, kA):
        for g in range(2):
            cols += [base + h * 64 + i for h in range(4 * g, 4 * g + 4) for i in range(32)]
            cols += [base + h * 64 + 32 + i for h in range(4 * g, 4 * g + 4) for i in range(32)]
    for half in range(2):
        cols += [qR + h * 32 + half * 16 + i for h in range(4) for i in range(16)]
        cols += [kR + h * 32 + half * 16 + i for h in range(4) for i in range(16)]
    cols += list(range(gR, gR + 256))
    cols += list(range(qG, qG + 128))
    cols += list(range(kG, kG + 128))
    cols += list(range(rG, rG + 256))
    cols += list(range(aG, aG + 16))
    cols += list(range(vA, vA + 512))
    cols += list(range(vR, vR + 256))
    cols += list(range(vG, vG + 256))
    assert len(cols) == PW and len(set(cols)) == PW
    return np.array(cols)


class Buf:
    __slots__ = ("w", "r", "pw", "pr", "name")

    def __init__(self, name=""):
        self.w = {}
        self.r = {}
        self.pw = None
        self.pr = set()
        self.name = name


class TK:
    CE = ("pe", "act", "dve", "pool")

    def __init__(self, nc, es):
        self.nc = nc
        self.E = {"pe": nc.tensor, "act": nc.scalar, "dve": nc.vector, "pool": nc.gpsimd, "sp": nc.sync}
        self.sem = {}
        self.val = {}
        self.seen = {e: {} for e in self.E}
        for e in self.CE:
            self._mk(es, "c_" + e)
        self.dq = {}
        for q, n in (("sp", 16), ("pool", 10)):
            self.dq[q] = [self._mk(es, f"d_{q}{i}") for i in range(n)]
        self.dqi = {q: 0 for q in self.dq}
        self.pending = {e: [] for e in self.CE}
        self.nins = 0

    def _mk(self, es, name):
        self.sem[name] = es.enter_context(self.nc.semaphore(name))
        self.val[name] = 0
        return name

    def wait(self, e, s, v):
        if v <= self.seen[e].get(s, 0):
            return
        self.E[e].wait_ge(self.sem[s], v)
        self.seen[e][s] = v
        self.nins += 1

    def _deps(self, e, reads, writes, dma=False):
        own = "c_" + e
        deps = {}
        for b in reads:
            assert b.pw in (None, e), f"read of {b.name} with pending writer {b.pw}"
            for s, v in b.w.items():
                if s == own and (e == "pe" and not dma):
                    continue
                if v > deps.get(s, 0):
                    deps[s] = v
        for b in writes:
            assert b.pw in (None, e), f"write of {b.name} with pending writer {b.pw}"
            assert not (b.pr - {e}), f"write of {b.name} with pending readers {b.pr}"
            for dd in (b.w, b.r):
                for s, v in dd.items():
                    if s == own and not dma:
                        continue
                    if v > deps.get(s, 0):
                        deps[s] = v
        for s, v in deps.items():
            self.wait(e, s, v)

    def op(self, e, emit, reads=(), writes=(), tick=True):
        self._deps(e, reads, writes)
        ins = emit()
        self.nins += 1
        own = "c_" + e
        self.pending[e].append((reads, writes))
        if tick:
            self.val[own] += 1
            ins.then_inc(self.sem[own], 1)
            v = self.val[own]
            for rs, ws in self.pending[e]:
                for b in rs:
                    b.r[own] = v
                    b.pr.discard(e)
                for b in ws:
                    b.w[own] = v
                    b.pw = None
            self.pending[e] = []
        else:
            for b in reads:
                b.pr.add(e)
            for b in writes:
                b.pw = e
        return ins

    def dma(self, q, out, in_, reads=(), writes=()):
        self._deps(q, reads, writes, dma=True)
        names = self.dq[q]
        nm = names[self.dqi[q] % len(names)]
        self.dqi[q] += 1
        self.wait(q, nm, self.val[nm])
        ins = self.E[q].dma_start(out=out, in_=in_)
        self.nins += 1
        self.val[nm] += 16
        ins.then_inc(self.sem[nm], 16)
        v = self.val[nm]
        for b in reads:
            b.r[nm] = v
        for b in writes:
            b.w[nm] = v
        return ins

    def barrier(self):
        for e in self.CE:
            assert not self.pending[e], f"pending un-ticked ops on {e}"
        for e in self.E:
            for s, v in self.val.items():
                if s == "c_" + e:
                    continue
                self.wait(e, s, v)

    def mm(self, out, lhsT, rhs, start, stop, reads, writes, tick=False):
        return self.op("pe", lambda: self.nc.tensor.matmul(out, lhsT=lhsT, rhs=rhs, start=start, stop=stop,
                                                           skip_group_check=True), reads, writes, tick)

    def act(self, out, in_, func, reads, writes, **kw):
        return self.op("act", lambda: self.nc.scalar.activation(out=out, in_=in_, func=func, **kw), reads, writes)

    def tt(self, e, out, in0, in1, op, reads, writes):
        return self.op(e, lambda: self.E[e].tensor_tensor(out=out, in0=in0, in1=in1, op=op), reads, writes)

    def stt(self, out, in0, scalar, in1, op0, op1, reads, writes):
        return self.op("dve", lambda: self.nc.vector.scalar_tensor_tensor(out=out, in0=in0, scalar=scalar, in1=in1,
                                                                         op0=op0, op1=op1), reads, writes)

    def ts(self, e, out, in0, s1, s2, op0, op1, reads, writes):
        return self.op(e, lambda: self.E[e].tensor_scalar(out=out, in0=in0, scalar1=s1, scalar2=s2, op0=op0, op1=op1),
                       reads, writes)

    def copy(self, e, out, in_, reads, writes):
        if e == "act":
            return self.act(out, in_, AF.Copy, reads, writes)
        return self.op(e, lambda: self.E[e].tensor_copy(out=out, in_=in_), reads, writes)

    def memset(self, e, ap, val, writes):
        return self.op(e, lambda: self.E[e].memset(ap, val), (), writes)


class Ctx:
    pass


_UNIQ = [0]


def sbt(nc, es, name, shape, dt):
    _UNIQ[0] += 1
    return es.enter_context(nc.sbuf_tensor(f"{name}_{_UNIQ[0]}", shape, dt))


def pst(nc, es, name, shape, dt=F32):
    _UNIQ[0] += 1
    return es.enter_context(nc.psum_tensor(f"{name}_{_UNIQ[0]}", shape, dt))


def layer_norm(g, es_name, z, zB, N, gcol, bcol, dst_ap, stat_ps, stat_bufs, tmp):
    tk, nc = g.tk, g.nc
    zb, zq, msq, var, rstd, nmr = tmp["zb"], tmp["zq"], tmp["msq"], tmp["var"], tmp["rstd"], tmp["nmr"]
    B = tmp["B"]
    tk.act(zb[:, :, 0:N], z[:, :, 0:N], AF.Copy, [zB], [B["zb"]])
    tk.act(zq[:, :, 0:N], z[:, :, 0:N], AF.Square, [zB], [B["zq"]])
    mean_ps, e2_ps = stat_ps
    mB, eB = stat_bufs
    for c in range(8):
        tk.mm(mean_ps[:, 0:N], g.ones_b[:], zb[:, c, 0:N], c == 0, c == 7, [g.cstB, B["zb"]], [mB], tick=(c == 7))
    for c in range(8):
        tk.mm(e2_ps[:, 0:N], g.ones_b[:], zq[:, c, 0:N], c == 0, c == 7, [g.cstB, B["zq"]], [eB], tick=(c == 7))
    tk.act(msq[:, 0:N], mean_ps[:, 0:N], AF.Square, [mB], [B["msq"]])
    tk.tt("dve", var[:, 0:N], e2_ps[:, 0:N], msq[:, 0:N], ALU.subtract, [eB, B["msq"]], [B["var"]])
    tk.act(var[:, 0:N], var[:, 0:N], AF.Ln, [B["var"]], [B["var"]], bias=g.eps_ln[:, 0:1])
    tk.act(rstd[:, 0:N], var[:, 0:N], AF.Exp, [B["var"]], [B["rstd"]], scale=-0.5)
    tk.stt(nmr[:, 0:N], mean_ps[:, 0:N], -1.0, rstd[:, 0:N], ALU.mult, ALU.mult, [mB, B["rstd"]], [B["nmr"]])
    for c in range(8):
        e1 = "dve" if c % 2 == 0 else "pool"
        tk.tt(e1, z[:, c, 0:N], z[:, c, 0:N], rstd[:, 0:N], ALU.mult, [zB, B["rstd"]], [zB])
        tk.tt("pool", z[:, c, 0:N], z[:, c, 0:N], nmr[:, 0:N], ALU.add, [zB, B["nmr"]], [zB])
        tk.act(z[:, c, 0:N], z[:, c, 0:N], AF.Identity, [zB, g.plB], [zB], scale=gcol[:, c:c + 1], bias=bcol[:, c:c + 1])
    tk.dma("sp", dst_ap.rearrange("(c p) t -> p c t", p=128), z[:, :, 0:N], [zB], [])


def head_norm(g, src, srcB, K, lhs1, lhs2, M, stat_ps, stat_bufs, tmp, N=512):
    tk = g.tk
    B = tmp["B"]
    sq, msq, var, dd = tmp["sq"], tmp["msq"], tmp["var"], tmp["dd"]
    mean_ps, e2_ps = stat_ps
    mB, eB = stat_bufs
    tk.act(sq[0:K, 0:N], src, AF.Square, [srcB], [B["sq"]])
    tk.mm(mean_ps[0:M, 0:N], lhs1, src, True, True, [g.cstB, srcB], [mB], tick=True)
    tk.mm(e2_ps[0:M, 0:N], lhs2, sq[0:K, 0:N], True, True, [g.cstB, B["sq"]], [eB], tick=True)
    tk.act(msq[0:M, 0:N], mean_ps[0:M, 0:N], AF.Square, [mB], [B["msq"]])
    tk.tt("dve", var[0:M, 0:N], e2_ps[0:M, 0:N], msq[0:M, 0:N], ALU.subtract, [eB, B["msq"]], [B["var"]])
    return mean_ps, mB


def build(depth=DEPTH, debug=False):
    nc = bass.Bass("TRN2", target_bir_lowering=False)
    g = Ctx()
    g.nc = nc
    dkind = "ExternalOutput" if debug else "Internal"
    xin = nc.dram_tensor("xin", [D, S], F32, kind="ExternalInput").ap()
    win = nc.dram_tensor("win", [DEPTH, D, PW], F32, kind="ExternalInput").ap()
    walpha = nc.dram_tensor("walpha", [DEPTH, 16, 128], F32, kind="ExternalInput").ap()
    wout = nc.dram_tensor("wout", [DEPTH, D, D], F32, kind="ExternalInput").ap()
    wup = nc.dram_tensor("wup", [DEPTH, D, 2 * DFF], F32, kind="ExternalInput").ap()
    wdown = nc.dram_tensor("wdown", [DEPTH, DFF, D], F32, kind="ExternalInput").ap()
    pl = nc.dram_tensor("pl", [DEPTH, 128, NPL], F32, kind="ExternalInput").ap()
    cst = nc.dram_tensor("cst", [128, NCST], F32, kind="ExternalInput").ap()
    rope = nc.dram_tensor("rope", [4, 128, S], F32, kind="ExternalInput").ap()
    out = nc.dram_tensor("out", [D, S], F32, kind="ExternalOutput").ap()
    XA = nc.dram_tensor("XA", [D, S], F32, kind="Internal").ap()
    X1F = nc.dram_tensor("X1F", [D, S], F32, kind=dkind).ap()
    YT = nc.dram_tensor("YT", [D, S], BF16, kind=dkind).ap()
    FMS = nc.dram_tensor("FMS", [NFM, 128, S], BF16, kind=dkind).ap()
    LA = nc.dram_tensor("LA", [128, S], F32, kind=dkind).ap()
    VA = nc.dram_tensor("VA", [S, 8, 65], BF16, kind=dkind).ap()
    VRG = nc.dram_tensor("VRG", [S, 512], BF16, kind=dkind).ap()

    with ExitStack() as es:
        tk = TK(nc, es)
        g.tk = tk
        block = es.enter_context(nc.Block())

        @block.sync
        def _(sync):
            cst_sb = sbt(nc, es, "cst_sb", [128, NCST], F32)
            g.cstB = Buf("cst")
            g.cst = cst_sb
            tk.dma("sp", cst_sb[:], cst[:, :], [], [g.cstB])
            g.maskb = sbt(nc, es, "maskb", [128, 256], BF16)
            g.ident_b = sbt(nc, es, "ident_b", [128, 128], BF16)
            g.ones_b = sbt(nc, es, "ones_b", [128, 128], BF16)
            g.eps_ln = sbt(nc, es, "eps_ln", [128, 1], F32)
            tk.copy("dve", g.maskb[:], cst_sb[:, C_MASKB:C_MASKB + 256], [g.cstB], [g.cstB])
            tk.copy("dve", g.ident_b[:], cst_sb[:, C_IDENT:C_IDENT + 128], [g.cstB], [g.cstB])
            tk.copy("dve", g.ones_b[:], cst_sb[:, C_ONES1024:C_ONES1024 + 128], [g.cstB], [g.cstB])
            tk.memset("dve", g.eps_ln[:], LN_EPS, [g.cstB])
            g.eps_hn = sbt(nc, es, "eps_hn", [128, 1], F32)
            tk.memset("dve", g.eps_hn[:], HN_EPS, [g.cstB])
            g.one_col = cst_sb[:, C_ONES:C_ONES + 1]
            g.pl_sb = sbt(nc, es, "pl_sb", [128, NPL], F32)
            g.negb = sbt(nc, es, "negb", [128, 1], F32)
            g.plB = Buf("pl")
            tk.barrier()

            for l in range(depth):
                xsrc = xin if l == 0 else XA
                xdst = out if l == depth - 1 else XA
                tk.dma("sp", g.pl_sb[:], pl[l], [], [g.plB])
                tk.ts("dve", g.negb[:], g.pl_sb[:, P_BA:P_BA + 1], -1.0, None, ALU.mult, ALU.bypass, [g.plB], [g.plB])
                phase_P(g, l, xsrc, win, walpha, rope, FMS, LA, VA, VRG)
                tk.barrier()
                phase_A(g, l, FMS, VA, YT)
                tk.barrier()
                phase_L(g, l, FMS, LA, VRG, YT)
                tk.barrier()
                phase_O(g, l, xsrc, wout, YT, X1F)
                tk.barrier()
                phase_F(g, l, wup, wdown, X1F, xdst)
                tk.barrier()
    g.nins = tk.nins
    return nc, g


def phase_P(g, l, xsrc, win, walpha, rope, FMS, LA, VA, VRG):
    tk, nc = g.tk, g.nc
    with ExitStack() as es:
        wfm = sbt(nc, es, "wfm", [128, 8, 2064], BF16)
        wtm = sbt(nc, es, "wtm", [128, 8, 1024], BF16)
        wal = sbt(nc, es, "wal", [16, 128], F32)
        wB = Buf("w")
        xt = [sbt(nc, es, f"xt{i}", [128, 8, 512], BF16) for i in range(2)]
        xtB = [Buf(f"xt{i}") for i in range(2)]
        rp = [sbt(nc, es, f"rp{i}", [128, 4, 512], F32) for i in range(2)]
        rpB = [Buf(f"rp{i}") for i in range(2)]
        NSO = 6
        so = [sbt(nc, es, f"so{i}", [128, 512], BF16) for i in range(NSO)]
        soB = [Buf(f"so{i}") for i in range(NSO)]
        tmpf = [[sbt(nc, es, f"rt{s}_{i}", [128, 512], F32) for i in range(4)] for s in range(2)]
        tmpB = [[Buf(f"rt{s}_{i}") for i in range(4)] for s in range(2)]
        vst = [sbt(nc, es, f"vst{i}", [128, 8, 65], BF16) for i in range(2)]
        vstB = [Buf(f"vst{i}") for i in range(2)]
        vrg = [sbt(nc, es, f"vrg{i}", [128, 512], BF16) for i in range(2)]
        vrgB = [Buf(f"vrg{i}") for i in range(2)]
        ag = sbt(nc, es, "ag", [16, 512], F32)
        agB = Buf("ag")
        ez = sbt(nc, es, "ez", [128, 512], F32)
        ezB = Buf("ez")
        lst = [sbt(nc, es, f"lst{i}", [128, 512], F32) for i in range(2)]
        lstB = [Buf(f"lst{i}") for i in range(2)]
        pb = [pst(nc, es, f"pb{i}", [128, 512]) for i in range(8)]
        pbB = [Buf(f"pb{i}") for i in range(8)]
        st = {"bank": 0, "so": 0, "ts": 0, "v": 0, "ev": 0}

        def nbank():
            i = st["bank"] % 8
            st["bank"] += 1
            return i

        def nso():
            i = st["so"] % NSO
            st["so"] += 1
            return i

        for i in range(2):
            tk.memset("pool", vst[i][:], 1.0, [vstB[i]])
        for kc in range(8):
            tk.dma("pool", wfm[:, kc, :], win[l, kc * 128:(kc + 1) * 128, 0:2064], [], [wB])
            tk.dma("pool", wtm[:, kc, :], win[l, kc * 128:(kc + 1) * 128, 2064:PW], [], [wB])
        tk.dma("sp", wal[:], walpha[l], [], [wB])
        xv = xsrc.rearrange("(c p) t -> p c t", p=128)
        rv = rope.rearrange("f p t -> p f t")

        def load(T):
            tk.dma("pool", xt[T % 2][:], xv[:, :, T * 512:(T + 1) * 512], [], [xtB[T % 2]])
            tk.dma("sp", rp[T % 2][:], rv[:, :, T * 512:(T + 1) * 512], [], [rpB[T % 2]])

        load(0)
        for T in range(8):
            if T + 1 < 8:
                load(T + 1)
            x_, xB_ = xt[T % 2], xtB[T % 2]
            r_, rB_ = rp[T % 2], rpB[T % 2]
            tsl = slice(T * 512, (T + 1) * 512)

            def fm_mm(tile, bank):
                for kc in range(8):
                    tk.mm(pb[bank][:, :], wfm[:, kc, tile * 128:(tile + 1) * 128], x_[:, kc, :], kc == 0, kc == 7,
                          [wB, xB_], [pbB[bank]], tick=(kc == 7))

            def store_fm(tile, si):
                tk.dma("sp", FMS[tile, :, tsl], so[si][:], [soB[si]], [])

            def tm_block(blk):
                for half in range(2):
                    b = nbank()
                    for kc in range(8):
                        tk.mm(pb[b][:, :], x_[:, kc, blk * 128:(blk + 1) * 128], wtm[:, kc, half * 512:(half + 1) * 512],
                              kc == 0, kc == 7, [wB, xB_], [pbB[b]], tick=(kc == 7))
                    vi = st["v"] % 2
                    eng = "act" if st["ev"] % 2 == 0 else "dve"
                    st["ev"] += 1
                    rows = slice(T * 512 + blk * 128, T * 512 + (blk + 1) * 128)
                    if half == 0:
                        tk.copy(eng, vst[vi][:, :, 0:64], pb[b][:, :].rearrange("p (h d) -> p h d", d=64), [pbB[b]], [vstB[vi]])
                        tk.dma("sp", VA[rows, :, :], vst[vi][:], [vstB[vi]], [])
                    else:
                        tk.copy(eng, vrg[vi][:], pb[b][:, :], [pbB[b]], [vrgB[vi]])
                        tk.dma("sp", VRG[rows, :], vrg[vi][:], [vrgB[vi]], [])
                        st["v"] += 1

            pairs = [(0, 1, 0), (2, 3, 0), (4, 5, 0), (6, 7, 0), (8, 9, 2)]
            for pi, (ta, tb, ro) in enumerate(pairs):
                ba, bb = nbank(), nbank()
                fm_mm(ta, ba)
                fm_mm(tb, bb)
                C_ = r_[:, ro, :]
                S_ = r_[:, ro + 1, :]
                s = st["ts"] % 2
                st["ts"] += 1
                t1, t2, t3, t4 = tmpf[s]
                b1, b2, b3, b4 = tmpB[s]
                tk.tt("dve", t1[:], pb[ba][:, :], C_, ALU.mult, [pbB[ba], rB_], [b1])
                tk.tt("dve", t2[:], pb[bb][:, :], S_, ALU.mult, [pbB[bb], rB_], [b2])
                tk.tt("dve", t3[:], pb[ba][:, :], S_, ALU.mult, [pbB[ba], rB_], [b3])
                tk.tt("dve", t4[:], pb[bb][:, :], C_, ALU.mult, [pbB[bb], rB_], [b4])
                sa = nso()
                tk.tt("pool", so[sa][:], t1[:], t2[:], ALU.subtract, [b1, b2], [soB[sa]])
                store_fm(ta, sa)
                sb_ = nso()
                tk.tt("pool", so[sb_][:], t3[:], t4[:], ALU.add, [b3, b4], [soB[sb_]])
                store_fm(tb, sb_)
                if pi < 4:
                    tm_block(pi)
            for tile, kind in ((10, "silu"), (11, "silu"), (12, "copy"), (13, "copy"), (14, "silu"), (15, "silu")):
                b = nbank()
                fm_mm(tile, b)
                si = nso()
                tk.act(so[si][:], pb[b][:, :], AF.Silu if kind == "silu" else AF.Copy, [pbB[b]], [soB[si]])
                store_fm(tile, si)
            b = nbank()
            for kc in range(8):
                tk.mm(pb[b][0:16, :], wfm[:, kc, 2048:2064], x_[:, kc, :], kc == 0, kc == 7, [wB, xB_], [pbB[b]], tick=(kc == 7))
            tk.act(ag[:], pb[b][0:16, :], AF.Copy, [pbB[b]], [agB])
            b2_ = nbank()
            tk.mm(pb[b2_][:, :], wal[:], ag[:], True, True, [wB, agB], [pbB[b2_]], tick=True)
            tk.act(ez[:], pb[b2_][:, :], AF.Exp, [pbB[b2_], g.plB], [ezB], scale=-1.0, bias=g.negb[:, 0:1])
            li = T % 2
            tk.act(lst[li][:], ez[:], AF.Ln, [ezB, g.cstB], [lstB[li]], bias=g.one_col)
            tk.dma("sp", LA[:, tsl], lst[li][:], [lstB[li]], [])


def phase_A(g, l, FMS, VA, YT):
    tk, nc = g.tk, g.nc
    with ExitStack() as es:
        qh = [sbt(nc, es, f"qh{i}", [64, S], BF16) for i in range(2)]
        kh = [sbt(nc, es, f"kh{i}", [64, S], BF16) for i in range(2)]
        v3 = [sbt(nc, es, f"v3{i}", [128, 3, 32, 65], BF16) for i in range(2)]
        inB = [Buf(f"ain{i}") for i in range(2)]
        acc = [sbt(nc, es, f"acc{i}", [65, S], F32) for i in range(2)]
        accB = [Buf(f"acc{i}") for i in range(2)]
        PT = [sbt(nc, es, f"PT{i}", [128, 1024], BF16) for i in range(3)]
        PTB = [Buf(f"PT{i}") for i in range(3)]
        yst = [sbt(nc, es, f"yst{i}", [64, S], BF16) for i in range(2)]
        ystB = [Buf(f"yst{i}") for i in range(2)]
        tmp = {"B": {k: Buf("a_" + k) for k in ("sq", "msq", "var", "dd")}}
        for k in ("sq", "msq", "var", "dd"):
            tmp[k] = sbt(nc, es, "a_" + k, [65, 512], F32)
        ST = [pst(nc, es, f"ST{i}", [128, 1024]) for i in range(2)]
        STB = [Buf(f"ST{i}") for i in range(2)]
        Op = [pst(nc, es, f"Op{i}", [128, 512]) for i in range(2)]
        OpB = [Buf(f"Op{i}") for i in range(2)]
        stat = [pst(nc, es, f"astat{i}", [128, 512]) for i in range(2)]
        statB = [Buf(f"astat{i}") for i in range(2)]
        cs = g.cst
        A1 = cs[0:65, C_A1:C_A1 + 64]
        A2 = cs[0:65, C_A2:C_A2 + 64]

        def load_head(h):
            i = h % 2
            gI, hh = h // 4, h % 4
            rows = slice(hh * 32, hh * 32 + 32)
            tk.dma("sp", qh[i][0:32, :], FMS[2 * gI, rows, :], [], [inB[i]])
            tk.dma("sp", qh[i][32:64, :], FMS[2 * gI + 1, rows, :], [], [inB[i]])
            tk.dma("sp", kh[i][0:32, :], FMS[4 + 2 * gI, rows, :], [], [inB[i]])
            tk.dma("sp", kh[i][32:64, :], FMS[4 + 2 * gI + 1, rows, :], [], [inB[i]])
            for di, d in enumerate((1, 4, 16)):
                src = VA[:, h, :].rearrange("(n j r) c -> j r n c", j=128, r=d)
                dst = v3[i][:, di, :, :].rearrange("p (r n) c -> p r n c", r=d)
                tk.dma("sp", dst, src, [], [inB[i]])

        batches = []
        for h in range(8):
            for di, d in enumerate((1, 4, 16)):
                nb = 32 // d
                blocks = [(r, n, r * nb + n) for r in range(d) for n in range(nb)]
                for b0 in range(0, 32, 4):
                    batches.append((h, di, d, nb, blocks[b0:b0 + 4]))
        NB = len(batches)

        def emit_ST(gi):
            h, di, d, nb, blks = batches[gi]
            i = h % 2
            sbuf = gi % 2
            qv = qh[i][:, :].rearrange("p (m r) -> p r m", r=d)
            kv = kh[i][:, :].rearrange("p (m r) -> p r m", r=d)
            for j, (r, n, b) in enumerate(blks):
                qn = 256 if n < nb - 1 else 128
                o_ = ST[sbuf][:, j * 256:j * 256 + qn]
                tk.mm(o_, kv[:, r, n * 128:(n + 1) * 128], qv[:, r, n * 128:n * 128 + qn], True, False,
                      [inB[i]], [STB[sbuf]])
                tk.mm(o_, g.ident_b[:], g.maskb[:, 0:qn], False, True, [g.cstB], [STB[sbuf]], tick=(j == 3))

        def emit_exp(gi):
            tk.act(PT[gi % 3][:], ST[gi % 2][:, :], AF.Exp, [STB[gi % 2]], [PTB[gi % 3]], scale=SC_A)

        def emit_PV(gi):
            h, di, d, nb, blks = batches[gi]
            i = h % 2
            ob = gi % 2
            cur = PT[gi % 3]
            prv = PT[(gi - 1) % 3]
            for j, (r, n, b) in enumerate(blks):
                o_ = Op[ob][0:65, j * 128:(j + 1) * 128]
                rd = [inB[i], PTB[gi % 3]]
                if n > 0:
                    if j > 0:
                        pprev = cur[:, (j - 1) * 256 + 128:(j - 1) * 256 + 256]
                    else:
                        pprev = prv[:, 3 * 256 + 128:4 * 256]
                        rd = rd + [PTB[(gi - 1) % 3]]
                    tk.mm(o_, v3[i][:, di, b - 1, :], pprev, True, False, rd, [OpB[ob]])
                    tk.mm(o_, v3[i][:, di, b, :], cur[:, j * 256:j * 256 + 128], False, True, rd, [OpB[ob]], tick=(j == 3))
                else:
                    tk.mm(o_, v3[i][:, di, b, :], cur[:, j * 256:j * 256 + 128], True, True, rd, [OpB[ob]], tick=(j == 3))

        def emit_evac(gi):
            h, di, d, nb, blks = batches[gi]
            a, aB = acc[h % 2], accB[h % 2]
            ob = gi % 2
            o_ = Op[ob][0:65, :]
            r0, n0, b0 = blks[0]
            if d == 1:
                tk.copy("dve", a[:, n0 * 128:n0 * 128 + 512], o_, [OpB[ob]], [aB])
            elif d == 4:
                av = a[:, :].rearrange("p (m q) -> p q m", q=4)[:, r0, n0 * 128:n0 * 128 + 512]
                tk.tt("dve", av, av, o_, ALU.add, [OpB[ob], aB], [aB])
            else:
                av = a[:, :].rearrange("p (m q) -> p q m", q=16)[:, r0:r0 + 2, :]
                tk.tt("dve", av, av, o_.rearrange("p (a m) -> p a m", a=2), ALU.add, [OpB[ob], aB], [aB])

        def emit_post(h, t):
            a, aB = acc[h % 2], accB[h % 2]
            B = tmp["B"]
            src = a[0:65, t * 512:(t + 1) * 512]
            mean_ps, mB = head_norm(g, src, aB, 65, A1, A2, 64, (stat[0], stat[1]), (statB[0], statB[1]), tmp)
            var, dd = tmp["var"], tmp["dd"]
            tk.act(var[0:64, :], var[0:64, :], AF.Ln, [B["var"]], [B["var"]])
            tk.act(var[0:64, :], var[0:64, :], AF.Exp, [B["var"]], [B["var"]], scale=-0.5)
            tk.tt("dve", dd[0:64, :], a[0:64, t * 512:(t + 1) * 512], mean_ps[0:64, :], ALU.subtract, [aB, mB], [B["dd"]])
            y, yB = yst[h % 2], ystB[h % 2]
            tk.stt(y[:, t * 512:(t + 1) * 512], dd[0:64, :], g.pl_sb[0:64, P_MSA + h:P_MSA + h + 1], var[0:64, :],
                   ALU.mult, ALU.mult, [B["dd"], B["var"], g.plB], [yB])
            if t == 7:
                tk.dma("sp", YT[h * 64:(h + 1) * 64, :], y[:, :], [yB], [])

        load_head(0)
        posts = []
        emit_ST(0)
        for gi in range(NB):
            h = batches[gi][0]
            first_of_head = (gi % 24 == 0)
            if first_of_head and h + 1 < 8:
                load_head(h + 1)
            if gi + 1 < NB:
                emit_ST(gi + 1)
            emit_exp(gi)
            emit_PV(gi)
            emit_evac(gi)
            if posts:
                emit_post(*posts.pop(0))
            if gi % 24 == 23:
                posts += [(h, t) for t in range(8)]
        while posts:
            emit_post(*posts.pop(0))


def phase_L(g, l, FMS, LA, VRG, YT):
    tk, nc = g.tk, g.nc
    cs = g.cst
    with ExitStack() as es:
        qf = [sbt(nc, es, f"lq{i}", [128, 512], BF16) for i in range(2)]
        kf = [sbt(nc, es, f"lk{i}", [128, 512], BF16) for i in range(2)]
        vt = [sbt(nc, es, f"lv{i}", [128, 4, 256], BF16) for i in range(2)]
        gt = [sbt(nc, es, f"lg{i}", [128, 2, 512], BF16) for i in range(2)]
        lt = [sbt(nc, es, f"ll{i}", [128, 512], F32) for i in range(2)]
        inB = [Buf(f"lin{i}") for i in range(2)]
        names = ("cum", "E", "Einv", "Kd", "dec", "Qbd", "kt", "ktil", "ktok", "og", "sq", "msq", "var", "dd", "yy")
        B = {k: Buf("l_" + k) for k in names}
        cum = sbt(nc, es, "l_cum", [128, 512], F32)
        Eg = sbt(nc, es, "l_E", [128, 512], F32)
        Einvg = sbt(nc, es, "l_Einv", [128, 512], F32)
        Kdg = sbt(nc, es, "l_Kd", [128, 512], F32)
        decg = sbt(nc, es, "l_dec", [128, 4], F32)
        Qbd = sbt(nc, es, "l_Qbd", [128, 4, 512], BF16)
        kt = sbt(nc, es, "l_kt", [128, 512], BF16)
        ktil = sbt(nc, es, "l_ktil", [128, 512], BF16)
        ktok = sbt(nc, es, "l_ktok", [128, 4, 128], BF16)
        PT = [sbt(nc, es, f"l_PT{i}", [128, 4, 128], BF16) for i in range(2)]
        PTB = [Buf(f"l_PT{i}") for i in range(2)]
        stf = sbt(nc, es, "l_stf", [128, 256], F32)
        stfB = Buf("l_stf")
        stb = [sbt(nc, es, f"l_stb{i}", [128, 256], BF16) for i in range(2)]
        stbB = [Buf(f"l_stb{i}") for i in range(2)]
        og = sbt(nc, es, "l_og", [128, 512], F32)
        tmp = {"B": B}
        for k in ("sq", "msq", "var", "dd"):
            tmp[k] = sbt(nc, es, "l_" + k, [128, 512], F32)
        yy = [sbt(nc, es, f"l_yy{i}", [128, 512], BF16) for i in range(2)]
        yyB = [Buf(f"l_yy{i}") for i in range(2)]
        STp = [pst(nc, es, f"l_ST{i}", [128, 512]) for i in range(2)]
        STpB = [Buf(f"l_ST{i}") for i in range(2)]
        kvp = pst(nc, es, "l_kv", [128, 512])
        kvB = Buf("l_kvp")
        Opp = [pst(nc, es, f"l_O{i}", [128, 512]) for i in range(2)]
        OppB = [Buf(f"l_O{i}") for i in range(2)]
        trp = pst(nc, es, "l_tr", [128, 512], BF16)
        trB = Buf("l_tr")
        stat = [pst(nc, es, f"l_stat{i}", [128, 512]) for i in range(2)]
        statB = [Buf(f"l_stat{i}") for i in range(2)]
        B1 = cs[:, C_B1:C_B1 + 128]
        lmask4 = cs[:, C_LMASK4:C_LMASK4 + 512]
        ones = cs[:, C_ONES:C_ONES + 128]
        cnt = {"t": 0, "st": 0, "pt": 0, "yy": 0}

        for grp in range(2):
            hm = cs[:, C_HMR:C_HMR + 4] if grp == 0 else cs[:, C_HMG:C_HMG + 4]
            ch0 = 512 + grp * 256

            def load(T):
                i = cnt["t"] % 2
                tsl = slice(T * 512, (T + 1) * 512)
                if grp == 0:
                    tk.dma("sp", qf[i][0:64, :], FMS[8, 0:64, tsl], [], [inB[i]])
                    tk.dma("sp", qf[i][64:128, :], FMS[9, 0:64, tsl], [], [inB[i]])
                    tk.dma("sp", kf[i][0:64, :], FMS[8, 64:128, tsl], [], [inB[i]])
                    tk.dma("sp", kf[i][64:128, :], FMS[9, 64:128, tsl], [], [inB[i]])
                    g0 = 10
                else:
                    tk.dma("sp", qf[i][:], FMS[12, :, tsl], [], [inB[i]])
                    tk.dma("sp", kf[i][:], FMS[13, :, tsl], [], [inB[i]])
                    tk.dma("sp", lt[i][:], LA[:, tsl], [], [inB[i]])
                    g0 = 14
                tk.dma("sp", gt[i][:, 0, :], FMS[g0, :, tsl], [], [inB[i]])
                tk.dma("sp", gt[i][:, 1, :], FMS[g0 + 1, :, tsl], [], [inB[i]])
                tk.dma("sp", vt[i][:], VRG[tsl, grp * 256:(grp + 1) * 256].rearrange("(c s) v -> s c v", s=128), [], [inB[i]])
                return i

            tk.memset("dve", stf[:], 0.0, [stfB])
            sb0 = cnt["st"] % 2
            tk.memset("pool", stb[sb0][:], 0.0, [stbB[sb0]])
            nxt = load(0)
            for T in range(8):
                i = nxt
                cnt["t"] += 1
                if T + 1 < 8:
                    nxt = load(T + 1)
                q_, k_, v_, g_, l_, iB = qf[i], kf[i], vt[i], gt[i], lt[i], inB[i]
                if grp == 0:
                    E_ = cs[:, C_ER:C_ER + 512]
                    Einv_ = cs[:, C_EINVR:C_EINVR + 512]
                    Kd_ = cs[:, C_KDR:C_KDR + 512]
                    dec_ = cs[:, C_DECR:C_DECR + 4]
                    EB = EinvB = KdB = decB = g.cstB
                else:
                    for c in range(4):
                        csl = slice(c * 128, (c + 1) * 128)
                        tk.op("dve", lambda csl=csl: nc.vector.tensor_tensor_scan(
                            out=cum[:, csl], data0=ones, data1=l_[:, csl], initial=0.0, op0=ALU.mult, op1=ALU.add),
                            [iB, g.cstB], [B["cum"]])
                    tk.act(Eg[:], cum[:], AF.Exp, [B["cum"]], [B["E"]], scale=-1.0 / 16)
                    tk.act(Einvg[:], cum[:], AF.Exp, [B["cum"]], [B["Einv"]], scale=1.0 / 16)
                    tk.act(decg[:], cum[:].rearrange("p (c s) -> p c s", s=128)[:, :, 127], AF.Exp, [B["cum"]], [B["dec"]],
                           scale=-1.0 / 16)
                    for c in range(4):
                        csl = slice(c * 128, (c + 1) * 128)
                        tk.ts("pool", Kdg[:, csl], Einvg[:, csl], decg[:, c:c + 1], None, ALU.mult, ALU.bypass,
                              [B["Einv"], B["dec"]], [B["Kd"]])
                    E_, Einv_, Kd_, dec_ = Eg[:], Einvg[:], Kdg[:], decg[:]
                    EB, EinvB, KdB, decB = B["E"], B["Einv"], B["Kd"], B["dec"]
                for h in range(4):
                    tk.stt(Qbd[:, h, :], q_[:], hm[:, h:h + 1], E_, ALU.mult, ALU.mult, [iB, g.cstB, EB], [B["Qbd"]])
                tk.tt("pool", kt[:], k_[:], Einv_, ALU.mult, [iB, EinvB], [B["kt"]])
                tk.tt("pool", ktil[:], k_[:], Kd_, ALU.mult, [iB, KdB], [B["ktil"]])
                for c in range(4):
                    tk.op("pe", lambda c=c: nc.tensor.transpose(out=trp[:, c * 128:(c + 1) * 128],
                                                                in_=ktil[:, c * 128:(c + 1) * 128], identity=g.ident_b[:]),
                          [B["ktil"], g.cstB], [trB], tick=(c == 3))
                tk.copy("act", ktok[:].rearrange("p c s -> p (c s)"), trp[:, :], [trB], [B["ktok"]])
                for c in range(4):
                    csl = slice(c * 128, (c + 1) * 128)
                    sp_ = cnt["pt"] % 2
                    cnt["pt"] += 1
                    tk.mm(STp[sp_][:, :], kt[:, csl], Qbd[:, :, csl], True, True, [B["kt"], B["Qbd"]], [STpB[sp_]], tick=True)
                    tk.tt("dve", PT[sp_][:], STp[sp_][:, :].rearrange("p (h c) -> p h c", h=4),
                          lmask4.rearrange("p (h c) -> p h c", h=4),
                          ALU.mult, [STpB[sp_], g.cstB], [PTB[sp_]])
                    kvs = slice((c % 2) * 256, (c % 2) * 256 + 256)
                    tk.mm(kvp[:, kvs], ktok[:, c, :], v_[:, c, :], True, True, [B["ktok"], iB], [kvB], tick=True)
                    sbi = cnt["st"] % 2
                    for h in range(4):
                        j, half = h // 2, h % 2
                        o_ = Opp[j][half * 64:(half + 1) * 64, csl]
                        tk.mm(o_, v_[:, c, h * 64:(h + 1) * 64], PT[sp_][:, h, :], True, False, [iB, PTB[sp_]], [OppB[j]])
                        tk.mm(o_, stb[sbi][:, h * 64:(h + 1) * 64], Qbd[:, h, csl], False, True,
                              [stbB[sbi], B["Qbd"]], [OppB[j]], tick=(h % 2 == 1))
                    tk.stt(stf[:], stf[:], dec_[:, c:c + 1], kvp[:, kvs], ALU.mult, ALU.add, [stfB, decB, kvB], [stfB])
                    cnt["st"] += 1
                    sbn = cnt["st"] % 2
                    tk.copy("pool", stb[sbn][:], stf[:], [stfB], [stbB[sbn]])
                for j in range(2):
                    tk.copy("act", og[:], Opp[j][:, :], [OppB[j]], [B["og"]])
                    mean_ps, mB = head_norm(g, og[:], B["og"], 128, B1, B1, 128, (stat[0], stat[1]),
                                            (statB[0], statB[1]), tmp)
                    var, dd = tmp["var"], tmp["dd"]
                    tk.act(var[:], var[:], AF.Ln, [B["var"]], [B["var"]], bias=g.eps_hn[:, 0:1])
                    tk.act(var[:], var[:], AF.Exp, [B["var"]], [B["var"]], scale=-0.5)
                    tk.tt("dve", dd[:], og[:], mean_ps[:, :], ALU.subtract, [B["og"], mB], [B["dd"]])
                    tk.stt(dd[:], dd[:], g.pl_sb[:, P_MSRG + grp * 2 + j:P_MSRG + grp * 2 + j + 1], var[:],
                           ALU.mult, ALU.mult, [B["dd"], B["var"], g.plB], [B["dd"]])
                    yi = cnt["yy"] % 2
                    cnt["yy"] += 1
                    tk.tt("pool", yy[yi][:], dd[:], g_[:, j, :], ALU.mult, [B["dd"], iB], [yyB[yi]])
                    tk.dma("sp", YT[ch0 + j * 128:ch0 + (j + 1) * 128, T * 512:(T + 1) * 512], yy[yi][:], [yyB[yi]], [])


def ln_alloc(nc, es, pfx, N):
    tmp = {"B": {k: Buf(pfx + k) for k in ("zb", "zq", "msq", "var", "rstd", "nmr")}}
    tmp["zb"] = sbt(nc, es, pfx + "zb", [128, 8, N], BF16)
    tmp["zq"] = sbt(nc, es, pfx + "zq", [128, 8, N], BF16)
    for k in ("msq", "var", "rstd", "nmr"):
        tmp[k] = sbt(nc, es, pfx + k, [128, N], F32)
    return tmp


def phase_O(g, l, xsrc, wout, YT, X1F):
    tk, nc = g.tk, g.nc
    with ExitStack() as es:
        wo = sbt(nc, es, "wo", [128, 8, D], BF16)
        wB = Buf("wo")
        yt = [sbt(nc, es, f"o_yt{i}", [128, 8, 512], BF16) for i in range(2)]
        xr = [sbt(nc, es, f"o_xr{i}", [128, 8, 512], F32) for i in range(2)]
        inB = [Buf(f"o_in{i}") for i in range(2)]
        z = [sbt(nc, es, f"o_z{i}", [128, 8, 512], F32) for i in range(2)]
        zB = [Buf(f"o_z{i}") for i in range(2)]
        tmp = ln_alloc(nc, es, "o_", 512)
        pb = [pst(nc, es, f"o_pb{i}", [128, 512]) for i in range(6)]
        pbB = [Buf(f"o_pb{i}") for i in range(6)]
        stat = [pst(nc, es, f"o_stat{i}", [128, 512]) for i in range(2)]
        statB = [Buf(f"o_stat{i}") for i in range(2)]
        for kc in range(8):
            tk.dma("pool", wo[:, kc, :], wout[l, kc * 128:(kc + 1) * 128, :], [], [wB])
        yv = YT.rearrange("(c p) t -> p c t", p=128)
        xv = xsrc.rearrange("(c p) t -> p c t", p=128)

        def load(T):
            tk.dma("sp", yt[T % 2][:], yv[:, :, T * 512:(T + 1) * 512], [], [inB[T % 2]])
            tk.dma("sp", xr[T % 2][:], xv[:, :, T * 512:(T + 1) * 512], [], [inB[T % 2]])

        load(0)
        nb = 0
        for T in range(8):
            if T + 1 < 8:
                load(T + 1)
            i = T % 2
            for oc in range(8):
                b = nb % 6
                nb += 1
                for kc in range(8):
                    tk.mm(pb[b][:, :], wo[:, kc, oc * 128:(oc + 1) * 128], yt[i][:, kc, :], kc == 0, kc == 7,
                          [wB, inB[i]], [pbB[b]], tick=(kc == 7))
                tk.stt(z[i][:, oc, :], xr[i][:, oc, :], ALPHA, pb[b][:, :], ALU.mult, ALU.add, [inB[i], pbB[b]], [zB[i]])
            layer_norm(g, "o", z[i], zB[i], 512, g.pl_sb[:, P_LN1G:P_LN1G + 8], g.pl_sb[:, P_LN1B:P_LN1B + 8],
                       X1F[:, T * 512:(T + 1) * 512], (stat[0], stat[1]), (statB[0], statB[1]), tmp)


def phase_F(g, l, wup, wdown, X1F, xdst):
    tk, nc = g.tk, g.nc
    NT = 256
    with ExitStack() as es:
        wu = sbt(nc, es, "wu", [128, 8, 2 * DFF], BF16)
        wd = sbt(nc, es, "wd", [128, 22, D], BF16)
        wB = Buf("wf")
        xb = [sbt(nc, es, f"f_xb{i}", [128, 8, NT + 2], BF16) for i in range(2)]
        xr = [sbt(nc, es, f"f_xr{i}", [128, 8, NT], F32) for i in range(2)]
        inB = [Buf(f"f_in{i}") for i in range(2)]
        hT = sbt(nc, es, "f_hT", [128, 22, NT], BF16)
        hB = Buf("f_hT")
        z = sbt(nc, es, "f_z", [128, 8, NT], F32)
        zB = Buf("f_z")
        og = [sbt(nc, es, f"f_og{i}", [128, NT], F32) for i in range(2)]
        ov = [sbt(nc, es, f"f_ov{i}", [128, NT], F32) for i in range(2)]
        sg = [sbt(nc, es, f"f_sg{i}", [128, NT], F32) for i in range(2)]
        ogB = [Buf(f"f_og{i}") for i in range(2)]
        ovB = [Buf(f"f_ov{i}") for i in range(2)]
        sgB = [Buf(f"f_sg{i}") for i in range(2)]
        tmp = ln_alloc(nc, es, "f_", NT)
        pb = [pst(nc, es, f"f_pb{i}", [128, 512]) for i in range(6)]
        pbB = [Buf(f"f_pb{i}") for i in range(6)]
        stat = [pst(nc, es, f"f_stat{i}", [128, 512]) for i in range(2)]
        statB = [Buf(f"f_stat{i}") for i in range(2)]
        for kc in range(8):
            for hf in range(2):
                tk.dma("pool", wu[:, kc, hf * DFF:(hf + 1) * DFF], wup[l, kc * 128:(kc + 1) * 128, hf * DFF:(hf + 1) * DFF],
                       [], [wB])
        for c in range(22):
            tk.dma("pool", wd[:, c, :], wdown[l, c * 128:(c + 1) * 128, :], [], [wB])
        xv = X1F.rearrange("(c p) t -> p c t", p=128)
        cw = g.pl_sb[:, P_CW:P_CW + 132]
        cb = g.pl_sb[:, P_CB:P_CB + 44]

        def load(T):
            i = T % 2
            t0 = T * NT
            if T == 0:
                tk.memset("dve", xb[i][:, :, 0:2], 0.0, [inB[i]])
                tk.dma("pool", xb[i][:, :, 2:NT + 2], xv[:, :, 0:NT], [], [inB[i]])
            else:
                tk.dma("pool", xb[i][:, :, :], xv[:, :, t0 - 2:t0 + NT], [], [inB[i]])
            tk.dma("sp", xr[i][:], xv[:, :, t0:t0 + NT], [], [inB[i]])

        load(0)
        nb = 0
        ce = 0
        for T in range(S // NT):
            if T + 1 < S // NT:
                load(T + 1)
            i = T % 2
            for c in range(22):
                bg = nb % 6
                bv = (nb + 1) % 6
                nb += 2
                for (bank, cc) in ((bg, c), (bv, 22 + c)):
                    for kc in range(8):
                        tk.mm(pb[bank][:, 0:NT + 2], wu[:, kc, cc * 128:(cc + 1) * 128], xb[i][:, kc, :], kc == 0, kc == 7,
                              [wB, inB[i]], [pbB[bank]], tick=(kc == 7))
                s = ce % 2
                ce += 1
                for (bank, cc, dst, dB) in ((bg, c, og[s], ogB[s]), (bv, 22 + c, ov[s], ovB[s])):
                    w0 = cw[:, cc * 3 + 0:cc * 3 + 1]
                    w1 = cw[:, cc * 3 + 1:cc * 3 + 2]
                    w2 = cw[:, cc * 3 + 2:cc * 3 + 3]
                    tk.act(dst[:], pb[bank][:, 2:NT + 2], AF.Identity, [pbB[bank], g.plB], [dB], scale=w2, bias=cb[:, cc:cc + 1])
                    tk.stt(dst[:], pb[bank][:, 1:NT + 1], w1, dst[:], ALU.mult, ALU.add, [pbB[bank], g.plB, dB], [dB])
                    tk.stt(dst[:], pb[bank][:, 0:NT], w0, dst[:], ALU.mult, ALU.add, [pbB[bank], g.plB, dB], [dB])
                tk.act(sg[s][:], og[s][:], AF.Silu, [ogB[s]], [sgB[s]])
                tk.tt("pool", hT[:, c, :], sg[s][:], ov[s][:], ALU.mult, [sgB[s], ovB[s]], [hB])
            for oc in range(8):
                b = nb % 6
                nb += 1
                for c in range(22):
                    tk.mm(pb[b][:, 0:NT], wd[:, c, oc * 128:(oc + 1) * 128], hT[:, c, :], c == 0, c == 21,
                          [wB, hB], [pbB[b]], tick=(c == 21))
                tk.stt(z[:, oc, :], xr[i][:, oc, :], ALPHA, pb[b][:, 0:NT], ALU.mult, ALU.add, [inB[i], pbB[b]], [zB])
            layer_norm(g, "f", z, zB, NT, g.pl_sb[:, P_LN2G:P_LN2G + 8], g.pl_sb[:, P_LN2B:P_LN2B + 8],
                       xdst[:, T * NT:(T + 1) * NT], (stat[0], stat[1]), (statB[0], statB[1]), tmp)


_CACHE = {}


def _prep_weights(w_in, w_alpha, b_alpha, mix_scale, w_out, ln1_g, ln1_b, w_up, conv_w, conv_b, w_down, ln2_g, ln2_b):
    f = lambda a: np.ascontiguousarray(np.asarray(a, dtype=np.float32))
    win_p = f(np.asarray(w_in)[:, :, win_perm()])
    plb = np.zeros((DEPTH, 128, NPL), np.float32)
    ms = np.asarray(mix_scale, np.float32)
    for l in range(DEPTH):
        plb[l, 0:64, P_MSA:P_MSA + 8] = ms[l, 0:512].reshape(8, 64).T
        plb[l, :, P_MSRG:P_MSRG + 4] = ms[l, 512:1024].reshape(4, 128).T
        plb[l, :, P_LN1G:P_LN1G + 8] = np.asarray(ln1_g)[l].reshape(8, 128).T
        plb[l, :, P_LN1B:P_LN1B + 8] = np.asarray(ln1_b)[l].reshape(8, 128).T
        plb[l, :, P_LN2G:P_LN2G + 8] = np.asarray(ln2_g)[l].reshape(8, 128).T
        plb[l, :, P_LN2B:P_LN2B + 8] = np.asarray(ln2_b)[l].reshape(8, 128).T
        cwl = np.asarray(conv_w)[l].reshape(3, 44, 128)
        plb[l, :, P_CW:P_CW + 132] = cwl.transpose(2, 1, 0).reshape(128, 132)
        plb[l, :, P_CB:P_CB + 44] = np.asarray(conv_b)[l].reshape(44, 128).T
        plb[l, :, P_BA] = np.asarray(b_alpha)[l]
    return dict(win=win_p, walpha=f(w_alpha), wout=f(w_out), wup=f(w_up), wdown=f(w_down), pl=plb)


def kernel(x, w_in, w_alpha, b_alpha, mix_scale, w_out, ln1_g, ln1_b, w_up, conv_w, conv_b, w_down, ln2_g, ln2_b):
    x = np.asarray(x, dtype=np.float32)
    if "nc" not in _CACHE:
        _CACHE["nc"] = build(DEPTH)[0]
        _CACHE["cst"] = make_consts()
    nc = _CACHE["nc"]
    cst, rope = _CACHE["cst"]
    wd = _prep_weights(w_in, w_alpha, b_alpha, mix_scale, w_out, ln1_g, ln1_b, w_up, conv_w, conv_b, w_down, ln2_g, ln2_b)
    in_maps = []
    for b in range(8):
        m = dict(wd)
        m["xin"] = np.ascontiguousarray(x[b].T)
        m["cst"] = cst
        m["rope"] = rope
        in_maps.append(m)
    res = run_bass_kernel_spmd(nc, in_maps, core_ids=list(range(8)))
    outp = np.stack([np.asarray(r["out"], dtype=np.float32).T for r in res.results], axis=0)
    return np.ascontiguousarray(outp)
```

```python
import numpy as np
from contextlib import ExitStack
import concourse.bass as bass
import concourse.mybir as mybir
from concourse.bass_utils import run_bass_kernel_spmd

F32 = mybir.dt.float32
BF16 = mybir.dt.bfloat16
AF = mybir.ActivationFunctionType
ALU = mybir.AluOpType

S = 4096
D = 1024
DEPTH = 4
DFF = 2816
PW = 3088
ALPHA = float((2 * DEPTH) ** 0.25)
LN_EPS = 1e-5
HN_EPS = 1e-6
NEG = -30000.0
NFM = 16
SC_A = 64 ** -0.5
SC_L = 32 ** -0.5

C_MASKB, C_IDENT, C_LMASK, C_A1, C_A2, C_B1, C_ONES1024, C_HMR, C_HMG, C_DECR, C_ER, C_EINVR, C_KDR, C_ONES = (
    0, 256, 384, 512, 576, 640, 768, 896, 900, 904, 908, 1420, 1932, 2444)
C_LMASK4 = 2572
NCST = 3084

P_MSA, P_MSRG, P_LN1G, P_LN1B, P_LN2G, P_LN2B, P_CW, P_CB, P_BA = 0, 8, 12, 20, 28, 36, 44, 176, 220
NPL = 221


def make_consts():
    c = np.zeros((128, NCST), np.float32)
    j = np.arange(128)[:, None]
    i = np.arange(128)[None, :]
    c[:, C_MASKB:C_MASKB + 128] = np.where(j <= i, 0.0, NEG)
    c[:, C_MASKB + 128:C_MASKB + 256] = np.where(j >= i, 0.0, NEG)
    c[:, C_IDENT:C_IDENT + 128] = np.eye(128, dtype=np.float32)
    c[:, C_LMASK:C_LMASK + 128] = (j <= i).astype(np.float32)
    for h in range(4):
        c[:, C_LMASK4 + h * 128:C_LMASK4 + (h + 1) * 128] = (j <= i).astype(np.float32)
    c[0:64, C_A1:C_A1 + 64] = 1.0 / 64
    c[0:64, C_A2:C_A2 + 64] = 1.0 / 64
    c[64, C_A2:C_A2 + 64] = HN_EPS
    for h in range(2):
        c[h * 64:(h + 1) * 64, C_B1 + h * 64:C_B1 + (h + 1) * 64] = 1.0 / 64
    c[:, C_ONES1024:C_ONES1024 + 128] = 1.0 / 1024
    p = np.arange(128)
    headR = (p % 64) // 16
    headG = p // 32
    for h in range(4):
        c[:, C_HMR + h] = (headR == h) * SC_L
        c[:, C_HMG + h] = (headG == h) * SC_L
    lg = np.log(1.0 - np.power(2.0, -5.0 - np.arange(4, dtype=np.float64)))
    lgp = lg[headR][:, None]
    idx = (np.arange(512) % 128)[None, :].astype(np.float64)
    c[:, C_DECR:C_DECR + 4] = np.exp(lgp * 128.0)
    c[:, C_ER:C_ER + 512] = np.exp(lgp * (idx + 1.0))
    c[:, C_EINVR:C_EINVR + 512] = np.exp(-lgp * (idx + 1.0))
    c[:, C_KDR:C_KDR + 512] = np.exp(lgp * (127.0 - idx))
    c[:, C_ONES:C_ONES + 128] = 1.0
    rope = np.zeros((4, 128, S), np.float32)
    pos = np.arange(S, dtype=np.float32)[None, :]
    invA = (1.0 / (10000.0 ** (np.arange(0, 64, 2, dtype=np.float32) / 64))).astype(np.float32)
    invR = (1.0 / (10000.0 ** (np.arange(0, 32, 2, dtype=np.float32) / 32))).astype(np.float32)
    angA = (pos * invA[p % 32][:, None]).astype(np.float32)
    angR = (pos * invR[p % 16][:, None]).astype(np.float32)
    rope[0], rope[1] = np.cos(angA), np.sin(angA)
    rope[2], rope[3] = np.cos(angR), np.sin(angR)
    return c, rope


def win_perm():
    qA, kA, vA, qR, kR, vR, gR, qG, kG, vG, rG, aG = 0, 512, 1024, 1536, 1664, 1792, 2048, 2304, 2432, 2560, 2816, 3072
    cols = []
    for base in (qA, kA):
        for g in range(2):
            cols += [base + h * 64 + i for h in range(4 * g, 4 * g + 4) for i in range(32)]
            cols += [base + h * 64 + 32 + i for h in range(4 * g, 4 * g + 4) for i in range(32)]
    for half in range(2):
        cols += [qR + h * 32 + half * 16 + i for h in range(4) for i in range(16)]
        cols += [kR + h * 32 + half * 16 + i for h in range(4) for i in range(16)]
    cols += list(range(gR, gR + 256))
    cols += list(range(qG, qG + 128))
    cols += list(range(kG, kG + 128))
    cols += list(range(rG, rG + 256))
    cols += list(range(aG, aG + 16))
    cols += list(range(vA, vA + 512))
    cols += list(range(vR, vR + 256))
    cols += list(range(vG, vG + 256))
    assert len(cols) == PW and len(set(cols)) == PW
    return np.array(cols)


class Buf:
    __slots__ = ("w", "r", "pw", "pr", "name")

    def __init__(self, name=""):
        self.w = {}
        self.r = {}
        self.pw = None
        self.pr = set()
        self.name = name


class TK:
    CE = ("pe", "act", "dve", "pool")

    def __init__(self, nc, es):
        self.nc = nc
        self.E = {"pe": nc.tensor, "act": nc.scalar, "dve": nc.vector, "pool": nc.gpsimd, "sp": nc.sync}
        self.sem = {}
        self.val = {}
        self.seen = {e: {} for e in self.E}
        for e in self.CE:
            self._mk(es, "c_" + e)
        self.dq = {}
        for q, n in (("sp", 16), ("pool", 10)):
            self.dq[q] = [self._mk(es, f"d_{q}{i}") for i in range(n)]
        self.dqi = {q: 0 for q in self.dq}
        self.pending = {e: [] for e in self.CE}
        self.nins = 0

    def _mk(self, es, name):
        self.sem[name] = es.enter_context(self.nc.semaphore(name))
        self.val[name] = 0
        return name

    def wait(self, e, s, v):
        if v <= self.seen[e].get(s, 0):
            return
        self.E[e].wait_ge(self.sem[s], v)
        self.seen[e][s] = v
        self.nins += 1

    def _deps(self, e, reads, writes, dma=False):
        own = "c_" + e
        deps = {}
        for b in reads:
            assert b.pw in (None, e), f"read of {b.name} with pending writer {b.pw}"
            for s, v in b.w.items():
                if s == own and (e == "pe" and not dma):
                    continue
                if v > deps.get(s, 0):
                    deps[s] = v
        for b in writes:
            assert b.pw in (None, e), f"write of {b.name} with pending writer {b.pw}"
            assert not (b.pr - {e}), f"write of {b.name} with pending readers {b.pr}"
            for dd in (b.w, b.r):
                for s, v in dd.items():
                    if s == own and not dma:
                        continue
                    if v > deps.get(s, 0):
                        deps[s] = v
        for s, v in deps.items():
            self.wait(e, s, v)

    def op(self, e, emit, reads=(), writes=(), tick=True):
        self._deps(e, reads, writes)
        ins = emit()
        self.nins += 1
        own = "c_" + e
        self.pending[e].append((reads, writes))
        if tick:
            self.val[own] += 1
            ins.then_inc(self.sem[own], 1)
            v = self.val[own]
            for rs, ws in self.pending[e]:
                for b in rs:
                    b.r[own] = v
                    b.pr.discard(e)
                for b in ws:
                    b.w[own] = v
                    b.pw = None
            self.pending[e] = []
        else:
            for b in reads:
                b.pr.add(e)
            for b in writes:
                b.pw = e
        return ins

    def dma(self, q, out, in_, reads=(), writes=()):
        self._deps(q, reads, writes, dma=True)
        names = self.dq[q]
        nm = names[self.dqi[q] % len(names)]
        self.dqi[q] += 1
        self.wait(q, nm, self.val[nm])
        ins = self.E[q].dma_start(out=out, in_=in_)
        self.nins += 1
        self.val[nm] += 16
        ins.then_inc(self.sem[nm], 16)
        v = self.val[nm]
        for b in reads:
            b.r[nm] = v
        for b in writes:
            b.w[nm] = v
        return ins

    def barrier(self):
        for e in self.CE:
            assert not self.pending[e], f"pending un-ticked ops on {e}"
        for e in self.E:
            for s, v in self.val.items():
                if s == "c_" + e:
                    continue
                self.wait(e, s, v)

    def mm(self, out, lhsT, rhs, start, stop, reads, writes, tick=False):
        return self.op("pe", lambda: self.nc.tensor.matmul(out, lhsT=lhsT, rhs=rhs, start=start, stop=stop,
                                                           skip_group_check=True), reads, writes, tick)

    def act(self, out, in_, func, reads, writes, **kw):
        return self.op("act", lambda: self.nc.scalar.activation(out=out, in_=in_, func=func, **kw), reads, writes)

    def tt(self, e, out, in0, in1, op, reads, writes):
        return self.op(e, lambda: self.E[e].tensor_tensor(out=out, in0=in0, in1=in1, op=op), reads, writes)

    def stt(self, out, in0, scalar, in1, op0, op1, reads, writes):
        return self.op("dve", lambda: self.nc.vector.scalar_tensor_tensor(out=out, in0=in0, scalar=scalar, in1=in1,
                                                                         op0=op0, op1=op1), reads, writes)

    def ts(self, e, out, in0, s1, s2, op0, op1, reads, writes):
        return self.op(e, lambda: self.E[e].tensor_scalar(out=out, in0=in0, scalar1=s1, scalar2=s2, op0=op0, op1=op1),
                       reads, writes)

    def copy(self, e, out, in_, reads, writes):
        if e == "act":
            return self.act(out, in_, AF.Copy, reads, writes)
        return self.op(e, lambda: self.E[e].tensor_copy(out=out, in_=in_), reads, writes)

    def memset(self, e, ap, val, writes):
        return self.op(e, lambda: self.E[e].memset(ap, val), (), writes)


class Ctx:
    pass


_UNIQ = [0]


def sbt(nc, es, name, shape, dt):
    _UNIQ[0] += 1
    return es.enter_context(nc.sbuf_tensor(f"{name}_{_UNIQ[0]}", shape, dt))


def pst(nc, es, name, shape, dt=F32):
    _UNIQ[0] += 1
    return es.enter_context(nc.psum_tensor(f"{name}_{_UNIQ[0]}", shape, dt))


def layer_norm(g, es_name, z, zB, N, gcol, bcol, dst_ap, stat_ps, stat_bufs, tmp):
    tk, nc = g.tk, g.nc
    zb, zq, msq, var, rstd, nmr = tmp["zb"], tmp["zq"], tmp["msq"], tmp["var"], tmp["rstd"], tmp["nmr"]
    B = tmp["B"]
    tk.act(zb[:, :, 0:N], z[:, :, 0:N], AF.Copy, [zB], [B["zb"]])
    tk.act(zq[:, :, 0:N], z[:, :, 0:N], AF.Square, [zB], [B["zq"]])
    mean_ps, e2_ps = stat_ps
    mB, eB = stat_bufs
    for c in range(8):
        tk.mm(mean_ps[:, 0:N], g.ones_b[:], zb[:, c, 0:N], c == 0, c == 7, [g.cstB, B["zb"]], [mB], tick=(c == 7))
    for c in range(8):
        tk.mm(e2_ps[:, 0:N], g.ones_b[:], zq[:, c, 0:N], c == 0, c == 7, [g.cstB, B["zq"]], [eB], tick=(c == 7))
    tk.act(msq[:, 0:N], mean_ps[:, 0:N], AF.Square, [mB], [B["msq"]])
    tk.tt("dve", var[:, 0:N], e2_ps[:, 0:N], msq[:, 0:N], ALU.subtract, [eB, B["msq"]], [B["var"]])
    tk.act(var[:, 0:N], var[:, 0:N], AF.Ln, [B["var"]], [B["var"]], bias=g.eps_ln[:, 0:1])
    tk.act(rstd[:, 0:N], var[:, 0:N], AF.Exp, [B["var"]], [B["rstd"]], scale=-0.5)
    tk.stt(nmr[:, 0:N], mean_ps[:, 0:N], -1.0, rstd[:, 0:N], ALU.mult, ALU.mult, [mB, B["rstd"]], [B["nmr"]])
    for c in range(8):
        e1 = "dve" if c % 2 == 0 else "pool"
        tk.tt(e1, z[:, c, 0:N], z[:, c, 0:N], rstd[:, 0:N], ALU.mult, [zB, B["rstd"]], [zB])
        tk.tt("pool", z[:, c, 0:N], z[:, c, 0:N], nmr[:, 0:N], ALU.add, [zB, B["nmr"]], [zB])
        tk.act(z[:, c, 0:N], z[:, c, 0:N], AF.Identity, [zB, g.plB], [zB], scale=gcol[:, c:c + 1], bias=bcol[:, c:c + 1])
    tk.dma("sp", dst_ap.rearrange("(c p) t -> p c t", p=128), z[:, :, 0:N], [zB], [])


def head_norm(g, src, srcB, K, lhs1, lhs2, M, stat_ps, stat_bufs, tmp, N=512):
    tk = g.tk
    B = tmp["B"]
    sq, msq, var, dd = tmp["sq"], tmp["msq"], tmp["var"], tmp["dd"]
    mean_ps, e2_ps = stat_ps
    mB, eB = stat_bufs
    tk.act(sq[0:K, 0:N], src, AF.Square, [srcB], [B["sq"]])
    tk.mm(mean_ps[0:M, 0:N], lhs1, src, True, True, [g.cstB, srcB], [mB], tick=True)
    tk.mm(e2_ps[0:M, 0:N], lhs2, sq[0:K, 0:N], True, True, [g.cstB, B["sq"]], [eB], tick=True)
    tk.act(msq[0:M, 0:N], mean_ps[0:M, 0:N], AF.Square, [mB], [B["msq"]])
    tk.tt("dve", var[0:M, 0:N], e2_ps[0:M, 0:N], msq[0:M, 0:N], ALU.subtract, [eB, B["msq"]], [B["var"]])
    return mean_ps, mB


def build(depth=DEPTH, debug=False):
    nc = bass.Bass("TRN2", target_bir_lowering=False)
    g = Ctx()
    g.nc = nc
    dkind = "ExternalOutput" if debug else "Internal"
    xin = nc.dram_tensor("xin", [D, S], F32, kind="ExternalInput").ap()
    win = nc.dram_tensor("win", [DEPTH, D, PW], F32, kind="ExternalInput").ap()
    walpha = nc.dram_tensor("walpha", [DEPTH, 16, 128], F32, kind="ExternalInput").ap()
    wout = nc.dram_tensor("wout", [DEPTH, D, D], F32, kind="ExternalInput").ap()
    wup = nc.dram_tensor("wup", [DEPTH, D, 2 * DFF], F32, kind="ExternalInput").ap()
    wdown = nc.dram_tensor("wdown", [DEPTH, DFF, D], F32, kind="ExternalInput").ap()
    pl = nc.dram_tensor("pl", [DEPTH, 128, NPL], F32, kind="ExternalInput").ap()
    cst = nc.dram_tensor("cst", [128, NCST], F32, kind="ExternalInput").ap()
    rope = nc.dram_tensor("rope", [4, 128, S], F32, kind="ExternalInput").ap()
    out = nc.dram_tensor("out", [D, S], F32, kind="ExternalOutput").ap()
    XA = nc.dram_tensor("XA", [D, S], F32, kind="Internal").ap()
    X1F = nc.dram_tensor("X1F", [D, S], F32, kind=dkind).ap()
    YT = nc.dram_tensor("YT", [D, S], BF16, kind=dkind).ap()
    FMS = nc.dram_tensor("FMS", [NFM, 128, S], BF16, kind=dkind).ap()
    LA = nc.dram_tensor("LA", [128, S], F32, kind=dkind).ap()
    VA = nc.dram_tensor("VA", [S, 8, 65], BF16, kind=dkind).ap()
    VRG = nc.dram_tensor("VRG", [S, 512], BF16, kind=dkind).ap()
    HT = nc.dram_tensor("HT", [DFF, S], BF16, kind="Internal").ap()

    with ExitStack() as es:
        tk = TK(nc, es)
        g.tk = tk
        block = es.enter_context(nc.Block())

        @block.sync
        def _(sync):
            cst_sb = sbt(nc, es, "cst_sb", [128, NCST], F32)
            g.cstB = Buf("cst")
            g.cst = cst_sb
            tk.dma("sp", cst_sb[:], cst[:, :], [], [g.cstB])
            g.maskb = sbt(nc, es, "maskb", [128, 256], BF16)
            g.ident_b = sbt(nc, es, "ident_b", [128, 128], BF16)
            g.ones_b = sbt(nc, es, "ones_b", [128, 128], BF16)
            g.eps_ln = sbt(nc, es, "eps_ln", [128, 1], F32)
            tk.copy("dve", g.maskb[:], cst_sb[:, C_MASKB:C_MASKB + 256], [g.cstB], [g.cstB])
            tk.copy("dve", g.ident_b[:], cst_sb[:, C_IDENT:C_IDENT + 128], [g.cstB], [g.cstB])
            tk.copy("dve", g.ones_b[:], cst_sb[:, C_ONES1024:C_ONES1024 + 128], [g.cstB], [g.cstB])
            tk.memset("dve", g.eps_ln[:], LN_EPS, [g.cstB])
            g.eps_hn = sbt(nc, es, "eps_hn", [128, 1], F32)
            tk.memset("dve", g.eps_hn[:], HN_EPS, [g.cstB])
            g.one_col = cst_sb[:, C_ONES:C_ONES + 1]
            g.pl_sb = sbt(nc, es, "pl_sb", [128, NPL], F32)
            g.negb = sbt(nc, es, "negb", [128, 1], F32)
            g.plB = Buf("pl")
            tk.barrier()

            for l in range(depth):
                xsrc = xin if l == 0 else XA
                xdst = out if l == depth - 1 else XA
                tk.dma("sp", g.pl_sb[:], pl[l], [], [g.plB])
                tk.ts("dve", g.negb[:], g.pl_sb[:, P_BA:P_BA + 1], -1.0, None, ALU.mult, ALU.bypass, [g.plB], [g.plB])
                phase_P(g, l, xsrc, win, walpha, rope, FMS, LA, VA, VRG)
                tk.barrier()
                phase_A(g, l, FMS, VA, YT)
                tk.barrier()
                phase_L(g, l, FMS, LA, VRG, YT)
                tk.barrier()
                phase_O(g, l, xsrc, wout, YT, X1F)
                tk.barrier()
                phase_F(g, l, wup, wdown, X1F, xdst, HT)
                tk.barrier()
    g.nins = tk.nins
    return nc, g


def phase_P(g, l, xsrc, win, walpha, rope, FMS, LA, VA, VRG):
    tk, nc = g.tk, g.nc
    with ExitStack() as es:
        wfm = sbt(nc, es, "wfm", [128, 8, 2064], BF16)
        wtm = sbt(nc, es, "wtm", [128, 8, 1024], BF16)
        wal = sbt(nc, es, "wal", [16, 128], F32)
        wB = Buf("w")
        xt = [sbt(nc, es, f"xt{i}", [128, 8, 512], BF16) for i in range(2)]
        xtB = [Buf(f"xt{i}") for i in range(2)]
        rp = [sbt(nc, es, f"rp{i}", [128, 4, 512], F32) for i in range(2)]
        rpB = [Buf(f"rp{i}") for i in range(2)]
        NSO = 6
        so = [sbt(nc, es, f"so{i}", [128, 512], BF16) for i in range(NSO)]
        soB = [Buf(f"so{i}") for i in range(NSO)]
        tmpf = [[sbt(nc, es, f"rt{s}_{i}", [128, 512], F32) for i in range(4)] for s in range(2)]
        tmpB = [[Buf(f"rt{s}_{i}") for i in range(4)] for s in range(2)]
        vst = [sbt(nc, es, f"vst{i}", [128, 8, 65], BF16) for i in range(2)]
        vstB = [Buf(f"vst{i}") for i in range(2)]
        vrg = [sbt(nc, es, f"vrg{i}", [128, 512], BF16) for i in range(2)]
        vrgB = [Buf(f"vrg{i}") for i in range(2)]
        ag = sbt(nc, es, "ag", [16, 512], F32)
        agB = Buf("ag")
        ez = sbt(nc, es, "ez", [128, 512], F32)
        ezB = Buf("ez")
        lst = [sbt(nc, es, f"lst{i}", [128, 512], F32) for i in range(2)]
        lstB = [Buf(f"lst{i}") for i in range(2)]
        pb = [pst(nc, es, f"pb{i}", [128, 512]) for i in range(8)]
        pbB = [Buf(f"pb{i}") for i in range(8)]
        st = {"bank": 0, "so": 0, "ts": 0, "v": 0, "ev": 0}

        def nbank():
            i = st["bank"] % 8
            st["bank"] += 1
            return i

        def nso():
            i = st["so"] % NSO
            st["so"] += 1
            return i

        for i in range(2):
            tk.memset("pool", vst[i][:], 1.0, [vstB[i]])
        for kc in range(8):
            tk.dma("pool", wfm[:, kc, :], win[l, kc * 128:(kc + 1) * 128, 0:2064], [], [wB])
            tk.dma("pool", wtm[:, kc, :], win[l, kc * 128:(kc + 1) * 128, 2064:PW], [], [wB])
        tk.dma("sp", wal[:], walpha[l], [], [wB])
        xv = xsrc.rearrange("(c p) t -> p c t", p=128)
        rv = rope.rearrange("f p t -> p f t")

        def load(T):
            tk.dma("pool", xt[T % 2][:], xv[:, :, T * 512:(T + 1) * 512], [], [xtB[T % 2]])
            tk.dma("sp", rp[T % 2][:], rv[:, :, T * 512:(T + 1) * 512], [], [rpB[T % 2]])

        load(0)
        for T in range(8):
            if T + 1 < 8:
                load(T + 1)
            x_, xB_ = xt[T % 2], xtB[T % 2]
            r_, rB_ = rp[T % 2], rpB[T % 2]
            tsl = slice(T * 512, (T + 1) * 512)

            def fm_mm(tile, bank):
                for kc in range(8):
                    tk.mm(pb[bank][:, :], wfm[:, kc, tile * 128:(tile + 1) * 128], x_[:, kc, :], kc == 0, kc == 7,
                          [wB, xB_], [pbB[bank]], tick=(kc == 7))

            def store_fm(tile, si):
                tk.dma("sp", FMS[tile, :, tsl], so[si][:], [soB[si]], [])

            def tm_block(blk):
                for half in range(2):
                    b = nbank()
                    for kc in range(8):
                        tk.mm(pb[b][:, :], x_[:, kc, blk * 128:(blk + 1) * 128], wtm[:, kc, half * 512:(half + 1) * 512],
                              kc == 0, kc == 7, [wB, xB_], [pbB[b]], tick=(kc == 7))
                    vi = st["v"] % 2
                    eng = "act" if st["ev"] % 2 == 0 else "dve"
                    st["ev"] += 1
                    rows = slice(T * 512 + blk * 128, T * 512 + (blk + 1) * 128)
                    if half == 0:
                        tk.copy(eng, vst[vi][:, :, 0:64], pb[b][:, :].rearrange("p (h d) -> p h d", d=64), [pbB[b]], [vstB[vi]])
                        tk.dma("sp", VA[rows, :, :], vst[vi][:], [vstB[vi]], [])
                    else:
                        tk.copy(eng, vrg[vi][:], pb[b][:, :], [pbB[b]], [vrgB[vi]])
                        tk.dma("sp", VRG[rows, :], vrg[vi][:], [vrgB[vi]], [])
                        st["v"] += 1

            pairs = [(0, 1, 0), (2, 3, 0), (4, 5, 0), (6, 7, 0), (8, 9, 2)]
            for pi, (ta, tb, ro) in enumerate(pairs):
                ba, bb = nbank(), nbank()
                fm_mm(ta, ba)
                fm_mm(tb, bb)
                C_ = r_[:, ro, :]
                S_ = r_[:, ro + 1, :]
                s = st["ts"] % 2
                st["ts"] += 1
                t1, t2, t3, t4 = tmpf[s]
                b1, b2, b3, b4 = tmpB[s]
                tk.tt("dve", t1[:], pb[ba][:, :], C_, ALU.mult, [pbB[ba], rB_], [b1])
                tk.tt("dve", t2[:], pb[bb][:, :], S_, ALU.mult, [pbB[bb], rB_], [b2])
                tk.tt("dve", t3[:], pb[ba][:, :], S_, ALU.mult, [pbB[ba], rB_], [b3])
                tk.tt("dve", t4[:], pb[bb][:, :], C_, ALU.mult, [pbB[bb], rB_], [b4])
                sa = nso()
                tk.tt("pool", so[sa][:], t1[:], t2[:], ALU.subtract, [b1, b2], [soB[sa]])
                store_fm(ta, sa)
                sb_ = nso()
                tk.tt("pool", so[sb_][:], t3[:], t4[:], ALU.add, [b3, b4], [soB[sb_]])
                store_fm(tb, sb_)
                if pi < 4:
                    tm_block(pi)
            for tile, kind in ((10, "silu"), (11, "silu"), (12, "copy"), (13, "copy"), (14, "silu"), (15, "silu")):
                b = nbank()
                fm_mm(tile, b)
                si = nso()
                tk.act(so[si][:], pb[b][:, :], AF.Silu if kind == "silu" else AF.Copy, [pbB[b]], [soB[si]])
                store_fm(tile, si)
            b = nbank()
            for kc in range(8):
                tk.mm(pb[b][0:16, :], wfm[:, kc, 2048:2064], x_[:, kc, :], kc == 0, kc == 7, [wB, xB_], [pbB[b]], tick=(kc == 7))
            tk.act(ag[:], pb[b][0:16, :], AF.Copy, [pbB[b]], [agB])
            b2_ = nbank()
            tk.mm(pb[b2_][:, :], wal[:], ag[:], True, True, [wB, agB], [pbB[b2_]], tick=True)
            tk.act(ez[:], pb[b2_][:, :], AF.Exp, [pbB[b2_], g.plB], [ezB], scale=-1.0, bias=g.negb[:, 0:1])
            li = T % 2
            tk.act(lst[li][:], ez[:], AF.Ln, [ezB, g.cstB], [lstB[li]], bias=g.one_col)
            tk.dma("sp", LA[:, tsl], lst[li][:], [lstB[li]], [])


def phase_A(g, l, FMS, VA, YT):
    tk, nc = g.tk, g.nc
    with ExitStack() as es:
        qh = [sbt(nc, es, f"qh{i}", [64, S], BF16) for i in range(2)]
        kh = [sbt(nc, es, f"kh{i}", [64, S], BF16) for i in range(2)]
        v3 = [sbt(nc, es, f"v3{i}", [128, 3, 32, 65], BF16) for i in range(2)]
        inB = [Buf(f"ain{i}") for i in range(2)]
        acc = [sbt(nc, es, f"acc{i}", [65, S], F32) for i in range(2)]
        accB = [Buf(f"acc{i}") for i in range(2)]
        PT = [sbt(nc, es, f"PT{i}", [128, 1024], BF16) for i in range(3)]
        PTB = [Buf(f"PT{i}") for i in range(3)]
        yst = [sbt(nc, es, f"yst{i}", [64, S], BF16) for i in range(2)]
        ystB = [Buf(f"yst{i}") for i in range(2)]
        tmp = {"B": {k: Buf("a_" + k) for k in ("sq", "msq", "var", "dd")}}
        for k in ("sq", "msq", "var", "dd"):
            tmp[k] = sbt(nc, es, "a_" + k, [65, 512], F32)
        ST = [pst(nc, es, f"ST{i}", [128, 1024]) for i in range(2)]
        STB = [Buf(f"ST{i}") for i in range(2)]
        Op = [pst(nc, es, f"Op{i}", [128, 512]) for i in range(2)]
        OpB = [Buf(f"Op{i}") for i in range(2)]
        stat = [pst(nc, es, f"astat{i}", [128, 512]) for i in range(2)]
        statB = [Buf(f"astat{i}") for i in range(2)]
        cs = g.cst
        A1 = cs[0:65, C_A1:C_A1 + 64]
        A2 = cs[0:65, C_A2:C_A2 + 64]

        def load_head(h):
            i = h % 2
            gI, hh = h // 4, h % 4
            rows = slice(hh * 32, hh * 32 + 32)
            tk.dma("sp", qh[i][0:32, :], FMS[2 * gI, rows, :], [], [inB[i]])
            tk.dma("sp", qh[i][32:64, :], FMS[2 * gI + 1, rows, :], [], [inB[i]])
            tk.dma("sp", kh[i][0:32, :], FMS[4 + 2 * gI, rows, :], [], [inB[i]])
            tk.dma("sp", kh[i][32:64, :], FMS[4 + 2 * gI + 1, rows, :], [], [inB[i]])
            for di, d in enumerate((1, 4, 16)):
                src = VA[:, h, :].rearrange("(n j r) c -> j r n c", j=128, r=d)
                dst = v3[i][:, di, :, :].rearrange("p (r n) c -> p r n c", r=d)
                tk.dma("sp", dst, src, [], [inB[i]])

        batches = []
        for h in range(8):
            for di, d in enumerate((1, 4, 16)):
                nb = 32 // d
                blocks = [(r, n, r * nb + n) for r in range(d) for n in range(nb)]
                for b0 in range(0, 32, 4):
                    batches.append((h, di, d, nb, blocks[b0:b0 + 4]))
        NB = len(batches)

        def emit_ST(gi):
            h, di, d, nb, blks = batches[gi]
            i = h % 2
            sbuf = gi % 2
            qv = qh[i][:, :].rearrange("p (m r) -> p r m", r=d)
            kv = kh[i][:, :].rearrange("p (m r) -> p r m", r=d)
            for j, (r, n, b) in enumerate(blks):
                qn = 256 if n < nb - 1 else 128
                o_ = ST[sbuf][:, j * 256:j * 256 + qn]
                tk.mm(o_, kv[:, r, n * 128:(n + 1) * 128], qv[:, r, n * 128:n * 128 + qn], True, False,
                      [inB[i]], [STB[sbuf]])
                tk.mm(o_, g.ident_b[:], g.maskb[:, 0:qn], False, True, [g.cstB], [STB[sbuf]], tick=(j == 3))

        def emit_exp(gi):
            tk.act(PT[gi % 3][:], ST[gi % 2][:, :], AF.Exp, [STB[gi % 2]], [PTB[gi % 3]], scale=SC_A)

        def emit_PV(gi):
            h, di, d, nb, blks = batches[gi]
            i = h % 2
            ob = gi % 2
            cur = PT[gi % 3]
            prv = PT[(gi - 1) % 3]
            for j, (r, n, b) in enumerate(blks):
                o_ = Op[ob][0:65, j * 128:(j + 1) * 128]
                rd = [inB[i], PTB[gi % 3]]
                if n > 0:
                    if j > 0:
                        pprev = cur[:, (j - 1) * 256 + 128:(j - 1) * 256 + 256]
                    else:
                        pprev = prv[:, 3 * 256 + 128:4 * 256]
                        rd = rd + [PTB[(gi - 1) % 3]]
                    tk.mm(o_, v3[i][:, di, b - 1, :], pprev, True, False, rd, [OpB[ob]])
                    tk.mm(o_, v3[i][:, di, b, :], cur[:, j * 256:j * 256 + 128], False, True, rd, [OpB[ob]], tick=(j == 3))
                else:
                    tk.mm(o_, v3[i][:, di, b, :], cur[:, j * 256:j * 256 + 128], True, True, rd, [OpB[ob]], tick=(j == 3))

        def emit_evac(gi):
            h, di, d, nb, blks = batches[gi]
            a, aB = acc[h % 2], accB[h % 2]
            ob = gi % 2
            o_ = Op[ob][0:65, :]
            r0, n0, b0 = blks[0]
            if d == 1:
                tk.copy("dve", a[:, n0 * 128:n0 * 128 + 512], o_, [OpB[ob]], [aB])
            elif d == 4:
                av = a[:, :].rearrange("p (m q) -> p q m", q=4)[:, r0, n0 * 128:n0 * 128 + 512]
                tk.tt("dve", av, av, o_, ALU.add, [OpB[ob], aB], [aB])
            else:
                av = a[:, :].rearrange("p (m q) -> p q m", q=16)[:, r0:r0 + 2, :]
                tk.tt("dve", av, av, o_.rearrange("p (a m) -> p a m", a=2), ALU.add, [OpB[ob], aB], [aB])

        def emit_post(h, t):
            a, aB = acc[h % 2], accB[h % 2]
            B = tmp["B"]
            src = a[0:65, t * 512:(t + 1) * 512]
            mean_ps, mB = head_norm(g, src, aB, 65, A1, A2, 64, (stat[0], stat[1]), (statB[0], statB[1]), tmp)
            var, dd = tmp["var"], tmp["dd"]
            tk.act(var[0:64, :], var[0:64, :], AF.Ln, [B["var"]], [B["var"]])
            tk.act(var[0:64, :], var[0:64, :], AF.Exp, [B["var"]], [B["var"]], scale=-0.5)
            tk.tt("dve", dd[0:64, :], a[0:64, t * 512:(t + 1) * 512], mean_ps[0:64, :], ALU.subtract, [aB, mB], [B["dd"]])
            y, yB = yst[h % 2], ystB[h % 2]
            tk.stt(y[:, t * 512:(t + 1) * 512], dd[0:64, :], g.pl_sb[0:64, P_MSA + h:P_MSA + h + 1], var[0:64, :],
                   ALU.mult, ALU.mult, [B["dd"], B["var"], g.plB], [yB])
            if t == 7:
                tk.dma("sp", YT[h * 64:(h + 1) * 64, :], y[:, :], [yB], [])

        load_head(0)
        posts = []
        emit_ST(0)
        for gi in range(NB):
            h = batches[gi][0]
            first_of_head = (gi % 24 == 0)
            if first_of_head and h + 1 < 8:
                load_head(h + 1)
            if gi + 1 < NB:
                emit_ST(gi + 1)
            emit_exp(gi)
            emit_PV(gi)
            emit_evac(gi)
            if posts:
                emit_post(*posts.pop(0))
            if gi % 24 == 23:
                posts += [(h, t) for t in range(8)]
        while posts:
            emit_post(*posts.pop(0))


def phase_L(g, l, FMS, LA, VRG, YT):
    tk, nc = g.tk, g.nc
    cs = g.cst
    with ExitStack() as es:
        G = []
        for grp in range(2):
            d = Ctx()
            n = f"L{grp}_"
            d.qf = [sbt(nc, es, n + f"q{i}", [128, 512], BF16) for i in range(2)]
            d.kf = [sbt(nc, es, n + f"k{i}", [128, 512], BF16) for i in range(2)]
            d.vt = [sbt(nc, es, n + f"v{i}", [128, 4, 256], BF16) for i in range(2)]
            d.gt = [sbt(nc, es, n + f"g{i}", [128, 2, 512], BF16) for i in range(2)]
            d.lt = [sbt(nc, es, n + f"l{i}", [128, 512], F32) for i in range(2)] if grp == 1 else None
            d.inB = [Buf(n + f"in{i}") for i in range(2)]
            names = ("cum", "E", "Einv", "Kd", "dec", "Qbd", "kt", "ktil", "ktok", "og0", "og1", "sq", "msq", "var", "dd")
            d.B = {k: Buf(n + k) for k in names}
            if grp == 1:
                d.cum = sbt(nc, es, n + "cum", [128, 512], F32)
                d.Eg = sbt(nc, es, n + "E", [128, 512], F32)
                d.Einvg = sbt(nc, es, n + "Einv", [128, 512], F32)
                d.Kdg = sbt(nc, es, n + "Kd", [128, 512], F32)
                d.decg = sbt(nc, es, n + "dec", [128, 4], F32)
            d.Qbd = sbt(nc, es, n + "Qbd", [128, 4, 512], BF16)
            d.kt = sbt(nc, es, n + "kt", [128, 512], BF16)
            d.ktil = sbt(nc, es, n + "ktil", [128, 512], BF16)
            d.ktok = sbt(nc, es, n + "ktok", [128, 4, 128], BF16)
            d.stf = sbt(nc, es, n + "stf", [128, 256], F32)
            d.stfB = Buf(n + "stf")
            d.stb = [sbt(nc, es, n + f"stb{i}", [128, 256], BF16) for i in range(2)]
            d.stbB = [Buf(n + f"stb{i}") for i in range(2)]
            d.nst = 0
            d.og = [sbt(nc, es, n + f"og{j}", [128, 512], F32) for j in range(2)]
            d.tmp = {"B": d.B}
            for k in ("sq", "msq", "var", "dd"):
                d.tmp[k] = sbt(nc, es, n + k, [128, 512], F32)
            d.yy = [sbt(nc, es, n + f"yy{i}", [128, 512], BF16) for i in range(2)]
            d.yyB = [Buf(n + f"yy{i}") for i in range(2)]
            d.nyy = 0
            d.hm = cs[:, C_HMR:C_HMR + 4] if grp == 0 else cs[:, C_HMG:C_HMG + 4]
            d.ch0 = 512 + grp * 256
            G.append(d)
        PT = [sbt(nc, es, f"l_PT{i}", [128, 4, 128], BF16) for i in range(2)]
        PTB = [Buf(f"l_PT{i}") for i in range(2)]
        STp = [pst(nc, es, f"l_ST{i}", [128, 512]) for i in range(2)]
        STpB = [Buf(f"l_ST{i}") for i in range(2)]
        kvp = pst(nc, es, "l_kv", [128, 512])
        kvB = Buf("l_kvp")
        Opp = [pst(nc, es, f"l_O{i}", [128, 512]) for i in range(2)]
        OppB = [Buf(f"l_O{i}") for i in range(2)]
        trp = pst(nc, es, "l_tr", [128, 1024], BF16)
        trB = Buf("l_tr")
        stat = [pst(nc, es, f"l_stat{i}", [128, 512]) for i in range(2)]
        statB = [Buf(f"l_stat{i}") for i in range(2)]
        B1 = cs[:, C_B1:C_B1 + 128]
        lmask4 = cs[:, C_LMASK4:C_LMASK4 + 512]
        ones = cs[:, C_ONES:C_ONES + 128]
        cnt = {"pt": 0}

        def load(T, grp):
            d = G[grp]
            i = T % 2
            tsl = slice(T * 512, (T + 1) * 512)
            iB = d.inB[i]
            if grp == 0:
                tk.dma("sp", d.qf[i][0:64, :], FMS[8, 0:64, tsl], [], [iB])
                tk.dma("sp", d.qf[i][64:128, :], FMS[9, 0:64, tsl], [], [iB])
                tk.dma("sp", d.kf[i][0:64, :], FMS[8, 64:128, tsl], [], [iB])
                tk.dma("sp", d.kf[i][64:128, :], FMS[9, 64:128, tsl], [], [iB])
                g0 = 10
            else:
                tk.dma("sp", d.qf[i][:], FMS[12, :, tsl], [], [iB])
                tk.dma("sp", d.kf[i][:], FMS[13, :, tsl], [], [iB])
                tk.dma("sp", d.lt[i][:], LA[:, tsl], [], [iB])
                g0 = 14
            tk.dma("sp", d.gt[i][:, 0, :], FMS[g0, :, tsl], [], [iB])
            tk.dma("sp", d.gt[i][:, 1, :], FMS[g0 + 1, :, tsl], [], [iB])
            tk.dma("sp", d.vt[i][:], VRG[tsl, grp * 256:(grp + 1) * 256].rearrange("(c s) v -> s c v", s=128), [], [iB])

        def tables(d, grp):
            if grp == 0:
                return (cs[:, C_ER:C_ER + 512], cs[:, C_EINVR:C_EINVR + 512], cs[:, C_KDR:C_KDR + 512],
                        cs[:, C_DECR:C_DECR + 4], g.cstB, g.cstB, g.cstB, g.cstB)
            B = d.B
            return d.Eg[:], d.Einvg[:], d.Kdg[:], d.decg[:], B["E"], B["Einv"], B["Kd"], B["dec"]

        def prep(T, grp):
            d = G[grp]
            B = d.B
            i = T % 2
            iB = d.inB[i]
            q_, k_ = d.qf[i], d.kf[i]
            if T == 0:
                tk.memset("dve", d.stf[:], 0.0, [d.stfB])
                tk.memset("pool", d.stb[0][:], 0.0, [d.stbB[0]])
                d.nst = 0
            if grp == 1:
                l_ = d.lt[i]
                for c in range(4):
                    csl = slice(c * 128, (c + 1) * 128)
                    tk.op("dve", lambda csl=csl: nc.vector.tensor_tensor_scan(
                        out=d.cum[:, csl], data0=ones, data1=l_[:, csl], initial=0.0, op0=ALU.mult, op1=ALU.add),
                        [iB, g.cstB], [B["cum"]])
                tk.act(d.Eg[:], d.cum[:], AF.Exp, [B["cum"]], [B["E"]], scale=-1.0 / 16)
                tk.act(d.Einvg[:], d.cum[:], AF.Exp, [B["cum"]], [B["Einv"]], scale=1.0 / 16)
                tk.act(d.decg[:], d.cum[:].rearrange("p (c s) -> p c s", s=128)[:, :, 127], AF.Exp, [B["cum"]], [B["dec"]],
                       scale=-1.0 / 16)
                for c in range(4):
                    csl = slice(c * 128, (c + 1) * 128)
                    tk.ts("pool", d.Kdg[:, csl], d.Einvg[:, csl], d.decg[:, c:c + 1], None, ALU.mult, ALU.bypass,
                          [B["Einv"], B["dec"]], [B["Kd"]])
            E_, Einv_, Kd_, dec_, EB, EinvB, KdB, decB = tables(d, grp)
            for h in range(4):
                tk.stt(d.Qbd[:, h, :], q_[:], d.hm[:, h:h + 1], E_, ALU.mult, ALU.mult, [iB, g.cstB, EB], [B["Qbd"]])
            tk.tt("pool", d.kt[:], k_[:], Einv_, ALU.mult, [iB, EinvB], [B["kt"]])
            tk.tt("pool", d.ktil[:], k_[:], Kd_, ALU.mult, [iB, KdB], [B["ktil"]])

        def core(T, grp):
            d = G[grp]
            B = d.B
            i = T % 2
            iB = d.inB[i]
            v_ = d.vt[i]
            E_, Einv_, Kd_, dec_, EB, EinvB, KdB, decB = tables(d, grp)
            for c in range(4):
                tk.op("pe", lambda c=c: nc.tensor.transpose(out=trp[:, c * 128:(c + 1) * 128],
                                                            in_=d.ktil[:, c * 128:(c + 1) * 128], identity=g.ident_b[:]),
                      [B["ktil"], g.cstB], [trB], tick=(c == 3))
            tk.copy("act", d.ktok[:].rearrange("p c s -> p (c s)"), trp[:, 0:512], [trB], [B["ktok"]])
            sps = []

            def emit_ST(c):
                csl = slice(c * 128, (c + 1) * 128)
                sp_ = cnt["pt"] % 2
                cnt["pt"] += 1
                sps.append(sp_)
                tk.mm(STp[sp_][:, :], d.kt[:, csl], d.Qbd[:, :, csl], True, True, [B["kt"], B["Qbd"]], [STpB[sp_]], tick=True)
                tk.tt("dve", PT[sp_][:], STp[sp_][:, :].rearrange("p (h c) -> p h c", h=4),
                      lmask4.rearrange("p (h c) -> p h c", h=4), ALU.mult, [STpB[sp_], g.cstB], [PTB[sp_]])

            emit_ST(0)
            for c in range(4):
                csl = slice(c * 128, (c + 1) * 128)
                if c + 1 < 4:
                    emit_ST(c + 1)
                sp_ = sps[c]
                kvs = slice((c % 2) * 256, (c % 2) * 256 + 256)
                tk.mm(kvp[:, kvs], d.ktok[:, c, :], v_[:, c, :], True, True, [B["ktok"], iB], [kvB], tick=True)
                sbi = d.nst % 2
                for h in range(4):
                    j, half = h // 2, h % 2
                    o_ = Opp[j][half * 64:(half + 1) * 64, csl]
                    tk.mm(o_, v_[:, c, h * 64:(h + 1) * 64], PT[sp_][:, h, :], True, False, [iB, PTB[sp_]], [OppB[j]])
                    tk.mm(o_, d.stb[sbi][:, h * 64:(h + 1) * 64], d.Qbd[:, h, csl], False, True,
                          [d.stbB[sbi], B["Qbd"]], [OppB[j]], tick=(h % 2 == 1))
                tk.stt(d.stf[:], d.stf[:], dec_[:, c:c + 1], kvp[:, kvs], ALU.mult, ALU.add, [d.stfB, decB, kvB], [d.stfB])
                d.nst += 1
                sbn = d.nst % 2
                tk.copy("dve", d.stb[sbn][:], d.stf[:], [d.stfB], [d.stbB[sbn]])
            for j in range(2):
                tk.copy("act", d.og[j][:], Opp[j][:, :], [OppB[j]], [B[f"og{j}"]])

        def norm(T, grp):
            d = G[grp]
            B = d.B
            i = T % 2
            iB = d.inB[i]
            g_ = d.gt[i]
            for j in range(2):
                ogB = B[f"og{j}"]
                mean_ps, mB = head_norm(g, d.og[j][:], ogB, 128, B1, B1, 128, (stat[0], stat[1]), (statB[0], statB[1]), d.tmp)
                var, dd = d.tmp["var"], d.tmp["dd"]
                tk.act(var[:], var[:], AF.Ln, [B["var"]], [B["var"]], bias=g.eps_hn[:, 0:1])
                tk.act(var[:], var[:], AF.Exp, [B["var"]], [B["var"]], scale=-0.5)
                tk.tt("dve", dd[:], d.og[j][:], mean_ps[:, :], ALU.subtract, [ogB, mB], [B["dd"]])
                tk.stt(dd[:], dd[:], g.pl_sb[:, P_MSRG + grp * 2 + j:P_MSRG + grp * 2 + j + 1], var[:],
                       ALU.mult, ALU.mult, [B["dd"], B["var"], g.plB], [B["dd"]])
                yi = d.nyy % 2
                d.nyy += 1
                tk.tt("pool", d.yy[yi][:], dd[:], g_[:, j, :], ALU.mult, [B["dd"], iB], [d.yyB[yi]])
                tk.dma("sp", YT[d.ch0 + j * 128:d.ch0 + (j + 1) * 128, T * 512:(T + 1) * 512], d.yy[yi][:], [d.yyB[yi]], [])

        items = [(T, grp) for T in range(8) for grp in range(2)]
        NI = len(items)
        load(*items[0])
        load(*items[1])
        for s in range(NI + 2):
            if s + 2 < NI:
                pass
            if s < NI:
                prep(*items[s])
            if 0 <= s - 1 < NI:
                core(*items[s - 1])
            if 0 <= s - 2 < NI:
                norm(*items[s - 2])
                if s < NI:
                    pass
            if s + 2 < NI:
                load(*items[s + 2])


def ln_alloc(nc, es, pfx, N):
    tmp = {"B": {k: Buf(pfx + k) for k in ("zb", "zq", "msq", "var", "rstd", "nmr")}}
    tmp["zb"] = sbt(nc, es, pfx + "zb", [128, 8, N], BF16)
    tmp["zq"] = sbt(nc, es, pfx + "zq", [128, 8, N], BF16)
    for k in ("msq", "var", "rstd", "nmr"):
        tmp[k] = sbt(nc, es, pfx + k, [128, N], F32)
    return tmp


def ln_part1(g, z, zB, N, tmp):
    tk = g.tk
    B = tmp["B"]
    tk.act(tmp["zb"][:, :, 0:N], z[:, :, 0:N], AF.Copy, [zB], [B["zb"]])
    tk.act(tmp["zq"][:, :, 0:N], z[:, :, 0:N], AF.Square, [zB], [B["zq"]])


def ln_part2(g, z, zB, N, gcol, bcol, dst_ap, stat_ps, stat_bufs, tmp):
    tk = g.tk
    zb, zq, msq, var, rstd, nmr = tmp["zb"], tmp["zq"], tmp["msq"], tmp["var"], tmp["rstd"], tmp["nmr"]
    B = tmp["B"]
    mean_ps, e2_ps = stat_ps
    mB, eB = stat_bufs
    for c in range(8):
        tk.mm(mean_ps[:, 0:N], g.ones_b[:], zb[:, c, 0:N], c == 0, c == 7, [g.cstB, B["zb"]], [mB], tick=(c == 7))
    for c in range(8):
        tk.mm(e2_ps[:, 0:N], g.ones_b[:], zq[:, c, 0:N], c == 0, c == 7, [g.cstB, B["zq"]], [eB], tick=(c == 7))
    tk.act(msq[:, 0:N], mean_ps[:, 0:N], AF.Square, [mB], [B["msq"]])
    tk.tt("dve", var[:, 0:N], e2_ps[:, 0:N], msq[:, 0:N], ALU.subtract, [eB, B["msq"]], [B["var"]])
    tk.act(var[:, 0:N], var[:, 0:N], AF.Ln, [B["var"]], [B["var"]], bias=g.eps_ln[:, 0:1])
    tk.act(rstd[:, 0:N], var[:, 0:N], AF.Exp, [B["var"]], [B["rstd"]], scale=-0.5)
    tk.stt(nmr[:, 0:N], mean_ps[:, 0:N], -1.0, rstd[:, 0:N], ALU.mult, ALU.mult, [mB, B["rstd"]], [B["nmr"]])
    for c in range(8):
        e1 = "dve" if c % 2 == 0 else "pool"
        tk.tt(e1, z[:, c, 0:N], z[:, c, 0:N], rstd[:, 0:N], ALU.mult, [zB, B["rstd"]], [zB])
        tk.tt("pool", z[:, c, 0:N], z[:, c, 0:N], nmr[:, 0:N], ALU.add, [zB, B["nmr"]], [zB])
        tk.act(z[:, c, 0:N], z[:, c, 0:N], AF.Identity, [zB, g.plB], [zB], scale=gcol[:, c:c + 1], bias=bcol[:, c:c + 1])
    tk.dma("sp", dst_ap.rearrange("(c p) t -> p c t", p=128), z[:, :, 0:N], [zB], [])


def phase_O(g, l, xsrc, wout, YT, X1F):
    tk, nc = g.tk, g.nc
    with ExitStack() as es:
        wo = sbt(nc, es, "wo", [128, 8, D], BF16)
        wB = Buf("wo")
        yt = [sbt(nc, es, f"o_yt{i}", [128, 8, 512], BF16) for i in range(2)]
        xr = [sbt(nc, es, f"o_xr{i}", [128, 8, 512], F32) for i in range(2)]
        inB = [Buf(f"o_in{i}") for i in range(2)]
        z = [sbt(nc, es, f"o_z{i}", [128, 8, 512], F32) for i in range(2)]
        zB = [Buf(f"o_z{i}") for i in range(2)]
        tmp = ln_alloc(nc, es, "o_", 512)
        pb = [pst(nc, es, f"o_pb{i}", [128, 512]) for i in range(6)]
        pbB = [Buf(f"o_pb{i}") for i in range(6)]
        stat = [pst(nc, es, f"o_stat{i}", [128, 512]) for i in range(2)]
        statB = [Buf(f"o_stat{i}") for i in range(2)]
        for kc in range(8):
            tk.dma("pool", wo[:, kc, :], wout[l, kc * 128:(kc + 1) * 128, :], [], [wB])
        yv = YT.rearrange("(c p) t -> p c t", p=128)
        xv = xsrc.rearrange("(c p) t -> p c t", p=128)
        gcol, bcol = g.pl_sb[:, P_LN1G:P_LN1G + 8], g.pl_sb[:, P_LN1B:P_LN1B + 8]

        def load(T):
            tk.dma("sp", yt[T % 2][:], yv[:, :, T * 512:(T + 1) * 512], [], [inB[T % 2]])
            tk.dma("sp", xr[T % 2][:], xv[:, :, T * 512:(T + 1) * 512], [], [inB[T % 2]])

        def fin(T):
            i = T % 2
            ln_part2(g, z[i], zB[i], 512, gcol, bcol, X1F[:, T * 512:(T + 1) * 512], (stat[0], stat[1]),
                     (statB[0], statB[1]), tmp)

        load(0)
        nb = 0
        pend = None
        for T in range(8):
            if T + 1 < 8:
                load(T + 1)
            i = T % 2
            for oc in range(8):
                b = nb % 6
                nb += 1
                for kc in range(8):
                    tk.mm(pb[b][:, :], wo[:, kc, oc * 128:(oc + 1) * 128], yt[i][:, kc, :], kc == 0, kc == 7,
                          [wB, inB[i]], [pbB[b]], tick=(kc == 7))
                tk.stt(z[i][:, oc, :], xr[i][:, oc, :], ALPHA, pb[b][:, :], ALU.mult, ALU.add, [inB[i], pbB[b]], [zB[i]])
                if oc == 3 and pend is not None:
                    fin(pend)
                    pend = None
            ln_part1(g, z[i], zB[i], 512, tmp)
            pend = T
        fin(pend)


def phase_F(g, l, wup, wdown, X1F, xdst, HT):
    tk, nc = g.tk, g.nc
    NT = 256
    NTI = S // NT
    xv = X1F.rearrange("(c p) t -> p c t", p=128)
    cw = g.pl_sb[:, P_CW:P_CW + 132]
    cb = g.pl_sb[:, P_CB:P_CB + 44]
    with ExitStack() as eso:
        wd = sbt(nc, eso, "wd", [128, 22, D], BF16)
        wdB = Buf("wd")
        with ExitStack() as es:
            x1b = sbt(nc, es, "f_x1b", [128, 8, S + 2], BF16)
            xB = [Buf(f"f_x1b{T}") for T in range(NTI)]
            haloB = Buf("f_halo")
            NW = 3
            wch = [sbt(nc, es, f"f_wch{i}", [128, 8, 256], BF16) for i in range(NW)]
            wchB = [Buf(f"f_wch{i}") for i in range(NW)]
            og = [sbt(nc, es, f"f_og{i}", [128, NT], F32) for i in range(2)]
            a2 = [sbt(nc, es, f"f_a2{i}", [128, NT], F32) for i in range(2)]
            ov = [sbt(nc, es, f"f_ov{i}", [128, NT], F32) for i in range(2)]
            sg = [sbt(nc, es, f"f_sg{i}", [128, NT], F32) for i in range(2)]
            ogB = [Buf(f"f_og{i}") for i in range(2)]
            a2B = [Buf(f"f_a2{i}") for i in range(2)]
            ovB = [Buf(f"f_ov{i}") for i in range(2)]
            sgB = [Buf(f"f_sg{i}") for i in range(2)]
            hst = [sbt(nc, es, f"f_hst{i}", [128, S], BF16) for i in range(2)]
            hstB = [Buf(f"f_hst{i}") for i in range(2)]
            pb = [pst(nc, es, f"f_pb{i}", [128, 512]) for i in range(8)]
            pbB = [Buf(f"f_pb{i}") for i in range(8)]
            wv = wup[l].rearrange("(kc p) n -> p kc n", p=128)

            def loadw(c):
                i = c % NW
                tk.dma("pool", wch[i][:, :, 0:128], wv[:, :, c * 128:(c + 1) * 128], [], [wchB[i]])
                tk.dma("pool", wch[i][:, :, 128:256], wv[:, :, (22 + c) * 128:(23 + c) * 128], [], [wchB[i]])

            tk.memset("dve", x1b[:, :, 0:2], 0.0, [haloB])
            loadw(0)
            for T in range(NTI):
                tk.dma("pool", x1b[:, :, 2 + T * NT:2 + (T + 1) * NT], xv[:, :, T * NT:(T + 1) * NT], [], [xB[T]])
                if T == 1:
                    loadw(1)
            nb = 0
            ce = 0
            tail = [None]
            for c in range(22):
                if c + 2 < 22:
                    loadw(c + 2)
                tk.dma("pool", wd[:, c, :], wdown[l, c * 128:(c + 1) * 128, :], [], [wdB])
                wi = c % NW
                hs, hsB = hst[c % 2], hstB[c % 2]
                for T in range(NTI):
                    bg = nb % 8
                    bv = (nb + 1) % 8
                    nb += 2
                    xrd = [wchB[wi], xB[T], xB[T - 1] if T > 0 else haloB]
                    for (bank, off) in ((bg, 0), (bv, 128)):
                        for kc in range(8):
                            tk.mm(pb[bank][:, 0:NT + 2], wch[wi][:, kc, off:off + 128], x1b[:, kc, T * NT:T * NT + NT + 2],
                                  kc == 0, kc == 7, xrd, [pbB[bank]], tick=(kc == 7))
                    s = ce % 2
                    ce += 1
                    G_, V_ = pb[bg], pb[bv]
                    cg, cv_ = c, 22 + c
                    wg = [cw[:, cg * 3 + j:cg * 3 + j + 1] for j in range(3)]
                    wv_ = [cw[:, cv_ * 3 + j:cv_ * 3 + j + 1] for j in range(3)]
                    tk.act(og[s][:], G_[:, 2:NT + 2], AF.Identity, [pbB[bg], g.plB], [ogB[s]], scale=wg[2], bias=cb[:, cg:cg + 1])
                    tk.act(a2[s][:], G_[:, 1:NT + 1], AF.Identity, [pbB[bg], g.plB], [a2B[s]], scale=wg[1])
                    tk.stt(og[s][:], G_[:, 0:NT], wg[0], og[s][:], ALU.mult, ALU.add, [pbB[bg], g.plB, ogB[s], a2B[s]], [ogB[s]])
                    tk.act(ov[s][:], V_[:, 2:NT + 2], AF.Identity, [pbB[bv], g.plB], [ovB[s]], scale=wv_[2], bias=cb[:, cv_:cv_ + 1])
                    tk.stt(ov[s][:], V_[:, 1:NT + 1], wv_[1], ov[s][:], ALU.mult, ALU.add, [pbB[bv], g.plB, ovB[s]], [ovB[s]])
                    tk.stt(ov[s][:], V_[:, 0:NT], wv_[0], ov[s][:], ALU.mult, ALU.add, [pbB[bv], g.plB, ovB[s]], [ovB[s]])
                    tk.tt("pool", og[s][:], og[s][:], a2[s][:], ALU.add, [ogB[s], a2B[s]], [ogB[s]])
                    if tail[0] is not None:
                        tail[0]()

                    def mk(s=s, hs=hs, hsB=hsB, T=T):
                        def f():
                            tk.act(sg[s][:], og[s][:], AF.Silu, [ogB[s]], [sgB[s]])
                            tk.tt("pool", hs[:, T * NT:(T + 1) * NT], sg[s][:], ov[s][:], ALU.mult, [sgB[s], ovB[s]], [hsB])
                        return f
                    tail[0] = mk()
                    if T == NTI - 1:
                        tail[0]()
                        tail[0] = None
                tk.dma("sp", HT[c * 128:(c + 1) * 128, :], hs[:, :], [hsB], [])
        tk.barrier()
        with ExitStack() as es:
            ht = [sbt(nc, es, f"d_ht{i}", [128, 22, 512], BF16) for i in range(2)]
            xr = [sbt(nc, es, f"d_xr{i}", [128, 8, 512], F32) for i in range(2)]
            inB = [Buf(f"d_in{i}") for i in range(2)]
            z = [sbt(nc, es, f"d_z{i}", [128, 8, 512], F32) for i in range(2)]
            zB = [Buf(f"d_z{i}") for i in range(2)]
            tmp = ln_alloc(nc, es, "d_", 512)
            pb = [pst(nc, es, f"d_pb{i}", [128, 512]) for i in range(6)]
            pbB = [Buf(f"d_pb{i}") for i in range(6)]
            stat = [pst(nc, es, f"d_stat{i}", [128, 512]) for i in range(2)]
            statB = [Buf(f"d_stat{i}") for i in range(2)]
            hv = HT.rearrange("(c p) t -> p c t", p=128)
            gcol, bcol = g.pl_sb[:, P_LN2G:P_LN2G + 8], g.pl_sb[:, P_LN2B:P_LN2B + 8]

            def load(T):
                tk.dma("sp", ht[T % 2][:], hv[:, :, T * 512:(T + 1) * 512], [], [inB[T % 2]])
                tk.dma("sp", xr[T % 2][:], xv[:, :, T * 512:(T + 1) * 512], [], [inB[T % 2]])

            def fin(T):
                i = T % 2
                ln_part2(g, z[i], zB[i], 512, gcol, bcol, xdst[:, T * 512:(T + 1) * 512], (stat[0], stat[1]),
                         (statB[0], statB[1]), tmp)

            load(0)
            nb = 0
            pend = None
            for T in range(8):
                if T + 1 < 8:
                    load(T + 1)
                i = T % 2
                for oc in range(8):
                    b = nb % 6
                    nb += 1
                    for c in range(22):
                        tk.mm(pb[b][:, :], wd[:, c, oc * 128:(oc + 1) * 128], ht[i][:, c, :], c == 0, c == 21,
                              [wdB, inB[i]], [pbB[b]], tick=(c == 21))
                    tk.stt(z[i][:, oc, :], xr[i][:, oc, :], ALPHA, pb[b][:, :], ALU.mult, ALU.add, [inB[i], pbB[b]], [zB[i]])
                    if oc == 3 and pend is not None:
                        fin(pend)
                        pend = None
                ln_part1(g, z[i], zB[i], 512, tmp)
                pend = T
            fin(pend)


_CACHE = {}


def _prep_weights(w_in, w_alpha, b_alpha, mix_scale, w_out, ln1_g, ln1_b, w_up, conv_w, conv_b, w_down, ln2_g, ln2_b):
    f = lambda a: np.ascontiguousarray(np.asarray(a, dtype=np.float32))
    win_p = f(np.asarray(w_in)[:, :, win_perm()])
    plb = np.zeros((DEPTH, 128, NPL), np.float32)
    ms = np.asarray(mix_scale, np.float32)
    for l in range(DEPTH):
        plb[l, 0:64, P_MSA:P_MSA + 8] = ms[l, 0:512].reshape(8, 64).T
        plb[l, :, P_MSRG:P_MSRG + 4] = ms[l, 512:1024].reshape(4, 128).T
        plb[l, :, P_LN1G:P_LN1G + 8] = np.asarray(ln1_g)[l].reshape(8, 128).T
        plb[l, :, P_LN1B:P_LN1B + 8] = np.asarray(ln1_b)[l].reshape(8, 128).T
        plb[l, :, P_LN2G:P_LN2G + 8] = np.asarray(ln2_g)[l].reshape(8, 128).T
        plb[l, :, P_LN2B:P_LN2B + 8] = np.asarray(ln2_b)[l].reshape(8, 128).T
        cwl = np.asarray(conv_w)[l].reshape(3, 44, 128)
        plb[l, :, P_CW:P_CW + 132] = cwl.transpose(2, 1, 0).reshape(128, 132)
        plb[l, :, P_CB:P_CB + 44] = np.asarray(conv_b)[l].reshape(44, 128).T
        plb[l, :, P_BA] = np.asarray(b_alpha)[l]
    return dict(win=win_p, walpha=f(w_alpha), wout=f(w_out), wup=f(w_up), wdown=f(w_down), pl=plb)


def kernel(x, w_in, w_alpha, b_alpha, mix_scale, w_out, ln1_g, ln1_b, w_up, conv_w, conv_b, w_down, ln2_g, ln2_b):
    x = np.asarray(x, dtype=np.float32)
    if "nc" not in _CACHE:
        _CACHE["nc"] = build(DEPTH)[0]
        _CACHE["cst"] = make_consts()
    nc = _CACHE["nc"]
    cst, rope = _CACHE["cst"]
    wd = _prep_weights(w_in, w_alpha, b_alpha, mix_scale, w_out, ln1_g, ln1_b, w_up, conv_w, conv_b, w_down, ln2_g, ln2_b)
    in_maps = []
    for b in range(8):
        m = dict(wd)
        m["xin"] = np.ascontiguousarray(x[b].T)
        m["cst"] = cst
        m["rope"] = rope
        in_maps.append(m)
    res = run_bass_kernel_spmd(nc, in_maps, core_ids=list(range(8)))
    outp = np.stack([np.asarray(r["out"], dtype=np.float32).T for r in res.results], axis=0)
    return np.ascontiguousarray(outp)
```

```python
import numpy as np
from contextlib import ExitStack
import concourse.bass as bass
import concourse.mybir as mybir
from concourse.bass_utils import run_bass_kernel_spmd

F32 = mybir.dt.float32
BF16 = mybir.dt.bfloat16
AF = mybir.ActivationFunctionType
ALU = mybir.AluOpType

S = 4096
D = 1024
DEPTH = 4
DFF = 2816
PW = 3088
ALPHA = float((2 * DEPTH) ** 0.25)
LN_EPS = 1e-5
HN_EPS = 1e-6
NEG = -30000.0
NFM = 16
SC_A = 64 ** -0.5
SC_L = 32 ** -0.5

C_MASKB, C_IDENT, C_LMASK, C_A1, C_A2, C_B1, C_ONES1024, C_HMR, C_HMG, C_DECR, C_ER, C_EINVR, C_KDR, C_ONES = (
    0, 256, 384, 512, 576, 640, 768, 896, 900, 904, 908, 1420, 1932, 2444)
C_LMASK4 = 2572
C_MASK01 = 3084
NCST = 3340

P_MSA, P_MSRG, P_LN1G, P_LN1B, P_LN2G, P_LN2B, P_CW, P_CB, P_BA = 0, 8, 12, 20, 28, 36, 44, 176, 220
NPL = 221


def make_consts():
    c = np.zeros((128, NCST), np.float32)
    j = np.arange(128)[:, None]
    i = np.arange(128)[None, :]
    c[:, C_MASKB:C_MASKB + 128] = np.where(j <= i, 0.0, NEG)
    c[:, C_MASKB + 128:C_MASKB + 256] = np.where(j >= i, 0.0, NEG)
    c[:, C_IDENT:C_IDENT + 128] = np.eye(128, dtype=np.float32)
    c[:, C_MASK01:C_MASK01 + 128] = (j <= i).astype(np.float32)
    c[:, C_MASK01 + 128:C_MASK01 + 256] = (j >= i).astype(np.float32)
    c[:, C_LMASK:C_LMASK + 128] = (j <= i).astype(np.float32)
    for h in range(4):
        c[:, C_LMASK4 + h * 128:C_LMASK4 + (h + 1) * 128] = (j <= i).astype(np.float32)
    c[0:64, C_A1:C_A1 + 64] = 1.0 / 64
    c[0:64, C_A2:C_A2 + 64] = 1.0 / 64
    c[64, C_A2:C_A2 + 64] = HN_EPS
    for h in range(2):
        c[h * 64:(h + 1) * 64, C_B1 + h * 64:C_B1 + (h + 1) * 64] = 1.0 / 64
    c[:, C_ONES1024:C_ONES1024 + 128] = 1.0 / 1024
    p = np.arange(128)
    headR = (p % 64) // 16
    headG = p // 32
    for h in range(4):
        c[:, C_HMR + h] = (headR == h) * SC_L
        c[:, C_HMG + h] = (headG == h) * SC_L
    lg = np.log(1.0 - np.power(2.0, -5.0 - np.arange(4, dtype=np.float64)))
    lgp = lg[headR][:, None]
    idx = (np.arange(512) % 128)[None, :].astype(np.float64)
    c[:, C_DECR:C_DECR + 4] = np.exp(lgp * 128.0)
    c[:, C_ER:C_ER + 512] = np.exp(lgp * (idx + 1.0))
    c[:, C_EINVR:C_EINVR + 512] = np.exp(-lgp * (idx + 1.0))
    c[:, C_KDR:C_KDR + 512] = np.exp(lgp * (127.0 - idx))
    c[:, C_ONES:C_ONES + 128] = 1.0
    rope = np.zeros((4, 128, S), np.float32)
    pos = np.arange(S, dtype=np.float32)[None, :]
    invA = (1.0 / (10000.0 ** (np.arange(0, 64, 2, dtype=np.float32) / 64))).astype(np.float32)
    invR = (1.0 / (10000.0 ** (np.arange(0, 32, 2, dtype=np.float32) / 32))).astype(np.float32)
    angA = (pos * invA[p % 32][:, None]).astype(np.float32)
    angR = (pos * invR[p % 16][:, None]).astype(np.float32)
    rope[0], rope[1] = np.cos(angA), np.sin(angA)
    rope[2], rope[3] = np.cos(angR), np.sin(angR)
    return c, rope


def win_perm():
    qA, kA, vA, qR, kR, vR, gR, qG, kG, vG, rG, aG = 0, 512, 1024, 1536, 1664, 1792, 2048, 2304, 2432, 2560, 2816, 3072
    cols = []
    for base in (qA, kA):
        for g in range(2):
            cols += [base + h * 64 + i for h in range(4 * g, 4 * g + 4) for i in range(32)]
            cols += [base + h * 64 + 32 + i for h in range(4 * g, 4 * g + 4) for i in range(32)]
    for half in range(2):
        cols += [qR + h * 32 + half * 16 + i for h in range(4) for i in range(16)]
        cols += [kR + h * 32 + half * 16 + i for h in range(4) for i in range(16)]
    cols += list(range(gR, gR + 256))
    cols += list(range(qG, qG + 128))
    cols += list(range(kG, kG + 128))
    cols += list(range(rG, rG + 256))
    cols += list(range(aG, aG + 16))
    cols += list(range(vA, vA + 512))
    cols += list(range(vR, vR + 256))
    cols += list(range(vG, vG + 256))
    assert len(cols) == PW and len(set(cols)) == PW
    return np.array(cols)


class Buf:
    __slots__ = ("w", "r", "pw", "pr", "name")

    def __init__(self, name=""):
        self.w = {}
        self.r = {}
        self.pw = None
        self.pr = set()
        self.name = name


class TK:
    CE = ("pe", "act", "dve", "pool")

    def __init__(self, nc, es):
        self.nc = nc
        self.E = {"pe": nc.tensor, "act": nc.scalar, "dve": nc.vector, "pool": nc.gpsimd, "sp": nc.sync}
        self.sem = {}
        self.val = {}
        self.seen = {e: {} for e in self.E}
        for e in self.CE:
            self._mk(es, "c_" + e)
        self.dq = {}
        for q, n in (("sp", 16), ("pool", 10)):
            self.dq[q] = [self._mk(es, f"d_{q}{i}") for i in range(n)]
        self.dqi = {q: 0 for q in self.dq}
        self.pending = {e: [] for e in self.CE}
        self.nins = 0

    def _mk(self, es, name):
        self.sem[name] = es.enter_context(self.nc.semaphore(name))
        self.val[name] = 0
        return name

    def wait(self, e, s, v):
        if v <= self.seen[e].get(s, 0):
            return
        self.E[e].wait_ge(self.sem[s], v)
        self.seen[e][s] = v
        self.nins += 1

    def _deps(self, e, reads, writes, dma=False):
        own = "c_" + e
        deps = {}
        for b in reads:
            assert b.pw in (None, e), f"read of {b.name} with pending writer {b.pw}"
            for s, v in b.w.items():
                if s == own and (e == "pe" and not dma):
                    continue
                if v > deps.get(s, 0):
                    deps[s] = v
        for b in writes:
            assert b.pw in (None, e), f"write of {b.name} with pending writer {b.pw}"
            assert not (b.pr - {e}), f"write of {b.name} with pending readers {b.pr}"
            for dd in (b.w, b.r):
                for s, v in dd.items():
                    if s == own and not dma:
                        continue
                    if v > deps.get(s, 0):
                        deps[s] = v
        for s, v in deps.items():
            self.wait(e, s, v)

    def op(self, e, emit, reads=(), writes=(), tick=True):
        self._deps(e, reads, writes)
        ins = emit()
        self.nins += 1
        own = "c_" + e
        self.pending[e].append((reads, writes))
        if tick:
            self.val[own] += 1
            ins.then_inc(self.sem[own], 1)
            v = self.val[own]
            for rs, ws in self.pending[e]:
                for b in rs:
                    b.r[own] = v
                    b.pr.discard(e)
                for b in ws:
                    b.w[own] = v
                    b.pw = None
            self.pending[e] = []
        else:
            for b in reads:
                b.pr.add(e)
            for b in writes:
                b.pw = e
        return ins

    def dma(self, q, out, in_, reads=(), writes=()):
        self._deps(q, reads, writes, dma=True)
        names = self.dq[q]
        nm = names[self.dqi[q] % len(names)]
        self.dqi[q] += 1
        self.wait(q, nm, self.val[nm])
        ins = self.E[q].dma_start(out=out, in_=in_)
        self.nins += 1
        self.val[nm] += 16
        ins.then_inc(self.sem[nm], 16)
        v = self.val[nm]
        for b in reads:
            b.r[nm] = v
        for b in writes:
            b.w[nm] = v
        return ins

    def barrier(self):
        for e in self.CE:
            assert not self.pending[e], f"pending un-ticked ops on {e}"
        for e in self.E:
            for s, v in self.val.items():
                if s == "c_" + e:
                    continue
                self.wait(e, s, v)

    def mm(self, out, lhsT, rhs, start, stop, reads, writes, tick=False):
        return self.op("pe", lambda: self.nc.tensor.matmul(out, lhsT=lhsT, rhs=rhs, start=start, stop=stop,
                                                           skip_group_check=True), reads, writes, tick)

    def act(self, out, in_, func, reads, writes, **kw):
        return self.op("act", lambda: self.nc.scalar.activation(out=out, in_=in_, func=func, **kw), reads, writes)

    def tt(self, e, out, in0, in1, op, reads, writes):
        return self.op(e, lambda: self.E[e].tensor_tensor(out=out, in0=in0, in1=in1, op=op), reads, writes)

    def stt(self, out, in0, scalar, in1, op0, op1, reads, writes):
        return self.op("dve", lambda: self.nc.vector.scalar_tensor_tensor(out=out, in0=in0, scalar=scalar, in1=in1,
                                                                         op0=op0, op1=op1), reads, writes)

    def ts(self, e, out, in0, s1, s2, op0, op1, reads, writes):
        return self.op(e, lambda: self.E[e].tensor_scalar(out=out, in0=in0, scalar1=s1, scalar2=s2, op0=op0, op1=op1),
                       reads, writes)

    def copy(self, e, out, in_, reads, writes):
        if e == "act":
            return self.act(out, in_, AF.Copy, reads, writes)
        return self.op(e, lambda: self.E[e].tensor_copy(out=out, in_=in_), reads, writes)

    def memset(self, e, ap, val, writes):
        return self.op(e, lambda: self.E[e].memset(ap, val), (), writes)


class Ctx:
    pass


_UNIQ = [0]


def sbt(nc, es, name, shape, dt):
    _UNIQ[0] += 1
    return es.enter_context(nc.sbuf_tensor(f"{name}_{_UNIQ[0]}", shape, dt))


def pst(nc, es, name, shape, dt=F32):
    _UNIQ[0] += 1
    return es.enter_context(nc.psum_tensor(f"{name}_{_UNIQ[0]}", shape, dt))


def layer_norm(g, es_name, z, zB, N, gcol, bcol, dst_ap, stat_ps, stat_bufs, tmp):
    tk, nc = g.tk, g.nc
    zb, zq, msq, var, rstd, nmr = tmp["zb"], tmp["zq"], tmp["msq"], tmp["var"], tmp["rstd"], tmp["nmr"]
    B = tmp["B"]
    tk.act(zb[:, :, 0:N], z[:, :, 0:N], AF.Copy, [zB], [B["zb"]])
    tk.act(zq[:, :, 0:N], z[:, :, 0:N], AF.Square, [zB], [B["zq"]])
    mean_ps, e2_ps = stat_ps
    mB, eB = stat_bufs
    for c in range(8):
        tk.mm(mean_ps[:, 0:N], g.ones_b[:], zb[:, c, 0:N], c == 0, c == 7, [g.cstB, B["zb"]], [mB], tick=(c == 7))
    for c in range(8):
        tk.mm(e2_ps[:, 0:N], g.ones_b[:], zq[:, c, 0:N], c == 0, c == 7, [g.cstB, B["zq"]], [eB], tick=(c == 7))
    tk.act(msq[:, 0:N], mean_ps[:, 0:N], AF.Square, [mB], [B["msq"]])
    tk.tt("dve", var[:, 0:N], e2_ps[:, 0:N], msq[:, 0:N], ALU.subtract, [eB, B["msq"]], [B["var"]])
    tk.act(var[:, 0:N], var[:, 0:N], AF.Ln, [B["var"]], [B["var"]], bias=g.eps_ln[:, 0:1])
    tk.act(rstd[:, 0:N], var[:, 0:N], AF.Exp, [B["var"]], [B["rstd"]], scale=-0.5)
    tk.stt(nmr[:, 0:N], mean_ps[:, 0:N], -1.0, rstd[:, 0:N], ALU.mult, ALU.mult, [mB, B["rstd"]], [B["nmr"]])
    for c in range(8):
        e1 = "dve" if c % 2 == 0 else "pool"
        tk.tt(e1, z[:, c, 0:N], z[:, c, 0:N], rstd[:, 0:N], ALU.mult, [zB, B["rstd"]], [zB])
        tk.tt("pool", z[:, c, 0:N], z[:, c, 0:N], nmr[:, 0:N], ALU.add, [zB, B["nmr"]], [zB])
        tk.act(z[:, c, 0:N], z[:, c, 0:N], AF.Identity, [zB, g.plB], [zB], scale=gcol[:, c:c + 1], bias=bcol[:, c:c + 1])
    tk.dma("sp", dst_ap.rearrange("(c p) t -> p c t", p=128), z[:, :, 0:N], [zB], [])


def head_norm(g, src, srcB, K, lhs1, lhs2, M, stat_ps, stat_bufs, tmp, N=512):
    tk = g.tk
    B = tmp["B"]
    sq, msq, var, dd = tmp["sq"], tmp["msq"], tmp["var"], tmp["dd"]
    mean_ps, e2_ps = stat_ps
    mB, eB = stat_bufs
    tk.act(sq[0:K, 0:N], src, AF.Square, [srcB], [B["sq"]])
    tk.mm(mean_ps[0:M, 0:N], lhs1, src, True, True, [g.cstB, srcB], [mB], tick=True)
    tk.mm(e2_ps[0:M, 0:N], lhs2, sq[0:K, 0:N], True, True, [g.cstB, B["sq"]], [eB], tick=True)
    tk.act(msq[0:M, 0:N], mean_ps[0:M, 0:N], AF.Square, [mB], [B["msq"]])
    tk.tt("dve", var[0:M, 0:N], e2_ps[0:M, 0:N], msq[0:M, 0:N], ALU.subtract, [eB, B["msq"]], [B["var"]])
    return mean_ps, mB


def build(depth=DEPTH, debug=False):
    nc = bass.Bass("TRN2", target_bir_lowering=False)
    g = Ctx()
    g.nc = nc
    dkind = "ExternalOutput" if debug else "Internal"
    xin = nc.dram_tensor("xin", [D, S], F32, kind="ExternalInput").ap()
    win = nc.dram_tensor("win", [DEPTH, D, PW], F32, kind="ExternalInput").ap()
    walpha = nc.dram_tensor("walpha", [DEPTH, 16, 128], F32, kind="ExternalInput").ap()
    wout = nc.dram_tensor("wout", [DEPTH, D, D], F32, kind="ExternalInput").ap()
    wup = nc.dram_tensor("wup", [DEPTH, D, 2 * DFF], F32, kind="ExternalInput").ap()
    wdown = nc.dram_tensor("wdown", [DEPTH, DFF, D], F32, kind="ExternalInput").ap()
    pl = nc.dram_tensor("pl", [DEPTH, 128, NPL], F32, kind="ExternalInput").ap()
    cst = nc.dram_tensor("cst", [128, NCST], F32, kind="ExternalInput").ap()
    rope = nc.dram_tensor("rope", [4, 128, S], F32, kind="ExternalInput").ap()
    out = nc.dram_tensor("out", [D, S], F32, kind="ExternalOutput").ap()
    XA = nc.dram_tensor("XA", [D, S], F32, kind="Internal").ap()
    X1F = nc.dram_tensor("X1F", [D, S], F32, kind=dkind).ap()
    YT = nc.dram_tensor("YT", [D, S], BF16, kind=dkind).ap()
    FMS = nc.dram_tensor("FMS", [NFM, 128, S], BF16, kind=dkind).ap()
    LA = nc.dram_tensor("LA", [128, S], F32, kind=dkind).ap()
    VA = nc.dram_tensor("VA", [S, 8, 65], BF16, kind=dkind).ap()
    VRG = nc.dram_tensor("VRG", [S, 512], BF16, kind=dkind).ap()
    HT = nc.dram_tensor("HT", [DFF, S], BF16, kind="Internal").ap()

    with ExitStack() as es:
        tk = TK(nc, es)
        g.tk = tk
        block = es.enter_context(nc.Block())

        @block.sync
        def _(sync):
            cst_sb = sbt(nc, es, "cst_sb", [128, NCST], F32)
            g.cstB = Buf("cst")
            g.cst = cst_sb
            tk.dma("sp", cst_sb[:], cst[:, :], [], [g.cstB])
            g.maskb = sbt(nc, es, "maskb", [128, 256], BF16)
            g.ident_b = sbt(nc, es, "ident_b", [128, 128], BF16)
            g.ones_b = sbt(nc, es, "ones_b", [128, 128], BF16)
            g.eps_ln = sbt(nc, es, "eps_ln", [128, 1], F32)
            tk.copy("dve", g.maskb[:], cst_sb[:, C_MASKB:C_MASKB + 256], [g.cstB], [g.cstB])
            tk.copy("dve", g.ident_b[:], cst_sb[:, C_IDENT:C_IDENT + 128], [g.cstB], [g.cstB])
            tk.copy("dve", g.ones_b[:], cst_sb[:, C_ONES1024:C_ONES1024 + 128], [g.cstB], [g.cstB])
            tk.memset("dve", g.eps_ln[:], LN_EPS, [g.cstB])
            g.mask01 = sbt(nc, es, "mask01", [128, 4, 256], BF16)
            for jj in range(4):
                tk.copy("dve", g.mask01[:, jj, :], cst_sb[:, C_MASK01:C_MASK01 + 256], [g.cstB], [g.cstB])
            g.eps_hn = sbt(nc, es, "eps_hn", [128, 1], F32)
            tk.memset("dve", g.eps_hn[:], HN_EPS, [g.cstB])
            g.one_col = cst_sb[:, C_ONES:C_ONES + 1]
            g.pl_sb = sbt(nc, es, "pl_sb", [128, NPL], F32)
            g.negb = sbt(nc, es, "negb", [128, 1], F32)
            g.plB = Buf("pl")
            tk.barrier()

            for l in range(depth):
                xsrc = xin if l == 0 else XA
                xdst = out if l == depth - 1 else XA
                tk.dma("sp", g.pl_sb[:], pl[l], [], [g.plB])
                tk.ts("dve", g.negb[:], g.pl_sb[:, P_BA:P_BA + 1], -1.0, None, ALU.mult, ALU.bypass, [g.plB], [g.plB])
                phase_P(g, l, xsrc, win, walpha, rope, FMS, LA, VA, VRG)
                tk.barrier()
                phase_A(g, l, FMS, VA, YT)
                tk.barrier()
                phase_L(g, l, FMS, LA, VRG, YT)
                tk.barrier()
                phase_O(g, l, xsrc, wout, YT, X1F)
                tk.barrier()
                phase_F(g, l, wup, wdown, X1F, xdst, HT)
                tk.barrier()
    g.nins = tk.nins
    return nc, g


def phase_P(g, l, xsrc, win, walpha, rope, FMS, LA, VA, VRG):
    tk, nc = g.tk, g.nc
    with ExitStack() as es:
        wfm = sbt(nc, es, "wfm", [128, 8, 2064], BF16)
        wtm = sbt(nc, es, "wtm", [128, 8, 1024], BF16)
        wal = sbt(nc, es, "wal", [16, 128], F32)
        wB = Buf("w")
        xt = [sbt(nc, es, f"xt{i}", [128, 8, 512], BF16) for i in range(2)]
        xtB = [Buf(f"xt{i}") for i in range(2)]
        rp = [sbt(nc, es, f"rp{i}", [128, 4, 512], F32) for i in range(2)]
        rpB = [Buf(f"rp{i}") for i in range(2)]
        NSO = 6
        so = [sbt(nc, es, f"so{i}", [128, 512], BF16) for i in range(NSO)]
        soB = [Buf(f"so{i}") for i in range(NSO)]
        tmpf = [[sbt(nc, es, f"rt{s}_{i}", [128, 512], F32) for i in range(4)] for s in range(2)]
        tmpB = [[Buf(f"rt{s}_{i}") for i in range(4)] for s in range(2)]
        vst = [sbt(nc, es, f"vst{i}", [128, 8, 65], BF16) for i in range(2)]
        vstB = [Buf(f"vst{i}") for i in range(2)]
        vrg = [sbt(nc, es, f"vrg{i}", [128, 512], BF16) for i in range(2)]
        vrgB = [Buf(f"vrg{i}") for i in range(2)]
        ag = sbt(nc, es, "ag", [16, 512], F32)
        agB = Buf("ag")
        ez = sbt(nc, es, "ez", [128, 512], F32)
        ezB = Buf("ez")
        lst = [sbt(nc, es, f"lst{i}", [128, 512], F32) for i in range(2)]
        lstB = [Buf(f"lst{i}") for i in range(2)]
        pb = [pst(nc, es, f"pb{i}", [128, 512]) for i in range(8)]
        pbB = [Buf(f"pb{i}") for i in range(8)]
        st = {"bank": 0, "so": 0, "ts": 0, "v": 0, "ev": 0}

        def nbank():
            i = st["bank"] % 8
            st["bank"] += 1
            return i

        def nso():
            i = st["so"] % NSO
            st["so"] += 1
            return i

        for i in range(2):
            tk.memset("pool", vst[i][:], 1.0, [vstB[i]])
        for kc in range(8):
            tk.dma("pool", wfm[:, kc, :], win[l, kc * 128:(kc + 1) * 128, 0:2064], [], [wB])
            tk.dma("pool", wtm[:, kc, :], win[l, kc * 128:(kc + 1) * 128, 2064:PW], [], [wB])
        tk.dma("sp", wal[:], walpha[l], [], [wB])
        xv = xsrc.rearrange("(c p) t -> p c t", p=128)
        rv = rope.rearrange("f p t -> p f t")

        def load(T):
            tk.dma("pool", xt[T % 2][:], xv[:, :, T * 512:(T + 1) * 512], [], [xtB[T % 2]])
            tk.dma("sp", rp[T % 2][:], rv[:, :, T * 512:(T + 1) * 512], [], [rpB[T % 2]])

        load(0)
        for T in range(8):
            if T + 1 < 8:
                load(T + 1)
            x_, xB_ = xt[T % 2], xtB[T % 2]
            r_, rB_ = rp[T % 2], rpB[T % 2]
            tsl = slice(T * 512, (T + 1) * 512)

            def fm_mm(tile, bank):
                for kc in range(8):
                    tk.mm(pb[bank][:, :], wfm[:, kc, tile * 128:(tile + 1) * 128], x_[:, kc, :], kc == 0, kc == 7,
                          [wB, xB_], [pbB[bank]], tick=(kc == 7))

            def store_fm(tile, si):
                tk.dma("sp", FMS[tile, :, tsl], so[si][:], [soB[si]], [])

            def tm_block(blk):
                for half in range(2):
                    b = nbank()
                    for kc in range(8):
                        tk.mm(pb[b][:, :], x_[:, kc, blk * 128:(blk + 1) * 128], wtm[:, kc, half * 512:(half + 1) * 512],
                              kc == 0, kc == 7, [wB, xB_], [pbB[b]], tick=(kc == 7))
                    vi = st["v"] % 2
                    eng = "act" if st["ev"] % 2 == 0 else "dve"
                    st["ev"] += 1
                    rows = slice(T * 512 + blk * 128, T * 512 + (blk + 1) * 128)
                    if half == 0:
                        tk.copy(eng, vst[vi][:, :, 0:64], pb[b][:, :].rearrange("p (h d) -> p h d", d=64), [pbB[b]], [vstB[vi]])
                        tk.dma("sp", VA[rows, :, :], vst[vi][:], [vstB[vi]], [])
                    else:
                        tk.copy(eng, vrg[vi][:], pb[b][:, :], [pbB[b]], [vrgB[vi]])
                        tk.dma("sp", VRG[rows, :], vrg[vi][:], [vrgB[vi]], [])
                        st["v"] += 1

            pairs = [(0, 1, 0), (2, 3, 0), (4, 5, 0), (6, 7, 0), (8, 9, 2)]
            for pi, (ta, tb, ro) in enumerate(pairs):
                ba, bb = nbank(), nbank()
                fm_mm(ta, ba)
                fm_mm(tb, bb)
                C_ = r_[:, ro, :]
                S_ = r_[:, ro + 1, :]
                s = st["ts"] % 2
                st["ts"] += 1
                t1, t2, t3, t4 = tmpf[s]
                b1, b2, b3, b4 = tmpB[s]
                tk.tt("dve", t1[:], pb[ba][:, :], C_, ALU.mult, [pbB[ba], rB_], [b1])
                tk.tt("dve", t2[:], pb[bb][:, :], S_, ALU.mult, [pbB[bb], rB_], [b2])
                tk.tt("dve", t3[:], pb[ba][:, :], S_, ALU.mult, [pbB[ba], rB_], [b3])
                tk.tt("dve", t4[:], pb[bb][:, :], C_, ALU.mult, [pbB[bb], rB_], [b4])
                sa = nso()
                tk.tt("pool", so[sa][:], t1[:], t2[:], ALU.subtract, [b1, b2], [soB[sa]])
                store_fm(ta, sa)
                sb_ = nso()
                tk.tt("pool", so[sb_][:], t3[:], t4[:], ALU.add, [b3, b4], [soB[sb_]])
                store_fm(tb, sb_)
                if pi < 4:
                    tm_block(pi)
            for tile, kind in ((10, "silu"), (11, "silu"), (12, "copy"), (13, "copy"), (14, "silu"), (15, "silu")):
                b = nbank()
                fm_mm(tile, b)
                si = nso()
                tk.act(so[si][:], pb[b][:, :], AF.Silu if kind == "silu" else AF.Copy, [pbB[b]], [soB[si]])
                store_fm(tile, si)
            b = nbank()
            for kc in range(8):
                tk.mm(pb[b][0:16, :], wfm[:, kc, 2048:2064], x_[:, kc, :], kc == 0, kc == 7, [wB, xB_], [pbB[b]], tick=(kc == 7))
            tk.act(ag[:], pb[b][0:16, :], AF.Copy, [pbB[b]], [agB])
            b2_ = nbank()
            tk.mm(pb[b2_][:, :], wal[:], ag[:], True, True, [wB, agB], [pbB[b2_]], tick=True)
            tk.act(ez[:], pb[b2_][:, :], AF.Exp, [pbB[b2_], g.plB], [ezB], scale=-1.0, bias=g.negb[:, 0:1])
            li = T % 2
            tk.act(lst[li][:], ez[:], AF.Ln, [ezB, g.cstB], [lstB[li]], bias=g.one_col)
            tk.dma("sp", LA[:, tsl], lst[li][:], [lstB[li]], [])


def phase_A(g, l, FMS, VA, YT):
    tk, nc = g.tk, g.nc
    with ExitStack() as es:
        qh = [sbt(nc, es, f"qh{i}", [128, S], BF16) for i in range(2)]
        kh = [sbt(nc, es, f"kh{i}", [128, S], BF16) for i in range(2)]
        v3 = [sbt(nc, es, f"v3{i}", [128, 3, 32, 65], BF16) for i in range(2)]
        inB = [Buf(f"ain{i}") for i in range(2)]
        acc = [sbt(nc, es, f"acc{i}", [65, S], F32) for i in range(2)]
        accB = [Buf(f"acc{i}") for i in range(2)]
        PT = [sbt(nc, es, f"PT{i}", [128, 1024], BF16) for i in range(3)]
        PTB = [[Buf(f"PT{i}a"), Buf(f"PT{i}b")] for i in range(3)]
        yst = [sbt(nc, es, f"yst{i}", [64, S], BF16) for i in range(2)]
        ystB = [Buf(f"yst{i}") for i in range(2)]
        tmp = {"B": {k: Buf("a_" + k) for k in ("sq", "msq", "var", "dd")}}
        for k in ("sq", "msq", "var", "dd"):
            tmp[k] = sbt(nc, es, "a_" + k, [65, 512], F32)
        ST = [pst(nc, es, f"ST{i}", [128, 1024]) for i in range(2)]
        STB = [Buf(f"ST{i}") for i in range(2)]
        Op = [pst(nc, es, f"Op{i}", [128, 512]) for i in range(2)]
        OpB = [Buf(f"Op{i}") for i in range(2)]
        stat = [pst(nc, es, f"astat{i}", [128, 512]) for i in range(2)]
        statB = [Buf(f"astat{i}") for i in range(2)]
        cs = g.cst
        A1 = cs[0:65, C_A1:C_A1 + 64]
        A2 = cs[0:65, C_A2:C_A2 + 64]

        def load_head(h):
            i = h % 2
            gI, hh = h // 4, h % 4
            rows = slice(hh * 32, hh * 32 + 32)
            tk.dma("sp", qh[i][0:32, :], FMS[2 * gI, rows, :], [], [inB[i]])
            tk.dma("sp", qh[i][32:64, :], FMS[2 * gI + 1, rows, :], [], [inB[i]])
            tk.dma("sp", kh[i][0:32, :], FMS[4 + 2 * gI, rows, :], [], [inB[i]])
            tk.dma("sp", kh[i][32:64, :], FMS[4 + 2 * gI + 1, rows, :], [], [inB[i]])
            for di, d in enumerate((1, 4, 16)):
                src = VA[:, h, :].rearrange("(n j r) c -> j r n c", j=128, r=d)
                dst = v3[i][:, di, :, :].rearrange("p (r n) c -> p r n c", r=d)
                tk.dma("sp", dst, src, [], [inB[i]])

        batches = []
        for h in range(8):
            for di, d in enumerate((1, 4, 16)):
                nb = 32 // d
                blocks = [(r, n, r * nb + n) for r in range(d) for n in range(nb)]
                for b0 in range(0, 32, 4):
                    batches.append((h, di, d, nb, blocks[b0:b0 + 4]))
        NB = len(batches)

        def emit_ST(gi):
            h, di, d, nb, blks = batches[gi]
            i = h % 2
            sbuf = gi % 2
            qv = qh[i][:, :].rearrange("p (m r) -> p r m", r=d)
            kv = kh[i][:, :].rearrange("p (m r) -> p r m", r=d)
            for j, (r, n, b) in enumerate(blks):
                qn = 256 if n < nb - 1 else 128
                o_ = ST[sbuf][:, j * 256:j * 256 + qn]
                tk.mm(o_, kv[:, r, n * 128:(n + 1) * 128], qv[:, r, n * 128:n * 128 + qn], True, True,
                      [inB[i]], [STB[sbuf]], tick=(j == 3))

        def emit_exp(gi):
            p3 = gi % 3
            tk.act(PT[p3][:], ST[gi % 2][:, :], AF.Exp, [STB[gi % 2]], PTB[p3], scale=SC_A)
            m01 = g.mask01[:].rearrange("p a b -> p (a b)")
            tk.tt("dve", PT[p3][:, 0:512], PT[p3][:, 0:512], m01[:, 0:512], ALU.mult, [PTB[p3][0], g.cstB], [PTB[p3][0]])
            tk.tt("pool", PT[p3][:, 512:1024], PT[p3][:, 512:1024], m01[:, 512:1024], ALU.mult, [PTB[p3][1], g.cstB], [PTB[p3][1]])

        def emit_PV(gi):
            h, di, d, nb, blks = batches[gi]
            i = h % 2
            ob = gi % 2
            cur = PT[gi % 3]
            prv = PT[(gi - 1) % 3]
            for j, (r, n, b) in enumerate(blks):
                o_ = Op[ob][0:65, j * 128:(j + 1) * 128]
                rd = [inB[i]] + PTB[gi % 3]
                if n > 0:
                    if j > 0:
                        pprev = cur[:, (j - 1) * 256 + 128:(j - 1) * 256 + 256]
                    else:
                        pprev = prv[:, 3 * 256 + 128:4 * 256]
                        rd = rd + PTB[(gi - 1) % 3]
                    tk.mm(o_, v3[i][:, di, b - 1, :], pprev, True, False, rd, [OpB[ob]])
                    tk.mm(o_, v3[i][:, di, b, :], cur[:, j * 256:j * 256 + 128], False, True, rd, [OpB[ob]], tick=(j == 3))
                else:
                    tk.mm(o_, v3[i][:, di, b, :], cur[:, j * 256:j * 256 + 128], True, True, rd, [OpB[ob]], tick=(j == 3))

        def emit_evac(gi):
            h, di, d, nb, blks = batches[gi]
            a, aB = acc[h % 2], accB[h % 2]
            ob = gi % 2
            o_ = Op[ob][0:65, :]
            r0, n0, b0 = blks[0]
            if d == 1:
                tk.copy("dve", a[:, n0 * 128:n0 * 128 + 512], o_, [OpB[ob]], [aB])
            elif d == 4:
                av = a[:, :].rearrange("p (m q) -> p q m", q=4)[:, r0, n0 * 128:n0 * 128 + 512]
                tk.tt("dve", av, av, o_, ALU.add, [OpB[ob], aB], [aB])
            else:
                av = a[:, :].rearrange("p (m q) -> p q m", q=16)[:, r0:r0 + 2, :]
                tk.tt("dve", av, av, o_.rearrange("p (a m) -> p a m", a=2), ALU.add, [OpB[ob], aB], [aB])

        def emit_post(h, t):
            a, aB = acc[h % 2], accB[h % 2]
            B = tmp["B"]
            src = a[0:65, t * 512:(t + 1) * 512]
            mean_ps, mB = head_norm(g, src, aB, 65, A1, A2, 64, (stat[0], stat[1]), (statB[0], statB[1]), tmp)
            var, dd = tmp["var"], tmp["dd"]
            tk.act(var[0:64, :], var[0:64, :], AF.Ln, [B["var"]], [B["var"]])
            tk.act(var[0:64, :], var[0:64, :], AF.Exp, [B["var"]], [B["var"]], scale=-0.5)
            tk.tt("dve", dd[0:64, :], a[0:64, t * 512:(t + 1) * 512], mean_ps[0:64, :], ALU.subtract, [aB, mB], [B["dd"]])
            y, yB = yst[h % 2], ystB[h % 2]
            tk.stt(y[:, t * 512:(t + 1) * 512], dd[0:64, :], g.pl_sb[0:64, P_MSA + h:P_MSA + h + 1], var[0:64, :],
                   ALU.mult, ALU.mult, [B["dd"], B["var"], g.plB], [yB])
            if t == 7:
                tk.dma("sp", YT[h * 64:(h + 1) * 64, :], y[:, :], [yB], [])

        for i in range(2):
            tk.memset("dve", qh[i][64:128, :], 0.0, [inB[i]])
            tk.memset("pool", kh[i][64:128, :], 0.0, [inB[i]])
        load_head(0)
        posts = []
        emit_ST(0)
        emit_ST(1)
        emit_exp(0)
        for gi in range(NB):
            h = batches[gi][0]
            first_of_head = (gi % 24 == 0)
            if first_of_head and h + 1 < 8:
                load_head(h + 1)
            if gi + 2 < NB:
                emit_ST(gi + 2)
            if gi + 1 < NB:
                emit_exp(gi + 1)
            emit_PV(gi)
            emit_evac(gi)
            if posts:
                emit_post(*posts.pop(0))
            if gi % 24 == 23:
                posts += [(h, t) for t in range(8)]
        while posts:
            emit_post(*posts.pop(0))


def phase_L(g, l, FMS, LA, VRG, YT):
    tk, nc = g.tk, g.nc
    cs = g.cst
    with ExitStack() as es:
        G = []
        for grp in range(2):
            d = Ctx()
            n = f"L{grp}_"
            d.qf = [sbt(nc, es, n + f"q{i}", [128, 512], BF16) for i in range(2)]
            d.kf = [sbt(nc, es, n + f"k{i}", [128, 512], BF16) for i in range(2)]
            d.vt = [sbt(nc, es, n + f"v{i}", [128, 4, 256], BF16) for i in range(2)]
            d.gt = [sbt(nc, es, n + f"g{i}", [128, 2, 512], BF16) for i in range(2)]
            d.lt = [sbt(nc, es, n + f"l{i}", [128, 512], F32) for i in range(2)] if grp == 1 else None
            d.inB = [Buf(n + f"in{i}") for i in range(2)]
            names = ("cum", "E", "Einv", "Kd", "dec", "Qbd", "kt", "ktil", "ktok", "og0", "og1", "sq", "msq", "var", "dd")
            d.B = {k: Buf(n + k) for k in names}
            if grp == 1:
                d.cum = sbt(nc, es, n + "cum", [128, 512], F32)
                d.Eg = sbt(nc, es, n + "E", [128, 512], F32)
                d.Einvg = sbt(nc, es, n + "Einv", [128, 512], F32)
                d.Kdg = sbt(nc, es, n + "Kd", [128, 512], F32)
                d.decg = sbt(nc, es, n + "dec", [128, 4], F32)
            d.Qbd = sbt(nc, es, n + "Qbd", [128, 4, 512], BF16)
            d.kt = sbt(nc, es, n + "kt", [128, 512], BF16)
            d.ktil = sbt(nc, es, n + "ktil", [128, 512], BF16)
            d.ktok = sbt(nc, es, n + "ktok", [128, 4, 128], BF16)
            d.stf = sbt(nc, es, n + "stf", [128, 256], F32)
            d.stfB = Buf(n + "stf")
            d.stb = [sbt(nc, es, n + f"stb{i}", [128, 256], BF16) for i in range(2)]
            d.stbB = [Buf(n + f"stb{i}") for i in range(2)]
            d.nst = 0
            d.og = [sbt(nc, es, n + f"og{j}", [128, 512], F32) for j in range(2)]
            d.tmp = {"B": d.B}
            for k in ("sq", "msq", "var", "dd"):
                d.tmp[k] = sbt(nc, es, n + k, [128, 512], F32)
            d.yy = [sbt(nc, es, n + f"yy{i}", [128, 512], BF16) for i in range(2)]
            d.yyB = [Buf(n + f"yy{i}") for i in range(2)]
            d.nyy = 0
            d.hm = cs[:, C_HMR:C_HMR + 4] if grp == 0 else cs[:, C_HMG:C_HMG + 4]
            d.ch0 = 512 + grp * 256
            G.append(d)
        PT = [sbt(nc, es, f"l_PT{i}", [128, 4, 128], BF16) for i in range(2)]
        PTB = [Buf(f"l_PT{i}") for i in range(2)]
        STp = [pst(nc, es, f"l_ST{i}", [128, 512]) for i in range(2)]
        STpB = [Buf(f"l_ST{i}") for i in range(2)]
        kvp = pst(nc, es, "l_kv", [128, 512])
        kvB = Buf("l_kvp")
        Opp = [pst(nc, es, f"l_O{i}", [128, 512]) for i in range(2)]
        OppB = [Buf(f"l_O{i}") for i in range(2)]
        trp = pst(nc, es, "l_tr", [128, 1024], BF16)
        trB = Buf("l_tr")
        stat = [pst(nc, es, f"l_stat{i}", [128, 512]) for i in range(2)]
        statB = [Buf(f"l_stat{i}") for i in range(2)]
        B1 = cs[:, C_B1:C_B1 + 128]
        lmask4 = cs[:, C_LMASK4:C_LMASK4 + 512]
        ones = cs[:, C_ONES:C_ONES + 128]
        cnt = {"pt": 0}

        def load(T, grp):
            d = G[grp]
            i = T % 2
            tsl = slice(T * 512, (T + 1) * 512)
            iB = d.inB[i]
            if grp == 0:
                tk.dma("sp", d.qf[i][0:64, :], FMS[8, 0:64, tsl], [], [iB])
                tk.dma("sp", d.qf[i][64:128, :], FMS[9, 0:64, tsl], [], [iB])
                tk.dma("sp", d.kf[i][0:64, :], FMS[8, 64:128, tsl], [], [iB])
                tk.dma("sp", d.kf[i][64:128, :], FMS[9, 64:128, tsl], [], [iB])
                g0 = 10
            else:
                tk.dma("sp", d.qf[i][:], FMS[12, :, tsl], [], [iB])
                tk.dma("sp", d.kf[i][:], FMS[13, :, tsl], [], [iB])
                tk.dma("sp", d.lt[i][:], LA[:, tsl], [], [iB])
                g0 = 14
            tk.dma("sp", d.gt[i][:, 0, :], FMS[g0, :, tsl], [], [iB])
            tk.dma("sp", d.gt[i][:, 1, :], FMS[g0 + 1, :, tsl], [], [iB])
            tk.dma("sp", d.vt[i][:], VRG[tsl, grp * 256:(grp + 1) * 256].rearrange("(c s) v -> s c v", s=128), [], [iB])

        def tables(d, grp):
            if grp == 0:
                return (cs[:, C_ER:C_ER + 512], cs[:, C_EINVR:C_EINVR + 512], cs[:, C_KDR:C_KDR + 512],
                        cs[:, C_DECR:C_DECR + 4], g.cstB, g.cstB, g.cstB, g.cstB)
            B = d.B
            return d.Eg[:], d.Einvg[:], d.Kdg[:], d.decg[:], B["E"], B["Einv"], B["Kd"], B["dec"]

        def prep(T, grp):
            d = G[grp]
            B = d.B
            i = T % 2
            iB = d.inB[i]
            q_, k_ = d.qf[i], d.kf[i]
            if T == 0:
                tk.memset("dve", d.stf[:], 0.0, [d.stfB])
                tk.memset("pool", d.stb[0][:], 0.0, [d.stbB[0]])
                d.nst = 0
            if grp == 1:
                l_ = d.lt[i]
                for c in range(4):
                    csl = slice(c * 128, (c + 1) * 128)
                    tk.op("dve", lambda csl=csl: nc.vector.tensor_tensor_scan(
                        out=d.cum[:, csl], data0=ones, data1=l_[:, csl], initial=0.0, op0=ALU.mult, op1=ALU.add),
                        [iB, g.cstB], [B["cum"]])
                tk.act(d.Eg[:], d.cum[:], AF.Exp, [B["cum"]], [B["E"]], scale=-1.0 / 16)
                tk.act(d.Einvg[:], d.cum[:], AF.Exp, [B["cum"]], [B["Einv"]], scale=1.0 / 16)
                tk.act(d.decg[:], d.cum[:].rearrange("p (c s) -> p c s", s=128)[:, :, 127], AF.Exp, [B["cum"]], [B["dec"]],
                       scale=-1.0 / 16)
                for c in range(4):
                    csl = slice(c * 128, (c + 1) * 128)
                    tk.ts("pool", d.Kdg[:, csl], d.Einvg[:, csl], d.decg[:, c:c + 1], None, ALU.mult, ALU.bypass,
                          [B["Einv"], B["dec"]], [B["Kd"]])
            E_, Einv_, Kd_, dec_, EB, EinvB, KdB, decB = tables(d, grp)
            for h in range(4):
                tk.stt(d.Qbd[:, h, :], q_[:], d.hm[:, h:h + 1], E_, ALU.mult, ALU.mult, [iB, g.cstB, EB], [B["Qbd"]])
            tk.tt("pool", d.kt[:], k_[:], Einv_, ALU.mult, [iB, EinvB], [B["kt"]])
            tk.tt("pool", d.ktil[:], k_[:], Kd_, ALU.mult, [iB, KdB], [B["ktil"]])

        def core(T, grp):
            d = G[grp]
            B = d.B
            i = T % 2
            iB = d.inB[i]
            v_ = d.vt[i]
            E_, Einv_, Kd_, dec_, EB, EinvB, KdB, decB = tables(d, grp)
            for c in range(4):
                tk.op("pe", lambda c=c: nc.tensor.transpose(out=trp[:, c * 128:(c + 1) * 128],
                                                            in_=d.ktil[:, c * 128:(c + 1) * 128], identity=g.ident_b[:]),
                      [B["ktil"], g.cstB], [trB], tick=(c == 3))
            tk.copy("act", d.ktok[:].rearrange("p c s -> p (c s)"), trp[:, 0:512], [trB], [B["ktok"]])
            sps = []

            def emit_ST(c):
                csl = slice(c * 128, (c + 1) * 128)
                sp_ = cnt["pt"] % 2
                cnt["pt"] += 1
                sps.append(sp_)
                tk.mm(STp[sp_][:, :], d.kt[:, csl], d.Qbd[:, :, csl], True, True, [B["kt"], B["Qbd"]], [STpB[sp_]], tick=True)
                tk.tt("dve", PT[sp_][:], STp[sp_][:, :].rearrange("p (h c) -> p h c", h=4),
                      lmask4.rearrange("p (h c) -> p h c", h=4), ALU.mult, [STpB[sp_], g.cstB], [PTB[sp_]])

            emit_ST(0)
            for c in range(4):
                csl = slice(c * 128, (c + 1) * 128)
                if c + 1 < 4:
                    emit_ST(c + 1)
                sp_ = sps[c]
                kvs = slice((c % 2) * 256, (c % 2) * 256 + 256)
                tk.mm(kvp[:, kvs], d.ktok[:, c, :], v_[:, c, :], True, True, [B["ktok"], iB], [kvB], tick=True)
                sbi = d.nst % 2
                for h in range(4):
                    j, half = h // 2, h % 2
                    o_ = Opp[j][half * 64:(half + 1) * 64, csl]
                    tk.mm(o_, v_[:, c, h * 64:(h + 1) * 64], PT[sp_][:, h, :], True, False, [iB, PTB[sp_]], [OppB[j]])
                    tk.mm(o_, d.stb[sbi][:, h * 64:(h + 1) * 64], d.Qbd[:, h, csl], False, True,
                          [d.stbB[sbi], B["Qbd"]], [OppB[j]], tick=(h % 2 == 1))
                tk.stt(d.stf[:], d.stf[:], dec_[:, c:c + 1], kvp[:, kvs], ALU.mult, ALU.add, [d.stfB, decB, kvB], [d.stfB])
                d.nst += 1
                sbn = d.nst % 2
                tk.copy("dve", d.stb[sbn][:], d.stf[:], [d.stfB], [d.stbB[sbn]])
            for j in range(2):
                tk.copy("act", d.og[j][:], Opp[j][:, :], [OppB[j]], [B[f"og{j}"]])

        def norm(T, grp):
            d = G[grp]
            B = d.B
            i = T % 2
            iB = d.inB[i]
            g_ = d.gt[i]
            for j in range(2):
                ogB = B[f"og{j}"]
                mean_ps, mB = head_norm(g, d.og[j][:], ogB, 128, B1, B1, 128, (stat[0], stat[1]), (statB[0], statB[1]), d.tmp)
                var, dd = d.tmp["var"], d.tmp["dd"]
                tk.act(var[:], var[:], AF.Ln, [B["var"]], [B["var"]], bias=g.eps_hn[:, 0:1])
                tk.act(var[:], var[:], AF.Exp, [B["var"]], [B["var"]], scale=-0.5)
                tk.tt("dve", dd[:], d.og[j][:], mean_ps[:, :], ALU.subtract, [ogB, mB], [B["dd"]])
                tk.stt(dd[:], dd[:], g.pl_sb[:, P_MSRG + grp * 2 + j:P_MSRG + grp * 2 + j + 1], var[:],
                       ALU.mult, ALU.mult, [B["dd"], B["var"], g.plB], [B["dd"]])
                yi = d.nyy % 2
                d.nyy += 1
                tk.tt("pool", d.yy[yi][:], dd[:], g_[:, j, :], ALU.mult, [B["dd"], iB], [d.yyB[yi]])
                tk.dma("sp", YT[d.ch0 + j * 128:d.ch0 + (j + 1) * 128, T * 512:(T + 1) * 512], d.yy[yi][:], [d.yyB[yi]], [])

        items = [(T, grp) for T in range(8) for grp in range(2)]
        NI = len(items)
        load(*items[0])
        load(*items[1])
        for s in range(NI + 2):
            if s + 2 < NI:
                pass
            if s < NI:
                prep(*items[s])
            if 0 <= s - 1 < NI:
                core(*items[s - 1])
            if 0 <= s - 2 < NI:
                norm(*items[s - 2])
                if s < NI:
                    pass
            if s + 2 < NI:
                load(*items[s + 2])


def ln_alloc(nc, es, pfx, N):
    tmp = {"B": {k: Buf(pfx + k) for k in ("zb", "zq", "msq", "var", "rstd", "nmr")}}
    tmp["zb"] = sbt(nc, es, pfx + "zb", [128, 8, N], BF16)
    tmp["zq"] = sbt(nc, es, pfx + "zq", [128, 8, N], BF16)
    for k in ("msq", "var", "rstd", "nmr"):
        tmp[k] = sbt(nc, es, pfx + k, [128, N], F32)
    return tmp


def ln_part1(g, z, zB, N, tmp):
    tk = g.tk
    B = tmp["B"]
    tk.act(tmp["zb"][:, :, 0:N], z[:, :, 0:N], AF.Copy, [zB], [B["zb"]])
    tk.act(tmp["zq"][:, :, 0:N], z[:, :, 0:N], AF.Square, [zB], [B["zq"]])


def ln_part2(g, z, zB, N, gcol, bcol, dst_ap, stat_ps, stat_bufs, tmp):
    tk = g.tk
    zb, zq, msq, var, rstd, nmr = tmp["zb"], tmp["zq"], tmp["msq"], tmp["var"], tmp["rstd"], tmp["nmr"]
    B = tmp["B"]
    mean_ps, e2_ps = stat_ps
    mB, eB = stat_bufs
    for c in range(8):
        tk.mm(mean_ps[:, 0:N], g.ones_b[:], zb[:, c, 0:N], c == 0, c == 7, [g.cstB, B["zb"]], [mB], tick=(c == 7))
    for c in range(8):
        tk.mm(e2_ps[:, 0:N], g.ones_b[:], zq[:, c, 0:N], c == 0, c == 7, [g.cstB, B["zq"]], [eB], tick=(c == 7))
    tk.act(msq[:, 0:N], mean_ps[:, 0:N], AF.Square, [mB], [B["msq"]])
    tk.tt("dve", var[:, 0:N], e2_ps[:, 0:N], msq[:, 0:N], ALU.subtract, [eB, B["msq"]], [B["var"]])
    tk.act(var[:, 0:N], var[:, 0:N], AF.Ln, [B["var"]], [B["var"]], bias=g.eps_ln[:, 0:1])
    tk.act(rstd[:, 0:N], var[:, 0:N], AF.Exp, [B["var"]], [B["rstd"]], scale=-0.5)
    tk.stt(nmr[:, 0:N], mean_ps[:, 0:N], -1.0, rstd[:, 0:N], ALU.mult, ALU.mult, [mB, B["rstd"]], [B["nmr"]])
    for c in range(8):
        e1 = "dve" if c % 2 == 0 else "pool"
        tk.tt(e1, z[:, c, 0:N], z[:, c, 0:N], rstd[:, 0:N], ALU.mult, [zB, B["rstd"]], [zB])
        tk.tt("pool", z[:, c, 0:N], z[:, c, 0:N], nmr[:, 0:N], ALU.add, [zB, B["nmr"]], [zB])
        tk.act(z[:, c, 0:N], z[:, c, 0:N], AF.Identity, [zB, g.plB], [zB], scale=gcol[:, c:c + 1], bias=bcol[:, c:c + 1])
    tk.dma("sp", dst_ap.rearrange("(c p) t -> p c t", p=128), z[:, :, 0:N], [zB], [])


def phase_O(g, l, xsrc, wout, YT, X1F):
    tk, nc = g.tk, g.nc
    with ExitStack() as es:
        wo = sbt(nc, es, "wo", [128, 8, D], BF16)
        wB = Buf("wo")
        yt = [sbt(nc, es, f"o_yt{i}", [128, 8, 512], BF16) for i in range(2)]
        xr = [sbt(nc, es, f"o_xr{i}", [128, 8, 512], F32) for i in range(2)]
        inB = [Buf(f"o_in{i}") for i in range(2)]
        z = [sbt(nc, es, f"o_z{i}", [128, 8, 512], F32) for i in range(2)]
        zB = [Buf(f"o_z{i}") for i in range(2)]
        tmp = ln_alloc(nc, es, "o_", 512)
        pb = [pst(nc, es, f"o_pb{i}", [128, 512]) for i in range(6)]
        pbB = [Buf(f"o_pb{i}") for i in range(6)]
        stat = [pst(nc, es, f"o_stat{i}", [128, 512]) for i in range(2)]
        statB = [Buf(f"o_stat{i}") for i in range(2)]
        for kc in range(8):
            tk.dma("pool", wo[:, kc, :], wout[l, kc * 128:(kc + 1) * 128, :], [], [wB])
        yv = YT.rearrange("(c p) t -> p c t", p=128)
        xv = xsrc.rearrange("(c p) t -> p c t", p=128)
        gcol, bcol = g.pl_sb[:, P_LN1G:P_LN1G + 8], g.pl_sb[:, P_LN1B:P_LN1B + 8]

        def load(T):
            tk.dma("sp", yt[T % 2][:], yv[:, :, T * 512:(T + 1) * 512], [], [inB[T % 2]])
            tk.dma("sp", xr[T % 2][:], xv[:, :, T * 512:(T + 1) * 512], [], [inB[T % 2]])

        def fin(T):
            i = T % 2
            ln_part2(g, z[i], zB[i], 512, gcol, bcol, X1F[:, T * 512:(T + 1) * 512], (stat[0], stat[1]),
                     (statB[0], statB[1]), tmp)

        load(0)
        nb = 0
        pend = None
        for T in range(8):
            if T + 1 < 8:
                load(T + 1)
            i = T % 2
            for oc in range(8):
                b = nb % 6
                nb += 1
                for kc in range(8):
                    tk.mm(pb[b][:, :], wo[:, kc, oc * 128:(oc + 1) * 128], yt[i][:, kc, :], kc == 0, kc == 7,
                          [wB, inB[i]], [pbB[b]], tick=(kc == 7))
                tk.stt(z[i][:, oc, :], xr[i][:, oc, :], ALPHA, pb[b][:, :], ALU.mult, ALU.add, [inB[i], pbB[b]], [zB[i]])
                if oc == 3 and pend is not None:
                    fin(pend)
                    pend = None
            ln_part1(g, z[i], zB[i], 512, tmp)
            pend = T
        fin(pend)


def phase_F(g, l, wup, wdown, X1F, xdst, HT):
    tk, nc = g.tk, g.nc
    NT = 256
    NTI = S // NT
    xv = X1F.rearrange("(c p) t -> p c t", p=128)
    cw = g.pl_sb[:, P_CW:P_CW + 132]
    cb = g.pl_sb[:, P_CB:P_CB + 44]
    with ExitStack() as eso:
        wd = sbt(nc, eso, "wd", [128, 22, D], BF16)
        wdB = Buf("wd")
        with ExitStack() as es:
            x1b = sbt(nc, es, "f_x1b", [128, 8, S + 2], BF16)
            xB = [Buf(f"f_x1b{T}") for T in range(NTI)]
            haloB = Buf("f_halo")
            NW = 3
            wch = [sbt(nc, es, f"f_wch{i}", [128, 8, 256], BF16) for i in range(NW)]
            wchB = [Buf(f"f_wch{i}") for i in range(NW)]
            og = [sbt(nc, es, f"f_og{i}", [128, NT], F32) for i in range(2)]
            a2 = [sbt(nc, es, f"f_a2{i}", [128, NT], F32) for i in range(2)]
            ov = [sbt(nc, es, f"f_ov{i}", [128, NT], F32) for i in range(2)]
            sg = [sbt(nc, es, f"f_sg{i}", [128, NT], F32) for i in range(2)]
            ogB = [Buf(f"f_og{i}") for i in range(2)]
            a2B = [Buf(f"f_a2{i}") for i in range(2)]
            ovB = [Buf(f"f_ov{i}") for i in range(2)]
            sgB = [Buf(f"f_sg{i}") for i in range(2)]
            hst = [sbt(nc, es, f"f_hst{i}", [128, S], BF16) for i in range(2)]
            hstB = [Buf(f"f_hst{i}") for i in range(2)]
            pb = [pst(nc, es, f"f_pb{i}", [128, 512]) for i in range(8)]
            pbB = [Buf(f"f_pb{i}") for i in range(8)]
            wv = wup[l].rearrange("(kc p) n -> p kc n", p=128)

            def loadw(c):
                i = c % NW
                tk.dma("pool", wch[i][:, :, 0:128], wv[:, :, c * 128:(c + 1) * 128], [], [wchB[i]])
                tk.dma("pool", wch[i][:, :, 128:256], wv[:, :, (22 + c) * 128:(23 + c) * 128], [], [wchB[i]])

            tk.memset("dve", x1b[:, :, 0:2], 0.0, [haloB])
            loadw(0)
            for T in range(NTI):
                tk.dma("pool", x1b[:, :, 2 + T * NT:2 + (T + 1) * NT], xv[:, :, T * NT:(T + 1) * NT], [], [xB[T]])
                if T == 1:
                    loadw(1)
            nb = 0
            ce = 0
            tail = [None]
            for c in range(22):
                if c + 2 < 22:
                    loadw(c + 2)
                tk.dma("pool", wd[:, c, :], wdown[l, c * 128:(c + 1) * 128, :], [], [wdB])
                wi = c % NW
                hs, hsB = hst[c % 2], hstB[c % 2]
                for T in range(NTI):
                    bg = nb % 8
                    bv = (nb + 1) % 8
                    nb += 2
                    xrd = [wchB[wi], xB[T], xB[T - 1] if T > 0 else haloB]
                    for (bank, off) in ((bg, 0), (bv, 128)):
                        for kc in range(8):
                            tk.mm(pb[bank][:, 0:NT + 2], wch[wi][:, kc, off:off + 128], x1b[:, kc, T * NT:T * NT + NT + 2],
                                  kc == 0, kc == 7, xrd, [pbB[bank]], tick=(kc == 7))
                    s = ce % 2
                    ce += 1
                    G_, V_ = pb[bg], pb[bv]
                    cg, cv_ = c, 22 + c
                    wg = [cw[:, cg * 3 + j:cg * 3 + j + 1] for j in range(3)]
                    wv_ = [cw[:, cv_ * 3 + j:cv_ * 3 + j + 1] for j in range(3)]
                    tk.act(og[s][:], G_[:, 2:NT + 2], AF.Identity, [pbB[bg], g.plB], [ogB[s]], scale=wg[2], bias=cb[:, cg:cg + 1])
                    tk.act(a2[s][:], G_[:, 1:NT + 1], AF.Identity, [pbB[bg], g.plB], [a2B[s]], scale=wg[1])
                    tk.stt(og[s][:], G_[:, 0:NT], wg[0], og[s][:], ALU.mult, ALU.add, [pbB[bg], g.plB, ogB[s], a2B[s]], [ogB[s]])
                    tk.act(ov[s][:], V_[:, 2:NT + 2], AF.Identity, [pbB[bv], g.plB], [ovB[s]], scale=wv_[2], bias=cb[:, cv_:cv_ + 1])
                    tk.stt(ov[s][:], V_[:, 1:NT + 1], wv_[1], ov[s][:], ALU.mult, ALU.add, [pbB[bv], g.plB, ovB[s]], [ovB[s]])
                    tk.stt(ov[s][:], V_[:, 0:NT], wv_[0], ov[s][:], ALU.mult, ALU.add, [pbB[bv], g.plB, ovB[s]], [ovB[s]])
                    tk.tt("pool", og[s][:], og[s][:], a2[s][:], ALU.add, [ogB[s], a2B[s]], [ogB[s]])
                    if tail[0] is not None:
                        tail[0]()

                    def mk(s=s, hs=hs, hsB=hsB, T=T):
                        def f():
                            tk.act(sg[s][:], og[s][:], AF.Silu, [ogB[s]], [sgB[s]])
                            tk.tt("pool", hs[:, T * NT:(T + 1) * NT], sg[s][:], ov[s][:], ALU.mult, [sgB[s], ovB[s]], [hsB])
                        return f
                    tail[0] = mk()
                    if T == NTI - 1:
                        tail[0]()
                        tail[0] = None
                tk.dma("sp", HT[c * 128:(c + 1) * 128, :], hs[:, :], [hsB], [])
        tk.barrier()
        with ExitStack() as es:
            ht = [sbt(nc, es, f"d_ht{i}", [128, 22, 512], BF16) for i in range(2)]
            xr = [sbt(nc, es, f"d_xr{i}", [128, 8, 512], F32) for i in range(2)]
            inB = [Buf(f"d_in{i}") for i in range(2)]
            z = [sbt(nc, es, f"d_z{i}", [128, 8, 512], F32) for i in range(2)]
            zB = [Buf(f"d_z{i}") for i in range(2)]
            tmp = ln_alloc(nc, es, "d_", 512)
            pb = [pst(nc, es, f"d_pb{i}", [128, 512]) for i in range(6)]
            pbB = [Buf(f"d_pb{i}") for i in range(6)]
            stat = [pst(nc, es, f"d_stat{i}", [128, 512]) for i in range(2)]
            statB = [Buf(f"d_stat{i}") for i in range(2)]
            hv = HT.rearrange("(c p) t -> p c t", p=128)
            gcol, bcol = g.pl_sb[:, P_LN2G:P_LN2G + 8], g.pl_sb[:, P_LN2B:P_LN2B + 8]

            def load(T):
                tk.dma("sp", ht[T % 2][:], hv[:, :, T * 512:(T + 1) * 512], [], [inB[T % 2]])
                tk.dma("sp", xr[T % 2][:], xv[:, :, T * 512:(T + 1) * 512], [], [inB[T % 2]])

            def fin(T):
                i = T % 2
                ln_part2(g, z[i], zB[i], 512, gcol, bcol, xdst[:, T * 512:(T + 1) * 512], (stat[0], stat[1]),
                         (statB[0], statB[1]), tmp)

            load(0)
            nb = 0
            pend = None
            for T in range(8):
                if T + 1 < 8:
                    load(T + 1)
                i = T % 2
                for oc in range(8):
                    b = nb % 6
                    nb += 1
                    for c in range(22):
                        tk.mm(pb[b][:, :], wd[:, c, oc * 128:(oc + 1) * 128], ht[i][:, c, :], c == 0, c == 21,
                              [wdB, inB[i]], [pbB[b]], tick=(c == 21))
                    tk.stt(z[i][:, oc, :], xr[i][:, oc, :], ALPHA, pb[b][:, :], ALU.mult, ALU.add, [inB[i], pbB[b]], [zB[i]])
                    if oc == 3 and pend is not None:
                        fin(pend)
                        pend = None
                ln_part1(g, z[i], zB[i], 512, tmp)
                pend = T
            fin(pend)


_CACHE = {}


def _prep_weights(w_in, w_alpha, b_alpha, mix_scale, w_out, ln1_g, ln1_b, w_up, conv_w, conv_b, w_down, ln2_g, ln2_b):
    f = lambda a: np.ascontiguousarray(np.asarray(a, dtype=np.float32))
    win_p = f(np.asarray(w_in)[:, :, win_perm()])
    plb = np.zeros((DEPTH, 128, NPL), np.float32)
    ms = np.asarray(mix_scale, np.float32)
    for l in range(DEPTH):
        plb[l, 0:64, P_MSA:P_MSA + 8] = ms[l, 0:512].reshape(8, 64).T
        plb[l, :, P_MSRG:P_MSRG + 4] = ms[l, 512:1024].reshape(4, 128).T
        plb[l, :, P_LN1G:P_LN1G + 8] = np.asarray(ln1_g)[l].reshape(8, 128).T
        plb[l, :, P_LN1B:P_LN1B + 8] = np.asarray(ln1_b)[l].reshape(8, 128).T
        plb[l, :, P_LN2G:P_LN2G + 8] = np.asarray(ln2_g)[l].reshape(8, 128).T
        plb[l, :, P_LN2B:P_LN2B + 8] = np.asarray(ln2_b)[l].reshape(8, 128).T
        cwl = np.asarray(conv_w)[l].reshape(3, 44, 128)
        plb[l, :, P_CW:P_CW + 132] = cwl.transpose(2, 1, 0).reshape(128, 132)
        plb[l, :, P_CB:P_CB + 44] = np.asarray(conv_b)[l].reshape(44, 128).T
        plb[l, :, P_BA] = np.asarray(b_alpha)[l]
    return dict(win=win_p, walpha=f(w_alpha), wout=f(w_out), wup=f(w_up), wdown=f(w_down), pl=plb)


def kernel(x, w_in, w_alpha, b_alpha, mix_scale, w_out, ln1_g, ln1_b, w_up, conv_w, conv_b, w_down, ln2_g, ln2_b):
    x = np.asarray(x, dtype=np.float32)
    if "nc" not in _CACHE:
        _CACHE["nc"] = build(DEPTH)[0]
        _CACHE["cst"] = make_consts()
    nc = _CACHE["nc"]
    cst, rope = _CACHE["cst"]
    wd = _prep_weights(w_in, w_alpha, b_alpha, mix_scale, w_out, ln1_g, ln1_b, w_up, conv_w, conv_b, w_down, ln2_g, ln2_b)
    in_maps = []
    for b in range(8):
        m = dict(wd)
        m["xin"] = np.ascontiguousarray(x[b].T)
        m["cst"] = cst
        m["rope"] = rope
        in_maps.append(m)
    res = run_bass_kernel_spmd(nc, in_maps, core_ids=list(range(8)))
    outp = np.stack([np.asarray(r["out"], dtype=np.float32).T for r in res.results], axis=0)
    return np.ascontiguousarray(outp)
```

```python
import numpy as np
from contextlib import ExitStack
import concourse.bass as bass
import concourse.mybir as mybir
from concourse.bass_utils import run_bass_kernel_spmd

F32 = mybir.dt.float32
BF16 = mybir.dt.bfloat16
AF = mybir.ActivationFunctionType
ALU = mybir.AluOpType

S = 4096
D = 1024
DEPTH = 4
DFF = 2816
PW = 3088
ALPHA = float((2 * DEPTH) ** 0.25)
LN_EPS = 1e-5
HN_EPS = 1e-6
NEG = -30000.0
NFM = 16
SC_A = 64 ** -0.5
SC_L = 32 ** -0.5

C_MASKB, C_IDENT, C_LMASK, C_A1, C_A2, C_B1, C_ONES1024, C_HMR, C_HMG, C_DECR, C_ER, C_EINVR, C_KDR, C_ONES = (
    0, 256, 384, 512, 576, 640, 768, 896, 900, 904, 908, 1420, 1932, 2444)
C_LMASK4 = 2572
C_MASK01 = 3084
NCST = 3340

P_MSA, P_MSRG, P_LN1G, P_LN1B, P_LN2G, P_LN2B, P_CW, P_CB, P_BA = 0, 8, 12, 20, 28, 36, 44, 176, 220
NPL = 221


def make_consts():
    c = np.zeros((128, NCST), np.float32)
    j = np.arange(128)[:, None]
    i = np.arange(128)[None, :]
    c[:, C_MASKB:C_MASKB + 128] = np.where(j <= i, 0.0, NEG)
    c[:, C_MASKB + 128:C_MASKB + 256] = np.where(j >= i, 0.0, NEG)
    c[:, C_IDENT:C_IDENT + 128] = np.eye(128, dtype=np.float32)
    c[:, C_MASK01:C_MASK01 + 128] = (j <= i).astype(np.float32)
    c[:, C_MASK01 + 128:C_MASK01 + 256] = (j >= i).astype(np.float32)
    c[:, C_LMASK:C_LMASK + 128] = (j <= i).astype(np.float32)
    for h in range(4):
        c[:, C_LMASK4 + h * 128:C_LMASK4 + (h + 1) * 128] = (j <= i).astype(np.float32)
    c[0:64, C_A1:C_A1 + 64] = 1.0 / 64
    c[0:64, C_A2:C_A2 + 64] = 1.0 / 64
    c[64, C_A2:C_A2 + 64] = HN_EPS
    for h in range(2):
        c[h * 64:(h + 1) * 64, C_B1 + h * 64:C_B1 + (h + 1) * 64] = 1.0 / 64
    c[:, C_ONES1024:C_ONES1024 + 128] = 1.0 / 1024
    p = np.arange(128)
    headR = (p % 64) // 16
    headG = p // 32
    for h in range(4):
        c[:, C_HMR + h] = (headR == h) * SC_L
        c[:, C_HMG + h] = (headG == h) * SC_L
    lg = np.log(1.0 - np.power(2.0, -5.0 - np.arange(4, dtype=np.float64)))
    lgp = lg[headR][:, None]
    idx = (np.arange(512) % 128)[None, :].astype(np.float64)
    c[:, C_DECR:C_DECR + 4] = np.exp(lgp * 128.0)
    c[:, C_ER:C_ER + 512] = np.exp(lgp * (idx + 1.0))
    c[:, C_EINVR:C_EINVR + 512] = np.exp(-lgp * (idx + 1.0))
    c[:, C_KDR:C_KDR + 512] = np.exp(lgp * (127.0 - idx))
    c[:, C_ONES:C_ONES + 128] = 1.0
    rope = np.zeros((4, 128, S), np.float32)
    pos = np.arange(S, dtype=np.float32)[None, :]
    invA = (1.0 / (10000.0 ** (np.arange(0, 64, 2, dtype=np.float32) / 64))).astype(np.float32)
    invR = (1.0 / (10000.0 ** (np.arange(0, 32, 2, dtype=np.float32) / 32))).astype(np.float32)
    angA = (pos * invA[p % 32][:, None]).astype(np.float32)
    angR = (pos * invR[p % 16][:, None]).astype(np.float32)
    rope[0], rope[1] = np.cos(angA), np.sin(angA)
    rope[2], rope[3] = np.cos(angR), np.sin(angR)
    return c, rope


def win_perm():
    qA, kA, vA, qR, kR, vR, gR, qG, kG, vG, rG, aG = 0, 512, 1024, 1536, 1664, 1792, 2048, 2304, 2432, 2560, 2816, 3072
    cols = []
    for base in (qA, kA):
        for g in range(2):
            cols += [base + h * 64 + i for h in range(4 * g, 4 * g + 4) for i in range(32)]
            cols += [base + h * 64 + 32 + i for h in range(4 * g, 4 * g + 4) for i in range(32)]
    for half in range(2):
        cols += [qR + h * 32 + half * 16 + i for h in range(4) for i in range(16)]
        cols += [kR + h * 32 + half * 16 + i for h in range(4) for i in range(16)]
    cols += list(range(gR, gR + 256))
    cols += list(range(qG, qG + 128))
    cols += list(range(kG, kG + 128))
    cols += list(range(rG, rG + 256))
    cols += list(range(aG, aG + 16))
    cols += list(range(vA, vA + 512))
    cols += list(range(vR, vR + 256))
    cols += list(range(vG, vG + 256))
    assert len(cols) == PW and len(set(cols)) == PW
    return np.array(cols)


class Buf:
    __slots__ = ("w", "r", "pw", "pr", "name")

    def __init__(self, name=""):
        self.w = {}
        self.r = {}
        self.pw = None
        self.pr = set()
        self.name = name


class TK:
    CE = ("pe", "act", "dve", "pool")

    def __init__(self, nc, es):
        self.nc = nc
        self.E = {"pe": nc.tensor, "act": nc.scalar, "dve": nc.vector, "pool": nc.gpsimd, "sp": nc.sync}
        self.sem = {}
        self.val = {}
        self.seen = {e: {} for e in self.E}
        for e in self.CE:
            self._mk(es, "c_" + e)
        self.dq = {}
        for q, n in (("sp", 8), ("pool", 8)):
            self.dq[q] = [self._mk(es, f"d_{q}{i}") for i in range(n)]
        self.dqi = {q: 0 for q in self.dq}
        self.pending = {e: [] for e in self.CE}
        self.nins = 0

    def _mk(self, es, name):
        self.sem[name] = es.enter_context(self.nc.semaphore(name))
        self.val[name] = 0
        return name

    def wait(self, e, s, v):
        if v <= self.seen[e].get(s, 0):
            return
        self.E[e].wait_ge(self.sem[s], v)
        self.seen[e][s] = v
        self.nins += 1

    def _deps(self, e, reads, writes, dma=False):
        own = "c_" + e
        deps = {}
        for b in reads:
            assert b.pw in (None, e), f"read of {b.name} with pending writer {b.pw}"
            for s, v in b.w.items():
                if s == own and (e == "pe" and not dma):
                    continue
                if v > deps.get(s, 0):
                    deps[s] = v
        for b in writes:
            assert b.pw in (None, e), f"write of {b.name} with pending writer {b.pw}"
            assert not (b.pr - {e}), f"write of {b.name} with pending readers {b.pr}"
            for dd in (b.w, b.r):
                for s, v in dd.items():
                    if s == own and not dma:
                        continue
                    if v > deps.get(s, 0):
                        deps[s] = v
        for s, v in deps.items():
            self.wait(e, s, v)

    def op(self, e, emit, reads=(), writes=(), tick=True):
        self._deps(e, reads, writes)
        ins = emit()
        self.nins += 1
        own = "c_" + e
        self.pending[e].append((reads, writes))
        if tick:
            self.val[own] += 1
            ins.then_inc(self.sem[own], 1)
            v = self.val[own]
            for rs, ws in self.pending[e]:
                for b in rs:
                    b.r[own] = v
                    b.pr.discard(e)
                for b in ws:
                    b.w[own] = v
                    b.pw = None
            self.pending[e] = []
        else:
            for b in reads:
                b.pr.add(e)
            for b in writes:
                b.pw = e
        return ins

    def dma(self, q, out, in_, reads=(), writes=()):
        self._deps(q, reads, writes, dma=True)
        names = self.dq[q]
        nm = names[self.dqi[q] % len(names)]
        self.dqi[q] += 1
        self.wait(q, nm, self.val[nm])
        ins = self.E[q].dma_start(out=out, in_=in_)
        self.nins += 1
        self.val[nm] += 16
        ins.then_inc(self.sem[nm], 16)
        v = self.val[nm]
        for b in reads:
            b.r[nm] = v
        for b in writes:
            b.w[nm] = v
        return ins

    def barrier(self):
        for e in self.CE:
            assert not self.pending[e], f"pending un-ticked ops on {e}"
        for e in self.E:
            for s, v in self.val.items():
                if s == "c_" + e:
                    continue
                self.wait(e, s, v)

    def mm(self, out, lhsT, rhs, start, stop, reads, writes, tick=False):
        return self.op("pe", lambda: self.nc.tensor.matmul(out, lhsT=lhsT, rhs=rhs, start=start, stop=stop,
                                                           skip_group_check=True), reads, writes, tick)

    def act(self, out, in_, func, reads, writes, **kw):
        return self.op("act", lambda: self.nc.scalar.activation(out=out, in_=in_, func=func, **kw), reads, writes)

    def tt(self, e, out, in0, in1, op, reads, writes):
        return self.op(e, lambda: self.E[e].tensor_tensor(out=out, in0=in0, in1=in1, op=op), reads, writes)

    def stt(self, out, in0, scalar, in1, op0, op1, reads, writes):
        return self.op("dve", lambda: self.nc.vector.scalar_tensor_tensor(out=out, in0=in0, scalar=scalar, in1=in1,
                                                                         op0=op0, op1=op1), reads, writes)

    def ts(self, e, out, in0, s1, s2, op0, op1, reads, writes):
        return self.op(e, lambda: self.E[e].tensor_scalar(out=out, in0=in0, scalar1=s1, scalar2=s2, op0=op0, op1=op1),
                       reads, writes)

    def copy(self, e, out, in_, reads, writes):
        if e == "act":
            return self.act(out, in_, AF.Copy, reads, writes)
        return self.op(e, lambda: self.E[e].tensor_copy(out=out, in_=in_), reads, writes)

    def memset(self, e, ap, val, writes):
        return self.op(e, lambda: self.E[e].memset(ap, val), (), writes)


class Ctx:
    pass


_UNIQ = [0]


def sbt(nc, es, name, shape, dt):
    _UNIQ[0] += 1
    return es.enter_context(nc.sbuf_tensor(f"{name}_{_UNIQ[0]}", shape, dt))


def pst(nc, es, name, shape, dt=F32):
    _UNIQ[0] += 1
    return es.enter_context(nc.psum_tensor(f"{name}_{_UNIQ[0]}", shape, dt))


def layer_norm(g, es_name, z, zB, N, gcol, bcol, dst_ap, stat_ps, stat_bufs, tmp):
    tk, nc = g.tk, g.nc
    zb, zq, msq, var, rstd, nmr = tmp["zb"], tmp["zq"], tmp["msq"], tmp["var"], tmp["rstd"], tmp["nmr"]
    B = tmp["B"]
    tk.act(zb[:, :, 0:N], z[:, :, 0:N], AF.Copy, [zB], [B["zb"]])
    tk.act(zq[:, :, 0:N], z[:, :, 0:N], AF.Square, [zB], [B["zq"]])
    mean_ps, e2_ps = stat_ps
    mB, eB = stat_bufs
    for c in range(8):
        tk.mm(mean_ps[:, 0:N], g.ones_b[:], zb[:, c, 0:N], c == 0, c == 7, [g.cstB, B["zb"]], [mB], tick=(c == 7))
    for c in range(8):
        tk.mm(e2_ps[:, 0:N], g.ones_b[:], zq[:, c, 0:N], c == 0, c == 7, [g.cstB, B["zq"]], [eB], tick=(c == 7))
    tk.act(msq[:, 0:N], mean_ps[:, 0:N], AF.Square, [mB], [B["msq"]])
    tk.tt("dve", var[:, 0:N], e2_ps[:, 0:N], msq[:, 0:N], ALU.subtract, [eB, B["msq"]], [B["var"]])
    tk.act(var[:, 0:N], var[:, 0:N], AF.Ln, [B["var"]], [B["var"]], bias=g.eps_ln[:, 0:1])
    tk.act(rstd[:, 0:N], var[:, 0:N], AF.Exp, [B["var"]], [B["rstd"]], scale=-0.5)
    tk.stt(nmr[:, 0:N], mean_ps[:, 0:N], -1.0, rstd[:, 0:N], ALU.mult, ALU.mult, [mB, B["rstd"]], [B["nmr"]])
    for c in range(8):
        e1 = "dve" if c % 2 == 0 else "pool"
        tk.tt(e1, z[:, c, 0:N], z[:, c, 0:N], rstd[:, 0:N], ALU.mult, [zB, B["rstd"]], [zB])
        tk.tt("pool", z[:, c, 0:N], z[:, c, 0:N], nmr[:, 0:N], ALU.add, [zB, B["nmr"]], [zB])
        tk.act(z[:, c, 0:N], z[:, c, 0:N], AF.Identity, [zB, g.plB], [zB], scale=gcol[:, c:c + 1], bias=bcol[:, c:c + 1])
    tk.dma("sp", dst_ap.rearrange("(c p) t -> p c t", p=128), z[:, :, 0:N], [zB], [])


def head_norm(g, src, srcB, K, lhs1, lhs2, M, stat_ps, stat_bufs, tmp, N=512):
    tk = g.tk
    B = tmp["B"]
    sq, msq, var, dd = tmp["sq"], tmp["msq"], tmp["var"], tmp["dd"]
    mean_ps, e2_ps = stat_ps
    mB, eB = stat_bufs
    tk.act(sq[0:K, 0:N], src, AF.Square, [srcB], [B["sq"]])
    tk.mm(mean_ps[0:M, 0:N], lhs1, src, True, True, [g.cstB, srcB], [mB], tick=True)
    tk.mm(e2_ps[0:M, 0:N], lhs2, sq[0:K, 0:N], True, True, [g.cstB, B["sq"]], [eB], tick=True)
    tk.act(msq[0:M, 0:N], mean_ps[0:M, 0:N], AF.Square, [mB], [B["msq"]])
    tk.tt("dve", var[0:M, 0:N], e2_ps[0:M, 0:N], msq[0:M, 0:N], ALU.subtract, [eB, B["msq"]], [B["var"]])
    return mean_ps, mB


def build(depth=DEPTH, debug=False):
    nc = bass.Bass("TRN2", target_bir_lowering=False)
    g = Ctx()
    g.nc = nc
    dkind = "ExternalOutput" if debug else "Internal"
    xin = nc.dram_tensor("xin", [D, S], F32, kind="ExternalInput").ap()
    win = nc.dram_tensor("win", [DEPTH, D, PW], F32, kind="ExternalInput").ap()
    walpha = nc.dram_tensor("walpha", [DEPTH, 16, 128], F32, kind="ExternalInput").ap()
    wout = nc.dram_tensor("wout", [DEPTH, D, D], F32, kind="ExternalInput").ap()
    wup = nc.dram_tensor("wup", [DEPTH, D, 2 * DFF], F32, kind="ExternalInput").ap()
    wdown = nc.dram_tensor("wdown", [DEPTH, DFF, D], F32, kind="ExternalInput").ap()
    pl = nc.dram_tensor("pl", [DEPTH, 128, NPL], F32, kind="ExternalInput").ap()
    cst = nc.dram_tensor("cst", [128, NCST], F32, kind="ExternalInput").ap()
    rope = nc.dram_tensor("rope", [4, 128, S], F32, kind="ExternalInput").ap()
    out = nc.dram_tensor("out", [D, S], F32, kind="ExternalOutput").ap()
    XA = nc.dram_tensor("XA", [D, S], F32, kind="Internal").ap()
    X1F = nc.dram_tensor("X1F", [D, S], F32, kind=dkind).ap()
    YT = nc.dram_tensor("YT", [D, S], BF16, kind=dkind).ap()
    FMS = nc.dram_tensor("FMS", [NFM, 128, S], BF16, kind=dkind).ap()
    LA = nc.dram_tensor("LA", [128, S], F32, kind=dkind).ap()
    VA = nc.dram_tensor("VA", [S, 8, 65], BF16, kind=dkind).ap()
    VRG = nc.dram_tensor("VRG", [S, 512], BF16, kind=dkind).ap()
    HT = nc.dram_tensor("HT", [DFF, S], BF16, kind="Internal").ap()
    X1B = nc.dram_tensor("X1B", [D, S], BF16, kind="Internal").ap()

    with ExitStack() as es:
        tk = TK(nc, es)
        g.tk = tk
        block = es.enter_context(nc.Block())

        @block.sync
        def _(sync):
            cst_sb = sbt(nc, es, "cst_sb", [128, NCST], F32)
            g.cstB = Buf("cst")
            g.cst = cst_sb
            tk.dma("sp", cst_sb[:], cst[:, :], [], [g.cstB])
            g.maskb = sbt(nc, es, "maskb", [128, 256], BF16)
            g.ident_b = sbt(nc, es, "ident_b", [128, 128], BF16)
            g.ones_b = sbt(nc, es, "ones_b", [128, 128], BF16)
            g.eps_ln = sbt(nc, es, "eps_ln", [128, 1], F32)
            tk.copy("dve", g.maskb[:], cst_sb[:, C_MASKB:C_MASKB + 256], [g.cstB], [g.cstB])
            tk.copy("dve", g.ident_b[:], cst_sb[:, C_IDENT:C_IDENT + 128], [g.cstB], [g.cstB])
            tk.copy("dve", g.ones_b[:], cst_sb[:, C_ONES1024:C_ONES1024 + 128], [g.cstB], [g.cstB])
            tk.memset("dve", g.eps_ln[:], LN_EPS, [g.cstB])
            g.mask01 = sbt(nc, es, "mask01", [128, 4, 256], BF16)
            for jj in range(4):
                tk.copy("dve", g.mask01[:, jj, :], cst_sb[:, C_MASK01:C_MASK01 + 256], [g.cstB], [g.cstB])
            g.eps_hn = sbt(nc, es, "eps_hn", [128, 1], F32)
            tk.memset("dve", g.eps_hn[:], HN_EPS, [g.cstB])
            g.one_col = cst_sb[:, C_ONES:C_ONES + 1]
            g.pl_sb = sbt(nc, es, "pl_sb", [128, NPL], F32)
            g.negb = sbt(nc, es, "negb", [128, 1], F32)
            g.plB = Buf("pl")
            tk.barrier()

            for l in range(depth):
                xsrc = xin if l == 0 else XA
                xdst = out if l == depth - 1 else XA
                tk.dma("sp", g.pl_sb[:], pl[l], [], [g.plB])
                tk.ts("dve", g.negb[:], g.pl_sb[:, P_BA:P_BA + 1], -1.0, None, ALU.mult, ALU.bypass, [g.plB], [g.plB])
                phase_P(g, l, xsrc, win, walpha, rope, FMS, LA, VA, VRG)
                tk.barrier()
                phase_A(g, l, FMS, VA, YT)
                tk.barrier()
                phase_L(g, l, FMS, LA, VRG, YT)
                tk.barrier()
                phase_O(g, l, xsrc, wout, YT, X1F, X1B)
                tk.barrier()
                phase_F(g, l, wup, wdown, X1F, X1B, xdst, HT)
                tk.barrier()
    g.nins = tk.nins
    return nc, g


def phase_P(g, l, xsrc, win, walpha, rope, FMS, LA, VA, VRG):
    tk, nc = g.tk, g.nc
    with ExitStack() as es:
        wfm = sbt(nc, es, "wfm", [128, 8, 2064], BF16)
        wtm = sbt(nc, es, "wtm", [128, 8, 1024], BF16)
        wal = sbt(nc, es, "wal", [16, 128], F32)
        wB = Buf("w")
        xt = [sbt(nc, es, f"xt{i}", [128, 8, 512], BF16) for i in range(2)]
        xtB = [Buf(f"xt{i}") for i in range(2)]
        rp = [sbt(nc, es, f"rp{i}", [128, 4, 512], F32) for i in range(2)]
        rpB = [Buf(f"rp{i}") for i in range(2)]
        NSO = 6
        so = [sbt(nc, es, f"so{i}", [128, 512], BF16) for i in range(NSO)]
        soB = [Buf(f"so{i}") for i in range(NSO)]
        tmpf = [[sbt(nc, es, f"rt{s}_{i}", [128, 512], F32) for i in range(4)] for s in range(2)]
        tmpB = [[Buf(f"rt{s}_{i}") for i in range(4)] for s in range(2)]
        vst = [sbt(nc, es, f"vst{i}", [128, 8, 65], BF16) for i in range(2)]
        vstB = [Buf(f"vst{i}") for i in range(2)]
        vrg = [sbt(nc, es, f"vrg{i}", [128, 512], BF16) for i in range(2)]
        vrgB = [Buf(f"vrg{i}") for i in range(2)]
        ag = sbt(nc, es, "ag", [16, 512], F32)
        agB = Buf("ag")
        ez = sbt(nc, es, "ez", [128, 512], F32)
        ezB = Buf("ez")
        lst = [sbt(nc, es, f"lst{i}", [128, 512], F32) for i in range(2)]
        lstB = [Buf(f"lst{i}") for i in range(2)]
        pb = [pst(nc, es, f"pb{i}", [128, 512]) for i in range(8)]
        pbB = [Buf(f"pb{i}") for i in range(8)]
        st = {"bank": 0, "so": 0, "ts": 0, "v": 0, "ev": 0}

        def nbank():
            i = st["bank"] % 8
            st["bank"] += 1
            return i

        def nso():
            i = st["so"] % NSO
            st["so"] += 1
            return i

        for i in range(2):
            tk.memset("pool", vst[i][:], 1.0, [vstB[i]])
        for kc in range(8):
            tk.dma("pool", wfm[:, kc, :], win[l, kc * 128:(kc + 1) * 128, 0:2064], [], [wB])
            tk.dma("pool", wtm[:, kc, :], win[l, kc * 128:(kc + 1) * 128, 2064:PW], [], [wB])
        tk.dma("sp", wal[:], walpha[l], [], [wB])
        xv = xsrc.rearrange("(c p) t -> p c t", p=128)
        rv = rope.rearrange("f p t -> p f t")

        def load(T):
            tk.dma("pool", xt[T % 2][:], xv[:, :, T * 512:(T + 1) * 512], [], [xtB[T % 2]])
            tk.dma("sp", rp[T % 2][:], rv[:, :, T * 512:(T + 1) * 512], [], [rpB[T % 2]])

        load(0)
        for T in range(8):
            if T + 1 < 8:
                load(T + 1)
            x_, xB_ = xt[T % 2], xtB[T % 2]
            r_, rB_ = rp[T % 2], rpB[T % 2]
            tsl = slice(T * 512, (T + 1) * 512)

            def fm_mm(tile, bank):
                for kc in range(8):
                    tk.mm(pb[bank][:, :], wfm[:, kc, tile * 128:(tile + 1) * 128], x_[:, kc, :], kc == 0, kc == 7,
                          [wB, xB_], [pbB[bank]], tick=(kc == 7))

            def store_fm(tile, si):
                tk.dma("sp", FMS[tile, :, tsl], so[si][:], [soB[si]], [])

            def tm_block(blk):
                for half in range(2):
                    b = nbank()
                    for kc in range(8):
                        tk.mm(pb[b][:, :], x_[:, kc, blk * 128:(blk + 1) * 128], wtm[:, kc, half * 512:(half + 1) * 512],
                              kc == 0, kc == 7, [wB, xB_], [pbB[b]], tick=(kc == 7))
                    vi = st["v"] % 2
                    eng = "act" if st["ev"] % 2 == 0 else "dve"
                    st["ev"] += 1
                    rows = slice(T * 512 + blk * 128, T * 512 + (blk + 1) * 128)
                    if half == 0:
                        tk.copy(eng, vst[vi][:, :, 0:64], pb[b][:, :].rearrange("p (h d) -> p h d", d=64), [pbB[b]], [vstB[vi]])
                        tk.dma("sp", VA[rows, :, :], vst[vi][:], [vstB[vi]], [])
                    else:
                        tk.copy(eng, vrg[vi][:], pb[b][:, :], [pbB[b]], [vrgB[vi]])
                        tk.dma("sp", VRG[rows, :], vrg[vi][:], [vrgB[vi]], [])
                        st["v"] += 1

            pairs = [(0, 1, 0), (2, 3, 0), (4, 5, 0), (6, 7, 0), (8, 9, 2)]
            for pi, (ta, tb, ro) in enumerate(pairs):
                ba, bb = nbank(), nbank()
                fm_mm(ta, ba)
                fm_mm(tb, bb)
                C_ = r_[:, ro, :]
                S_ = r_[:, ro + 1, :]
                s = st["ts"] % 2
                st["ts"] += 1
                t1, t2, t3, t4 = tmpf[s]
                b1, b2, b3, b4 = tmpB[s]
                tk.tt("dve", t1[:], pb[ba][:, :], C_, ALU.mult, [pbB[ba], rB_], [b1])
                tk.tt("dve", t2[:], pb[bb][:, :], S_, ALU.mult, [pbB[bb], rB_], [b2])
                tk.tt("dve", t3[:], pb[ba][:, :], S_, ALU.mult, [pbB[ba], rB_], [b3])
                tk.tt("dve", t4[:], pb[bb][:, :], C_, ALU.mult, [pbB[bb], rB_], [b4])
                sa = nso()
                tk.tt("pool", so[sa][:], t1[:], t2[:], ALU.subtract, [b1, b2], [soB[sa]])
                store_fm(ta, sa)
                sb_ = nso()
                tk.tt("pool", so[sb_][:], t3[:], t4[:], ALU.add, [b3, b4], [soB[sb_]])
                store_fm(tb, sb_)
                if pi < 4:
                    tm_block(pi)
            for tile, kind in ((10, "silu"), (11, "silu"), (12, "copy"), (13, "copy"), (14, "silu"), (15, "silu")):
                b = nbank()
                fm_mm(tile, b)
                si = nso()
                tk.act(so[si][:], pb[b][:, :], AF.Silu if kind == "silu" else AF.Copy, [pbB[b]], [soB[si]])
                store_fm(tile, si)
            b = nbank()
            for kc in range(8):
                tk.mm(pb[b][0:16, :], wfm[:, kc, 2048:2064], x_[:, kc, :], kc == 0, kc == 7, [wB, xB_], [pbB[b]], tick=(kc == 7))
            tk.act(ag[:], pb[b][0:16, :], AF.Copy, [pbB[b]], [agB])
            b2_ = nbank()
            tk.mm(pb[b2_][:, :], wal[:], ag[:], True, True, [wB, agB], [pbB[b2_]], tick=True)
            tk.act(ez[:], pb[b2_][:, :], AF.Exp, [pbB[b2_], g.plB], [ezB], scale=-1.0, bias=g.negb[:, 0:1])
            li = T % 2
            tk.act(lst[li][:], ez[:], AF.Ln, [ezB, g.cstB], [lstB[li]], bias=g.one_col)
            tk.dma("sp", LA[:, tsl], lst[li][:], [lstB[li]], [])


def phase_A(g, l, FMS, VA, YT):
    tk, nc = g.tk, g.nc
    with ExitStack() as es:
        qh = [sbt(nc, es, f"qh{i}", [128, S], BF16) for i in range(2)]
        kh = [sbt(nc, es, f"kh{i}", [128, S], BF16) for i in range(2)]
        v3 = [sbt(nc, es, f"v3{i}", [128, 3, 32, 65], BF16) for i in range(2)]
        inB = [Buf(f"ain{i}") for i in range(2)]
        acc = [sbt(nc, es, f"acc{i}", [65, S], F32) for i in range(2)]
        accB = [Buf(f"acc{i}") for i in range(2)]
        PT = [sbt(nc, es, f"PT{i}", [128, 1024], BF16) for i in range(3)]
        PTB = [[Buf(f"PT{i}a"), Buf(f"PT{i}b")] for i in range(3)]
        yst = [sbt(nc, es, f"yst{i}", [64, S], BF16) for i in range(2)]
        ystB = [Buf(f"yst{i}") for i in range(2)]
        tmp = {"B": {k: Buf("a_" + k) for k in ("sq", "msq", "var", "dd")}}
        for k in ("sq", "msq", "var", "dd"):
            tmp[k] = sbt(nc, es, "a_" + k, [65, 512], F32)
        ST = [pst(nc, es, f"ST{i}", [128, 1024]) for i in range(2)]
        STB = [Buf(f"ST{i}") for i in range(2)]
        Op = [pst(nc, es, f"Op{i}", [128, 512]) for i in range(2)]
        OpB = [Buf(f"Op{i}") for i in range(2)]
        stat = [pst(nc, es, f"astat{i}", [128, 512]) for i in range(2)]
        statB = [Buf(f"astat{i}") for i in range(2)]
        cs = g.cst
        A1 = cs[0:65, C_A1:C_A1 + 64]
        A2 = cs[0:65, C_A2:C_A2 + 64]

        def load_head(h):
            i = h % 2
            gI, hh = h // 4, h % 4
            rows = slice(hh * 32, hh * 32 + 32)
            tk.dma("sp", qh[i][0:32, :], FMS[2 * gI, rows, :], [], [inB[i]])
            tk.dma("sp", qh[i][32:64, :], FMS[2 * gI + 1, rows, :], [], [inB[i]])
            tk.dma("sp", kh[i][0:32, :], FMS[4 + 2 * gI, rows, :], [], [inB[i]])
            tk.dma("sp", kh[i][32:64, :], FMS[4 + 2 * gI + 1, rows, :], [], [inB[i]])
            for di, d in enumerate((1, 4, 16)):
                src = VA[:, h, :].rearrange("(n j r) c -> j r n c", j=128, r=d)
                dst = v3[i][:, di, :, :].rearrange("p (r n) c -> p r n c", r=d)
                nb_ = 32 // d
                if d == 1:
                    for q4 in range(4):
                        tk.dma("sp", dst[:, :, q4 * 8:(q4 + 1) * 8, :], src[:, :, q4 * 8:(q4 + 1) * 8, :], [], [inB[i]])
                elif d == 4:
                    for r_ in range(4):
                        tk.dma("sp", dst[:, r_, :, :], src[:, r_, :, :], [], [inB[i]])
                else:
                    for n_ in range(2):
                        for hf in range(2):
                            tk.dma("sp", dst[:, hf * 8:(hf + 1) * 8, n_, :], src[:, hf * 8:(hf + 1) * 8, n_, :], [], [inB[i]])

        batches = []
        for h in range(8):
            for di, d in enumerate((1, 4, 16)):
                nb = 32 // d
                blocks = [(r, n, r * nb + n) for r in range(d) for n in range(nb)]
                for b0 in range(0, 32, 4):
                    batches.append((h, di, d, nb, blocks[b0:b0 + 4]))
        NB = len(batches)

        def emit_ST(gi):
            h, di, d, nb, blks = batches[gi]
            i = h % 2
            sbuf = gi % 2
            qv = qh[i][:, :].rearrange("p (m r) -> p r m", r=d)
            kv = kh[i][:, :].rearrange("p (m r) -> p r m", r=d)
            for j, (r, n, b) in enumerate(blks):
                qn = 256 if n < nb - 1 else 128
                o_ = ST[sbuf][:, j * 256:j * 256 + qn]
                tk.mm(o_, kv[:, r, n * 128:(n + 1) * 128], qv[:, r, n * 128:n * 128 + qn], True, True,
                      [inB[i]], [STB[sbuf]], tick=(j == 3))

        def emit_exp(gi):
            p3 = gi % 3
            tk.act(PT[p3][:], ST[gi % 2][:, :], AF.Exp, [STB[gi % 2]], PTB[p3], scale=SC_A)
            m01 = g.mask01[:].rearrange("p a b -> p (a b)")
            tk.tt("dve", PT[p3][:, 0:512], PT[p3][:, 0:512], m01[:, 0:512], ALU.mult, [PTB[p3][0], g.cstB], [PTB[p3][0]])
            tk.tt("pool", PT[p3][:, 512:1024], PT[p3][:, 512:1024], m01[:, 512:1024], ALU.mult, [PTB[p3][1], g.cstB], [PTB[p3][1]])

        def emit_PV(gi):
            h, di, d, nb, blks = batches[gi]
            i = h % 2
            ob = gi % 2
            cur = PT[gi % 3]
            prv = PT[(gi - 1) % 3]
            for j, (r, n, b) in enumerate(blks):
                o_ = Op[ob][0:65, j * 128:(j + 1) * 128]
                rd = [inB[i]] + PTB[gi % 3]
                if n > 0:
                    if j > 0:
                        pprev = cur[:, (j - 1) * 256 + 128:(j - 1) * 256 + 256]
                    else:
                        pprev = prv[:, 3 * 256 + 128:4 * 256]
                        rd = rd + PTB[(gi - 1) % 3]
                    tk.mm(o_, v3[i][:, di, b - 1, :], pprev, True, False, rd, [OpB[ob]])
                    tk.mm(o_, v3[i][:, di, b, :], cur[:, j * 256:j * 256 + 128], False, True, rd, [OpB[ob]], tick=(j == 3))
                else:
                    tk.mm(o_, v3[i][:, di, b, :], cur[:, j * 256:j * 256 + 128], True, True, rd, [OpB[ob]], tick=(j == 3))

        def emit_evac(gi):
            h, di, d, nb, blks = batches[gi]
            a, aB = acc[h % 2], accB[h % 2]
            ob = gi % 2
            o_ = Op[ob][0:65, :]
            r0, n0, b0 = blks[0]
            if d == 1:
                tk.copy("dve", a[:, n0 * 128:n0 * 128 + 512], o_, [OpB[ob]], [aB])
            elif d == 4:
                av = a[:, :].rearrange("p (m q) -> p q m", q=4)[:, r0, n0 * 128:n0 * 128 + 512]
                tk.tt("dve", av, av, o_, ALU.add, [OpB[ob], aB], [aB])
            else:
                av = a[:, :].rearrange("p (m q) -> p q m", q=16)[:, r0:r0 + 2, :]
                tk.tt("dve", av, av, o_.rearrange("p (a m) -> p a m", a=2), ALU.add, [OpB[ob], aB], [aB])

        def emit_post(h, t):
            a, aB = acc[h % 2], accB[h % 2]
            B = tmp["B"]
            src = a[0:65, t * 512:(t + 1) * 512]
            mean_ps, mB = head_norm(g, src, aB, 65, A1, A2, 64, (stat[0], stat[1]), (statB[0], statB[1]), tmp)
            var, dd = tmp["var"], tmp["dd"]
            tk.act(var[0:64, :], var[0:64, :], AF.Ln, [B["var"]], [B["var"]])
            tk.act(var[0:64, :], var[0:64, :], AF.Exp, [B["var"]], [B["var"]], scale=-0.5)
            tk.tt("dve", dd[0:64, :], a[0:64, t * 512:(t + 1) * 512], mean_ps[0:64, :], ALU.subtract, [aB, mB], [B["dd"]])
            y, yB = yst[h % 2], ystB[h % 2]
            tk.stt(y[:, t * 512:(t + 1) * 512], dd[0:64, :], g.pl_sb[0:64, P_MSA + h:P_MSA + h + 1], var[0:64, :],
                   ALU.mult, ALU.mult, [B["dd"], B["var"], g.plB], [yB])
            if t == 7:
                tk.dma("sp", YT[h * 64:(h + 1) * 64, :], y[:, :], [yB], [])

        for i in range(2):
            tk.memset("dve", qh[i][64:128, :], 0.0, [inB[i]])
            tk.memset("pool", kh[i][64:128, :], 0.0, [inB[i]])
        load_head(0)
        posts = []
        emit_ST(0)
        emit_ST(1)
        emit_exp(0)
        for gi in range(NB):
            h = batches[gi][0]
            first_of_head = (gi % 24 == 0)
            if first_of_head and h + 1 < 8:
                load_head(h + 1)
            if gi + 2 < NB:
                emit_ST(gi + 2)
            if gi + 1 < NB:
                emit_exp(gi + 1)
            emit_PV(gi)
            emit_evac(gi)
            if posts:
                emit_post(*posts.pop(0))
            if gi % 24 == 23:
                posts += [(h, t) for t in range(8)]
        while posts:
            emit_post(*posts.pop(0))


def phase_L(g, l, FMS, LA, VRG, YT):
    tk, nc = g.tk, g.nc
    cs = g.cst
    with ExitStack() as es:
        G = []
        for grp in range(2):
            d = Ctx()
            n = f"L{grp}_"
            d.qf = [sbt(nc, es, n + f"q{i}", [128, 512], BF16) for i in range(2)]
            d.kf = [sbt(nc, es, n + f"k{i}", [128, 512], BF16) for i in range(2)]
            d.vt = [sbt(nc, es, n + f"v{i}", [128, 4, 256], BF16) for i in range(2)]
            d.gt = [sbt(nc, es, n + f"g{i}", [128, 2, 512], BF16) for i in range(2)]
            d.lt = [sbt(nc, es, n + f"l{i}", [128, 512], F32) for i in range(2)] if grp == 1 else None
            d.inB = [Buf(n + f"in{i}") for i in range(2)]
            names = ("cum", "E", "Einv", "Kd", "dec", "Qbd", "kt", "ktil", "ktok", "og0", "og1", "sq", "msq", "var", "dd")
            d.B = {k: Buf(n + k) for k in names}
            if grp == 1:
                d.cum = sbt(nc, es, n + "cum", [128, 512], F32)
                d.Eg = sbt(nc, es, n + "E", [128, 512], F32)
                d.Einvg = sbt(nc, es, n + "Einv", [128, 512], F32)
                d.Kdg = sbt(nc, es, n + "Kd", [128, 512], F32)
                d.decg = sbt(nc, es, n + "dec", [128, 4], F32)
            d.Qbd = sbt(nc, es, n + "Qbd", [128, 4, 512], BF16)
            d.kt = sbt(nc, es, n + "kt", [128, 512], BF16)
            d.ktil = sbt(nc, es, n + "ktil", [128, 512], BF16)
            d.ktok = sbt(nc, es, n + "ktok", [128, 4, 128], BF16)
            d.stf = sbt(nc, es, n + "stf", [128, 256], F32)
            d.stfB = Buf(n + "stf")
            d.stb = [sbt(nc, es, n + f"stb{i}", [128, 256], BF16) for i in range(2)]
            d.stbB = [Buf(n + f"stb{i}") for i in range(2)]
            d.nst = 0
            d.og = [sbt(nc, es, n + f"og{j}", [128, 512], F32) for j in range(2)]
            d.tmp = {"B": d.B}
            for k in ("sq", "msq", "var", "dd"):
                d.tmp[k] = sbt(nc, es, n + k, [128, 512], F32)
            d.yy = [sbt(nc, es, n + f"yy{i}", [128, 512], BF16) for i in range(2)]
            d.yyB = [Buf(n + f"yy{i}") for i in range(2)]
            d.nyy = 0
            d.hm = cs[:, C_HMR:C_HMR + 4] if grp == 0 else cs[:, C_HMG:C_HMG + 4]
            d.ch0 = 512 + grp * 256
            G.append(d)
        PT = [sbt(nc, es, f"l_PT{i}", [128, 4, 128], BF16) for i in range(2)]
        PTB = [Buf(f"l_PT{i}") for i in range(2)]
        STp = [pst(nc, es, f"l_ST{i}", [128, 512]) for i in range(2)]
        STpB = [Buf(f"l_ST{i}") for i in range(2)]
        kvp = pst(nc, es, "l_kv", [128, 512])
        kvB = Buf("l_kvp")
        Opp = [pst(nc, es, f"l_O{i}", [128, 512]) for i in range(2)]
        OppB = [Buf(f"l_O{i}") for i in range(2)]
        trp = pst(nc, es, "l_tr", [128, 1024], BF16)
        trB = Buf("l_tr")
        stat = [pst(nc, es, f"l_stat{i}", [128, 512]) for i in range(2)]
        statB = [Buf(f"l_stat{i}") for i in range(2)]
        B1 = cs[:, C_B1:C_B1 + 128]
        lmask4 = cs[:, C_LMASK4:C_LMASK4 + 512]
        ones = cs[:, C_ONES:C_ONES + 128]
        cnt = {"pt": 0}

        def load(T, grp):
            d = G[grp]
            i = T % 2
            tsl = slice(T * 512, (T + 1) * 512)
            iB = d.inB[i]
            if grp == 0:
                tk.dma("sp", d.qf[i][0:64, :], FMS[8, 0:64, tsl], [], [iB])
                tk.dma("sp", d.qf[i][64:128, :], FMS[9, 0:64, tsl], [], [iB])
                tk.dma("sp", d.kf[i][0:64, :], FMS[8, 64:128, tsl], [], [iB])
                tk.dma("sp", d.kf[i][64:128, :], FMS[9, 64:128, tsl], [], [iB])
                g0 = 10
            else:
                tk.dma("sp", d.qf[i][:], FMS[12, :, tsl], [], [iB])
                tk.dma("sp", d.kf[i][:], FMS[13, :, tsl], [], [iB])
                tk.dma("sp", d.lt[i][:], LA[:, tsl], [], [iB])
                g0 = 14
            tk.dma("sp", d.gt[i][:, 0, :], FMS[g0, :, tsl], [], [iB])
            tk.dma("sp", d.gt[i][:, 1, :], FMS[g0 + 1, :, tsl], [], [iB])
            tk.dma("sp", d.vt[i][:], VRG[tsl, grp * 256:(grp + 1) * 256].rearrange("(c s) v -> s c v", s=128), [], [iB])

        def tables(d, grp):
            if grp == 0:
                return (cs[:, C_ER:C_ER + 512], cs[:, C_EINVR:C_EINVR + 512], cs[:, C_KDR:C_KDR + 512],
                        cs[:, C_DECR:C_DECR + 4], g.cstB, g.cstB, g.cstB, g.cstB)
            B = d.B
            return d.Eg[:], d.Einvg[:], d.Kdg[:], d.decg[:], B["E"], B["Einv"], B["Kd"], B["dec"]

        def prep(T, grp):
            d = G[grp]
            B = d.B
            i = T % 2
            iB = d.inB[i]
            q_, k_ = d.qf[i], d.kf[i]
            if T == 0:
                tk.memset("dve", d.stf[:], 0.0, [d.stfB])
                tk.memset("pool", d.stb[0][:], 0.0, [d.stbB[0]])
                d.nst = 0
            if grp == 1:
                l_ = d.lt[i]
                for c in range(4):
                    csl = slice(c * 128, (c + 1) * 128)
                    tk.op("dve", lambda csl=csl: nc.vector.tensor_tensor_scan(
                        out=d.cum[:, csl], data0=ones, data1=l_[:, csl], initial=0.0, op0=ALU.mult, op1=ALU.add),
                        [iB, g.cstB], [B["cum"]])
                tk.act(d.Eg[:], d.cum[:], AF.Exp, [B["cum"]], [B["E"]], scale=-1.0 / 16)
                tk.act(d.Einvg[:], d.cum[:], AF.Exp, [B["cum"]], [B["Einv"]], scale=1.0 / 16)
                tk.act(d.decg[:], d.cum[:].rearrange("p (c s) -> p c s", s=128)[:, :, 127], AF.Exp, [B["cum"]], [B["dec"]],
                       scale=-1.0 / 16)
                for c in range(4):
                    csl = slice(c * 128, (c + 1) * 128)
                    tk.ts("pool", d.Kdg[:, csl], d.Einvg[:, csl], d.decg[:, c:c + 1], None, ALU.mult, ALU.bypass,
                          [B["Einv"], B["dec"]], [B["Kd"]])
            E_, Einv_, Kd_, dec_, EB, EinvB, KdB, decB = tables(d, grp)
            for h in range(4):
                tk.stt(d.Qbd[:, h, :], q_[:], d.hm[:, h:h + 1], E_, ALU.mult, ALU.mult, [iB, g.cstB, EB], [B["Qbd"]])
            tk.tt("pool", d.kt[:], k_[:], Einv_, ALU.mult, [iB, EinvB], [B["kt"]])
            tk.tt("pool", d.ktil[:], k_[:], Kd_, ALU.mult, [iB, KdB], [B["ktil"]])

        def core(T, grp):
            d = G[grp]
            B = d.B
            i = T % 2
            iB = d.inB[i]
            v_ = d.vt[i]
            E_, Einv_, Kd_, dec_, EB, EinvB, KdB, decB = tables(d, grp)
            for c in range(4):
                tk.op("pe", lambda c=c: nc.tensor.transpose(out=trp[:, c * 128:(c + 1) * 128],
                                                            in_=d.ktil[:, c * 128:(c + 1) * 128], identity=g.ident_b[:]),
                      [B["ktil"], g.cstB], [trB], tick=(c == 3))
            tk.copy("act", d.ktok[:].rearrange("p c s -> p (c s)"), trp[:, 0:512], [trB], [B["ktok"]])
            sps = []

            def emit_ST(c):
                csl = slice(c * 128, (c + 1) * 128)
                sp_ = cnt["pt"] % 2
                cnt["pt"] += 1
                sps.append(sp_)
                tk.mm(STp[sp_][:, :], d.kt[:, csl], d.Qbd[:, :, csl], True, True, [B["kt"], B["Qbd"]], [STpB[sp_]], tick=True)
                tk.tt("dve", PT[sp_][:], STp[sp_][:, :].rearrange("p (h c) -> p h c", h=4),
                      lmask4.rearrange("p (h c) -> p h c", h=4), ALU.mult, [STpB[sp_], g.cstB], [PTB[sp_]])

            emit_ST(0)
            for c in range(4):
                csl = slice(c * 128, (c + 1) * 128)
                if c + 1 < 4:
                    emit_ST(c + 1)
                sp_ = sps[c]
                kvs = slice((c % 2) * 256, (c % 2) * 256 + 256)
                tk.mm(kvp[:, kvs], d.ktok[:, c, :], v_[:, c, :], True, True, [B["ktok"], iB], [kvB], tick=True)
                sbi = d.nst % 2
                for h in range(4):
                    j, half = h // 2, h % 2
                    o_ = Opp[j][half * 64:(half + 1) * 64, csl]
                    tk.mm(o_, v_[:, c, h * 64:(h + 1) * 64], PT[sp_][:, h, :], True, False, [iB, PTB[sp_]], [OppB[j]])
                    tk.mm(o_, d.stb[sbi][:, h * 64:(h + 1) * 64], d.Qbd[:, h, csl], False, True,
                          [d.stbB[sbi], B["Qbd"]], [OppB[j]], tick=(h % 2 == 1))
                tk.stt(d.stf[:], d.stf[:], dec_[:, c:c + 1], kvp[:, kvs], ALU.mult, ALU.add, [d.stfB, decB, kvB], [d.stfB])
                d.nst += 1
                sbn = d.nst % 2
                tk.copy("dve", d.stb[sbn][:], d.stf[:], [d.stfB], [d.stbB[sbn]])
            for j in range(2):
                tk.copy("act", d.og[j][:], Opp[j][:, :], [OppB[j]], [B[f"og{j}"]])

        def norm(T, grp):
            d = G[grp]
            B = d.B
            i = T % 2
            iB = d.inB[i]
            g_ = d.gt[i]
            for j in range(2):
                ogB = B[f"og{j}"]
                mean_ps, mB = head_norm(g, d.og[j][:], ogB, 128, B1, B1, 128, (stat[0], stat[1]), (statB[0], statB[1]), d.tmp)
                var, dd = d.tmp["var"], d.tmp["dd"]
                tk.act(var[:], var[:], AF.Ln, [B["var"]], [B["var"]], bias=g.eps_hn[:, 0:1])
                tk.act(var[:], var[:], AF.Exp, [B["var"]], [B["var"]], scale=-0.5)
                tk.tt("dve", dd[:], d.og[j][:], mean_ps[:, :], ALU.subtract, [ogB, mB], [B["dd"]])
                tk.stt(dd[:], dd[:], g.pl_sb[:, P_MSRG + grp * 2 + j:P_MSRG + grp * 2 + j + 1], var[:],
                       ALU.mult, ALU.mult, [B["dd"], B["var"], g.plB], [B["dd"]])
                yi = d.nyy % 2
                d.nyy += 1
                tk.tt("pool", d.yy[yi][:], dd[:], g_[:, j, :], ALU.mult, [B["dd"], iB], [d.yyB[yi]])
                tk.dma("sp", YT[d.ch0 + j * 128:d.ch0 + (j + 1) * 128, T * 512:(T + 1) * 512], d.yy[yi][:], [d.yyB[yi]], [])

        items = [(T, grp) for T in range(8) for grp in range(2)]
        NI = len(items)
        load(*items[0])
        load(*items[1])
        for s in range(NI + 2):
            if s + 2 < NI:
                pass
            if s < NI:
                prep(*items[s])
            if 0 <= s - 1 < NI:
                core(*items[s - 1])
            if 0 <= s - 2 < NI:
                norm(*items[s - 2])
                if s < NI:
                    pass
            if s + 2 < NI:
                load(*items[s + 2])


def ln_alloc(nc, es, pfx, N):
    tmp = {"B": {k: Buf(pfx + k) for k in ("zb", "zq", "msq", "var", "rstd", "nmr")}}
    tmp["zb"] = sbt(nc, es, pfx + "zb", [128, 8, N], BF16)
    tmp["zq"] = sbt(nc, es, pfx + "zq", [128, 8, N], BF16)
    for k in ("msq", "var", "rstd", "nmr"):
        tmp[k] = sbt(nc, es, pfx + k, [128, N], F32)
    return tmp


def ln_part1(g, z, zB, N, tmp):
    tk = g.tk
    B = tmp["B"]
    tk.copy("dve", tmp["zb"][:, :, 0:N], z[:, :, 0:N], list(zB), [B["zb"]])
    tk.act(tmp["zq"][:, :, 0:N], z[:, :, 0:N], AF.Square, list(zB), [B["zq"]])


def ln_part2(g, z, zB, N, gcol, bcol, dst_ap, stat_ps, stat_bufs, tmp, dst_bf=None):
    tk = g.tk
    zb, zq, msq, var, rstd, nmr = tmp["zb"], tmp["zq"], tmp["msq"], tmp["var"], tmp["rstd"], tmp["nmr"]
    B = tmp["B"]
    mean_ps, e2_ps = stat_ps
    mB, eB = stat_bufs
    for c in range(8):
        tk.mm(mean_ps[:, 0:N], g.ones_b[:], zb[:, c, 0:N], c == 0, c == 7, [g.cstB, B["zb"]], [mB], tick=(c == 7))
    for c in range(8):
        tk.mm(e2_ps[:, 0:N], g.ones_b[:], zq[:, c, 0:N], c == 0, c == 7, [g.cstB, B["zq"]], [eB], tick=(c == 7))
    tk.act(msq[:, 0:N], mean_ps[:, 0:N], AF.Square, [mB], [B["msq"]])
    tk.tt("dve", var[:, 0:N], e2_ps[:, 0:N], msq[:, 0:N], ALU.subtract, [eB, B["msq"]], [B["var"]])
    tk.act(var[:, 0:N], var[:, 0:N], AF.Ln, [B["var"]], [B["var"]], bias=g.eps_ln[:, 0:1])
    tk.act(rstd[:, 0:N], var[:, 0:N], AF.Exp, [B["var"]], [B["rstd"]], scale=-0.5)
    tk.stt(nmr[:, 0:N], mean_ps[:, 0:N], -1.0, rstd[:, 0:N], ALU.mult, ALU.mult, [mB, B["rstd"]], [B["nmr"]])
    for c in range(8):
        e2 = "pool" if c % 2 == 0 else "dve"
        tk.tt("dve", z[:, c, 0:N], z[:, c, 0:N], rstd[:, 0:N], ALU.mult, [zB[c], B["rstd"]], [zB[c]])
        tk.tt(e2, z[:, c, 0:N], z[:, c, 0:N], nmr[:, 0:N], ALU.add, [zB[c], B["nmr"]], [zB[c]])
        tk.act(z[:, c, 0:N], z[:, c, 0:N], AF.Identity, [zB[c], g.plB], [zB[c]], scale=gcol[:, c:c + 1], bias=bcol[:, c:c + 1])
    tk.dma("sp", dst_ap.rearrange("(c p) t -> p c t", p=128), z[:, :, 0:N], list(zB), [])
    if dst_bf is not None:
        tk.copy("dve", zb[:, :, 0:N], z[:, :, 0:N], list(zB), [B["zb"]])
        tk.dma("sp", dst_bf.rearrange("(c p) t -> p c t", p=128), zb[:, :, 0:N], [B["zb"]], [])


def phase_O(g, l, xsrc, wout, YT, X1F, X1B):
    tk, nc = g.tk, g.nc
    with ExitStack() as es:
        wo = sbt(nc, es, "wo", [128, 8, D], BF16)
        wB = Buf("wo")
        yt = [sbt(nc, es, f"o_yt{i}", [128, 8, 512], BF16) for i in range(2)]
        xr = [sbt(nc, es, f"o_xr{i}", [128, 8, 512], F32) for i in range(2)]
        inB = [Buf(f"o_in{i}") for i in range(2)]
        z = [sbt(nc, es, f"o_z{i}", [128, 8, 512], F32) for i in range(2)]
        zB = [[Buf(f"o_z{i}_{c}") for c in range(8)] for i in range(2)]
        tmp = ln_alloc(nc, es, "o_", 512)
        pb = [pst(nc, es, f"o_pb{i}", [128, 512]) for i in range(6)]
        pbB = [Buf(f"o_pb{i}") for i in range(6)]
        stat = [pst(nc, es, f"o_stat{i}", [128, 512]) for i in range(2)]
        statB = [Buf(f"o_stat{i}") for i in range(2)]
        for kc in range(8):
            tk.dma("pool", wo[:, kc, :], wout[l, kc * 128:(kc + 1) * 128, :], [], [wB])
        yv = YT.rearrange("(c p) t -> p c t", p=128)
        xv = xsrc.rearrange("(c p) t -> p c t", p=128)
        gcol, bcol = g.pl_sb[:, P_LN1G:P_LN1G + 8], g.pl_sb[:, P_LN1B:P_LN1B + 8]

        def load(T):
            tk.dma("sp", yt[T % 2][:], yv[:, :, T * 512:(T + 1) * 512], [], [inB[T % 2]])
            tk.dma("sp", xr[T % 2][:], xv[:, :, T * 512:(T + 1) * 512], [], [inB[T % 2]])

        def fin(T):
            i = T % 2
            ln_part2(g, z[i], zB[i], 512, gcol, bcol, X1F[:, T * 512:(T + 1) * 512], (stat[0], stat[1]),
                     (statB[0], statB[1]), tmp, dst_bf=X1B[:, T * 512:(T + 1) * 512])

        load(0)
        nb = 0
        pend = None
        for T in range(8):
            if T + 1 < 8:
                load(T + 1)
            i = T % 2
            for oc in range(8):
                b = nb % 6
                nb += 1
                for kc in range(8):
                    tk.mm(pb[b][:, :], wo[:, kc, oc * 128:(oc + 1) * 128], yt[i][:, kc, :], kc == 0, kc == 7,
                          [wB, inB[i]], [pbB[b]], tick=(kc == 7))
                tk.stt(z[i][:, oc, :], xr[i][:, oc, :], ALPHA, pb[b][:, :], ALU.mult, ALU.add, [inB[i], pbB[b]], [zB[i][oc]])
                if oc == 3 and pend is not None:
                    fin(pend)
                    pend = None
            ln_part1(g, z[i], zB[i], 512, tmp)
            pend = T
        fin(pend)


def phase_F(g, l, wup, wdown, X1F, X1B, xdst, HT):
    tk, nc = g.tk, g.nc
    NT = 256
    NTI = S // NT
    xv = X1F.rearrange("(c p) t -> p c t", p=128)
    xbv = X1B.rearrange("(c p) t -> p c t", p=128)
    cw = g.pl_sb[:, P_CW:P_CW + 132]
    cb = g.pl_sb[:, P_CB:P_CB + 44]
    with ExitStack() as eso:
        wd = sbt(nc, eso, "wd", [128, 22, D], BF16)
        wdB = Buf("wd")
        with ExitStack() as es:
            x1b = sbt(nc, es, "f_x1b", [128, 8, S + 2], BF16)
            xB = [Buf(f"f_x1b{T}") for T in range(NTI)]
            haloB = Buf("f_halo")
            NW = 3
            wch = [sbt(nc, es, f"f_wch{i}", [128, 8, 256], BF16) for i in range(NW)]
            wchB = [Buf(f"f_wch{i}") for i in range(NW)]
            og = [sbt(nc, es, f"f_og{i}", [128, NT], F32) for i in range(2)]
            a2 = [sbt(nc, es, f"f_a2{i}", [128, NT], F32) for i in range(2)]
            ov = [sbt(nc, es, f"f_ov{i}", [128, NT], F32) for i in range(2)]
            sg = [sbt(nc, es, f"f_sg{i}", [128, NT], F32) for i in range(2)]
            ogB = [Buf(f"f_og{i}") for i in range(2)]
            a2B = [Buf(f"f_a2{i}") for i in range(2)]
            ovB = [Buf(f"f_ov{i}") for i in range(2)]
            sgB = [Buf(f"f_sg{i}") for i in range(2)]
            hst = [sbt(nc, es, f"f_hst{i}", [128, S], BF16) for i in range(2)]
            hstB = [Buf(f"f_hst{i}") for i in range(2)]
            pb = [pst(nc, es, f"f_pb{i}", [128, 512]) for i in range(8)]
            pbB = [Buf(f"f_pb{i}") for i in range(8)]
            wv = wup[l].rearrange("(kc p) n -> p kc n", p=128)

            def loadw(c):
                i = c % NW
                tk.dma("pool", wch[i][:, :, 0:128], wv[:, :, c * 128:(c + 1) * 128], [], [wchB[i]])
                tk.dma("pool", wch[i][:, :, 128:256], wv[:, :, (22 + c) * 128:(23 + c) * 128], [], [wchB[i]])

            tk.memset("dve", x1b[:, :, 0:2], 0.0, [haloB])
            loadw(0)
            for T in range(NTI):
                tk.dma("sp", x1b[:, :, 2 + T * NT:2 + (T + 1) * NT], xbv[:, :, T * NT:(T + 1) * NT], [], [xB[T]])
                if T == 1:
                    loadw(1)
            nb = 0
            ce = 0
            tail = [None]
            for c in range(22):
                if c + 2 < 22:
                    loadw(c + 2)
                tk.dma("pool", wd[:, c, :], wdown[l, c * 128:(c + 1) * 128, :], [], [wdB])
                wi = c % NW
                hs, hsB = hst[c % 2], hstB[c % 2]
                for T in range(NTI):
                    bg = nb % 8
                    bv = (nb + 1) % 8
                    nb += 2
                    xrd = [wchB[wi], xB[T], xB[T - 1] if T > 0 else haloB]
                    for (bank, off) in ((bg, 0), (bv, 128)):
                        for kc in range(8):
                            tk.mm(pb[bank][:, 0:NT + 2], wch[wi][:, kc, off:off + 128], x1b[:, kc, T * NT:T * NT + NT + 2],
                                  kc == 0, kc == 7, xrd, [pbB[bank]], tick=(kc == 7))
                    s = ce % 2
                    ce += 1
                    G_, V_ = pb[bg], pb[bv]
                    cg, cv_ = c, 22 + c
                    wg = [cw[:, cg * 3 + j:cg * 3 + j + 1] for j in range(3)]
                    wv_ = [cw[:, cv_ * 3 + j:cv_ * 3 + j + 1] for j in range(3)]
                    tk.act(og[s][:], G_[:, 2:NT + 2], AF.Identity, [pbB[bg], g.plB], [ogB[s]], scale=wg[2], bias=cb[:, cg:cg + 1])
                    tk.act(a2[s][:], G_[:, 1:NT + 1], AF.Identity, [pbB[bg], g.plB], [a2B[s]], scale=wg[1])
                    tk.stt(og[s][:], G_[:, 0:NT], wg[0], og[s][:], ALU.mult, ALU.add, [pbB[bg], g.plB, ogB[s], a2B[s]], [ogB[s]])
                    tk.act(ov[s][:], V_[:, 2:NT + 2], AF.Identity, [pbB[bv], g.plB], [ovB[s]], scale=wv_[2], bias=cb[:, cv_:cv_ + 1])
                    tk.stt(ov[s][:], V_[:, 1:NT + 1], wv_[1], ov[s][:], ALU.mult, ALU.add, [pbB[bv], g.plB, ovB[s]], [ovB[s]])
                    tk.stt(ov[s][:], V_[:, 0:NT], wv_[0], ov[s][:], ALU.mult, ALU.add, [pbB[bv], g.plB, ovB[s]], [ovB[s]])
                    tk.tt("pool", og[s][:], og[s][:], a2[s][:], ALU.add, [ogB[s], a2B[s]], [ogB[s]])
                    if tail[0] is not None:
                        tail[0]()

                    def mk(s=s, hs=hs, hsB=hsB, T=T):
                        def f():
                            tk.act(sg[s][:], og[s][:], AF.Silu, [ogB[s]], [sgB[s]])
                            tk.tt("pool", hs[:, T * NT:(T + 1) * NT], sg[s][:], ov[s][:], ALU.mult, [sgB[s], ovB[s]], [hsB])
                        return f
                    tail[0] = mk()
                    if T == NTI - 1:
                        tail[0]()
                        tail[0] = None
                tk.dma("sp", HT[c * 128:(c + 1) * 128, :], hs[:, :], [hsB], [])
        tk.barrier()
        with ExitStack() as es:
            ht = [sbt(nc, es, f"d_ht{i}", [128, 22, 512], BF16) for i in range(2)]
            xr = [sbt(nc, es, f"d_xr{i}", [128, 8, 512], F32) for i in range(2)]
            inB = [Buf(f"d_in{i}") for i in range(2)]
            z = [sbt(nc, es, f"d_z{i}", [128, 8, 512], F32) for i in range(2)]
            zB = [[Buf(f"d_z{i}_{c}") for c in range(8)] for i in range(2)]
            tmp = ln_alloc(nc, es, "d_", 512)
            pb = [pst(nc, es, f"d_pb{i}", [128, 512]) for i in range(6)]
            pbB = [Buf(f"d_pb{i}") for i in range(6)]
            stat = [pst(nc, es, f"d_stat{i}", [128, 512]) for i in range(2)]
            statB = [Buf(f"d_stat{i}") for i in range(2)]
            hv = HT.rearrange("(c p) t -> p c t", p=128)
            gcol, bcol = g.pl_sb[:, P_LN2G:P_LN2G + 8], g.pl_sb[:, P_LN2B:P_LN2B + 8]

            def load(T):
                tk.dma("sp", ht[T % 2][:], hv[:, :, T * 512:(T + 1) * 512], [], [inB[T % 2]])
                tk.dma("sp", xr[T % 2][:], xv[:, :, T * 512:(T + 1) * 512], [], [inB[T % 2]])

            def fin(T):
                i = T % 2
                ln_part2(g, z[i], zB[i], 512, gcol, bcol, xdst[:, T * 512:(T + 1) * 512], (stat[0], stat[1]),
                         (statB[0], statB[1]), tmp)

            load(0)
            nb = 0
            pend = None
            for T in range(8):
                if T + 1 < 8:
                    load(T + 1)
                i = T % 2
                for oc in range(8):
                    b = nb % 6
                    nb += 1
                    for c in range(22):
                        tk.mm(pb[b][:, :], wd[:, c, oc * 128:(oc + 1) * 128], ht[i][:, c, :], c == 0, c == 21,
                              [wdB, inB[i]], [pbB[b]], tick=(c == 21))
                    tk.stt(z[i][:, oc, :], xr[i][:, oc, :], ALPHA, pb[b][:, :], ALU.mult, ALU.add, [inB[i], pbB[b]], [zB[i][oc]])
                    if oc == 3 and pend is not None:
                        fin(pend)
                        pend = None
                ln_part1(g, z[i], zB[i], 512, tmp)
                pend = T
            fin(pend)


_CACHE = {}


def _prep_weights(w_in, w_alpha, b_alpha, mix_scale, w_out, ln1_g, ln1_b, w_up, conv_w, conv_b, w_down, ln2_g, ln2_b):
    f = lambda a: np.ascontiguousarray(np.asarray(a, dtype=np.float32))
    win_p = f(np.asarray(w_in)[:, :, win_perm()])
    plb = np.zeros((DEPTH, 128, NPL), np.float32)
    ms = np.asarray(mix_scale, np.float32)
    for l in range(DEPTH):
        plb[l, 0:64, P_MSA:P_MSA + 8] = ms[l, 0:512].reshape(8, 64).T
        plb[l, :, P_MSRG:P_MSRG + 4] = ms[l, 512:1024].reshape(4, 128).T
        plb[l, :, P_LN1G:P_LN1G + 8] = np.asarray(ln1_g)[l].reshape(8, 128).T
        plb[l, :, P_LN1B:P_LN1B + 8] = np.asarray(ln1_b)[l].reshape(8, 128).T
        plb[l, :, P_LN2G:P_LN2G + 8] = np.asarray(ln2_g)[l].reshape(8, 128).T
        plb[l, :, P_LN2B:P_LN2B + 8] = np.asarray(ln2_b)[l].reshape(8, 128).T
        cwl = np.asarray(conv_w)[l].reshape(3, 44, 128)
        plb[l, :, P_CW:P_CW + 132] = cwl.transpose(2, 1, 0).reshape(128, 132)
        plb[l, :, P_CB:P_CB + 44] = np.asarray(conv_b)[l].reshape(44, 128).T
        plb[l, :, P_BA] = np.asarray(b_alpha)[l]
    return dict(win=win_p, walpha=f(w_alpha), wout=f(w_out), wup=f(w_up), wdown=f(w_down), pl=plb)


def kernel(x, w_in, w_alpha, b_alpha, mix_scale, w_out, ln1_g, ln1_b, w_up, conv_w, conv_b, w_down, ln2_g, ln2_b):
    x = np.asarray(x, dtype=np.float32)
    if "nc" not in _CACHE:
        _CACHE["nc"] = build(DEPTH)[0]
        _CACHE["cst"] = make_consts()
    nc = _CACHE["nc"]
    cst, rope = _CACHE["cst"]
    wd = _prep_weights(w_in, w_alpha, b_alpha, mix_scale, w_out, ln1_g, ln1_b, w_up, conv_w, conv_b, w_down, ln2_g, ln2_b)
    in_maps = []
    for b in range(8):
        m = dict(wd)
        m["xin"] = np.ascontiguousarray(x[b].T)
        m["cst"] = cst
        m["rope"] = rope
        in_maps.append(m)
    res = run_bass_kernel_spmd(nc, in_maps, core_ids=list(range(8)))
    outp = np.stack([np.asarray(r["out"], dtype=np.float32).T for r in res.results], axis=0)
    return np.ascontiguousarray(outp)
```

```python
import numpy as np
from contextlib import ExitStack
import concourse.bass as bass
import concourse.mybir as mybir
from concourse.bass_utils import run_bass_kernel_spmd

F32 = mybir.dt.float32
BF16 = mybir.dt.bfloat16
AF = mybir.ActivationFunctionType
ALU = mybir.AluOpType

S = 4096
D = 1024
DEPTH = 4
DFF = 2816
PW = 3088
ALPHA = float((2 * DEPTH) ** 0.25)
LN_EPS = 1e-5
HN_EPS = 1e-6
NEG = -30000.0
NFM = 16
SC_A = 64 ** -0.5
SC_L = 32 ** -0.5

C_MASKB, C_IDENT, C_LMASK, C_A1, C_A2, C_B1, C_ONES1024, C_HMR, C_HMG, C_DECR, C_ER, C_EINVR, C_KDR, C_ONES = (
    0, 256, 384, 512, 576, 640, 768, 896, 900, 904, 908, 1420, 1932, 2444)
C_LMASK4 = 2572
C_MASK01 = 3084
NCST = 3340

P_MSA, P_MSRG, P_LN1G, P_LN1B, P_LN2G, P_LN2B, P_CW, P_CB, P_BA = 0, 8, 12, 20, 28, 36, 44, 176, 220
NPL = 221


def make_consts():
    c = np.zeros((128, NCST), np.float32)
    j = np.arange(128)[:, None]
    i = np.arange(128)[None, :]
    c[:, C_MASKB:C_MASKB + 128] = np.where(j <= i, 0.0, NEG)
    c[:, C_MASKB + 128:C_MASKB + 256] = np.where(j >= i, 0.0, NEG)
    c[:, C_IDENT:C_IDENT + 128] = np.eye(128, dtype=np.float32)
    c[:, C_MASK01:C_MASK01 + 128] = (j <= i).astype(np.float32)
    c[:, C_MASK01 + 128:C_MASK01 + 256] = (j >= i).astype(np.float32)
    c[:, C_LMASK:C_LMASK + 128] = (j <= i).astype(np.float32)
    for h in range(4):
        c[:, C_LMASK4 + h * 128:C_LMASK4 + (h + 1) * 128] = (j <= i).astype(np.float32)
    c[0:64, C_A1:C_A1 + 64] = 1.0 / 64
    c[0:64, C_A2:C_A2 + 64] = 1.0 / 64
    c[64, C_A2:C_A2 + 64] = HN_EPS
    for h in range(2):
        c[h * 64:(h + 1) * 64, C_B1 + h * 64:C_B1 + (h + 1) * 64] = 1.0 / 64
    c[:, C_ONES1024:C_ONES1024 + 128] = 1.0 / 1024
    p = np.arange(128)
    headR = (p % 64) // 16
    headG = p // 32
    for h in range(4):
        c[:, C_HMR + h] = (headR == h) * SC_L
        c[:, C_HMG + h] = (headG == h) * SC_L
    lg = np.log(1.0 - np.power(2.0, -5.0 - np.arange(4, dtype=np.float64)))
    lgp = lg[headR][:, None]
    idx = (np.arange(512) % 128)[None, :].astype(np.float64)
    c[:, C_DECR:C_DECR + 4] = np.exp(lgp * 128.0)
    c[:, C_ER:C_ER + 512] = np.exp(lgp * (idx + 1.0))
    c[:, C_EINVR:C_EINVR + 512] = np.exp(-lgp * (idx + 1.0))
    c[:, C_KDR:C_KDR + 512] = np.exp(lgp * (127.0 - idx))
    c[:, C_ONES:C_ONES + 128] = 1.0
    rope = np.zeros((4, 128, S), np.float32)
    pos = np.arange(S, dtype=np.float32)[None, :]
    invA = (1.0 / (10000.0 ** (np.arange(0, 64, 2, dtype=np.float32) / 64))).astype(np.float32)
    invR = (1.0 / (10000.0 ** (np.arange(0, 32, 2, dtype=np.float32) / 32))).astype(np.float32)
    angA = (pos * invA[p % 32][:, None]).astype(np.float32)
    angR = (pos * invR[p % 16][:, None]).astype(np.float32)
    rope[0], rope[1] = np.cos(angA), np.sin(angA)
    rope[2], rope[3] = np.cos(angR), np.sin(angR)
    return c, rope


def win_perm():
    qA, kA, vA, qR, kR, vR, gR, qG, kG, vG, rG, aG = 0, 512, 1024, 1536, 1664, 1792, 2048, 2304, 2432, 2560, 2816, 3072
    cols = []
    for base in (qA, kA):
        for g in range(2):
            cols += [base + h * 64 + i for h in range(4 * g, 4 * g + 4) for i in range(32)]
            cols += [base + h * 64 + 32 + i for h in range(4 * g, 4 * g + 4) for i in range(32)]
    for half in range(2):
        cols += [qR + h * 32 + half * 16 + i for h in range(4) for i in range(16)]
        cols += [kR + h * 32 + half * 16 + i for h in range(4) for i in range(16)]
    cols += list(range(gR, gR + 256))
    cols += list(range(qG, qG + 128))
    cols += list(range(kG, kG + 128))
    cols += list(range(rG, rG + 256))
    cols += list(range(aG, aG + 16))
    cols += list(range(vA, vA + 512))
    cols += list(range(vR, vR + 256))
    cols += list(range(vG, vG + 256))
    assert len(cols) == PW and len(set(cols)) == PW
    return np.array(cols)


class Buf:
    __slots__ = ("w", "r", "pw", "pr", "name")

    def __init__(self, name=""):
        self.w = {}
        self.r = {}
        self.pw = None
        self.pr = set()
        self.name = name


class TK:
    CE = ("pe", "act", "dve", "pool")

    def __init__(self, nc, es):
        self.nc = nc
        self.E = {"pe": nc.tensor, "act": nc.scalar, "dve": nc.vector, "pool": nc.gpsimd, "sp": nc.sync}
        self.sem = {}
        self.val = {}
        self.seen = {e: {} for e in self.E}
        for e in self.CE:
            self._mk(es, "c_" + e)
        self.dq = {}
        for q, n in (("sp", 8), ("pool", 8)):
            self.dq[q] = [self._mk(es, f"d_{q}{i}") for i in range(n)]
        self.dqi = {q: 0 for q in self.dq}
        self.pending = {e: [] for e in self.CE}
        self.nins = 0

    def _mk(self, es, name):
        self.sem[name] = es.enter_context(self.nc.semaphore(name))
        self.val[name] = 0
        return name

    def wait(self, e, s, v):
        if v <= self.seen[e].get(s, 0):
            return
        self.E[e].wait_ge(self.sem[s], v)
        self.seen[e][s] = v
        self.nins += 1

    def _deps(self, e, reads, writes, dma=False):
        own = "c_" + e
        deps = {}
        for b in reads:
            assert b.pw in (None, e), f"read of {b.name} with pending writer {b.pw}"
            for s, v in b.w.items():
                if s == own and (e == "pe" and not dma):
                    continue
                if v > deps.get(s, 0):
                    deps[s] = v
        for b in writes:
            assert b.pw in (None, e), f"write of {b.name} with pending writer {b.pw}"
            assert not (b.pr - {e}), f"write of {b.name} with pending readers {b.pr}"
            for dd in (b.w, b.r):
                for s, v in dd.items():
                    if s == own and not dma:
                        continue
                    if v > deps.get(s, 0):
                        deps[s] = v
        for s, v in deps.items():
            self.wait(e, s, v)

    def op(self, e, emit, reads=(), writes=(), tick=True):
        self._deps(e, reads, writes)
        ins = emit()
        self.nins += 1
        own = "c_" + e
        self.pending[e].append((reads, writes))
        if tick:
            self.val[own] += 1
            ins.then_inc(self.sem[own], 1)
            v = self.val[own]
            for rs, ws in self.pending[e]:
                for b in rs:
                    b.r[own] = v
                    b.pr.discard(e)
                for b in ws:
                    b.w[own] = v
                    b.pw = None
            self.pending[e] = []
        else:
            for b in reads:
                b.pr.add(e)
            for b in writes:
                b.pw = e
        return ins

    def dma(self, q, out, in_, reads=(), writes=()):
        self._deps(q, reads, writes, dma=True)
        names = self.dq[q]
        nm = names[self.dqi[q] % len(names)]
        self.dqi[q] += 1
        self.wait(q, nm, self.val[nm])
        ins = self.E[q].dma_start(out=out, in_=in_)
        self.nins += 1
        self.val[nm] += 16
        ins.then_inc(self.sem[nm], 16)
        v = self.val[nm]
        for b in reads:
            b.r[nm] = v
        for b in writes:
            b.w[nm] = v
        return ins

    def barrier(self):
        for e in self.CE:
            assert not self.pending[e], f"pending un-ticked ops on {e}"
        for e in self.E:
            for s, v in self.val.items():
                if s == "c_" + e:
                    continue
                self.wait(e, s, v)

    def mm(self, out, lhsT, rhs, start, stop, reads, writes, tick=False):
        return self.op("pe", lambda: self.nc.tensor.matmul(out, lhsT=lhsT, rhs=rhs, start=start, stop=stop,
                                                           skip_group_check=True), reads, writes, tick)

    def act(self, out, in_, func, reads, writes, **kw):
        return self.op("act", lambda: self.nc.scalar.activation(out=out, in_=in_, func=func, **kw), reads, writes)

    def tt(self, e, out, in0, in1, op, reads, writes):
        return self.op(e, lambda: self.E[e].tensor_tensor(out=out, in0=in0, in1=in1, op=op), reads, writes)

    def stt(self, out, in0, scalar, in1, op0, op1, reads, writes):
        return self.op("dve", lambda: self.nc.vector.scalar_tensor_tensor(out=out, in0=in0, scalar=scalar, in1=in1,
                                                                         op0=op0, op1=op1), reads, writes)

    def ts(self, e, out, in0, s1, s2, op0, op1, reads, writes):
        return self.op(e, lambda: self.E[e].tensor_scalar(out=out, in0=in0, scalar1=s1, scalar2=s2, op0=op0, op1=op1),
                       reads, writes)

    def copy(self, e, out, in_, reads, writes):
        if e == "act":
            return self.act(out, in_, AF.Copy, reads, writes)
        return self.op(e, lambda: self.E[e].tensor_copy(out=out, in_=in_), reads, writes)

    def memset(self, e, ap, val, writes):
        return self.op(e, lambda: self.E[e].memset(ap, val), (), writes)


class Ctx:
    pass


_UNIQ = [0]


def sbt(nc, es, name, shape, dt):
    _UNIQ[0] += 1
    return es.enter_context(nc.sbuf_tensor(f"{name}_{_UNIQ[0]}", shape, dt))


def pst(nc, es, name, shape, dt=F32):
    _UNIQ[0] += 1
    return es.enter_context(nc.psum_tensor(f"{name}_{_UNIQ[0]}", shape, dt))


def layer_norm(g, es_name, z, zB, N, gcol, bcol, dst_ap, stat_ps, stat_bufs, tmp):
    tk, nc = g.tk, g.nc
    zb, zq, msq, var, rstd, nmr = tmp["zb"], tmp["zq"], tmp["msq"], tmp["var"], tmp["rstd"], tmp["nmr"]
    B = tmp["B"]
    tk.act(zb[:, :, 0:N], z[:, :, 0:N], AF.Copy, [zB], [B["zb"]])
    tk.act(zq[:, :, 0:N], z[:, :, 0:N], AF.Square, [zB], [B["zq"]])
    mean_ps, e2_ps = stat_ps
    mB, eB = stat_bufs
    for c in range(8):
        tk.mm(mean_ps[:, 0:N], g.ones_b[:], zb[:, c, 0:N], c == 0, c == 7, [g.cstB, B["zb"]], [mB], tick=(c == 7))
    for c in range(8):
        tk.mm(e2_ps[:, 0:N], g.ones_b[:], zq[:, c, 0:N], c == 0, c == 7, [g.cstB, B["zq"]], [eB], tick=(c == 7))
    tk.act(msq[:, 0:N], mean_ps[:, 0:N], AF.Square, [mB], [B["msq"]])
    tk.tt("dve", var[:, 0:N], e2_ps[:, 0:N], msq[:, 0:N], ALU.subtract, [eB, B["msq"]], [B["var"]])
    tk.act(var[:, 0:N], var[:, 0:N], AF.Ln, [B["var"]], [B["var"]], bias=g.eps_ln[:, 0:1])
    tk.act(rstd[:, 0:N], var[:, 0:N], AF.Exp, [B["var"]], [B["rstd"]], scale=-0.5)
    tk.stt(nmr[:, 0:N], mean_ps[:, 0:N], -1.0, rstd[:, 0:N], ALU.mult, ALU.mult, [mB, B["rstd"]], [B["nmr"]])
    for c in range(8):
        e1 = "dve" if c % 2 == 0 else "pool"
        tk.tt(e1, z[:, c, 0:N], z[:, c, 0:N], rstd[:, 0:N], ALU.mult, [zB, B["rstd"]], [zB])
        tk.tt("pool", z[:, c, 0:N], z[:, c, 0:N], nmr[:, 0:N], ALU.add, [zB, B["nmr"]], [zB])
        tk.act(z[:, c, 0:N], z[:, c, 0:N], AF.Identity, [zB, g.plB], [zB], scale=gcol[:, c:c + 1], bias=bcol[:, c:c + 1])
    tk.dma("sp", dst_ap.rearrange("(c p) t -> p c t", p=128), z[:, :, 0:N], [zB], [])


def head_norm(g, src, srcB, K, lhs1, lhs2, M, stat_ps, stat_bufs, tmp, N=512):
    tk = g.tk
    B = tmp["B"]
    sq, msq, var, dd = tmp["sq"], tmp["msq"], tmp["var"], tmp["dd"]
    mean_ps, e2_ps = stat_ps
    mB, eB = stat_bufs
    srcb = tmp["srcb"]
    tk.act(sq[0:K, 0:N], src, AF.Square, [srcB], [B["sq"]])
    tk.act(srcb[0:K, 0:N], src, AF.Copy, [srcB], [B["srcb"]])
    tk.mm(mean_ps[0:M, 0:N], lhs1, srcb[0:K, 0:N], True, True, [g.cstB, B["srcb"]], [mB], tick=True)
    tk.mm(e2_ps[0:M, 0:N], lhs2, sq[0:K, 0:N], True, True, [g.cstB, B["sq"]], [eB], tick=True)
    tk.act(msq[0:M, 0:N], mean_ps[0:M, 0:N], AF.Square, [mB], [B["msq"]])
    tk.tt("dve", var[0:M, 0:N], e2_ps[0:M, 0:N], msq[0:M, 0:N], ALU.subtract, [eB, B["msq"]], [B["var"]])
    return mean_ps, mB


def build(depth=DEPTH, debug=False):
    nc = bass.Bass("TRN2", target_bir_lowering=False)
    g = Ctx()
    g.nc = nc
    dkind = "ExternalOutput" if debug else "Internal"
    xin = nc.dram_tensor("xin", [D, S], F32, kind="ExternalInput").ap()
    win = nc.dram_tensor("win", [DEPTH, D, PW], F32, kind="ExternalInput").ap()
    walpha = nc.dram_tensor("walpha", [DEPTH, 16, 128], F32, kind="ExternalInput").ap()
    wout = nc.dram_tensor("wout", [DEPTH, D, D], F32, kind="ExternalInput").ap()
    wup = nc.dram_tensor("wup", [DEPTH, D, 2 * DFF], F32, kind="ExternalInput").ap()
    wdown = nc.dram_tensor("wdown", [DEPTH, DFF, D], F32, kind="ExternalInput").ap()
    pl = nc.dram_tensor("pl", [DEPTH, 128, NPL], F32, kind="ExternalInput").ap()
    cst = nc.dram_tensor("cst", [128, NCST], F32, kind="ExternalInput").ap()
    rope = nc.dram_tensor("rope", [4, 128, S], F32, kind="ExternalInput").ap()
    out = nc.dram_tensor("out", [D, S], F32, kind="ExternalOutput").ap()
    XA = nc.dram_tensor("XA", [D, S], F32, kind="Internal").ap()
    X1F = nc.dram_tensor("X1F", [D, S], F32, kind=dkind).ap()
    YT = nc.dram_tensor("YT", [D, S], BF16, kind=dkind).ap()
    FMS = nc.dram_tensor("FMS", [NFM, 128, S], BF16, kind=dkind).ap()
    LA = nc.dram_tensor("LA", [128, S], F32, kind=dkind).ap()
    VA = nc.dram_tensor("VA", [S, 8, 65], BF16, kind=dkind).ap()
    VRG = nc.dram_tensor("VRG", [S, 512], BF16, kind=dkind).ap()
    HT = nc.dram_tensor("HT", [DFF, S], BF16, kind="Internal").ap()
    X1B = nc.dram_tensor("X1B", [D, S], BF16, kind="Internal").ap()

    with ExitStack() as es:
        tk = TK(nc, es)
        g.tk = tk
        block = es.enter_context(nc.Block())

        @block.sync
        def _(sync):
            cst_sb = sbt(nc, es, "cst_sb", [128, NCST], F32)
            g.cstB = Buf("cst")
            g.cst = cst_sb
            tk.dma("sp", cst_sb[:], cst[:, :], [], [g.cstB])
            g.maskb = sbt(nc, es, "maskb", [128, 256], BF16)
            g.ident_b = sbt(nc, es, "ident_b", [128, 128], BF16)
            g.ones_b = sbt(nc, es, "ones_b", [128, 128], BF16)
            g.eps_ln = sbt(nc, es, "eps_ln", [128, 1], F32)
            tk.copy("dve", g.maskb[:], cst_sb[:, C_MASKB:C_MASKB + 256], [g.cstB], [g.cstB])
            tk.copy("dve", g.ident_b[:], cst_sb[:, C_IDENT:C_IDENT + 128], [g.cstB], [g.cstB])
            tk.copy("dve", g.ones_b[:], cst_sb[:, C_ONES1024:C_ONES1024 + 128], [g.cstB], [g.cstB])
            tk.memset("dve", g.eps_ln[:], LN_EPS, [g.cstB])
            g.mask01 = sbt(nc, es, "mask01", [128, 4, 256], BF16)
            for jj in range(4):
                tk.copy("dve", g.mask01[:, jj, :], cst_sb[:, C_MASK01:C_MASK01 + 256], [g.cstB], [g.cstB])
            g.A1b = sbt(nc, es, "A1b", [128, 64], BF16)
            g.A2b = sbt(nc, es, "A2b", [128, 64], BF16)
            g.B1b = sbt(nc, es, "B1b", [128, 128], BF16)
            tk.copy("dve", g.A1b[:], cst_sb[:, C_A1:C_A1 + 64], [g.cstB], [g.cstB])
            tk.copy("dve", g.A2b[:], cst_sb[:, C_A2:C_A2 + 64], [g.cstB], [g.cstB])
            tk.copy("dve", g.B1b[:], cst_sb[:, C_B1:C_B1 + 128], [g.cstB], [g.cstB])
            g.eps_hn = sbt(nc, es, "eps_hn", [128, 1], F32)
            tk.memset("dve", g.eps_hn[:], HN_EPS, [g.cstB])
            g.one_col = cst_sb[:, C_ONES:C_ONES + 1]
            g.pl_sb = sbt(nc, es, "pl_sb", [128, NPL], F32)
            g.negb = sbt(nc, es, "negb", [128, 1], F32)
            g.plB = Buf("pl")
            tk.barrier()

            for l in range(depth):
                xsrc = xin if l == 0 else XA
                xdst = out if l == depth - 1 else XA
                tk.dma("sp", g.pl_sb[:], pl[l], [], [g.plB])
                tk.ts("dve", g.negb[:], g.pl_sb[:, P_BA:P_BA + 1], -1.0, None, ALU.mult, ALU.bypass, [g.plB], [g.plB])
                phase_P(g, l, xsrc, win, walpha, rope, FMS, LA, VA, VRG)
                tk.barrier()
                phase_A(g, l, FMS, VA, YT)
                tk.barrier()
                phase_L(g, l, FMS, LA, VRG, YT)
                tk.barrier()
                phase_O(g, l, xsrc, wout, YT, X1F, X1B)
                tk.barrier()
                phase_F(g, l, wup, wdown, X1F, X1B, xdst, HT)
                tk.barrier()
    g.nins = tk.nins
    return nc, g


def phase_P(g, l, xsrc, win, walpha, rope, FMS, LA, VA, VRG):
    tk, nc = g.tk, g.nc
    with ExitStack() as es:
        wfm = sbt(nc, es, "wfm", [128, 8, 2064], BF16)
        wtm = sbt(nc, es, "wtm", [128, 8, 1024], BF16)
        wal = sbt(nc, es, "wal", [16, 128], F32)
        wB = Buf("w")
        xt = [sbt(nc, es, f"xt{i}", [128, 8, 512], BF16) for i in range(2)]
        xtB = [Buf(f"xt{i}") for i in range(2)]
        rp = [sbt(nc, es, f"rp{i}", [128, 4, 512], F32) for i in range(2)]
        rpB = [Buf(f"rp{i}") for i in range(2)]
        NSO = 6
        so = [sbt(nc, es, f"so{i}", [128, 512], BF16) for i in range(NSO)]
        soB = [Buf(f"so{i}") for i in range(NSO)]
        tmpf = [[sbt(nc, es, f"rt{s}_{i}", [128, 512], F32) for i in range(4)] for s in range(2)]
        tmpB = [[Buf(f"rt{s}_{i}") for i in range(4)] for s in range(2)]
        vst = [sbt(nc, es, f"vst{i}", [128, 8, 65], BF16) for i in range(2)]
        vstB = [Buf(f"vst{i}") for i in range(2)]
        vrg = [sbt(nc, es, f"vrg{i}", [128, 512], BF16) for i in range(2)]
        vrgB = [Buf(f"vrg{i}") for i in range(2)]
        ag = sbt(nc, es, "ag", [16, 512], F32)
        agB = Buf("ag")
        ez = sbt(nc, es, "ez", [128, 512], F32)
        ezB = Buf("ez")
        lst = [sbt(nc, es, f"lst{i}", [128, 512], F32) for i in range(2)]
        lstB = [Buf(f"lst{i}") for i in range(2)]
        pb = [pst(nc, es, f"pb{i}", [128, 512]) for i in range(8)]
        pbB = [Buf(f"pb{i}") for i in range(8)]
        st = {"bank": 0, "so": 0, "ts": 0, "v": 0, "ev": 0}

        def nbank():
            i = st["bank"] % 8
            st["bank"] += 1
            return i

        def nso():
            i = st["so"] % NSO
            st["so"] += 1
            return i

        for i in range(2):
            tk.memset("pool", vst[i][:], 1.0, [vstB[i]])
        for kc in range(8):
            tk.dma("pool", wfm[:, kc, :], win[l, kc * 128:(kc + 1) * 128, 0:2064], [], [wB])
            tk.dma("pool", wtm[:, kc, :], win[l, kc * 128:(kc + 1) * 128, 2064:PW], [], [wB])
        tk.dma("sp", wal[:], walpha[l], [], [wB])
        xv = xsrc.rearrange("(c p) t -> p c t", p=128)
        rv = rope.rearrange("f p t -> p f t")

        def load(T):
            tk.dma("pool", xt[T % 2][:], xv[:, :, T * 512:(T + 1) * 512], [], [xtB[T % 2]])
            tk.dma("sp", rp[T % 2][:], rv[:, :, T * 512:(T + 1) * 512], [], [rpB[T % 2]])

        load(0)
        for T in range(8):
            if T + 1 < 8:
                load(T + 1)
            x_, xB_ = xt[T % 2], xtB[T % 2]
            r_, rB_ = rp[T % 2], rpB[T % 2]
            tsl = slice(T * 512, (T + 1) * 512)

            def fm_mm(tile, bank):
                for kc in range(8):
                    tk.mm(pb[bank][:, :], wfm[:, kc, tile * 128:(tile + 1) * 128], x_[:, kc, :], kc == 0, kc == 7,
                          [wB, xB_], [pbB[bank]], tick=(kc == 7))

            def store_fm(tile, si):
                tk.dma("sp", FMS[tile, :, tsl], so[si][:], [soB[si]], [])

            def tm_block(blk):
                for half in range(2):
                    b = nbank()
                    for kc in range(8):
                        tk.mm(pb[b][:, :], x_[:, kc, blk * 128:(blk + 1) * 128], wtm[:, kc, half * 512:(half + 1) * 512],
                              kc == 0, kc == 7, [wB, xB_], [pbB[b]], tick=(kc == 7))
                    vi = st["v"] % 2
                    eng = "act" if st["ev"] % 2 == 0 else "dve"
                    st["ev"] += 1
                    rows = slice(T * 512 + blk * 128, T * 512 + (blk + 1) * 128)
                    if half == 0:
                        tk.copy(eng, vst[vi][:, :, 0:64], pb[b][:, :].rearrange("p (h d) -> p h d", d=64), [pbB[b]], [vstB[vi]])
                        tk.dma("sp", VA[rows, :, :], vst[vi][:], [vstB[vi]], [])
                    else:
                        tk.copy(eng, vrg[vi][:], pb[b][:, :], [pbB[b]], [vrgB[vi]])
                        tk.dma("sp", VRG[rows, :], vrg[vi][:], [vrgB[vi]], [])
                        st["v"] += 1

            pairs = [(0, 1, 0), (2, 3, 0), (4, 5, 0), (6, 7, 0), (8, 9, 2)]
            for pi, (ta, tb, ro) in enumerate(pairs):
                ba, bb = nbank(), nbank()
                fm_mm(ta, ba)
                fm_mm(tb, bb)
                C_ = r_[:, ro, :]
                S_ = r_[:, ro + 1, :]
                s = st["ts"] % 2
                st["ts"] += 1
                t1, t2, t3, t4 = tmpf[s]
                b1, b2, b3, b4 = tmpB[s]
                tk.tt("dve", t1[:], pb[ba][:, :], C_, ALU.mult, [pbB[ba], rB_], [b1])
                tk.tt("dve", t2[:], pb[bb][:, :], S_, ALU.mult, [pbB[bb], rB_], [b2])
                tk.tt("dve", t3[:], pb[ba][:, :], S_, ALU.mult, [pbB[ba], rB_], [b3])
                tk.tt("dve", t4[:], pb[bb][:, :], C_, ALU.mult, [pbB[bb], rB_], [b4])
                sa = nso()
                tk.tt("pool", so[sa][:], t1[:], t2[:], ALU.subtract, [b1, b2], [soB[sa]])
                store_fm(ta, sa)
                sb_ = nso()
                tk.tt("pool", so[sb_][:], t3[:], t4[:], ALU.add, [b3, b4], [soB[sb_]])
                store_fm(tb, sb_)
                if pi < 4:
                    tm_block(pi)
            for tile, kind in ((10, "silu"), (11, "silu"), (12, "copy"), (13, "copy"), (14, "silu"), (15, "silu")):
                b = nbank()
                fm_mm(tile, b)
                si = nso()
                tk.act(so[si][:], pb[b][:, :], AF.Silu if kind == "silu" else AF.Copy, [pbB[b]], [soB[si]])
                store_fm(tile, si)
            b = nbank()
            for kc in range(8):
                tk.mm(pb[b][0:16, :], wfm[:, kc, 2048:2064], x_[:, kc, :], kc == 0, kc == 7, [wB, xB_], [pbB[b]], tick=(kc == 7))
            tk.act(ag[:], pb[b][0:16, :], AF.Copy, [pbB[b]], [agB])
            b2_ = nbank()
            tk.mm(pb[b2_][:, :], wal[:], ag[:], True, True, [wB, agB], [pbB[b2_]], tick=True)
            tk.act(ez[:], pb[b2_][:, :], AF.Exp, [pbB[b2_], g.plB], [ezB], scale=-1.0, bias=g.negb[:, 0:1])
            li = T % 2
            tk.act(lst[li][:], ez[:], AF.Ln, [ezB, g.cstB], [lstB[li]], bias=g.one_col)
            tk.dma("sp", LA[:, tsl], lst[li][:], [lstB[li]], [])


def phase_A(g, l, FMS, VA, YT):
    tk, nc = g.tk, g.nc
    with ExitStack() as es:
        qh = [sbt(nc, es, f"qh{i}", [128, S], BF16) for i in range(2)]
        kh = [sbt(nc, es, f"kh{i}", [128, S], BF16) for i in range(2)]
        v3 = [sbt(nc, es, f"v3{i}", [128, 3, 32, 65], BF16) for i in range(2)]
        inB = [Buf(f"ain{i}") for i in range(2)]
        acc = [sbt(nc, es, f"acc{i}", [65, S], F32) for i in range(2)]
        accB = [Buf(f"acc{i}") for i in range(2)]
        PT = [sbt(nc, es, f"PT{i}", [128, 1024], BF16) for i in range(3)]
        PTB = [[Buf(f"PT{i}a"), Buf(f"PT{i}b")] for i in range(3)]
        yst = [sbt(nc, es, f"yst{i}", [64, S], BF16) for i in range(2)]
        ystB = [Buf(f"yst{i}") for i in range(2)]
        tmp = {"B": {k: Buf("a_" + k) for k in ("sq", "srcb", "msq", "var", "dd")}}
        for k in ("msq", "var", "dd"):
            tmp[k] = sbt(nc, es, "a_" + k, [65, 512], F32)
        for k in ("sq", "srcb"):
            tmp[k] = sbt(nc, es, "a_" + k, [65, 512], BF16)
        ST = [pst(nc, es, f"ST{i}", [128, 1024]) for i in range(2)]
        STB = [Buf(f"ST{i}") for i in range(2)]
        Op = [pst(nc, es, f"Op{i}", [128, 512]) for i in range(2)]
        OpB = [Buf(f"Op{i}") for i in range(2)]
        stat = [pst(nc, es, f"astat{i}", [128, 512]) for i in range(2)]
        statB = [Buf(f"astat{i}") for i in range(2)]
        cs = g.cst
        A1 = g.A1b[0:65, :]
        A2 = g.A2b[0:65, :]

        def load_head(h):
            i = h % 2
            gI, hh = h // 4, h % 4
            rows = slice(hh * 32, hh * 32 + 32)
            tk.dma("sp", qh[i][0:32, :], FMS[2 * gI, rows, :], [], [inB[i]])
            tk.dma("sp", qh[i][32:64, :], FMS[2 * gI + 1, rows, :], [], [inB[i]])
            tk.dma("sp", kh[i][0:32, :], FMS[4 + 2 * gI, rows, :], [], [inB[i]])
            tk.dma("sp", kh[i][32:64, :], FMS[4 + 2 * gI + 1, rows, :], [], [inB[i]])
            for di, d in enumerate((1, 4, 16)):
                src = VA[:, h, :].rearrange("(n j r) c -> j r n c", j=128, r=d)
                dst = v3[i][:, di, :, :].rearrange("p (r n) c -> p r n c", r=d)
                nb_ = 32 // d
                if d == 1:
                    for q4 in range(4):
                        tk.dma("sp", dst[:, :, q4 * 8:(q4 + 1) * 8, :], src[:, :, q4 * 8:(q4 + 1) * 8, :], [], [inB[i]])
                elif d == 4:
                    for r_ in range(4):
                        tk.dma("sp", dst[:, r_, :, :], src[:, r_, :, :], [], [inB[i]])
                else:
                    for n_ in range(2):
                        for hf in range(2):
                            tk.dma("sp", dst[:, hf * 8:(hf + 1) * 8, n_, :], src[:, hf * 8:(hf + 1) * 8, n_, :], [], [inB[i]])

        batches = []
        for h in range(8):
            for di, d in enumerate((1, 4, 16)):
                nb = 32 // d
                blocks = [(r, n, r * nb + n) for r in range(d) for n in range(nb)]
                for b0 in range(0, 32, 4):
                    batches.append((h, di, d, nb, blocks[b0:b0 + 4]))
        NB = len(batches)

        def emit_ST(gi):
            h, di, d, nb, blks = batches[gi]
            i = h % 2
            sbuf = gi % 2
            qv = qh[i][:, :].rearrange("p (m r) -> p r m", r=d)
            kv = kh[i][:, :].rearrange("p (m r) -> p r m", r=d)
            for j, (r, n, b) in enumerate(blks):
                qn = 256 if n < nb - 1 else 128
                o_ = ST[sbuf][:, j * 256:j * 256 + qn]
                tk.mm(o_, kv[:, r, n * 128:(n + 1) * 128], qv[:, r, n * 128:n * 128 + qn], True, True,
                      [inB[i]], [STB[sbuf]], tick=(j == 3))

        def emit_exp(gi):
            p3 = gi % 3
            tk.act(PT[p3][:], ST[gi % 2][:, :], AF.Exp, [STB[gi % 2]], PTB[p3], scale=SC_A)
            m01 = g.mask01[:].rearrange("p a b -> p (a b)")
            tk.tt("dve", PT[p3][:, 0:512], PT[p3][:, 0:512], m01[:, 0:512], ALU.mult, [PTB[p3][0], g.cstB], [PTB[p3][0]])
            tk.tt("dve", PT[p3][:, 512:1024], PT[p3][:, 512:1024], m01[:, 512:1024], ALU.mult, [PTB[p3][1], g.cstB], [PTB[p3][1]])

        def emit_PV(gi):
            h, di, d, nb, blks = batches[gi]
            i = h % 2
            ob = gi % 2
            cur = PT[gi % 3]
            prv = PT[(gi - 1) % 3]
            for j, (r, n, b) in enumerate(blks):
                o_ = Op[ob][0:65, j * 128:(j + 1) * 128]
                rd = [inB[i]] + PTB[gi % 3]
                if n > 0:
                    if j > 0:
                        pprev = cur[:, (j - 1) * 256 + 128:(j - 1) * 256 + 256]
                    else:
                        pprev = prv[:, 3 * 256 + 128:4 * 256]
                        rd = rd + PTB[(gi - 1) % 3]
                    tk.mm(o_, v3[i][:, di, b - 1, :], pprev, True, False, rd, [OpB[ob]])
                    tk.mm(o_, v3[i][:, di, b, :], cur[:, j * 256:j * 256 + 128], False, True, rd, [OpB[ob]], tick=(j == 3))
                else:
                    tk.mm(o_, v3[i][:, di, b, :], cur[:, j * 256:j * 256 + 128], True, True, rd, [OpB[ob]], tick=(j == 3))

        def emit_evac(gi):
            h, di, d, nb, blks = batches[gi]
            a, aB = acc[h % 2], accB[h % 2]
            ob = gi % 2
            o_ = Op[ob][0:65, :]
            r0, n0, b0 = blks[0]
            if d == 1:
                tk.copy("dve", a[:, n0 * 128:n0 * 128 + 512], o_, [OpB[ob]], [aB])
            elif d == 4:
                av = a[:, :].rearrange("p (m q) -> p q m", q=4)[:, r0, n0 * 128:n0 * 128 + 512]
                tk.tt("dve", av, av, o_, ALU.add, [OpB[ob], aB], [aB])
            else:
                av = a[:, :].rearrange("p (m q) -> p q m", q=16)[:, r0:r0 + 2, :]
                tk.tt("dve", av, av, o_.rearrange("p (a m) -> p a m", a=2), ALU.add, [OpB[ob], aB], [aB])

        def emit_post(h, t, stage):
            a, aB = acc[h % 2], accB[h % 2]
            B = tmp["B"]
            K, M, N = 65, 64, 512
            src = a[0:65, t * 512:(t + 1) * 512]
            sq, srcb, msq, var, dd = tmp["sq"], tmp["srcb"], tmp["msq"], tmp["var"], tmp["dd"]
            mean_ps, e2_ps = stat[0], stat[1]
            mB, eB = statB[0], statB[1]
            if stage == 1:
                tk.act(sq[0:K, 0:N], src, AF.Square, [aB], [B["sq"]])
                tk.act(srcb[0:K, 0:N], src, AF.Copy, [aB], [B["srcb"]])
                tk.mm(mean_ps[0:M, 0:N], A1, srcb[0:K, 0:N], True, True, [g.cstB, B["srcb"]], [mB], tick=True)
                tk.mm(e2_ps[0:M, 0:N], A2, sq[0:K, 0:N], True, True, [g.cstB, B["sq"]], [eB], tick=True)
            elif stage == 2:
                tk.act(msq[0:M, 0:N], mean_ps[0:M, 0:N], AF.Square, [mB], [B["msq"]])
                tk.tt("dve", var[0:M, 0:N], e2_ps[0:M, 0:N], msq[0:M, 0:N], ALU.subtract, [eB, B["msq"]], [B["var"]])
                tk.tt("dve", dd[0:64, :], a[0:64, t * 512:(t + 1) * 512], mean_ps[0:64, :], ALU.subtract, [aB, mB, B["msq"]], [B["dd"]])
            else:
                tk.act(var[0:64, :], var[0:64, :], AF.Ln, [B["var"]], [B["var"]])
                tk.act(var[0:64, :], var[0:64, :], AF.Exp, [B["var"]], [B["var"]], scale=-0.5)
                y, yB = yst[h % 2], ystB[h % 2]
                tk.stt(y[:, t * 512:(t + 1) * 512], dd[0:64, :], g.pl_sb[0:64, P_MSA + h:P_MSA + h + 1], var[0:64, :],
                       ALU.mult, ALU.mult, [B["dd"], B["var"], g.plB], [yB])
                if t == 7:
                    tk.dma("sp", YT[h * 64:(h + 1) * 64, :], y[:, :], [yB], [])

        for i in range(2):
            tk.memset("dve", qh[i][64:128, :], 0.0, [inB[i]])
            tk.memset("pool", kh[i][64:128, :], 0.0, [inB[i]])
        load_head(0)
        posts = []
        cur = [None, 0]
        emit_ST(0)
        emit_ST(1)
        emit_exp(0)
        for gi in range(NB):
            h = batches[gi][0]
            first_of_head = (gi % 24 == 0)
            if first_of_head and h + 1 < 8:
                load_head(h + 1)
            if gi + 2 < NB:
                emit_ST(gi + 2)
            if gi + 1 < NB:
                emit_exp(gi + 1)
            emit_PV(gi)
            emit_evac(gi)
            if cur[0] is None and posts:
                cur[0] = posts.pop(0)
                cur[1] = 1
            if cur[0] is not None:
                emit_post(cur[0][0], cur[0][1], cur[1])
                cur[1] += 1
                if cur[1] > 3:
                    cur[0] = None
            if gi % 24 == 23:
                posts += [(h, t) for t in range(8)]
        while cur[0] is not None or posts:
            if cur[0] is None:
                cur[0] = posts.pop(0)
                cur[1] = 1
            emit_post(cur[0][0], cur[0][1], cur[1])
            cur[1] += 1
            if cur[1] > 3:
                cur[0] = None


def phase_L(g, l, FMS, LA, VRG, YT):
    tk, nc = g.tk, g.nc
    cs = g.cst
    with ExitStack() as es:
        G = []
        for grp in range(2):
            d = Ctx()
            n = f"L{grp}_"
            d.qf = [sbt(nc, es, n + f"q{i}", [128, 512], BF16) for i in range(2)]
            d.kf = [sbt(nc, es, n + f"k{i}", [128, 512], BF16) for i in range(2)]
            d.vt = [sbt(nc, es, n + f"v{i}", [128, 4, 256], BF16) for i in range(2)]
            d.gt = [sbt(nc, es, n + f"g{i}", [128, 2, 512], BF16) for i in range(2)]
            d.lt = [sbt(nc, es, n + f"l{i}", [128, 512], F32) for i in range(2)] if grp == 1 else None
            d.inB = [Buf(n + f"in{i}") for i in range(2)]
            names = ("cum", "E", "Einv", "Kd", "dec", "Qbd", "kt", "ktil", "ktok", "og0", "og1", "sq", "srcb", "msq", "var", "dd")
            d.B = {k: Buf(n + k) for k in names}
            if grp == 1:
                d.cum = sbt(nc, es, n + "cum", [128, 512], F32)
                d.Eg = sbt(nc, es, n + "E", [128, 512], F32)
                d.Einvg = sbt(nc, es, n + "Einv", [128, 512], F32)
                d.Kdg = sbt(nc, es, n + "Kd", [128, 512], F32)
                d.decg = sbt(nc, es, n + "dec", [128, 4], F32)
            d.Qbd = sbt(nc, es, n + "Qbd", [128, 4, 512], BF16)
            d.kt = sbt(nc, es, n + "kt", [128, 512], BF16)
            d.ktil = sbt(nc, es, n + "ktil", [128, 512], BF16)
            d.ktok = sbt(nc, es, n + "ktok", [128, 4, 128], BF16)
            d.stf = sbt(nc, es, n + "stf", [128, 256], F32)
            d.stfB = Buf(n + "stf")
            d.stb = [sbt(nc, es, n + f"stb{i}", [128, 256], BF16) for i in range(2)]
            d.stbB = [Buf(n + f"stb{i}") for i in range(2)]
            d.nst = 0
            d.og = [sbt(nc, es, n + f"og{j}", [128, 512], F32) for j in range(2)]
            d.tmp = {"B": d.B}
            for k in ("msq", "var", "dd"):
                d.tmp[k] = sbt(nc, es, n + k, [128, 512], F32)
            for k in ("sq", "srcb"):
                d.tmp[k] = sbt(nc, es, n + k, [128, 512], BF16)
            d.yy = [sbt(nc, es, n + f"yy{i}", [128, 512], BF16) for i in range(2)]
            d.yyB = [Buf(n + f"yy{i}") for i in range(2)]
            d.nyy = 0
            d.hm = cs[:, C_HMR:C_HMR + 4] if grp == 0 else cs[:, C_HMG:C_HMG + 4]
            d.ch0 = 512 + grp * 256
            G.append(d)
        PT = [sbt(nc, es, f"l_PT{i}", [128, 4, 128], BF16) for i in range(2)]
        PTB = [Buf(f"l_PT{i}") for i in range(2)]
        STp = [pst(nc, es, f"l_ST{i}", [128, 512]) for i in range(2)]
        STpB = [Buf(f"l_ST{i}") for i in range(2)]
        kvp = pst(nc, es, "l_kv", [128, 512])
        kvB = Buf("l_kvp")
        Opp = [pst(nc, es, f"l_O{i}", [128, 512]) for i in range(2)]
        OppB = [Buf(f"l_O{i}") for i in range(2)]
        trp = pst(nc, es, "l_tr", [128, 1024], BF16)
        trB = Buf("l_tr")
        stat = [pst(nc, es, f"l_stat{i}", [128, 512]) for i in range(2)]
        statB = [Buf(f"l_stat{i}") for i in range(2)]
        B1 = g.B1b[:, :]
        lmask4 = cs[:, C_LMASK4:C_LMASK4 + 512]
        ones = cs[:, C_ONES:C_ONES + 128]
        cnt = {"pt": 0}

        def load(T, grp):
            d = G[grp]
            i = T % 2
            tsl = slice(T * 512, (T + 1) * 512)
            iB = d.inB[i]
            if grp == 0:
                tk.dma("sp", d.qf[i][0:64, :], FMS[8, 0:64, tsl], [], [iB])
                tk.dma("sp", d.qf[i][64:128, :], FMS[9, 0:64, tsl], [], [iB])
                tk.dma("sp", d.kf[i][0:64, :], FMS[8, 64:128, tsl], [], [iB])
                tk.dma("sp", d.kf[i][64:128, :], FMS[9, 64:128, tsl], [], [iB])
                g0 = 10
            else:
                tk.dma("sp", d.qf[i][:], FMS[12, :, tsl], [], [iB])
                tk.dma("sp", d.kf[i][:], FMS[13, :, tsl], [], [iB])
                tk.dma("sp", d.lt[i][:], LA[:, tsl], [], [iB])
                g0 = 14
            tk.dma("sp", d.gt[i][:, 0, :], FMS[g0, :, tsl], [], [iB])
            tk.dma("sp", d.gt[i][:, 1, :], FMS[g0 + 1, :, tsl], [], [iB])
            tk.dma("sp", d.vt[i][:], VRG[tsl, grp * 256:(grp + 1) * 256].rearrange("(c s) v -> s c v", s=128), [], [iB])

        def tables(d, grp):
            if grp == 0:
                return (cs[:, C_ER:C_ER + 512], cs[:, C_EINVR:C_EINVR + 512], cs[:, C_KDR:C_KDR + 512],
                        cs[:, C_DECR:C_DECR + 4], g.cstB, g.cstB, g.cstB, g.cstB)
            B = d.B
            return d.Eg[:], d.Einvg[:], d.Kdg[:], d.decg[:], B["E"], B["Einv"], B["Kd"], B["dec"]

        def prep(T, grp):
            d = G[grp]
            B = d.B
            i = T % 2
            iB = d.inB[i]
            q_, k_ = d.qf[i], d.kf[i]
            if T == 0:
                tk.memset("dve", d.stf[:], 0.0, [d.stfB])
                tk.memset("pool", d.stb[0][:], 0.0, [d.stbB[0]])
                d.nst = 0
            if grp == 1:
                l_ = d.lt[i]
                for c in range(4):
                    csl = slice(c * 128, (c + 1) * 128)
                    tk.op("dve", lambda csl=csl: nc.vector.tensor_tensor_scan(
                        out=d.cum[:, csl], data0=ones, data1=l_[:, csl], initial=0.0, op0=ALU.mult, op1=ALU.add),
                        [iB, g.cstB], [B["cum"]])
                tk.act(d.Eg[:], d.cum[:], AF.Exp, [B["cum"]], [B["E"]], scale=-1.0 / 16)
                tk.act(d.Einvg[:], d.cum[:], AF.Exp, [B["cum"]], [B["Einv"]], scale=1.0 / 16)
                tk.act(d.decg[:], d.cum[:].rearrange("p (c s) -> p c s", s=128)[:, :, 127], AF.Exp, [B["cum"]], [B["dec"]],
                       scale=-1.0 / 16)
                for c in range(4):
                    csl = slice(c * 128, (c + 1) * 128)
                    tk.ts("pool", d.Kdg[:, csl], d.Einvg[:, csl], d.decg[:, c:c + 1], None, ALU.mult, ALU.bypass,
                          [B["Einv"], B["dec"]], [B["Kd"]])
            E_, Einv_, Kd_, dec_, EB, EinvB, KdB, decB = tables(d, grp)
            for h in range(4):
                tk.stt(d.Qbd[:, h, :], q_[:], d.hm[:, h:h + 1], E_, ALU.mult, ALU.mult, [iB, g.cstB, EB], [B["Qbd"]])
            tk.tt("pool", d.kt[:], k_[:], Einv_, ALU.mult, [iB, EinvB], [B["kt"]])
            tk.tt("pool", d.ktil[:], k_[:], Kd_, ALU.mult, [iB, KdB], [B["ktil"]])

        def core(T, grp):
            d = G[grp]
            B = d.B
            i = T % 2
            iB = d.inB[i]
            v_ = d.vt[i]
            E_, Einv_, Kd_, dec_, EB, EinvB, KdB, decB = tables(d, grp)
            for c in range(4):
                tk.op("pe", lambda c=c: nc.tensor.transpose(out=trp[:, c * 128:(c + 1) * 128],
                                                            in_=d.ktil[:, c * 128:(c + 1) * 128], identity=g.ident_b[:]),
                      [B["ktil"], g.cstB], [trB], tick=(c == 3))
            tk.copy("act", d.ktok[:].rearrange("p c s -> p (c s)"), trp[:, 0:512], [trB], [B["ktok"]])
            sps = []

            def emit_ST(c):
                csl = slice(c * 128, (c + 1) * 128)
                sp_ = cnt["pt"] % 2
                cnt["pt"] += 1
                sps.append(sp_)
                tk.mm(STp[sp_][:, :], d.kt[:, csl], d.Qbd[:, :, csl], True, True, [B["kt"], B["Qbd"]], [STpB[sp_]], tick=True)
                tk.tt("dve", PT[sp_][:], STp[sp_][:, :].rearrange("p (h c) -> p h c", h=4),
                      lmask4.rearrange("p (h c) -> p h c", h=4), ALU.mult, [STpB[sp_], g.cstB], [PTB[sp_]])

            emit_ST(0)
            for c in range(4):
                csl = slice(c * 128, (c + 1) * 128)
                if c + 1 < 4:
                    emit_ST(c + 1)
                sp_ = sps[c]
                kvs = slice((c % 2) * 256, (c % 2) * 256 + 256)
                tk.mm(kvp[:, kvs], d.ktok[:, c, :], v_[:, c, :], True, True, [B["ktok"], iB], [kvB], tick=True)
                sbi = d.nst % 2
                for h in range(4):
                    j, half = h // 2, h % 2
                    o_ = Opp[j][half * 64:(half + 1) * 64, csl]
                    tk.mm(o_, v_[:, c, h * 64:(h + 1) * 64], PT[sp_][:, h, :], True, False, [iB, PTB[sp_]], [OppB[j]])
                    tk.mm(o_, d.stb[sbi][:, h * 64:(h + 1) * 64], d.Qbd[:, h, csl], False, True,
                          [d.stbB[sbi], B["Qbd"]], [OppB[j]], tick=(h % 2 == 1))
                tk.stt(d.stf[:], d.stf[:], dec_[:, c:c + 1], kvp[:, kvs], ALU.mult, ALU.add, [d.stfB, decB, kvB], [d.stfB])
                d.nst += 1
                sbn = d.nst % 2
                tk.copy("dve", d.stb[sbn][:], d.stf[:], [d.stfB], [d.stbB[sbn]])
            for j in range(2):
                tk.copy("act", d.og[j][:], Opp[j][:, :], [OppB[j]], [B[f"og{j}"]])

        def norm(T, grp):
            d = G[grp]
            B = d.B
            i = T % 2
            iB = d.inB[i]
            g_ = d.gt[i]
            for j in range(2):
                ogB = B[f"og{j}"]
                mean_ps, mB = head_norm(g, d.og[j][:], ogB, 128, B1, B1, 128, (stat[0], stat[1]), (statB[0], statB[1]), d.tmp)
                var, dd = d.tmp["var"], d.tmp["dd"]
                tk.act(var[:], var[:], AF.Ln, [B["var"]], [B["var"]], bias=g.eps_hn[:, 0:1])
                tk.act(var[:], var[:], AF.Exp, [B["var"]], [B["var"]], scale=-0.5)
                tk.tt("dve", dd[:], d.og[j][:], mean_ps[:, :], ALU.subtract, [ogB, mB], [B["dd"]])
                tk.stt(dd[:], dd[:], g.pl_sb[:, P_MSRG + grp * 2 + j:P_MSRG + grp * 2 + j + 1], var[:],
                       ALU.mult, ALU.mult, [B["dd"], B["var"], g.plB], [B["dd"]])
                yi = d.nyy % 2
                d.nyy += 1
                tk.tt("pool", d.yy[yi][:], dd[:], g_[:, j, :], ALU.mult, [B["dd"], iB], [d.yyB[yi]])
                tk.dma("sp", YT[d.ch0 + j * 128:d.ch0 + (j + 1) * 128, T * 512:(T + 1) * 512], d.yy[yi][:], [d.yyB[yi]], [])

        items = [(T, grp) for T in range(8) for grp in range(2)]
        NI = len(items)
        load(*items[0])
        load(*items[1])
        for s in range(NI + 2):
            if s + 2 < NI:
                pass
            if s < NI:
                prep(*items[s])
            if 0 <= s - 1 < NI:
                core(*items[s - 1])
            if 0 <= s - 2 < NI:
                norm(*items[s - 2])
                if s < NI:
                    pass
            if s + 2 < NI:
                load(*items[s + 2])


def ln_alloc(nc, es, pfx, N):
    tmp = {"B": {k: Buf(pfx + k) for k in ("zb", "zq", "msq", "var", "rstd", "nmr")}}
    tmp["zb"] = sbt(nc, es, pfx + "zb", [128, 8, N], BF16)
    tmp["zq"] = sbt(nc, es, pfx + "zq", [128, 8, N], BF16)
    for k in ("msq", "var", "rstd", "nmr"):
        tmp[k] = sbt(nc, es, pfx + k, [128, N], F32)
    return tmp


def ln_part1(g, z, zB, N, tmp):
    tk = g.tk
    B = tmp["B"]
    tk.copy("dve", tmp["zb"][:, :, 0:N], z[:, :, 0:N], list(zB), [B["zb"]])
    tk.act(tmp["zq"][:, :, 0:N], z[:, :, 0:N], AF.Square, list(zB), [B["zq"]])


def ln_part2(g, z, zB, N, gcol, bcol, dst_ap, stat_ps, stat_bufs, tmp, dst_bf=None):
    tk = g.tk
    zb, zq, msq, var, rstd, nmr = tmp["zb"], tmp["zq"], tmp["msq"], tmp["var"], tmp["rstd"], tmp["nmr"]
    B = tmp["B"]
    mean_ps, e2_ps = stat_ps
    mB, eB = stat_bufs
    for c in range(8):
        tk.mm(mean_ps[:, 0:N], g.ones_b[:], zb[:, c, 0:N], c == 0, c == 7, [g.cstB, B["zb"]], [mB], tick=(c == 7))
    for c in range(8):
        tk.mm(e2_ps[:, 0:N], g.ones_b[:], zq[:, c, 0:N], c == 0, c == 7, [g.cstB, B["zq"]], [eB], tick=(c == 7))
    tk.act(msq[:, 0:N], mean_ps[:, 0:N], AF.Square, [mB], [B["msq"]])
    tk.tt("dve", var[:, 0:N], e2_ps[:, 0:N], msq[:, 0:N], ALU.subtract, [eB, B["msq"]], [B["var"]])
    tk.act(var[:, 0:N], var[:, 0:N], AF.Ln, [B["var"]], [B["var"]], bias=g.eps_ln[:, 0:1])
    tk.act(rstd[:, 0:N], var[:, 0:N], AF.Exp, [B["var"]], [B["rstd"]], scale=-0.5)
    tk.stt(nmr[:, 0:N], mean_ps[:, 0:N], -1.0, rstd[:, 0:N], ALU.mult, ALU.mult, [mB, B["rstd"]], [B["nmr"]])
    for c in range(8):
        e2 = "pool" if c % 2 == 0 else "dve"
        tk.tt("dve", z[:, c, 0:N], z[:, c, 0:N], rstd[:, 0:N], ALU.mult, [zB[c], B["rstd"]], [zB[c]])
        tk.tt(e2, z[:, c, 0:N], z[:, c, 0:N], nmr[:, 0:N], ALU.add, [zB[c], B["nmr"]], [zB[c]])
        tk.act(z[:, c, 0:N], z[:, c, 0:N], AF.Identity, [zB[c], g.plB], [zB[c]], scale=gcol[:, c:c + 1], bias=bcol[:, c:c + 1])
    tk.dma("sp", dst_ap.rearrange("(c p) t -> p c t", p=128), z[:, :, 0:N], list(zB), [])
    if dst_bf is not None:
        tk.copy("act", zb[:, :, 0:N], z[:, :, 0:N], list(zB), [B["zb"]])
        tk.dma("sp", dst_bf.rearrange("(c p) t -> p c t", p=128), zb[:, :, 0:N], [B["zb"]], [])


def phase_O(g, l, xsrc, wout, YT, X1F, X1B):
    tk, nc = g.tk, g.nc
    with ExitStack() as es:
        wo = sbt(nc, es, "wo", [128, 8, D], BF16)
        wB = Buf("wo")
        yt = [sbt(nc, es, f"o_yt{i}", [128, 8, 512], BF16) for i in range(2)]
        xr = [sbt(nc, es, f"o_xr{i}", [128, 8, 512], F32) for i in range(2)]
        inB = [Buf(f"o_in{i}") for i in range(2)]
        z = [sbt(nc, es, f"o_z{i}", [128, 8, 512], F32) for i in range(2)]
        zB = [[Buf(f"o_z{i}_{c}") for c in range(8)] for i in range(2)]
        tmp = ln_alloc(nc, es, "o_", 512)
        pb = [pst(nc, es, f"o_pb{i}", [128, 512]) for i in range(6)]
        pbB = [Buf(f"o_pb{i}") for i in range(6)]
        stat = [pst(nc, es, f"o_stat{i}", [128, 512]) for i in range(2)]
        statB = [Buf(f"o_stat{i}") for i in range(2)]
        for kc in range(8):
            tk.dma("pool", wo[:, kc, :], wout[l, kc * 128:(kc + 1) * 128, :], [], [wB])
        yv = YT.rearrange("(c p) t -> p c t", p=128)
        xv = xsrc.rearrange("(c p) t -> p c t", p=128)
        gcol, bcol = g.pl_sb[:, P_LN1G:P_LN1G + 8], g.pl_sb[:, P_LN1B:P_LN1B + 8]

        def load(T):
            tk.dma("sp", yt[T % 2][:], yv[:, :, T * 512:(T + 1) * 512], [], [inB[T % 2]])
            tk.dma("sp", xr[T % 2][:], xv[:, :, T * 512:(T + 1) * 512], [], [inB[T % 2]])

        def fin(T):
            i = T % 2
            ln_part2(g, z[i], zB[i], 512, gcol, bcol, X1F[:, T * 512:(T + 1) * 512], (stat[0], stat[1]),
                     (statB[0], statB[1]), tmp, dst_bf=X1B[:, T * 512:(T + 1) * 512])

        load(0)
        nb = 0
        pend = None
        for T in range(8):
            if T + 1 < 8:
                load(T + 1)
            i = T % 2
            for oc in range(8):
                b = nb % 6
                nb += 1
                for kc in range(8):
                    tk.mm(pb[b][:, :], wo[:, kc, oc * 128:(oc + 1) * 128], yt[i][:, kc, :], kc == 0, kc == 7,
                          [wB, inB[i]], [pbB[b]], tick=(kc == 7))
                tk.stt(z[i][:, oc, :], xr[i][:, oc, :], ALPHA, pb[b][:, :], ALU.mult, ALU.add, [inB[i], pbB[b]], [zB[i][oc]])
                if oc == 3 and pend is not None:
                    fin(pend)
                    pend = None
            ln_part1(g, z[i], zB[i], 512, tmp)
            pend = T
        fin(pend)


def phase_F(g, l, wup, wdown, X1F, X1B, xdst, HT):
    tk, nc = g.tk, g.nc
    NT = 256
    NTI = S // NT
    xv = X1F.rearrange("(c p) t -> p c t", p=128)
    xbv = X1B.rearrange("(c p) t -> p c t", p=128)
    cw = g.pl_sb[:, P_CW:P_CW + 132]
    cb = g.pl_sb[:, P_CB:P_CB + 44]
    with ExitStack() as eso:
        wd = sbt(nc, eso, "wd", [128, 22, D], BF16)
        wdB = Buf("wd")
        with ExitStack() as es:
            x1b = sbt(nc, es, "f_x1b", [128, 8, S + 2], BF16)
            xB = [Buf(f"f_x1b{T}") for T in range(NTI)]
            haloB = Buf("f_halo")
            NW = 3
            wch = [sbt(nc, es, f"f_wch{i}", [128, 8, 256], BF16) for i in range(NW)]
            wchB = [Buf(f"f_wch{i}") for i in range(NW)]
            og = [sbt(nc, es, f"f_og{i}", [128, NT], F32) for i in range(2)]
            a2 = [sbt(nc, es, f"f_a2{i}", [128, NT], F32) for i in range(2)]
            ov = [sbt(nc, es, f"f_ov{i}", [128, NT], F32) for i in range(2)]
            sg = [sbt(nc, es, f"f_sg{i}", [128, NT], F32) for i in range(2)]
            ogB = [Buf(f"f_og{i}") for i in range(2)]
            a2B = [Buf(f"f_a2{i}") for i in range(2)]
            ovB = [Buf(f"f_ov{i}") for i in range(2)]
            sgB = [Buf(f"f_sg{i}") for i in range(2)]
            hst = [sbt(nc, es, f"f_hst{i}", [128, S], BF16) for i in range(2)]
            hstB = [Buf(f"f_hst{i}") for i in range(2)]
            pb = [pst(nc, es, f"f_pb{i}", [128, 512]) for i in range(8)]
            pbB = [Buf(f"f_pb{i}") for i in range(8)]
            wv = wup[l].rearrange("(kc p) n -> p kc n", p=128)

            def loadw(c):
                i = c % NW
                tk.dma("pool", wch[i][:, :, 0:128], wv[:, :, c * 128:(c + 1) * 128], [], [wchB[i]])
                tk.dma("pool", wch[i][:, :, 128:256], wv[:, :, (22 + c) * 128:(23 + c) * 128], [], [wchB[i]])

            tk.memset("dve", x1b[:, :, 0:2], 0.0, [haloB])
            loadw(0)
            for T in range(NTI):
                tk.dma("sp", x1b[:, :, 2 + T * NT:2 + (T + 1) * NT], xbv[:, :, T * NT:(T + 1) * NT], [], [xB[T]])
                if T == 1:
                    loadw(1)
            nb = 0
            ce = 0
            tail = [None]
            for c in range(22):
                if c + 2 < 22:
                    loadw(c + 2)
                tk.dma("pool", wd[:, c, :], wdown[l, c * 128:(c + 1) * 128, :], [], [wdB])
                wi = c % NW
                hs, hsB = hst[c % 2], hstB[c % 2]
                for T in range(NTI):
                    bg = nb % 8
                    bv = (nb + 1) % 8
                    nb += 2
                    xrd = [wchB[wi], xB[T], xB[T - 1] if T > 0 else haloB]
                    for (bank, off) in ((bg, 0), (bv, 128)):
                        for kc in range(8):
                            tk.mm(pb[bank][:, 0:NT + 2], wch[wi][:, kc, off:off + 128], x1b[:, kc, T * NT:T * NT + NT + 2],
                                  kc == 0, kc == 7, xrd, [pbB[bank]], tick=(kc == 7))
                    s = ce % 2
                    ce += 1
                    G_, V_ = pb[bg], pb[bv]
                    cg, cv_ = c, 22 + c
                    wg = [cw[:, cg * 3 + j:cg * 3 + j + 1] for j in range(3)]
                    wv_ = [cw[:, cv_ * 3 + j:cv_ * 3 + j + 1] for j in range(3)]
                    tk.act(og[s][:], G_[:, 2:NT + 2], AF.Identity, [pbB[bg], g.plB], [ogB[s]], scale=wg[2], bias=cb[:, cg:cg + 1])
                    tk.act(a2[s][:], G_[:, 1:NT + 1], AF.Identity, [pbB[bg], g.plB], [a2B[s]], scale=wg[1])
                    tk.stt(og[s][:], G_[:, 0:NT], wg[0], og[s][:], ALU.mult, ALU.add, [pbB[bg], g.plB, ogB[s], a2B[s]], [ogB[s]])
                    tk.act(ov[s][:], V_[:, 2:NT + 2], AF.Identity, [pbB[bv], g.plB], [ovB[s]], scale=wv_[2], bias=cb[:, cv_:cv_ + 1])
                    tk.stt(ov[s][:], V_[:, 1:NT + 1], wv_[1], ov[s][:], ALU.mult, ALU.add, [pbB[bv], g.plB, ovB[s]], [ovB[s]])
                    tk.stt(ov[s][:], V_[:, 0:NT], wv_[0], ov[s][:], ALU.mult, ALU.add, [pbB[bv], g.plB, ovB[s]], [ovB[s]])
                    tk.tt("pool", og[s][:], og[s][:], a2[s][:], ALU.add, [ogB[s], a2B[s]], [ogB[s]])
                    if tail[0] is not None:
                        tail[0]()

                    def mk(s=s, hs=hs, hsB=hsB, T=T):
                        def f():
                            tk.act(sg[s][:], og[s][:], AF.Silu, [ogB[s]], [sgB[s]])
                            tk.tt("pool", hs[:, T * NT:(T + 1) * NT], sg[s][:], ov[s][:], ALU.mult, [sgB[s], ovB[s]], [hsB])
                        return f
                    tail[0] = mk()
                    if T == NTI - 1:
                        tail[0]()
                        tail[0] = None
                tk.dma("sp", HT[c * 128:(c + 1) * 128, :], hs[:, :], [hsB], [])
        tk.barrier()
        with ExitStack() as es:
            ht = [sbt(nc, es, f"d_ht{i}", [128, 22, 512], BF16) for i in range(2)]
            xr = [sbt(nc, es, f"d_xr{i}", [128, 8, 512], F32) for i in range(2)]
            inB = [Buf(f"d_in{i}") for i in range(2)]
            z = [sbt(nc, es, f"d_z{i}", [128, 8, 512], F32) for i in range(2)]
            zB = [[Buf(f"d_z{i}_{c}") for c in range(8)] for i in range(2)]
            tmp = ln_alloc(nc, es, "d_", 512)
            pb = [pst(nc, es, f"d_pb{i}", [128, 512]) for i in range(6)]
            pbB = [Buf(f"d_pb{i}") for i in range(6)]
            stat = [pst(nc, es, f"d_stat{i}", [128, 512]) for i in range(2)]
            statB = [Buf(f"d_stat{i}") for i in range(2)]
            hv = HT.rearrange("(c p) t -> p c t", p=128)
            gcol, bcol = g.pl_sb[:, P_LN2G:P_LN2G + 8], g.pl_sb[:, P_LN2B:P_LN2B + 8]

            def load(T):
                tk.dma("sp", ht[T % 2][:], hv[:, :, T * 512:(T + 1) * 512], [], [inB[T % 2]])
                tk.dma("sp", xr[T % 2][:], xv[:, :, T * 512:(T + 1) * 512], [], [inB[T % 2]])

            def fin(T):
                i = T % 2
                ln_part2(g, z[i], zB[i], 512, gcol, bcol, xdst[:, T * 512:(T + 1) * 512], (stat[0], stat[1]),
                         (statB[0], statB[1]), tmp)

            load(0)
            nb = 0
            pend = None
            for T in range(8):
                if T + 1 < 8:
                    load(T + 1)
                i = T % 2
                for oc in range(8):
                    b = nb % 6
                    nb += 1
                    for c in range(22):
                        tk.mm(pb[b][:, :], wd[:, c, oc * 128:(oc + 1) * 128], ht[i][:, c, :], c == 0, c == 21,
                              [wdB, inB[i]], [pbB[b]], tick=(c == 21))
                    tk.stt(z[i][:, oc, :], xr[i][:, oc, :], ALPHA, pb[b][:, :], ALU.mult, ALU.add, [inB[i], pbB[b]], [zB[i][oc]])
                    if oc == 3 and pend is not None:
                        fin(pend)
                        pend = None
                ln_part1(g, z[i], zB[i], 512, tmp)
                pend = T
            fin(pend)


_CACHE = {}


def _prep_weights(w_in, w_alpha, b_alpha, mix_scale, w_out, ln1_g, ln1_b, w_up, conv_w, conv_b, w_down, ln2_g, ln2_b):
    f = lambda a: np.ascontiguousarray(np.asarray(a, dtype=np.float32))
    win_p = f(np.asarray(w_in)[:, :, win_perm()])
    plb = np.zeros((DEPTH, 128, NPL), np.float32)
    ms = np.asarray(mix_scale, np.float32)
    for l in range(DEPTH):
        plb[l, 0:64, P_MSA:P_MSA + 8] = ms[l, 0:512].reshape(8, 64).T
        plb[l, :, P_MSRG:P_MSRG + 4] = ms[l, 512:1024].reshape(4, 128).T
        plb[l, :, P_LN1G:P_LN1G + 8] = np.asarray(ln1_g)[l].reshape(8, 128).T
        plb[l, :, P_LN1B:P_LN1B + 8] = np.asarray(ln1_b)[l].reshape(8, 128).T
        plb[l, :, P_LN2G:P_LN2G + 8] = np.asarray(ln2_g)[l].reshape(8, 128).T
        plb[l, :, P_LN2B:P_LN2B + 8] = np.asarray(ln2_b)[l].reshape(8, 128).T
        cwl = np.asarray(conv_w)[l].reshape(3, 44, 128)
        plb[l, :, P_CW:P_CW + 132] = cwl.transpose(2, 1, 0).reshape(128, 132)
        plb[l, :, P_CB:P_CB + 44] = np.asarray(conv_b)[l].reshape(44, 128).T
        plb[l, :, P_BA] = np.asarray(b_alpha)[l]
    return dict(win=win_p, walpha=f(w_alpha), wout=f(w_out), wup=f(w_up), wdown=f(w_down), pl=plb)


def kernel(x, w_in, w_alpha, b_alpha, mix_scale, w_out, ln1_g, ln1_b, w_up, conv_w, conv_b, w_down, ln2_g, ln2_b):
    x = np.asarray(x, dtype=np.float32)
    if "nc" not in _CACHE:
        _CACHE["nc"] = build(DEPTH)[0]
        _CACHE["cst"] = make_consts()
    nc = _CACHE["nc"]
    cst, rope = _CACHE["cst"]
    wd = _prep_weights(w_in, w_alpha, b_alpha, mix_scale, w_out, ln1_g, ln1_b, w_up, conv_w, conv_b, w_down, ln2_g, ln2_b)
    in_maps = []
    for b in range(8):
        m = dict(wd)
        m["xin"] = np.ascontiguousarray(x[b].T)
        m["cst"] = cst
        m["rope"] = rope
        in_maps.append(m)
    res = run_bass_kernel_spmd(nc, in_maps, core_ids=list(range(8)))
    outp = np.stack([np.asarray(r["out"], dtype=np.float32).T for r in res.results], axis=0)
    return np.ascontiguousarray(outp)
```

```python
import numpy as np
from contextlib import ExitStack
import concourse.bass as bass
import concourse.mybir as mybir
from concourse.bass_utils import run_bass_kernel_spmd

F32 = mybir.dt.float32
BF16 = mybir.dt.bfloat16
AF = mybir.ActivationFunctionType
ALU = mybir.AluOpType

S = 4096
D = 1024
DEPTH = 4
DFF = 2816
PW = 3088
ALPHA = float((2 * DEPTH) ** 0.25)
LN_EPS = 1e-5
HN_EPS = 1e-6
NEG = -30000.0
NFM = 16
SC_A = 64 ** -0.5
SC_L = 32 ** -0.5

C_MASKB, C_IDENT, C_LMASK, C_A1, C_A2, C_B1, C_ONES1024, C_HMR, C_HMG, C_DECR, C_ER, C_EINVR, C_KDR, C_ONES = (
    0, 256, 384, 512, 576, 640, 768, 896, 900, 904, 908, 1420, 1932, 2444)
C_LMASK4 = 2572
C_MASK01 = 3084
NCST = 3340

P_MSA, P_MSRG, P_LN1G, P_LN1B, P_LN2G, P_LN2B, P_CW, P_CB, P_BA = 0, 8, 12, 20, 28, 36, 44, 176, 220
NPL = 221


def make_consts():
    c = np.zeros((128, NCST), np.float32)
    j = np.arange(128)[:, None]
    i = np.arange(128)[None, :]
    c[:, C_MASKB:C_MASKB + 128] = np.where(j <= i, 0.0, NEG)
    c[:, C_MASKB + 128:C_MASKB + 256] = np.where(j >= i, 0.0, NEG)
    c[:, C_IDENT:C_IDENT + 128] = np.eye(128, dtype=np.float32)
    c[:, C_MASK01:C_MASK01 + 128] = (j <= i).astype(np.float32)
    c[:, C_MASK01 + 128:C_MASK01 + 256] = (j >= i).astype(np.float32)
    c[:, C_LMASK:C_LMASK + 128] = (j <= i).astype(np.float32)
    for h in range(4):
        c[:, C_LMASK4 + h * 128:C_LMASK4 + (h + 1) * 128] = (j <= i).astype(np.float32)
    c[0:64, C_A1:C_A1 + 64] = 1.0 / 64
    c[0:64, C_A2:C_A2 + 64] = 1.0 / 64
    c[64, C_A2:C_A2 + 64] = HN_EPS
    for h in range(2):
        c[h * 64:(h + 1) * 64, C_B1 + h * 64:C_B1 + (h + 1) * 64] = 1.0 / 64
    c[:, C_ONES1024:C_ONES1024 + 128] = 1.0 / 1024
    p = np.arange(128)
    headR = (p % 64) // 16
    headG = p // 32
    for h in range(4):
        c[:, C_HMR + h] = (headR == h) * SC_L
        c[:, C_HMG + h] = (headG == h) * SC_L
    lg = np.log(1.0 - np.power(2.0, -5.0 - np.arange(4, dtype=np.float64)))
    lgp = lg[headR][:, None]
    idx = (np.arange(512) % 128)[None, :].astype(np.float64)
    c[:, C_DECR:C_DECR + 4] = np.exp(lgp * 128.0)
    c[:, C_ER:C_ER + 512] = np.exp(lgp * (idx + 1.0))
    c[:, C_EINVR:C_EINVR + 512] = np.exp(-lgp * (idx + 1.0))
    c[:, C_KDR:C_KDR + 512] = np.exp(lgp * (127.0 - idx))
    c[:, C_ONES:C_ONES + 128] = 1.0
    rope = np.zeros((4, 128, S), np.float32)
    pos = np.arange(S, dtype=np.float32)[None, :]
    invA = (1.0 / (10000.0 ** (np.arange(0, 64, 2, dtype=np.float32) / 64))).astype(np.float32)
    invR = (1.0 / (10000.0 ** (np.arange(0, 32, 2, dtype=np.float32) / 32))).astype(np.float32)
    angA = (pos * invA[p % 32][:, None]).astype(np.float32)
    angR = (pos * invR[p % 16][:, None]).astype(np.float32)
    rope[0], rope[1] = np.cos(angA), np.sin(angA)
    rope[2], rope[3] = np.cos(angR), np.sin(angR)
    return c, rope


def win_perm():
    qA, kA, vA, qR, kR, vR, gR, qG, kG, vG, rG, aG = 0, 512, 1024, 1536, 1664, 1792, 2048, 2304, 2432, 2560, 2816, 3072
    cols = []
    for base in (qA, kA):
        for g in range(2):
            cols += [base + h * 64 + i for h in range(4 * g, 4 * g + 4) for i in range(32)]
            cols += [base + h * 64 + 32 + i for h in range(4 * g, 4 * g + 4) for i in range(32)]
    for half in range(2):
        cols += [qR + h * 32 + half * 16 + i for h in range(4) for i in range(16)]
        cols += [kR + h * 32 + half * 16 + i for h in range(4) for i in range(16)]
    cols += list(range(gR, gR + 256))
    cols += list(range(qG, qG + 128))
    cols += list(range(kG, kG + 128))
    cols += list(range(rG, rG + 256))
    cols += list(range(aG, aG + 16))
    cols += list(range(vA, vA + 512))
    cols += list(range(vR, vR + 256))
    cols += list(range(vG, vG + 256))
    assert len(cols) == PW and len(set(cols)) == PW
    return np.array(cols)


class Buf:
    __slots__ = ("w", "r", "pw", "pr", "name")

    def __init__(self, name=""):
        self.w = {}
        self.r = {}
        self.pw = None
        self.pr = set()
        self.name = name


class TK:
    CE = ("pe", "act", "dve", "pool")

    def __init__(self, nc, es):
        self.nc = nc
        self.E = {"pe": nc.tensor, "act": nc.scalar, "dve": nc.vector, "pool": nc.gpsimd, "sp": nc.sync}
        self.sem = {}
        self.val = {}
        self.seen = {e: {} for e in self.E}
        for e in self.CE:
            self._mk(es, "c_" + e)
        self.dq = {}
        for q, n in (("sp", 8), ("pool", 8)):
            self.dq[q] = [self._mk(es, f"d_{q}{i}") for i in range(n)]
        self.dqi = {q: 0 for q in self.dq}
        self.pending = {e: [] for e in self.CE}
        self.nins = 0

    def _mk(self, es, name):
        self.sem[name] = es.enter_context(self.nc.semaphore(name))
        self.val[name] = 0
        return name

    def wait(self, e, s, v):
        if v <= self.seen[e].get(s, 0):
            return
        self.E[e].wait_ge(self.sem[s], v)
        self.seen[e][s] = v
        self.nins += 1

    def _deps(self, e, reads, writes, dma=False):
        own = "c_" + e
        deps = {}
        for b in reads:
            assert b.pw in (None, e), f"read of {b.name} with pending writer {b.pw}"
            for s, v in b.w.items():
                if s == own and (e == "pe" and not dma):
                    continue
                if v > deps.get(s, 0):
                    deps[s] = v
        for b in writes:
            assert b.pw in (None, e), f"write of {b.name} with pending writer {b.pw}"
            assert not (b.pr - {e}), f"write of {b.name} with pending readers {b.pr}"
            for dd in (b.w, b.r):
                for s, v in dd.items():
                    if s == own and not dma:
                        continue
                    if v > deps.get(s, 0):
                        deps[s] = v
        for s, v in deps.items():
            self.wait(e, s, v)

    def op(self, e, emit, reads=(), writes=(), tick=True):
        self._deps(e, reads, writes)
        ins = emit()
        self.nins += 1
        own = "c_" + e
        self.pending[e].append((reads, writes))
        if tick:
            self.val[own] += 1
            ins.then_inc(self.sem[own], 1)
            v = self.val[own]
            for rs, ws in self.pending[e]:
                for b in rs:
                    b.r[own] = v
                    b.pr.discard(e)
                for b in ws:
                    b.w[own] = v
                    b.pw = None
            self.pending[e] = []
        else:
            for b in reads:
                b.pr.add(e)
            for b in writes:
                b.pw = e
        return ins

    def dma(self, q, out, in_, reads=(), writes=()):
        self._deps(q, reads, writes, dma=True)
        names = self.dq[q]
        nm = names[self.dqi[q] % len(names)]
        self.dqi[q] += 1
        self.wait(q, nm, self.val[nm])
        ins = self.E[q].dma_start(out=out, in_=in_)
        self.nins += 1
        self.val[nm] += 16
        ins.then_inc(self.sem[nm], 16)
        v = self.val[nm]
        for b in reads:
            b.r[nm] = v
        for b in writes:
            b.w[nm] = v
        return ins

    def barrier(self):
        for e in self.CE:
            assert not self.pending[e], f"pending un-ticked ops on {e}"
        for e in self.E:
            for s, v in self.val.items():
                if s == "c_" + e:
                    continue
                self.wait(e, s, v)

    def mm(self, out, lhsT, rhs, start, stop, reads, writes, tick=False):
        return self.op("pe", lambda: self.nc.tensor.matmul(out, lhsT=lhsT, rhs=rhs, start=start, stop=stop,
                                                           skip_group_check=True), reads, writes, tick)

    def act(self, out, in_, func, reads, writes, **kw):
        return self.op("act", lambda: self.nc.scalar.activation(out=out, in_=in_, func=func, **kw), reads, writes)

    def tt(self, e, out, in0, in1, op, reads, writes):
        return self.op(e, lambda: self.E[e].tensor_tensor(out=out, in0=in0, in1=in1, op=op), reads, writes)

    def stt(self, out, in0, scalar, in1, op0, op1, reads, writes):
        return self.op("dve", lambda: self.nc.vector.scalar_tensor_tensor(out=out, in0=in0, scalar=scalar, in1=in1,
                                                                         op0=op0, op1=op1), reads, writes)

    def ts(self, e, out, in0, s1, s2, op0, op1, reads, writes):
        return self.op(e, lambda: self.E[e].tensor_scalar(out=out, in0=in0, scalar1=s1, scalar2=s2, op0=op0, op1=op1),
                       reads, writes)

    def copy(self, e, out, in_, reads, writes):
        if e == "act":
            return self.act(out, in_, AF.Copy, reads, writes)
        return self.op(e, lambda: self.E[e].tensor_copy(out=out, in_=in_), reads, writes)

    def memset(self, e, ap, val, writes):
        return self.op(e, lambda: self.E[e].memset(ap, val), (), writes)


class Ctx:
    pass


_UNIQ = [0]


def sbt(nc, es, name, shape, dt):
    _UNIQ[0] += 1
    return es.enter_context(nc.sbuf_tensor(f"{name}_{_UNIQ[0]}", shape, dt))


def pst(nc, es, name, shape, dt=F32):
    _UNIQ[0] += 1
    return es.enter_context(nc.psum_tensor(f"{name}_{_UNIQ[0]}", shape, dt))


def layer_norm(g, es_name, z, zB, N, gcol, bcol, dst_ap, stat_ps, stat_bufs, tmp):
    tk, nc = g.tk, g.nc
    zb, zq, msq, var, rstd, nmr = tmp["zb"], tmp["zq"], tmp["msq"], tmp["var"], tmp["rstd"], tmp["nmr"]
    B = tmp["B"]
    tk.act(zb[:, :, 0:N], z[:, :, 0:N], AF.Copy, [zB], [B["zb"]])
    tk.act(zq[:, :, 0:N], z[:, :, 0:N], AF.Square, [zB], [B["zq"]])
    mean_ps, e2_ps = stat_ps
    mB, eB = stat_bufs
    for c in range(8):
        tk.mm(mean_ps[:, 0:N], g.ones_b[:], zb[:, c, 0:N], c == 0, c == 7, [g.cstB, B["zb"]], [mB], tick=(c == 7))
    for c in range(8):
        tk.mm(e2_ps[:, 0:N], g.ones_b[:], zq[:, c, 0:N], c == 0, c == 7, [g.cstB, B["zq"]], [eB], tick=(c == 7))
    tk.act(msq[:, 0:N], mean_ps[:, 0:N], AF.Square, [mB], [B["msq"]])
    tk.tt("dve", var[:, 0:N], e2_ps[:, 0:N], msq[:, 0:N], ALU.subtract, [eB, B["msq"]], [B["var"]])
    tk.act(var[:, 0:N], var[:, 0:N], AF.Ln, [B["var"]], [B["var"]], bias=g.eps_ln[:, 0:1])
    tk.act(rstd[:, 0:N], var[:, 0:N], AF.Exp, [B["var"]], [B["rstd"]], scale=-0.5)
    tk.stt(nmr[:, 0:N], mean_ps[:, 0:N], -1.0, rstd[:, 0:N], ALU.mult, ALU.mult, [mB, B["rstd"]], [B["nmr"]])
    for c in range(8):
        e1 = "dve" if c % 2 == 0 else "pool"
        tk.tt(e1, z[:, c, 0:N], z[:, c, 0:N], rstd[:, 0:N], ALU.mult, [zB, B["rstd"]], [zB])
        tk.tt("pool", z[:, c, 0:N], z[:, c, 0:N], nmr[:, 0:N], ALU.add, [zB, B["nmr"]], [zB])
        tk.act(z[:, c, 0:N], z[:, c, 0:N], AF.Identity, [zB, g.plB], [zB], scale=gcol[:, c:c + 1], bias=bcol[:, c:c + 1])
    tk.dma("sp", dst_ap.rearrange("(c p) t -> p c t", p=128), z[:, :, 0:N], [zB], [])


def head_norm(g, src, srcB, K, lhs1, lhs2, M, stat_ps, stat_bufs, tmp, N=512):
    tk = g.tk
    B = tmp["B"]
    sq, msq, var, dd = tmp["sq"], tmp["msq"], tmp["var"], tmp["dd"]
    mean_ps, e2_ps = stat_ps
    mB, eB = stat_bufs
    srcb = tmp["srcb"]
    tk.act(sq[0:K, 0:N], src, AF.Square, [srcB], [B["sq"]])
    tk.act(srcb[0:K, 0:N], src, AF.Copy, [srcB], [B["srcb"]])
    tk.mm(mean_ps[0:M, 0:N], lhs1, srcb[0:K, 0:N], True, True, [g.cstB, B["srcb"]], [mB], tick=True)
    tk.mm(e2_ps[0:M, 0:N], lhs2, sq[0:K, 0:N], True, True, [g.cstB, B["sq"]], [eB], tick=True)
    tk.act(msq[0:M, 0:N], mean_ps[0:M, 0:N], AF.Square, [mB], [B["msq"]])
    tk.tt("dve", var[0:M, 0:N], e2_ps[0:M, 0:N], msq[0:M, 0:N], ALU.subtract, [eB, B["msq"]], [B["var"]])
    return mean_ps, mB


def build(depth=DEPTH, debug=False):
    nc = bass.Bass("TRN2", target_bir_lowering=False)
    g = Ctx()
    g.nc = nc
    dkind = "ExternalOutput" if debug else "Internal"
    xin = nc.dram_tensor("xin", [D, S], F32, kind="ExternalInput").ap()
    win = nc.dram_tensor("win", [DEPTH, D, PW], F32, kind="ExternalInput").ap()
    walpha = nc.dram_tensor("walpha", [DEPTH, 16, 128], F32, kind="ExternalInput").ap()
    wout = nc.dram_tensor("wout", [DEPTH, D, D], F32, kind="ExternalInput").ap()
    wup = nc.dram_tensor("wup", [DEPTH, D, 2 * DFF], F32, kind="ExternalInput").ap()
    wdown = nc.dram_tensor("wdown", [DEPTH, DFF, D], F32, kind="ExternalInput").ap()
    pl = nc.dram_tensor("pl", [DEPTH, 128, NPL], F32, kind="ExternalInput").ap()
    cst = nc.dram_tensor("cst", [128, NCST], F32, kind="ExternalInput").ap()
    rope = nc.dram_tensor("rope", [4, 128, S], F32, kind="ExternalInput").ap()
    out = nc.dram_tensor("out", [D, S], F32, kind="ExternalOutput").ap()
    XA = nc.dram_tensor("XA", [D, S], F32, kind="Internal").ap()
    X1F = nc.dram_tensor("X1F", [D, S], F32, kind=dkind).ap()
    YT = nc.dram_tensor("YT", [D, S], BF16, kind=dkind).ap()
    FMS = nc.dram_tensor("FMS", [NFM, 128, S], BF16, kind=dkind).ap()
    LA = nc.dram_tensor("LA", [128, S], F32, kind=dkind).ap()
    VA = nc.dram_tensor("VA", [S, 8, 65], BF16, kind=dkind).ap()
    VRG = nc.dram_tensor("VRG", [S, 512], BF16, kind=dkind).ap()
    HT = nc.dram_tensor("HT", [DFF, S], BF16, kind="Internal").ap()
    X1B = nc.dram_tensor("X1B", [D, S], BF16, kind="Internal").ap()

    with ExitStack() as es:
        tk = TK(nc, es)
        g.tk = tk
        block = es.enter_context(nc.Block())

        @block.sync
        def _(sync):
            cst_sb = sbt(nc, es, "cst_sb", [128, NCST], F32)
            g.cstB = Buf("cst")
            g.cst = cst_sb
            tk.dma("sp", cst_sb[:], cst[:, :], [], [g.cstB])
            g.maskb = sbt(nc, es, "maskb", [128, 256], BF16)
            g.ident_b = sbt(nc, es, "ident_b", [128, 128], BF16)
            g.ones_b = sbt(nc, es, "ones_b", [128, 128], BF16)
            g.eps_ln = sbt(nc, es, "eps_ln", [128, 1], F32)
            tk.copy("dve", g.maskb[:], cst_sb[:, C_MASKB:C_MASKB + 256], [g.cstB], [g.cstB])
            tk.copy("dve", g.ident_b[:], cst_sb[:, C_IDENT:C_IDENT + 128], [g.cstB], [g.cstB])
            tk.copy("dve", g.ones_b[:], cst_sb[:, C_ONES1024:C_ONES1024 + 128], [g.cstB], [g.cstB])
            tk.memset("dve", g.eps_ln[:], LN_EPS, [g.cstB])
            g.mask01 = sbt(nc, es, "mask01", [128, 4, 256], BF16)
            for jj in range(4):
                tk.copy("dve", g.mask01[:, jj, :], cst_sb[:, C_MASK01:C_MASK01 + 256], [g.cstB], [g.cstB])
            g.A1b = sbt(nc, es, "A1b", [128, 64], BF16)
            g.A2b = sbt(nc, es, "A2b", [128, 64], BF16)
            g.B1b = sbt(nc, es, "B1b", [128, 128], BF16)
            tk.copy("dve", g.A1b[:], cst_sb[:, C_A1:C_A1 + 64], [g.cstB], [g.cstB])
            tk.copy("dve", g.A2b[:], cst_sb[:, C_A2:C_A2 + 64], [g.cstB], [g.cstB])
            tk.copy("dve", g.B1b[:], cst_sb[:, C_B1:C_B1 + 128], [g.cstB], [g.cstB])
            g.eps_hn = sbt(nc, es, "eps_hn", [128, 1], F32)
            tk.memset("dve", g.eps_hn[:], HN_EPS, [g.cstB])
            g.one_col = cst_sb[:, C_ONES:C_ONES + 1]
            g.wfm = sbt(nc, es, "wfm", [128, 8, 2064], BF16)
            g.wal = sbt(nc, es, "wal", [16, 128], F32)
            g.wB = Buf("w")
            g.wtB = Buf("wt")

            def load_win(l):
                for kc in range(8):
                    tk.dma("pool", g.wfm[:, kc, :], win[l, kc * 128:(kc + 1) * 128, 0:2064], [], [g.wB])
                tk.dma("sp", g.wal[:], walpha[l], [], [g.wB])
            g.load_win = load_win
            g.pl_sb = sbt(nc, es, "pl_sb", [128, NPL], F32)
            g.negb = sbt(nc, es, "negb", [128, 1], F32)
            g.plB = Buf("pl")
            tk.barrier()
            g.load_win(0)

            for l in range(depth):
                xsrc = xin if l == 0 else XA
                xdst = out if l == depth - 1 else XA
                tk.dma("sp", g.pl_sb[:], pl[l], [], [g.plB])
                tk.ts("dve", g.negb[:], g.pl_sb[:, P_BA:P_BA + 1], -1.0, None, ALU.mult, ALU.bypass, [g.plB], [g.plB])
                phase_P(g, l, xsrc, win, walpha, rope, FMS, LA, VA, VRG)
                tk.barrier()
                if l + 1 < depth:
                    g.load_win(l + 1)
                phase_A(g, l, FMS, VA, YT)
                tk.barrier()
                phase_L(g, l, FMS, LA, VRG, YT)
                tk.barrier()
                phase_O(g, l, xsrc, wout, YT, X1F, X1B)
                tk.barrier()
                phase_F(g, l, wup, wdown, X1F, X1B, xdst, HT)
                tk.barrier()
    g.nins = tk.nins
    return nc, g


def phase_P(g, l, xsrc, win, walpha, rope, FMS, LA, VA, VRG):
    tk, nc = g.tk, g.nc
    with ExitStack() as es:
        wfm, wal, wB = g.wfm, g.wal, g.wB
        wtm = sbt(nc, es, "wtm", [128, 8, 1024], BF16)
        for kc in range(8):
            tk.dma("pool", wtm[:, kc, :], win[l, kc * 128:(kc + 1) * 128, 2064:PW], [], [g.wtB])
        xt = [sbt(nc, es, f"xt{i}", [128, 8, 512], BF16) for i in range(2)]
        xtB = [Buf(f"xt{i}") for i in range(2)]
        rp = [sbt(nc, es, f"rp{i}", [128, 4, 512], F32) for i in range(2)]
        rpB = [Buf(f"rp{i}") for i in range(2)]
        NSO = 6
        so = [sbt(nc, es, f"so{i}", [128, 512], BF16) for i in range(NSO)]
        soB = [Buf(f"so{i}") for i in range(NSO)]
        tmpf = [[sbt(nc, es, f"rt{s}_{i}", [128, 512], F32) for i in range(4)] for s in range(2)]
        tmpB = [[Buf(f"rt{s}_{i}") for i in range(4)] for s in range(2)]
        vst = [sbt(nc, es, f"vst{i}", [128, 8, 65], BF16) for i in range(2)]
        vstB = [Buf(f"vst{i}") for i in range(2)]
        vrg = [sbt(nc, es, f"vrg{i}", [128, 512], BF16) for i in range(2)]
        vrgB = [Buf(f"vrg{i}") for i in range(2)]
        ag = sbt(nc, es, "ag", [16, 512], F32)
        agB = Buf("ag")
        ez = sbt(nc, es, "ez", [128, 512], F32)
        ezB = Buf("ez")
        lst = [sbt(nc, es, f"lst{i}", [128, 512], F32) for i in range(2)]
        lstB = [Buf(f"lst{i}") for i in range(2)]
        pb = [pst(nc, es, f"pb{i}", [128, 512]) for i in range(8)]
        pbB = [Buf(f"pb{i}") for i in range(8)]
        st = {"bank": 0, "so": 0, "ts": 0, "v": 0, "ev": 0}

        def nbank():
            i = st["bank"] % 8
            st["bank"] += 1
            return i

        def nso():
            i = st["so"] % NSO
            st["so"] += 1
            return i

        for i in range(2):
            tk.memset("pool", vst[i][:], 1.0, [vstB[i]])
        xv = xsrc.rearrange("(c p) t -> p c t", p=128)
        rv = rope.rearrange("f p t -> p f t")

        def load(T):
            tk.dma("pool", xt[T % 2][:], xv[:, :, T * 512:(T + 1) * 512], [], [xtB[T % 2]])
            tk.dma("sp", rp[T % 2][:], rv[:, :, T * 512:(T + 1) * 512], [], [rpB[T % 2]])

        load(0)
        for T in range(8):
            if T + 1 < 8:
                load(T + 1)
            x_, xB_ = xt[T % 2], xtB[T % 2]
            r_, rB_ = rp[T % 2], rpB[T % 2]
            tsl = slice(T * 512, (T + 1) * 512)

            def fm_mm(tile, bank):
                for kc in range(8):
                    tk.mm(pb[bank][:, :], wfm[:, kc, tile * 128:(tile + 1) * 128], x_[:, kc, :], kc == 0, kc == 7,
                          [wB, xB_], [pbB[bank]], tick=(kc == 7))

            def store_fm(tile, si):
                tk.dma("sp", FMS[tile, :, tsl], so[si][:], [soB[si]], [])

            def tm_block(blk):
                for half in range(2):
                    b = nbank()
                    for kc in range(8):
                        tk.mm(pb[b][:, :], x_[:, kc, blk * 128:(blk + 1) * 128], wtm[:, kc, half * 512:(half + 1) * 512],
                              kc == 0, kc == 7, [g.wtB, xB_], [pbB[b]], tick=(kc == 7))
                    vi = st["v"] % 2
                    eng = "act" if st["ev"] % 2 == 0 else "dve"
                    st["ev"] += 1
                    rows = slice(T * 512 + blk * 128, T * 512 + (blk + 1) * 128)
                    if half == 0:
                        tk.copy(eng, vst[vi][:, :, 0:64], pb[b][:, :].rearrange("p (h d) -> p h d", d=64), [pbB[b]], [vstB[vi]])
                        tk.dma("sp", VA[rows, :, :], vst[vi][:], [vstB[vi]], [])
                    else:
                        tk.copy(eng, vrg[vi][:], pb[b][:, :], [pbB[b]], [vrgB[vi]])
                        tk.dma("sp", VRG[rows, :], vrg[vi][:], [vrgB[vi]], [])
                        st["v"] += 1

            pairs = [(0, 1, 0), (2, 3, 0), (4, 5, 0), (6, 7, 0), (8, 9, 2)]
            for pi, (ta, tb, ro) in enumerate(pairs):
                ba, bb = nbank(), nbank()
                fm_mm(ta, ba)
                fm_mm(tb, bb)
                C_ = r_[:, ro, :]
                S_ = r_[:, ro + 1, :]
                s = st["ts"] % 2
                st["ts"] += 1
                t1, t2, t3, t4 = tmpf[s]
                b1, b2, b3, b4 = tmpB[s]
                tk.tt("dve", t1[:], pb[ba][:, :], C_, ALU.mult, [pbB[ba], rB_], [b1])
                tk.tt("dve", t2[:], pb[bb][:, :], S_, ALU.mult, [pbB[bb], rB_], [b2])
                tk.tt("dve", t3[:], pb[ba][:, :], S_, ALU.mult, [pbB[ba], rB_], [b3])
                tk.tt("dve", t4[:], pb[bb][:, :], C_, ALU.mult, [pbB[bb], rB_], [b4])
                sa = nso()
                tk.tt("pool", so[sa][:], t1[:], t2[:], ALU.subtract, [b1, b2], [soB[sa]])
                store_fm(ta, sa)
                sb_ = nso()
                tk.tt("pool", so[sb_][:], t3[:], t4[:], ALU.add, [b3, b4], [soB[sb_]])
                store_fm(tb, sb_)
                if pi < 4 and T > 0:
                    tm_block(pi)
            for tile, kind in ((10, "silu"), (11, "silu"), (12, "copy"), (13, "copy"), (14, "silu"), (15, "silu")):
                b = nbank()
                fm_mm(tile, b)
                si = nso()
                tk.act(so[si][:], pb[b][:, :], AF.Silu if kind == "silu" else AF.Copy, [pbB[b]], [soB[si]])
                store_fm(tile, si)
            if T == 0:
                for pi in range(4):
                    tm_block(pi)
            b = nbank()
            for kc in range(8):
                tk.mm(pb[b][0:16, :], wfm[:, kc, 2048:2064], x_[:, kc, :], kc == 0, kc == 7, [wB, xB_], [pbB[b]], tick=(kc == 7))
            tk.act(ag[:], pb[b][0:16, :], AF.Copy, [pbB[b]], [agB])
            b2_ = nbank()
            tk.mm(pb[b2_][:, :], wal[:], ag[:], True, True, [wB, agB], [pbB[b2_]], tick=True)
            tk.act(ez[:], pb[b2_][:, :], AF.Exp, [pbB[b2_], g.plB], [ezB], scale=-1.0, bias=g.negb[:, 0:1])
            li = T % 2
            tk.act(lst[li][:], ez[:], AF.Ln, [ezB, g.cstB], [lstB[li]], bias=g.one_col)
            tk.dma("sp", LA[:, tsl], lst[li][:], [lstB[li]], [])


def phase_A(g, l, FMS, VA, YT):
    tk, nc = g.tk, g.nc
    with ExitStack() as es:
        qh = [sbt(nc, es, f"qh{i}", [128, S], BF16) for i in range(2)]
        kh = [sbt(nc, es, f"kh{i}", [128, S], BF16) for i in range(2)]
        v3 = [sbt(nc, es, f"v3{i}", [128, 3, 32, 65], BF16) for i in range(2)]
        inB = [Buf(f"ain{i}") for i in range(2)]
        vB = [[Buf(f"av{i}_{di}") for di in range(3)] for i in range(2)]
        acc = [sbt(nc, es, f"acc{i}", [65, S], F32) for i in range(2)]
        accB = [Buf(f"acc{i}") for i in range(2)]
        PT = [sbt(nc, es, f"PT{i}", [128, 1024], BF16) for i in range(3)]
        PTB = [[Buf(f"PT{i}a"), Buf(f"PT{i}b")] for i in range(3)]
        yst = [sbt(nc, es, f"yst{i}", [64, S], BF16) for i in range(2)]
        ystB = [Buf(f"yst{i}") for i in range(2)]
        tmp = {"B": {k: Buf("a_" + k) for k in ("sq", "srcb", "msq", "var", "dd")}}
        for k in ("msq", "var", "dd"):
            tmp[k] = sbt(nc, es, "a_" + k, [65, 512], F32)
        for k in ("sq", "srcb"):
            tmp[k] = sbt(nc, es, "a_" + k, [65, 512], BF16)
        ST = [pst(nc, es, f"ST{i}", [128, 1024]) for i in range(2)]
        STB = [Buf(f"ST{i}") for i in range(2)]
        Op = [pst(nc, es, f"Op{i}", [128, 512]) for i in range(2)]
        OpB = [Buf(f"Op{i}") for i in range(2)]
        stat = [pst(nc, es, f"astat{i}", [128, 512]) for i in range(2)]
        statB = [Buf(f"astat{i}") for i in range(2)]
        cs = g.cst
        A1 = g.A1b[0:65, :]
        A2 = g.A2b[0:65, :]

        def load_head(h):
            i = h % 2
            gI, hh = h // 4, h % 4
            rows = slice(hh * 32, hh * 32 + 32)
            tk.dma("sp", qh[i][0:32, :], FMS[2 * gI, rows, :], [], [inB[i]])
            tk.dma("sp", qh[i][32:64, :], FMS[2 * gI + 1, rows, :], [], [inB[i]])
            tk.dma("sp", kh[i][0:32, :], FMS[4 + 2 * gI, rows, :], [], [inB[i]])
            tk.dma("sp", kh[i][32:64, :], FMS[4 + 2 * gI + 1, rows, :], [], [inB[i]])
            for di, d in enumerate((1, 4, 16)):
                src = VA[:, h, :].rearrange("(n j r) c -> j r n c", j=128, r=d)
                dst = v3[i][:, di, :, :].rearrange("p (r n) c -> p r n c", r=d)
                nb_ = 32 // d
                if d == 1:
                    for q4 in range(4):
                        tk.dma("sp", dst[:, :, q4 * 8:(q4 + 1) * 8, :], src[:, :, q4 * 8:(q4 + 1) * 8, :], [], [vB[i][di]])
                elif d == 4:
                    for r_ in range(4):
                        tk.dma("sp", dst[:, r_, :, :], src[:, r_, :, :], [], [vB[i][di]])
                else:
                    for n_ in range(2):
                        for hf in range(2):
                            tk.dma("sp", dst[:, hf * 8:(hf + 1) * 8, n_, :], src[:, hf * 8:(hf + 1) * 8, n_, :], [], [vB[i][di]])

        batches = []
        for h in range(8):
            for di, d in enumerate((1, 4, 16)):
                nb = 32 // d
                blocks = [(r, n, r * nb + n) for r in range(d) for n in range(nb)]
                for b0 in range(0, 32, 4):
                    batches.append((h, di, d, nb, blocks[b0:b0 + 4]))
        NB = len(batches)

        def emit_ST(gi):
            h, di, d, nb, blks = batches[gi]
            i = h % 2
            sbuf = gi % 2
            qv = qh[i][:, :].rearrange("p (m r) -> p r m", r=d)
            kv = kh[i][:, :].rearrange("p (m r) -> p r m", r=d)
            for j, (r, n, b) in enumerate(blks):
                qn = 256 if n < nb - 1 else 128
                o_ = ST[sbuf][:, j * 256:j * 256 + qn]
                tk.mm(o_, kv[:, r, n * 128:(n + 1) * 128], qv[:, r, n * 128:n * 128 + qn], True, True,
                      [inB[i]], [STB[sbuf]], tick=(j == 3))

        def emit_exp(gi):
            p3 = gi % 3
            tk.act(PT[p3][:], ST[gi % 2][:, :], AF.Exp, [STB[gi % 2]], PTB[p3], scale=SC_A)
            m01 = g.mask01[:].rearrange("p a b -> p (a b)")
            tk.tt("dve", PT[p3][:, 0:512], PT[p3][:, 0:512], m01[:, 0:512], ALU.mult, [PTB[p3][0], g.cstB], [PTB[p3][0]])
            tk.tt("dve", PT[p3][:, 512:1024], PT[p3][:, 512:1024], m01[:, 512:1024], ALU.mult, [PTB[p3][1], g.cstB], [PTB[p3][1]])

        def emit_PV(gi):
            h, di, d, nb, blks = batches[gi]
            i = h % 2
            ob = gi % 2
            cur = PT[gi % 3]
            prv = PT[(gi - 1) % 3]
            for j, (r, n, b) in enumerate(blks):
                o_ = Op[ob][0:65, j * 128:(j + 1) * 128]
                rd = [vB[i][di]] + PTB[gi % 3]
                if n > 0:
                    if j > 0:
                        pprev = cur[:, (j - 1) * 256 + 128:(j - 1) * 256 + 256]
                    else:
                        pprev = prv[:, 3 * 256 + 128:4 * 256]
                        rd = rd + PTB[(gi - 1) % 3]
                    tk.mm(o_, v3[i][:, di, b - 1, :], pprev, True, False, rd, [OpB[ob]])
                    tk.mm(o_, v3[i][:, di, b, :], cur[:, j * 256:j * 256 + 128], False, True, rd, [OpB[ob]], tick=(j == 3))
                else:
                    tk.mm(o_, v3[i][:, di, b, :], cur[:, j * 256:j * 256 + 128], True, True, rd, [OpB[ob]], tick=(j == 3))

        def emit_evac(gi):
            h, di, d, nb, blks = batches[gi]
            a, aB = acc[h % 2], accB[h % 2]
            ob = gi % 2
            o_ = Op[ob][0:65, :]
            r0, n0, b0 = blks[0]
            if d == 1:
                tk.copy("dve", a[:, n0 * 128:n0 * 128 + 512], o_, [OpB[ob]], [aB])
            elif d == 4:
                av = a[:, :].rearrange("p (m q) -> p q m", q=4)[:, r0, n0 * 128:n0 * 128 + 512]
                tk.tt("dve", av, av, o_, ALU.add, [OpB[ob], aB], [aB])
            else:
                av = a[:, :].rearrange("p (m q) -> p q m", q=16)[:, r0:r0 + 2, :]
                tk.tt("dve", av, av, o_.rearrange("p (a m) -> p a m", a=2), ALU.add, [OpB[ob], aB], [aB])

        def emit_post(h, t, stage):
            a, aB = acc[h % 2], accB[h % 2]
            B = tmp["B"]
            K, M, N = 65, 64, 512
            src = a[0:65, t * 512:(t + 1) * 512]
            sq, srcb, msq, var, dd = tmp["sq"], tmp["srcb"], tmp["msq"], tmp["var"], tmp["dd"]
            mean_ps, e2_ps = stat[0], stat[1]
            mB, eB = statB[0], statB[1]
            if stage == 1:
                tk.act(sq[0:K, 0:N], src, AF.Square, [aB], [B["sq"]])
                tk.act(srcb[0:K, 0:N], src, AF.Copy, [aB], [B["srcb"]])
                tk.mm(mean_ps[0:M, 0:N], A1, srcb[0:K, 0:N], True, True, [g.cstB, B["srcb"]], [mB], tick=True)
                tk.mm(e2_ps[0:M, 0:N], A2, sq[0:K, 0:N], True, True, [g.cstB, B["sq"]], [eB], tick=True)
            elif stage == 2:
                tk.act(msq[0:M, 0:N], mean_ps[0:M, 0:N], AF.Square, [mB], [B["msq"]])
                tk.tt("dve", var[0:M, 0:N], e2_ps[0:M, 0:N], msq[0:M, 0:N], ALU.subtract, [eB, B["msq"]], [B["var"]])
                tk.tt("dve", dd[0:64, :], a[0:64, t * 512:(t + 1) * 512], mean_ps[0:64, :], ALU.subtract, [aB, mB, B["msq"]], [B["dd"]])
            else:
                tk.act(var[0:64, :], var[0:64, :], AF.Ln, [B["var"]], [B["var"]])
                tk.act(var[0:64, :], var[0:64, :], AF.Exp, [B["var"]], [B["var"]], scale=-0.5)
                y, yB = yst[h % 2], ystB[h % 2]
                tk.stt(y[:, t * 512:(t + 1) * 512], dd[0:64, :], g.pl_sb[0:64, P_MSA + h:P_MSA + h + 1], var[0:64, :],
                       ALU.mult, ALU.mult, [B["dd"], B["var"], g.plB], [yB])
                if t == 7:
                    tk.dma("sp", YT[h * 64:(h + 1) * 64, :], y[:, :], [yB], [])

        for i in range(2):
            tk.memset("dve", qh[i][64:128, :], 0.0, [inB[i]])
            tk.memset("pool", kh[i][64:128, :], 0.0, [inB[i]])
        load_head(0)
        posts = []
        cur = [None, 0]
        emit_ST(0)
        emit_ST(1)
        emit_exp(0)
        for gi in range(NB):
            h = batches[gi][0]
            first_of_head = (gi % 24 == 0)
            if first_of_head and h + 1 < 8:
                load_head(h + 1)
            if gi + 2 < NB:
                emit_ST(gi + 2)
            if gi + 1 < NB:
                emit_exp(gi + 1)
            emit_PV(gi)
            emit_evac(gi)
            if cur[0] is None and posts:
                cur[0] = posts.pop(0)
                cur[1] = 1
            if cur[0] is not None:
                emit_post(cur[0][0], cur[0][1], cur[1])
                cur[1] += 1
                if cur[1] > 3:
                    cur[0] = None
            if gi % 24 == 23:
                posts += [(h, t) for t in range(8)]
        while cur[0] is not None or posts:
            if cur[0] is None:
                cur[0] = posts.pop(0)
                cur[1] = 1
            emit_post(cur[0][0], cur[0][1], cur[1])
            cur[1] += 1
            if cur[1] > 3:
                cur[0] = None


def phase_L(g, l, FMS, LA, VRG, YT):
    tk, nc = g.tk, g.nc
    cs = g.cst
    with ExitStack() as es:
        G = []
        for grp in range(2):
            d = Ctx()
            n = f"L{grp}_"
            d.qf = [sbt(nc, es, n + f"q{i}", [128, 512], BF16) for i in range(2)]
            d.kf = [sbt(nc, es, n + f"k{i}", [128, 512], BF16) for i in range(2)]
            d.vt = [sbt(nc, es, n + f"v{i}", [128, 4, 256], BF16) for i in range(2)]
            d.gt = [sbt(nc, es, n + f"g{i}", [128, 2, 512], BF16) for i in range(2)]
            d.lt = [sbt(nc, es, n + f"l{i}", [128, 512], F32) for i in range(2)] if grp == 1 else None
            d.inB = [Buf(n + f"in{i}") for i in range(2)]
            names = ("cum", "E", "Einv", "Kd", "dec", "Qbd", "kt", "ktil", "ktok", "og0", "og1", "sq", "srcb", "msq", "var", "dd")
            d.B = {k: Buf(n + k) for k in names}
            if grp == 1:
                d.cum = sbt(nc, es, n + "cum", [128, 512], F32)
                d.Eg = sbt(nc, es, n + "E", [128, 512], F32)
                d.Einvg = sbt(nc, es, n + "Einv", [128, 512], F32)
                d.Kdg = sbt(nc, es, n + "Kd", [128, 512], F32)
                d.decg = sbt(nc, es, n + "dec", [128, 4], F32)
            d.Qbd = sbt(nc, es, n + "Qbd", [128, 4, 512], BF16)
            d.kt = sbt(nc, es, n + "kt", [128, 512], BF16)
            d.ktil = sbt(nc, es, n + "ktil", [128, 512], BF16)
            d.ktok = sbt(nc, es, n + "ktok", [128, 4, 128], BF16)
            d.stf = sbt(nc, es, n + "stf", [128, 256], F32)
            d.stfB = Buf(n + "stf")
            d.stb = [sbt(nc, es, n + f"stb{i}", [128, 256], BF16) for i in range(2)]
            d.stbB = [Buf(n + f"stb{i}") for i in range(2)]
            d.nst = 0
            d.og = [sbt(nc, es, n + f"og{j}", [128, 512], F32) for j in range(2)]
            d.tmp = {"B": d.B}
            for k in ("msq", "var", "dd"):
                d.tmp[k] = sbt(nc, es, n + k, [128, 512], F32)
            for k in ("sq", "srcb"):
                d.tmp[k] = sbt(nc, es, n + k, [128, 512], BF16)
            d.yy = [sbt(nc, es, n + f"yy{i}", [128, 512], BF16) for i in range(2)]
            d.yyB = [Buf(n + f"yy{i}") for i in range(2)]
            d.nyy = 0
            d.hm = cs[:, C_HMR:C_HMR + 4] if grp == 0 else cs[:, C_HMG:C_HMG + 4]
            d.ch0 = 512 + grp * 256
            G.append(d)
        PT = [sbt(nc, es, f"l_PT{i}", [128, 4, 128], BF16) for i in range(2)]
        PTB = [Buf(f"l_PT{i}") for i in range(2)]
        STp = [pst(nc, es, f"l_ST{i}", [128, 512]) for i in range(2)]
        STpB = [Buf(f"l_ST{i}") for i in range(2)]
        kvp = pst(nc, es, "l_kv", [128, 512])
        kvB = Buf("l_kvp")
        Opp = [pst(nc, es, f"l_O{i}", [128, 512]) for i in range(2)]
        OppB = [Buf(f"l_O{i}") for i in range(2)]
        trp = pst(nc, es, "l_tr", [128, 1024], BF16)
        trB = Buf("l_tr")
        stat = [pst(nc, es, f"l_stat{i}", [128, 512]) for i in range(2)]
        statB = [Buf(f"l_stat{i}") for i in range(2)]
        B1 = g.B1b[:, :]
        lmask4 = cs[:, C_LMASK4:C_LMASK4 + 512]
        ones = cs[:, C_ONES:C_ONES + 128]
        cnt = {"pt": 0}

        def load(T, grp):
            d = G[grp]
            i = T % 2
            tsl = slice(T * 512, (T + 1) * 512)
            iB = d.inB[i]
            if grp == 0:
                tk.dma("sp", d.qf[i][0:64, :], FMS[8, 0:64, tsl], [], [iB])
                tk.dma("sp", d.qf[i][64:128, :], FMS[9, 0:64, tsl], [], [iB])
                tk.dma("sp", d.kf[i][0:64, :], FMS[8, 64:128, tsl], [], [iB])
                tk.dma("sp", d.kf[i][64:128, :], FMS[9, 64:128, tsl], [], [iB])
                g0 = 10
            else:
                tk.dma("sp", d.qf[i][:], FMS[12, :, tsl], [], [iB])
                tk.dma("sp", d.kf[i][:], FMS[13, :, tsl], [], [iB])
                tk.dma("sp", d.lt[i][:], LA[:, tsl], [], [iB])
                g0 = 14
            tk.dma("sp", d.gt[i][:, 0, :], FMS[g0, :, tsl], [], [iB])
            tk.dma("sp", d.gt[i][:, 1, :], FMS[g0 + 1, :, tsl], [], [iB])
            tk.dma("sp", d.vt[i][:], VRG[tsl, grp * 256:(grp + 1) * 256].rearrange("(c s) v -> s c v", s=128), [], [iB])

        def tables(d, grp):
            if grp == 0:
                return (cs[:, C_ER:C_ER + 512], cs[:, C_EINVR:C_EINVR + 512], cs[:, C_KDR:C_KDR + 512],
                        cs[:, C_DECR:C_DECR + 4], g.cstB, g.cstB, g.cstB, g.cstB)
            B = d.B
            return d.Eg[:], d.Einvg[:], d.Kdg[:], d.decg[:], B["E"], B["Einv"], B["Kd"], B["dec"]

        def prep(T, grp):
            d = G[grp]
            B = d.B
            i = T % 2
            iB = d.inB[i]
            q_, k_ = d.qf[i], d.kf[i]
            if T == 0:
                tk.memset("dve", d.stf[:], 0.0, [d.stfB])
                tk.memset("pool", d.stb[0][:], 0.0, [d.stbB[0]])
                d.nst = 0
            if grp == 1:
                l_ = d.lt[i]
                for c in range(4):
                    csl = slice(c * 128, (c + 1) * 128)
                    tk.op("dve", lambda csl=csl: nc.vector.tensor_tensor_scan(
                        out=d.cum[:, csl], data0=ones, data1=l_[:, csl], initial=0.0, op0=ALU.mult, op1=ALU.add),
                        [iB, g.cstB], [B["cum"]])
                tk.act(d.Eg[:], d.cum[:], AF.Exp, [B["cum"]], [B["E"]], scale=-1.0 / 16)
                tk.act(d.Einvg[:], d.cum[:], AF.Exp, [B["cum"]], [B["Einv"]], scale=1.0 / 16)
                tk.act(d.decg[:], d.cum[:].rearrange("p (c s) -> p c s", s=128)[:, :, 127], AF.Exp, [B["cum"]], [B["dec"]],
                       scale=-1.0 / 16)
                for c in range(4):
                    csl = slice(c * 128, (c + 1) * 128)
                    tk.ts("pool", d.Kdg[:, csl], d.Einvg[:, csl], d.decg[:, c:c + 1], None, ALU.mult, ALU.bypass,
                          [B["Einv"], B["dec"]], [B["Kd"]])
            E_, Einv_, Kd_, dec_, EB, EinvB, KdB, decB = tables(d, grp)
            for h in range(4):
                tk.stt(d.Qbd[:, h, :], q_[:], d.hm[:, h:h + 1], E_, ALU.mult, ALU.mult, [iB, g.cstB, EB], [B["Qbd"]])
            tk.tt("pool", d.kt[:], k_[:], Einv_, ALU.mult, [iB, EinvB], [B["kt"]])
            tk.tt("pool", d.ktil[:], k_[:], Kd_, ALU.mult, [iB, KdB], [B["ktil"]])

        def core(T, grp):
            d = G[grp]
            B = d.B
            i = T % 2
            iB = d.inB[i]
            v_ = d.vt[i]
            E_, Einv_, Kd_, dec_, EB, EinvB, KdB, decB = tables(d, grp)
            for c in range(4):
                tk.op("pe", lambda c=c: nc.tensor.transpose(out=trp[:, c * 128:(c + 1) * 128],
                                                            in_=d.ktil[:, c * 128:(c + 1) * 128], identity=g.ident_b[:]),
                      [B["ktil"], g.cstB], [trB], tick=(c == 3))
            tk.copy("act", d.ktok[:].rearrange("p c s -> p (c s)"), trp[:, 0:512], [trB], [B["ktok"]])
            sps = []

            def emit_ST(c):
                csl = slice(c * 128, (c + 1) * 128)
                sp_ = cnt["pt"] % 2
                cnt["pt"] += 1
                sps.append(sp_)
                tk.mm(STp[sp_][:, :], d.kt[:, csl], d.Qbd[:, :, csl], True, True, [B["kt"], B["Qbd"]], [STpB[sp_]], tick=True)
                tk.tt("dve", PT[sp_][:], STp[sp_][:, :].rearrange("p (h c) -> p h c", h=4),
                      lmask4.rearrange("p (h c) -> p h c", h=4), ALU.mult, [STpB[sp_], g.cstB], [PTB[sp_]])

            emit_ST(0)
            for c in range(4):
                csl = slice(c * 128, (c + 1) * 128)
                if c + 1 < 4:
                    emit_ST(c + 1)
                sp_ = sps[c]
                kvs = slice((c % 2) * 256, (c % 2) * 256 + 256)
                tk.mm(kvp[:, kvs], d.ktok[:, c, :], v_[:, c, :], True, True, [B["ktok"], iB], [kvB], tick=True)
                sbi = d.nst % 2
                for h in range(4):
                    j, half = h // 2, h % 2
                    o_ = Opp[j][half * 64:(half + 1) * 64, csl]
                    tk.mm(o_, v_[:, c, h * 64:(h + 1) * 64], PT[sp_][:, h, :], True, False, [iB, PTB[sp_]], [OppB[j]])
                    tk.mm(o_, d.stb[sbi][:, h * 64:(h + 1) * 64], d.Qbd[:, h, csl], False, True,
                          [d.stbB[sbi], B["Qbd"]], [OppB[j]], tick=(h % 2 == 1))
                tk.stt(d.stf[:], d.stf[:], dec_[:, c:c + 1], kvp[:, kvs], ALU.mult, ALU.add, [d.stfB, decB, kvB], [d.stfB])
                d.nst += 1
                sbn = d.nst % 2
                tk.copy("dve", d.stb[sbn][:], d.stf[:], [d.stfB], [d.stbB[sbn]])
            for j in range(2):
                tk.copy("act", d.og[j][:], Opp[j][:, :], [OppB[j]], [B[f"og{j}"]])

        def norm(T, grp):
            d = G[grp]
            B = d.B
            i = T % 2
            iB = d.inB[i]
            g_ = d.gt[i]
            for j in range(2):
                ogB = B[f"og{j}"]
                mean_ps, mB = head_norm(g, d.og[j][:], ogB, 128, B1, B1, 128, (stat[0], stat[1]), (statB[0], statB[1]), d.tmp)
                var, dd = d.tmp["var"], d.tmp["dd"]
                tk.act(var[:], var[:], AF.Ln, [B["var"]], [B["var"]], bias=g.eps_hn[:, 0:1])
                tk.act(var[:], var[:], AF.Exp, [B["var"]], [B["var"]], scale=-0.5)
                tk.tt("dve", dd[:], d.og[j][:], mean_ps[:, :], ALU.subtract, [ogB, mB], [B["dd"]])
                tk.stt(dd[:], dd[:], g.pl_sb[:, P_MSRG + grp * 2 + j:P_MSRG + grp * 2 + j + 1], var[:],
                       ALU.mult, ALU.mult, [B["dd"], B["var"], g.plB], [B["dd"]])
                yi = d.nyy % 2
                d.nyy += 1
                tk.tt("pool", d.yy[yi][:], dd[:], g_[:, j, :], ALU.mult, [B["dd"], iB], [d.yyB[yi]])
                tk.dma("sp", YT[d.ch0 + j * 128:d.ch0 + (j + 1) * 128, T * 512:(T + 1) * 512], d.yy[yi][:], [d.yyB[yi]], [])

        items = [(T, grp) for T in range(8) for grp in range(2)]
        NI = len(items)
        load(*items[0])
        load(*items[1])
        for s in range(NI + 2):
            if s + 2 < NI:
                pass
            if s < NI:
                prep(*items[s])
            if 0 <= s - 1 < NI:
                core(*items[s - 1])
            if 0 <= s - 2 < NI:
                norm(*items[s - 2])
                if s < NI:
                    pass
            if s + 2 < NI:
                load(*items[s + 2])


def ln_alloc(nc, es, pfx, N):
    tmp = {"B": {k: Buf(pfx + k) for k in ("zb", "zq", "msq", "var", "rstd", "nmr")}}
    tmp["zb"] = sbt(nc, es, pfx + "zb", [128, 8, N], BF16)
    tmp["zq"] = sbt(nc, es, pfx + "zq", [128, 8, N], BF16)
    for k in ("msq", "var", "rstd", "nmr"):
        tmp[k] = sbt(nc, es, pfx + k, [128, N], F32)
    return tmp


def ln_part1(g, z, zB, N, tmp):
    tk = g.tk
    B = tmp["B"]
    tk.copy("dve", tmp["zb"][:, :, 0:N], z[:, :, 0:N], list(zB), [B["zb"]])
    tk.act(tmp["zq"][:, :, 0:N], z[:, :, 0:N], AF.Square, list(zB), [B["zq"]])


def ln_part2(g, z, zB, N, gcol, bcol, dst_ap, stat_ps, stat_bufs, tmp, dst_bf=None):
    tk = g.tk
    zb, zq, msq, var, rstd, nmr = tmp["zb"], tmp["zq"], tmp["msq"], tmp["var"], tmp["rstd"], tmp["nmr"]
    B = tmp["B"]
    mean_ps, e2_ps = stat_ps
    mB, eB = stat_bufs
    for c in range(8):
        tk.mm(mean_ps[:, 0:N], g.ones_b[:], zb[:, c, 0:N], c == 0, c == 7, [g.cstB, B["zb"]], [mB], tick=(c == 7))
    for c in range(8):
        tk.mm(e2_ps[:, 0:N], g.ones_b[:], zq[:, c, 0:N], c == 0, c == 7, [g.cstB, B["zq"]], [eB], tick=(c == 7))
    tk.act(msq[:, 0:N], mean_ps[:, 0:N], AF.Square, [mB], [B["msq"]])
    tk.tt("dve", var[:, 0:N], e2_ps[:, 0:N], msq[:, 0:N], ALU.subtract, [eB, B["msq"]], [B["var"]])
    tk.act(var[:, 0:N], var[:, 0:N], AF.Ln, [B["var"]], [B["var"]], bias=g.eps_ln[:, 0:1])
    tk.act(rstd[:, 0:N], var[:, 0:N], AF.Exp, [B["var"]], [B["rstd"]], scale=-0.5)
    tk.stt(nmr[:, 0:N], mean_ps[:, 0:N], -1.0, rstd[:, 0:N], ALU.mult, ALU.mult, [mB, B["rstd"]], [B["nmr"]])
    for c in range(8):
        e2 = "pool" if c % 2 == 0 else "dve"
        tk.tt("dve", z[:, c, 0:N], z[:, c, 0:N], rstd[:, 0:N], ALU.mult, [zB[c], B["rstd"]], [zB[c]])
        tk.tt(e2, z[:, c, 0:N], z[:, c, 0:N], nmr[:, 0:N], ALU.add, [zB[c], B["nmr"]], [zB[c]])
        tk.act(z[:, c, 0:N], z[:, c, 0:N], AF.Identity, [zB[c], g.plB], [zB[c]], scale=gcol[:, c:c + 1], bias=bcol[:, c:c + 1])
    tk.dma("sp", dst_ap.rearrange("(c p) t -> p c t", p=128), z[:, :, 0:N], list(zB), [])
    if dst_bf is not None:
        tk.copy("act", zb[:, :, 0:N], z[:, :, 0:N], list(zB), [B["zb"]])
        tk.dma("sp", dst_bf.rearrange("(c p) t -> p c t", p=128), zb[:, :, 0:N], [B["zb"]], [])


def phase_O(g, l, xsrc, wout, YT, X1F, X1B):
    tk, nc = g.tk, g.nc
    with ExitStack() as es:
        wo = sbt(nc, es, "wo", [128, 8, D], BF16)
        wB = Buf("wo")
        yt = [sbt(nc, es, f"o_yt{i}", [128, 8, 512], BF16) for i in range(2)]
        xr = [sbt(nc, es, f"o_xr{i}", [128, 8, 512], F32) for i in range(2)]
        inB = [Buf(f"o_in{i}") for i in range(2)]
        z = [sbt(nc, es, f"o_z{i}", [128, 8, 512], F32) for i in range(2)]
        zB = [[Buf(f"o_z{i}_{c}") for c in range(8)] for i in range(2)]
        tmp = ln_alloc(nc, es, "o_", 512)
        pb = [pst(nc, es, f"o_pb{i}", [128, 512]) for i in range(6)]
        pbB = [Buf(f"o_pb{i}") for i in range(6)]
        stat = [pst(nc, es, f"o_stat{i}", [128, 512]) for i in range(2)]
        statB = [Buf(f"o_stat{i}") for i in range(2)]
        for kc in range(8):
            tk.dma("pool", wo[:, kc, :], wout[l, kc * 128:(kc + 1) * 128, :], [], [wB])
        yv = YT.rearrange("(c p) t -> p c t", p=128)
        xv = xsrc.rearrange("(c p) t -> p c t", p=128)
        gcol, bcol = g.pl_sb[:, P_LN1G:P_LN1G + 8], g.pl_sb[:, P_LN1B:P_LN1B + 8]

        def load(T):
            tk.dma("sp", yt[T % 2][:], yv[:, :, T * 512:(T + 1) * 512], [], [inB[T % 2]])
            tk.dma("sp", xr[T % 2][:], xv[:, :, T * 512:(T + 1) * 512], [], [inB[T % 2]])

        def fin(T):
            i = T % 2
            ln_part2(g, z[i], zB[i], 512, gcol, bcol, X1F[:, T * 512:(T + 1) * 512], (stat[0], stat[1]),
                     (statB[0], statB[1]), tmp, dst_bf=X1B[:, T * 512:(T + 1) * 512])

        load(0)
        nb = 0
        pend = None
        for T in range(8):
            if T + 1 < 8:
                load(T + 1)
            i = T % 2
            for oc in range(8):
                b = nb % 6
                nb += 1
                for kc in range(8):
                    tk.mm(pb[b][:, :], wo[:, kc, oc * 128:(oc + 1) * 128], yt[i][:, kc, :], kc == 0, kc == 7,
                          [wB, inB[i]], [pbB[b]], tick=(kc == 7))
                tk.stt(z[i][:, oc, :], xr[i][:, oc, :], ALPHA, pb[b][:, :], ALU.mult, ALU.add, [inB[i], pbB[b]], [zB[i][oc]])
                if oc == 3 and pend is not None:
                    fin(pend)
                    pend = None
            ln_part1(g, z[i], zB[i], 512, tmp)
            pend = T
        fin(pend)


def phase_F(g, l, wup, wdown, X1F, X1B, xdst, HT):
    tk, nc = g.tk, g.nc
    NT = 256
    NTI = S // NT
    xv = X1F.rearrange("(c p) t -> p c t", p=128)
    xbv = X1B.rearrange("(c p) t -> p c t", p=128)
    cw = g.pl_sb[:, P_CW:P_CW + 132]
    cb = g.pl_sb[:, P_CB:P_CB + 44]
    with ExitStack() as eso:
        wd = sbt(nc, eso, "wd", [128, 22, D], BF16)
        wdB = Buf("wd")
        with ExitStack() as es:
            x1b = sbt(nc, es, "f_x1b", [128, 8, S + 2], BF16)
            xB = [Buf(f"f_x1b{T}") for T in range(NTI)]
            haloB = Buf("f_halo")
            NW = 2
            wch = [sbt(nc, es, f"f_wch{i}", [128, 8, 256], BF16) for i in range(NW)]
            wchB = [Buf(f"f_wch{i}") for i in range(NW)]
            og = [sbt(nc, es, f"f_og{i}", [128, NT], F32) for i in range(2)]
            a2 = [sbt(nc, es, f"f_a2{i}", [128, NT], F32) for i in range(2)]
            ov = [sbt(nc, es, f"f_ov{i}", [128, NT], F32) for i in range(2)]
            sg = [sbt(nc, es, f"f_sg{i}", [128, NT], F32) for i in range(2)]
            ogB = [Buf(f"f_og{i}") for i in range(2)]
            a2B = [Buf(f"f_a2{i}") for i in range(2)]
            ovB = [Buf(f"f_ov{i}") for i in range(2)]
            sgB = [Buf(f"f_sg{i}") for i in range(2)]
            hst = [sbt(nc, es, f"f_hst{i}", [128, S], BF16) for i in range(2)]
            hstB = [Buf(f"f_hst{i}") for i in range(2)]
            pb = [pst(nc, es, f"f_pb{i}", [128, 512]) for i in range(8)]
            pbB = [Buf(f"f_pb{i}") for i in range(8)]
            wv = wup[l].rearrange("(kc p) n -> p kc n", p=128)

            def loadw(c):
                i = c % NW
                tk.dma("pool", wch[i][:, :, 0:128], wv[:, :, c * 128:(c + 1) * 128], [], [wchB[i]])
                tk.dma("pool", wch[i][:, :, 128:256], wv[:, :, (22 + c) * 128:(23 + c) * 128], [], [wchB[i]])

            tk.memset("dve", x1b[:, :, 0:2], 0.0, [haloB])
            loadw(0)
            for T in range(NTI):
                tk.dma("sp", x1b[:, :, 2 + T * NT:2 + (T + 1) * NT], xbv[:, :, T * NT:(T + 1) * NT], [], [xB[T]])
                if T == 1:
                    loadw(1)
            nb = 0
            ce = 0
            tail = [None]
            for c in range(22):
                if 1 <= c and c + 1 < 22:
                    loadw(c + 1)
                tk.dma("pool", wd[:, c, :], wdown[l, c * 128:(c + 1) * 128, :], [], [wdB])
                wi = c % NW
                hs, hsB = hst[c % 2], hstB[c % 2]
                for T in range(NTI):
                    bg = nb % 8
                    bv = (nb + 1) % 8
                    nb += 2
                    xrd = [wchB[wi], xB[T], xB[T - 1] if T > 0 else haloB]
                    for (bank, off) in ((bg, 0), (bv, 128)):
                        for kc in range(8):
                            tk.mm(pb[bank][:, 0:NT + 2], wch[wi][:, kc, off:off + 128], x1b[:, kc, T * NT:T * NT + NT + 2],
                                  kc == 0, kc == 7, xrd, [pbB[bank]], tick=(kc == 7))
                    s = ce % 2
                    ce += 1
                    G_, V_ = pb[bg], pb[bv]
                    cg, cv_ = c, 22 + c
                    wg = [cw[:, cg * 3 + j:cg * 3 + j + 1] for j in range(3)]
                    wv_ = [cw[:, cv_ * 3 + j:cv_ * 3 + j + 1] for j in range(3)]
                    tk.act(og[s][:], G_[:, 2:NT + 2], AF.Identity, [pbB[bg], g.plB], [ogB[s]], scale=wg[2], bias=cb[:, cg:cg + 1])
                    tk.act(a2[s][:], G_[:, 1:NT + 1], AF.Identity, [pbB[bg], g.plB], [a2B[s]], scale=wg[1])
                    tk.stt(og[s][:], G_[:, 0:NT], wg[0], og[s][:], ALU.mult, ALU.add, [pbB[bg], g.plB, ogB[s], a2B[s]], [ogB[s]])
                    tk.act(ov[s][:], V_[:, 2:NT + 2], AF.Identity, [pbB[bv], g.plB], [ovB[s]], scale=wv_[2], bias=cb[:, cv_:cv_ + 1])
                    tk.stt(ov[s][:], V_[:, 1:NT + 1], wv_[1], ov[s][:], ALU.mult, ALU.add, [pbB[bv], g.plB, ovB[s]], [ovB[s]])
                    tk.stt(ov[s][:], V_[:, 0:NT], wv_[0], ov[s][:], ALU.mult, ALU.add, [pbB[bv], g.plB, ovB[s]], [ovB[s]])
                    tk.tt("pool", og[s][:], og[s][:], a2[s][:], ALU.add, [ogB[s], a2B[s]], [ogB[s]])
                    if tail[0] is not None:
                        tail[0]()

                    def mk(s=s, hs=hs, hsB=hsB, T=T):
                        def f():
                            tk.act(sg[s][:], og[s][:], AF.Silu, [ogB[s]], [sgB[s]])
                            tk.tt("pool", hs[:, T * NT:(T + 1) * NT], sg[s][:], ov[s][:], ALU.mult, [sgB[s], ovB[s]], [hsB])
                        return f
                    tail[0] = mk()
                    if T == NTI - 1:
                        tail[0]()
                        tail[0] = None
                tk.dma("sp", HT[c * 128:(c + 1) * 128, :], hs[:, :], [hsB], [])
        tk.barrier()
        with ExitStack() as es:
            ht = [sbt(nc, es, f"d_ht{i}", [128, 22, 512], BF16) for i in range(2)]
            xr = [sbt(nc, es, f"d_xr{i}", [128, 512], F32) for i in range(4)]
            xrB = [Buf(f"d_xr{i}") for i in range(4)]
            inB = [Buf(f"d_in{i}") for i in range(2)]
            z = [sbt(nc, es, f"d_z{i}", [128, 8, 512], F32) for i in range(2)]
            zB = [[Buf(f"d_z{i}_{c}") for c in range(8)] for i in range(2)]
            tmp = ln_alloc(nc, es, "d_", 512)
            pb = [pst(nc, es, f"d_pb{i}", [128, 512]) for i in range(6)]
            pbB = [Buf(f"d_pb{i}") for i in range(6)]
            stat = [pst(nc, es, f"d_stat{i}", [128, 512]) for i in range(2)]
            statB = [Buf(f"d_stat{i}") for i in range(2)]
            hv = HT.rearrange("(c p) t -> p c t", p=128)
            gcol, bcol = g.pl_sb[:, P_LN2G:P_LN2G + 8], g.pl_sb[:, P_LN2B:P_LN2B + 8]

            def load(T):
                tk.dma("sp", ht[T % 2][:], hv[:, :, T * 512:(T + 1) * 512], [], [inB[T % 2]])

            def loadx(k):
                T_, oc_ = k // 8, k % 8
                tk.dma("sp", xr[k % 4][:], X1F[oc_ * 128:(oc_ + 1) * 128, T_ * 512:(T_ + 1) * 512], [], [xrB[k % 4]])

            def fin(T):
                i = T % 2
                ln_part2(g, z[i], zB[i], 512, gcol, bcol, xdst[:, T * 512:(T + 1) * 512], (stat[0], stat[1]),
                         (statB[0], statB[1]), tmp)

            load(0)
            for k in range(3):
                loadx(k)
            nb = 0
            pend = None
            for T in range(8):
                if T + 1 < 8:
                    load(T + 1)
                i = T % 2
                for oc in range(8):
                    b = nb % 6
                    k = nb
                    nb += 1
                    if k + 3 < 64:
                        loadx(k + 3)
                    for c in range(22):
                        tk.mm(pb[b][:, :], wd[:, c, oc * 128:(oc + 1) * 128], ht[i][:, c, :], c == 0, c == 21,
                              [wdB, inB[i]], [pbB[b]], tick=(c == 21))
                    tk.stt(z[i][:, oc, :], xr[k % 4][:], ALPHA, pb[b][:, :], ALU.mult, ALU.add, [xrB[k % 4], pbB[b]], [zB[i][oc]])
                    if oc == 3 and pend is not None:
                        fin(pend)
                        pend = None
                ln_part1(g, z[i], zB[i], 512, tmp)
                pend = T
            fin(pend)


_CACHE = {}


def _prep_weights(w_in, w_alpha, b_alpha, mix_scale, w_out, ln1_g, ln1_b, w_up, conv_w, conv_b, w_down, ln2_g, ln2_b):
    f = lambda a: np.ascontiguousarray(np.asarray(a, dtype=np.float32))
    win_p = f(np.asarray(w_in)[:, :, win_perm()])
    plb = np.zeros((DEPTH, 128, NPL), np.float32)
    ms = np.asarray(mix_scale, np.float32)
    for l in range(DEPTH):
        plb[l, 0:64, P_MSA:P_MSA + 8] = ms[l, 0:512].reshape(8, 64).T
        plb[l, :, P_MSRG:P_MSRG + 4] = ms[l, 512:1024].reshape(4, 128).T
        plb[l, :, P_LN1G:P_LN1G + 8] = np.asarray(ln1_g)[l].reshape(8, 128).T
        plb[l, :, P_LN1B:P_LN1B + 8] = np.asarray(ln1_b)[l].reshape(8, 128).T
        plb[l, :, P_LN2G:P_LN2G + 8] = np.asarray(ln2_g)[l].reshape(8, 128).T
        plb[l, :, P_LN2B:P_LN2B + 8] = np.asarray(ln2_b)[l].reshape(8, 128).T
        cwl = np.asarray(conv_w)[l].reshape(3, 44, 128)
        plb[l, :, P_CW:P_CW + 132] = cwl.transpose(2, 1, 0).reshape(128, 132)
        plb[l, :, P_CB:P_CB + 44] = np.asarray(conv_b)[l].reshape(44, 128).T
        plb[l, :, P_BA] = np.asarray(b_alpha)[l]
    return dict(win=win_p, walpha=f(w_alpha), wout=f(w_out), wup=f(w_up), wdown=f(w_down), pl=plb)


def kernel(x, w_in, w_alpha, b_alpha, mix_scale, w_out, ln1_g, ln1_b, w_up, conv_w, conv_b, w_down, ln2_g, ln2_b):
    x = np.asarray(x, dtype=np.float32)
    if "nc" not in _CACHE:
        _CACHE["nc"] = build(DEPTH)[0]
        _CACHE["cst"] = make_consts()
    nc = _CACHE["nc"]
    cst, rope = _CACHE["cst"]
    wd = _prep_weights(w_in, w_alpha, b_alpha, mix_scale, w_out, ln1_g, ln1_b, w_up, conv_w, conv_b, w_down, ln2_g, ln2_b)
    in_maps = []
    for b in range(8):
        m = dict(wd)
        m["xin"] = np.ascontiguousarray(x[b].T)
        m["cst"] = cst
        m["rope"] = rope
        in_maps.append(m)
    res = run_bass_kernel_spmd(nc, in_maps, core_ids=list(range(8)))
    outp = np.stack([np.asarray(r["out"], dtype=np.float32).T for r in res.results], axis=0)
    return np.ascontiguousarray(outp)
```

```python
import numpy as np
from contextlib import ExitStack
import concourse.bass as bass
import concourse.mybir as mybir
from concourse.bass_utils import run_bass_kernel_spmd

F32 = mybir.dt.float32
BF16 = mybir.dt.bfloat16
AF = mybir.ActivationFunctionType
ALU = mybir.AluOpType

S = 4096
D = 1024
DEPTH = 4
DFF = 2816
PW = 3088
ALPHA = float((2 * DEPTH) ** 0.25)
LN_EPS = 1e-5
HN_EPS = 1e-6
NEG = -30000.0
NFM = 16
SC_A = 64 ** -0.5
SC_L = 32 ** -0.5

C_MASKB, C_IDENT, C_LMASK, C_A1, C_A2, C_B1, C_ONES1024, C_HMR, C_HMG, C_DECR, C_ER, C_EINVR, C_KDR, C_ONES = (
    0, 256, 384, 512, 576, 640, 768, 896, 900, 904, 908, 1420, 1932, 2444)
C_LMASK4 = 2572
C_MASK01 = 3084
NCST = 3340

P_MSA, P_MSRG, P_LN1G, P_LN1B, P_LN2G, P_LN2B, P_CW, P_CB, P_BA = 0, 8, 12, 20, 28, 36, 44, 176, 220
NPL = 221


def make_consts():
    c = np.zeros((128, NCST), np.float32)
    j = np.arange(128)[:, None]
    i = np.arange(128)[None, :]
    c[:, C_MASKB:C_MASKB + 128] = np.where(j <= i, 0.0, NEG)
    c[:, C_MASKB + 128:C_MASKB + 256] = np.where(j >= i, 0.0, NEG)
    c[:, C_IDENT:C_IDENT + 128] = np.eye(128, dtype=np.float32)
    c[:, C_MASK01:C_MASK01 + 128] = (j <= i).astype(np.float32)
    c[:, C_MASK01 + 128:C_MASK01 + 256] = (j >= i).astype(np.float32)
    c[:, C_LMASK:C_LMASK + 128] = (j <= i).astype(np.float32)
    for h in range(4):
        c[:, C_LMASK4 + h * 128:C_LMASK4 + (h + 1) * 128] = (j <= i).astype(np.float32)
    c[0:64, C_A1:C_A1 + 64] = 1.0 / 64
    c[0:64, C_A2:C_A2 + 64] = 1.0 / 64
    c[64, C_A2:C_A2 + 64] = HN_EPS
    for h in range(2):
        c[h * 64:(h + 1) * 64, C_B1 + h * 64:C_B1 + (h + 1) * 64] = 1.0 / 64
    c[:, C_ONES1024:C_ONES1024 + 128] = 1.0 / 1024
    p = np.arange(128)
    headR = (p % 64) // 16
    headG = p // 32
    for h in range(4):
        c[:, C_HMR + h] = (headR == h) * SC_L
        c[:, C_HMG + h] = (headG == h) * SC_L
    lg = np.log(1.0 - np.power(2.0, -5.0 - np.arange(4, dtype=np.float64)))
    lgp = lg[headR][:, None]
    idx = (np.arange(512) % 128)[None, :].astype(np.float64)
    c[:, C_DECR:C_DECR + 4] = np.exp(lgp * 128.0)
    c[:, C_ER:C_ER + 512] = np.exp(lgp * (idx + 1.0))
    c[:, C_EINVR:C_EINVR + 512] = np.exp(-lgp * (idx + 1.0))
    c[:, C_KDR:C_KDR + 512] = np.exp(lgp * (127.0 - idx))
    c[:, C_ONES:C_ONES + 128] = 1.0
    rope = np.zeros((4, 128, S), np.float32)
    pos = np.arange(S, dtype=np.float32)[None, :]
    invA = (1.0 / (10000.0 ** (np.arange(0, 64, 2, dtype=np.float32) / 64))).astype(np.float32)
    invR = (1.0 / (10000.0 ** (np.arange(0, 32, 2, dtype=np.float32) / 32))).astype(np.float32)
    angA = (pos * invA[p % 32][:, None]).astype(np.float32)
    angR = (pos * invR[p % 16][:, None]).astype(np.float32)
    rope[0], rope[1] = np.cos(angA), np.sin(angA)
    rope[2], rope[3] = np.cos(angR), np.sin(angR)
    return c, rope


def win_perm():
    qA, kA, vA, qR, kR, vR, gR, qG, kG, vG, rG, aG = 0, 512, 1024, 1536, 1664, 1792, 2048, 2304, 2432, 2560, 2816, 3072
    cols = []
    for base in (qA, kA):
        for g in range(2):
            cols += [base + h * 64 + i for h in range(4 * g, 4 * g + 4) for i in range(32)]
            cols += [base + h * 64 + 32 + i for h in range(4 * g, 4 * g + 4) for i in range(32)]
    for half in range(2):
        cols += [qR + h * 32 + half * 16 + i for h in range(4) for i in range(16)]
        cols += [kR + h * 32 + half * 16 + i for h in range(4) for i in range(16)]
    cols += list(range(gR, gR + 256))
    cols += list(range(qG, qG + 128))
    cols += list(range(kG, kG + 128))
    cols += list(range(rG, rG + 256))
    cols += list(range(aG, aG + 16))
    cols += list(range(vA, vA + 512))
    cols += list(range(vR, vR + 256))
    cols += list(range(vG, vG + 256))
    assert len(cols) == PW and len(set(cols)) == PW
    return np.array(cols)


class Buf:
    __slots__ = ("w", "r", "pw", "pr", "name")

    def __init__(self, name=""):
        self.w = {}
        self.r = {}
        self.pw = None
        self.pr = set()
        self.name = name


class TK:
    CE = ("pe", "act", "dve", "pool")

    def __init__(self, nc, es):
        self.nc = nc
        self.E = {"pe": nc.tensor, "act": nc.scalar, "dve": nc.vector, "pool": nc.gpsimd, "sp": nc.sync}
        self.sem = {}
        self.val = {}
        self.seen = {e: {} for e in self.E}
        for e in self.CE:
            self._mk(es, "c_" + e)
        self.dq = {}
        for q, n in (("sp", 8), ("pool", 8)):
            self.dq[q] = [self._mk(es, f"d_{q}{i}") for i in range(n)]
        self.dqi = {q: 0 for q in self.dq}
        self.pending = {e: [] for e in self.CE}
        self.nins = 0

    def _mk(self, es, name):
        self.sem[name] = es.enter_context(self.nc.semaphore(name))
        self.val[name] = 0
        return name

    def wait(self, e, s, v):
        if v <= self.seen[e].get(s, 0):
            return
        self.E[e].wait_ge(self.sem[s], v)
        self.seen[e][s] = v
        self.nins += 1

    def _deps(self, e, reads, writes, dma=False):
        own = "c_" + e
        deps = {}
        for b in reads:
            assert b.pw in (None, e), f"read of {b.name} with pending writer {b.pw}"
            for s, v in b.w.items():
                if s == own and (e == "pe" and not dma):
                    continue
                if v > deps.get(s, 0):
                    deps[s] = v
        for b in writes:
            assert b.pw in (None, e), f"write of {b.name} with pending writer {b.pw}"
            assert not (b.pr - {e}), f"write of {b.name} with pending readers {b.pr}"
            for dd in (b.w, b.r):
                for s, v in dd.items():
                    if s == own and not dma:
                        continue
                    if v > deps.get(s, 0):
                        deps[s] = v
        for s, v in deps.items():
            self.wait(e, s, v)

    def op(self, e, emit, reads=(), writes=(), tick=True):
        self._deps(e, reads, writes)
        ins = emit()
        self.nins += 1
        own = "c_" + e
        self.pending[e].append((reads, writes))
        if tick:
            self.val[own] += 1
            ins.then_inc(self.sem[own], 1)
            v = self.val[own]
            for rs, ws in self.pending[e]:
                for b in rs:
                    b.r[own] = v
                    b.pr.discard(e)
                for b in ws:
                    b.w[own] = v
                    b.pw = None
            self.pending[e] = []
        else:
            for b in reads:
                b.pr.add(e)
            for b in writes:
                b.pw = e
        return ins

    def dma(self, q, out, in_, reads=(), writes=()):
        self._deps(q, reads, writes, dma=True)
        names = self.dq[q]
        nm = names[self.dqi[q] % len(names)]
        self.dqi[q] += 1
        self.wait(q, nm, self.val[nm])
        ins = self.E[q].dma_start(out=out, in_=in_)
        self.nins += 1
        self.val[nm] += 16
        ins.then_inc(self.sem[nm], 16)
        v = self.val[nm]
        for b in reads:
            b.r[nm] = v
        for b in writes:
            b.w[nm] = v
        return ins

    def barrier(self):
        for e in self.CE:
            assert not self.pending[e], f"pending un-ticked ops on {e}"
        for e in self.E:
            for s, v in self.val.items():
                if s == "c_" + e:
                    continue
                self.wait(e, s, v)

    def mm(self, out, lhsT, rhs, start, stop, reads, writes, tick=False):
        return self.op("pe", lambda: self.nc.tensor.matmul(out, lhsT=lhsT, rhs=rhs, start=start, stop=stop,
                                                           skip_group_check=True), reads, writes, tick)

    def act(self, out, in_, func, reads, writes, **kw):
        return self.op("act", lambda: self.nc.scalar.activation(out=out, in_=in_, func=func, **kw), reads, writes)

    def tt(self, e, out, in0, in1, op, reads, writes):
        return self.op(e, lambda: self.E[e].tensor_tensor(out=out, in0=in0, in1=in1, op=op), reads, writes)

    def stt(self, out, in0, scalar, in1, op0, op1, reads, writes):
        return self.op("dve", lambda: self.nc.vector.scalar_tensor_tensor(out=out, in0=in0, scalar=scalar, in1=in1,
                                                                         op0=op0, op1=op1), reads, writes)

    def ts(self, e, out, in0, s1, s2, op0, op1, reads, writes):
        return self.op(e, lambda: self.E[e].tensor_scalar(out=out, in0=in0, scalar1=s1, scalar2=s2, op0=op0, op1=op1),
                       reads, writes)

    def copy(self, e, out, in_, reads, writes):
        if e == "act":
            return self.act(out, in_, AF.Copy, reads, writes)
        return self.op(e, lambda: self.E[e].tensor_copy(out=out, in_=in_), reads, writes)

    def memset(self, e, ap, val, writes):
        return self.op(e, lambda: self.E[e].memset(ap, val), (), writes)


class Ctx:
    pass


_UNIQ = [0]


def sbt(nc, es, name, shape, dt):
    _UNIQ[0] += 1
    return es.enter_context(nc.sbuf_tensor(f"{name}_{_UNIQ[0]}", shape, dt))


def pst(nc, es, name, shape, dt=F32):
    _UNIQ[0] += 1
    return es.enter_context(nc.psum_tensor(f"{name}_{_UNIQ[0]}", shape, dt))


def layer_norm(g, es_name, z, zB, N, gcol, bcol, dst_ap, stat_ps, stat_bufs, tmp):
    tk, nc = g.tk, g.nc
    zb, zq, msq, var, rstd, nmr = tmp["zb"], tmp["zq"], tmp["msq"], tmp["var"], tmp["rstd"], tmp["nmr"]
    B = tmp["B"]
    tk.act(zb[:, :, 0:N], z[:, :, 0:N], AF.Copy, [zB], [B["zb"]])
    tk.act(zq[:, :, 0:N], z[:, :, 0:N], AF.Square, [zB], [B["zq"]])
    mean_ps, e2_ps = stat_ps
    mB, eB = stat_bufs
    for c in range(8):
        tk.mm(mean_ps[:, 0:N], g.ones_b[:], zb[:, c, 0:N], c == 0, c == 7, [g.cstB, B["zb"]], [mB], tick=(c == 7))
    for c in range(8):
        tk.mm(e2_ps[:, 0:N], g.ones_b[:], zq[:, c, 0:N], c == 0, c == 7, [g.cstB, B["zq"]], [eB], tick=(c == 7))
    tk.act(msq[:, 0:N], mean_ps[:, 0:N], AF.Square, [mB], [B["msq"]])
    tk.tt("dve", var[:, 0:N], e2_ps[:, 0:N], msq[:, 0:N], ALU.subtract, [eB, B["msq"]], [B["var"]])
    tk.act(var[:, 0:N], var[:, 0:N], AF.Ln, [B["var"]], [B["var"]], bias=g.eps_ln[:, 0:1])
    tk.act(rstd[:, 0:N], var[:, 0:N], AF.Exp, [B["var"]], [B["rstd"]], scale=-0.5)
    tk.stt(nmr[:, 0:N], mean_ps[:, 0:N], -1.0, rstd[:, 0:N], ALU.mult, ALU.mult, [mB, B["rstd"]], [B["nmr"]])
    for c in range(8):
        e1 = "dve" if c % 2 == 0 else "pool"
        tk.tt(e1, z[:, c, 0:N], z[:, c, 0:N], rstd[:, 0:N], ALU.mult, [zB, B["rstd"]], [zB])
        tk.tt("pool", z[:, c, 0:N], z[:, c, 0:N], nmr[:, 0:N], ALU.add, [zB, B["nmr"]], [zB])
        tk.act(z[:, c, 0:N], z[:, c, 0:N], AF.Identity, [zB, g.plB], [zB], scale=gcol[:, c:c + 1], bias=bcol[:, c:c + 1])
    tk.dma("sp", dst_ap.rearrange("(c p) t -> p c t", p=128), z[:, :, 0:N], [zB], [])


def head_norm(g, src, srcB, K, lhs1, lhs2, M, stat_ps, stat_bufs, tmp, N=512):
    tk = g.tk
    B = tmp["B"]
    sq, msq, var, dd = tmp["sq"], tmp["msq"], tmp["var"], tmp["dd"]
    mean_ps, e2_ps = stat_ps
    mB, eB = stat_bufs
    srcb = tmp["srcb"]
    tk.act(sq[0:K, 0:N], src, AF.Square, [srcB], [B["sq"]])
    tk.act(srcb[0:K, 0:N], src, AF.Copy, [srcB], [B["srcb"]])
    tk.mm(mean_ps[0:M, 0:N], lhs1, srcb[0:K, 0:N], True, True, [g.cstB, B["srcb"]], [mB], tick=True)
    tk.mm(e2_ps[0:M, 0:N], lhs2, sq[0:K, 0:N], True, True, [g.cstB, B["sq"]], [eB], tick=True)
    tk.act(msq[0:M, 0:N], mean_ps[0:M, 0:N], AF.Square, [mB], [B["msq"]])
    tk.tt("dve", var[0:M, 0:N], e2_ps[0:M, 0:N], msq[0:M, 0:N], ALU.subtract, [eB, B["msq"]], [B["var"]])
    return mean_ps, mB


def build(depth=DEPTH, debug=False):
    nc = bass.Bass("TRN2", target_bir_lowering=False)
    g = Ctx()
    g.nc = nc
    dkind = "ExternalOutput" if debug else "Internal"
    xin = nc.dram_tensor("xin", [D, S], F32, kind="ExternalInput").ap()
    win = nc.dram_tensor("win", [DEPTH, D, PW], F32, kind="ExternalInput").ap()
    walpha = nc.dram_tensor("walpha", [DEPTH, 16, 128], F32, kind="ExternalInput").ap()
    wout = nc.dram_tensor("wout", [DEPTH, D, D], F32, kind="ExternalInput").ap()
    wup = nc.dram_tensor("wup", [DEPTH, D, 2 * DFF], F32, kind="ExternalInput").ap()
    wdown = nc.dram_tensor("wdown", [DEPTH, DFF, D], F32, kind="ExternalInput").ap()
    pl = nc.dram_tensor("pl", [DEPTH, 128, NPL], F32, kind="ExternalInput").ap()
    cst = nc.dram_tensor("cst", [128, NCST], F32, kind="ExternalInput").ap()
    rope = nc.dram_tensor("rope", [4, 128, S], F32, kind="ExternalInput").ap()
    out = nc.dram_tensor("out", [D, S], F32, kind="ExternalOutput").ap()
    XA = nc.dram_tensor("XA", [D, S], F32, kind="Internal").ap()
    X1F = nc.dram_tensor("X1F", [D, S], F32, kind=dkind).ap()
    YT = nc.dram_tensor("YT", [D, S], BF16, kind=dkind).ap()
    FMS = nc.dram_tensor("FMS", [NFM, 128, S], BF16, kind=dkind).ap()
    LA = nc.dram_tensor("LA", [128, S], F32, kind=dkind).ap()
    VA = nc.dram_tensor("VA", [S, 8, 65], BF16, kind=dkind).ap()
    VRG = nc.dram_tensor("VRG", [S, 512], BF16, kind=dkind).ap()
    HT = nc.dram_tensor("HT", [DFF, S], BF16, kind="Internal").ap()
    X1B = nc.dram_tensor("X1B", [D, S], BF16, kind="Internal").ap()

    with ExitStack() as es:
        tk = TK(nc, es)
        g.tk = tk
        block = es.enter_context(nc.Block())

        @block.sync
        def _(sync):
            cst_sb = sbt(nc, es, "cst_sb", [128, NCST], F32)
            g.cstB = Buf("cst")
            g.cst = cst_sb
            tk.dma("sp", cst_sb[:], cst[:, :], [], [g.cstB])
            g.maskb = sbt(nc, es, "maskb", [128, 256], BF16)
            g.ident_b = sbt(nc, es, "ident_b", [128, 128], BF16)
            g.ones_b = sbt(nc, es, "ones_b", [128, 128], BF16)
            g.eps_ln = sbt(nc, es, "eps_ln", [128, 1], F32)
            tk.copy("dve", g.maskb[:], cst_sb[:, C_MASKB:C_MASKB + 256], [g.cstB], [g.cstB])
            tk.copy("dve", g.ident_b[:], cst_sb[:, C_IDENT:C_IDENT + 128], [g.cstB], [g.cstB])
            tk.copy("dve", g.ones_b[:], cst_sb[:, C_ONES1024:C_ONES1024 + 128], [g.cstB], [g.cstB])
            tk.memset("dve", g.eps_ln[:], LN_EPS, [g.cstB])
            g.mask01 = sbt(nc, es, "mask01", [128, 4, 256], BF16)
            for jj in range(4):
                tk.copy("dve", g.mask01[:, jj, :], cst_sb[:, C_MASK01:C_MASK01 + 256], [g.cstB], [g.cstB])
            g.A1b = sbt(nc, es, "A1b", [128, 64], BF16)
            g.A2b = sbt(nc, es, "A2b", [128, 64], BF16)
            g.B1b = sbt(nc, es, "B1b", [128, 128], BF16)
            tk.copy("dve", g.A1b[:], cst_sb[:, C_A1:C_A1 + 64], [g.cstB], [g.cstB])
            tk.copy("dve", g.A2b[:], cst_sb[:, C_A2:C_A2 + 64], [g.cstB], [g.cstB])
            tk.copy("dve", g.B1b[:], cst_sb[:, C_B1:C_B1 + 128], [g.cstB], [g.cstB])
            g.eps_hn = sbt(nc, es, "eps_hn", [128, 1], F32)
            tk.memset("dve", g.eps_hn[:], HN_EPS, [g.cstB])
            g.one_col = cst_sb[:, C_ONES:C_ONES + 1]
            g.wfm = sbt(nc, es, "wfm", [128, 8, 2064], BF16)
            g.wal = sbt(nc, es, "wal", [16, 128], F32)
            g.wB = Buf("w")
            g.wtB = Buf("wt")

            def load_win(l):
                for kc in range(8):
                    tk.dma("pool", g.wfm[:, kc, :], win[l, kc * 128:(kc + 1) * 128, 0:2064], [], [g.wB])
                tk.dma("sp", g.wal[:], walpha[l], [], [g.wB])
            g.load_win = load_win
            g.pl_sb = sbt(nc, es, "pl_sb", [128, NPL], F32)
            g.negb = sbt(nc, es, "negb", [128, 1], F32)
            g.plB = Buf("pl")
            tk.barrier()
            g.load_win(0)

            for l in range(depth):
                xsrc = xin if l == 0 else XA
                xdst = out if l == depth - 1 else XA
                tk.dma("sp", g.pl_sb[:], pl[l], [], [g.plB])
                tk.ts("dve", g.negb[:], g.pl_sb[:, P_BA:P_BA + 1], -1.0, None, ALU.mult, ALU.bypass, [g.plB], [g.plB])
                phase_P(g, l, xsrc, win, walpha, rope, FMS, LA, VA, VRG)
                tk.barrier()
                if l + 1 < depth:
                    g.load_win(l + 1)
                phase_A(g, l, FMS, VA, YT)
                tk.barrier()
                phase_L(g, l, FMS, LA, VRG, YT)
                tk.barrier()
                phase_O(g, l, xsrc, wout, YT, X1F, X1B)
                tk.barrier()
                phase_F(g, l, wup, wdown, X1F, X1B, xdst, HT)
                tk.barrier()
    g.nins = tk.nins
    return nc, g


def phase_P(g, l, xsrc, win, walpha, rope, FMS, LA, VA, VRG):
    tk, nc = g.tk, g.nc
    with ExitStack() as es:
        wfm, wal, wB = g.wfm, g.wal, g.wB
        wtm = sbt(nc, es, "wtm", [128, 8, 1024], BF16)
        for kc in range(8):
            tk.dma("pool", wtm[:, kc, :], win[l, kc * 128:(kc + 1) * 128, 2064:PW], [], [g.wtB])
        xt = [sbt(nc, es, f"xt{i}", [128, 8, 512], BF16) for i in range(2)]
        xtB = [Buf(f"xt{i}") for i in range(2)]
        rp = [sbt(nc, es, f"rp{i}", [128, 4, 512], F32) for i in range(2)]
        rpB = [Buf(f"rp{i}") for i in range(2)]
        NSO = 6
        so = [sbt(nc, es, f"so{i}", [128, 512], BF16) for i in range(NSO)]
        soB = [Buf(f"so{i}") for i in range(NSO)]
        tmpf = [[sbt(nc, es, f"rt{s}_{i}", [128, 512], F32) for i in range(4)] for s in range(2)]
        tmpB = [[Buf(f"rt{s}_{i}") for i in range(4)] for s in range(2)]
        vst = [sbt(nc, es, f"vst{i}", [128, 8, 65], BF16) for i in range(2)]
        vstB = [Buf(f"vst{i}") for i in range(2)]
        vrg = [sbt(nc, es, f"vrg{i}", [128, 512], BF16) for i in range(2)]
        vrgB = [Buf(f"vrg{i}") for i in range(2)]
        ag = sbt(nc, es, "ag", [16, 512], F32)
        agB = Buf("ag")
        ez = sbt(nc, es, "ez", [128, 512], F32)
        ezB = Buf("ez")
        lst = [sbt(nc, es, f"lst{i}", [128, 512], F32) for i in range(2)]
        lstB = [Buf(f"lst{i}") for i in range(2)]
        pb = [pst(nc, es, f"pb{i}", [128, 512]) for i in range(8)]
        pbB = [Buf(f"pb{i}") for i in range(8)]
        st = {"bank": 0, "so": 0, "ts": 0, "v": 0, "ev": 0}

        def nbank():
            i = st["bank"] % 8
            st["bank"] += 1
            return i

        def nso():
            i = st["so"] % NSO
            st["so"] += 1
            return i

        for i in range(2):
            tk.memset("pool", vst[i][:], 1.0, [vstB[i]])
        xv = xsrc.rearrange("(c p) t -> p c t", p=128)
        rv = rope.rearrange("f p t -> p f t")

        def load(T):
            tk.dma("pool", xt[T % 2][:], xv[:, :, T * 512:(T + 1) * 512], [], [xtB[T % 2]])
            tk.dma("sp", rp[T % 2][:], rv[:, :, T * 512:(T + 1) * 512], [], [rpB[T % 2]])

        load(0)
        for T in range(8):
            if T + 1 < 8:
                load(T + 1)
            x_, xB_ = xt[T % 2], xtB[T % 2]
            r_, rB_ = rp[T % 2], rpB[T % 2]
            tsl = slice(T * 512, (T + 1) * 512)

            def fm_mm(tile, bank):
                for kc in range(8):
                    tk.mm(pb[bank][:, :], wfm[:, kc, tile * 128:(tile + 1) * 128], x_[:, kc, :], kc == 0, kc == 7,
                          [wB, xB_], [pbB[bank]], tick=(kc == 7))

            def store_fm(tile, si):
                tk.dma("sp", FMS[tile, :, tsl], so[si][:], [soB[si]], [])

            def tm_block(blk):
                for half in range(2):
                    b = nbank()
                    for kc in range(8):
                        tk.mm(pb[b][:, :], x_[:, kc, blk * 128:(blk + 1) * 128], wtm[:, kc, half * 512:(half + 1) * 512],
                              kc == 0, kc == 7, [g.wtB, xB_], [pbB[b]], tick=(kc == 7))
                    vi = st["v"] % 2
                    eng = "act" if st["ev"] % 2 == 0 else "dve"
                    st["ev"] += 1
                    rows = slice(T * 512 + blk * 128, T * 512 + (blk + 1) * 128)
                    if half == 0:
                        tk.copy(eng, vst[vi][:, :, 0:64], pb[b][:, :].rearrange("p (h d) -> p h d", d=64), [pbB[b]], [vstB[vi]])
                        tk.dma("sp", VA[rows, :, :], vst[vi][:], [vstB[vi]], [])
                    else:
                        tk.copy(eng, vrg[vi][:], pb[b][:, :], [pbB[b]], [vrgB[vi]])
                        tk.dma("sp", VRG[rows, :], vrg[vi][:], [vrgB[vi]], [])
                        st["v"] += 1

            pairs = [(0, 1, 0), (2, 3, 0), (4, 5, 0), (6, 7, 0), (8, 9, 2)]
            for pi, (ta, tb, ro) in enumerate(pairs):
                ba, bb = nbank(), nbank()
                fm_mm(ta, ba)
                fm_mm(tb, bb)
                C_ = r_[:, ro, :]
                S_ = r_[:, ro + 1, :]
                s = st["ts"] % 2
                st["ts"] += 1
                t1, t2, t3, t4 = tmpf[s]
                b1, b2, b3, b4 = tmpB[s]
                tk.tt("dve", t1[:], pb[ba][:, :], C_, ALU.mult, [pbB[ba], rB_], [b1])
                tk.tt("dve", t2[:], pb[bb][:, :], S_, ALU.mult, [pbB[bb], rB_], [b2])
                tk.tt("dve", t3[:], pb[ba][:, :], S_, ALU.mult, [pbB[ba], rB_], [b3])
                tk.tt("dve", t4[:], pb[bb][:, :], C_, ALU.mult, [pbB[bb], rB_], [b4])
                sa = nso()
                tk.tt("pool", so[sa][:], t1[:], t2[:], ALU.subtract, [b1, b2], [soB[sa]])
                store_fm(ta, sa)
                sb_ = nso()
                tk.tt("pool", so[sb_][:], t3[:], t4[:], ALU.add, [b3, b4], [soB[sb_]])
                store_fm(tb, sb_)
                if pi < 4 and T > 0:
                    tm_block(pi)
            for tile, kind in ((10, "silu"), (11, "silu"), (12, "copy"), (13, "copy"), (14, "silu"), (15, "silu")):
                b = nbank()
                fm_mm(tile, b)
                si = nso()
                tk.act(so[si][:], pb[b][:, :], AF.Silu if kind == "silu" else AF.Copy, [pbB[b]], [soB[si]])
                store_fm(tile, si)
            if T == 0:
                for pi in range(4):
                    tm_block(pi)
            b = nbank()
            for kc in range(8):
                tk.mm(pb[b][0:16, :], wfm[:, kc, 2048:2064], x_[:, kc, :], kc == 0, kc == 7, [wB, xB_], [pbB[b]], tick=(kc == 7))
            tk.act(ag[:], pb[b][0:16, :], AF.Copy, [pbB[b]], [agB])
            b2_ = nbank()
            tk.mm(pb[b2_][:, :], wal[:], ag[:], True, True, [wB, agB], [pbB[b2_]], tick=True)
            tk.act(ez[:], pb[b2_][:, :], AF.Exp, [pbB[b2_], g.plB], [ezB], scale=-1.0, bias=g.negb[:, 0:1])
            li = T % 2
            tk.act(lst[li][:], ez[:], AF.Ln, [ezB, g.cstB], [lstB[li]], bias=g.one_col)
            tk.dma("sp", LA[:, tsl], lst[li][:], [lstB[li]], [])


def phase_A(g, l, FMS, VA, YT):
    tk, nc = g.tk, g.nc
    with ExitStack() as es:
        qh = [sbt(nc, es, f"qh{i}", [128, S], BF16) for i in range(2)]
        kh = [sbt(nc, es, f"kh{i}", [128, S], BF16) for i in range(2)]
        v3 = [sbt(nc, es, f"v3{i}", [128, 3, 32, 65], BF16) for i in range(2)]
        inB = [Buf(f"ain{i}") for i in range(2)]
        vB = [[Buf(f"av{i}_{di}") for di in range(3)] for i in range(2)]
        acc = [sbt(nc, es, f"acc{i}", [65, S], F32) for i in range(2)]
        accB = [Buf(f"acc{i}") for i in range(2)]
        PT = [sbt(nc, es, f"PT{i}", [128, 1024], BF16) for i in range(3)]
        PTB = [[Buf(f"PT{i}a"), Buf(f"PT{i}b")] for i in range(3)]
        yst = [sbt(nc, es, f"yst{i}", [64, S], BF16) for i in range(2)]
        ystB = [Buf(f"yst{i}") for i in range(2)]
        tmp = {"B": {k: Buf("a_" + k) for k in ("sq", "srcb", "msq", "var", "dd")}}
        for k in ("msq", "var", "dd"):
            tmp[k] = sbt(nc, es, "a_" + k, [65, 512], F32)
        for k in ("sq", "srcb"):
            tmp[k] = sbt(nc, es, "a_" + k, [65, 512], BF16)
        ST = [pst(nc, es, f"ST{i}", [128, 1024]) for i in range(2)]
        STB = [Buf(f"ST{i}") for i in range(2)]
        Op = [pst(nc, es, f"Op{i}", [128, 512]) for i in range(2)]
        OpB = [Buf(f"Op{i}") for i in range(2)]
        stat = [pst(nc, es, f"astat{i}", [128, 512]) for i in range(2)]
        statB = [Buf(f"astat{i}") for i in range(2)]
        cs = g.cst
        A1 = g.A1b[0:65, :]
        A2 = g.A2b[0:65, :]

        def load_head(h):
            i = h % 2
            gI, hh = h // 4, h % 4
            rows = slice(hh * 32, hh * 32 + 32)
            tk.dma("sp", qh[i][0:32, :], FMS[2 * gI, rows, :], [], [inB[i]])
            tk.dma("sp", qh[i][32:64, :], FMS[2 * gI + 1, rows, :], [], [inB[i]])
            tk.dma("sp", kh[i][0:32, :], FMS[4 + 2 * gI, rows, :], [], [inB[i]])
            tk.dma("sp", kh[i][32:64, :], FMS[4 + 2 * gI + 1, rows, :], [], [inB[i]])
            for di, d in enumerate((1, 4, 16)):
                src = VA[:, h, :].rearrange("(n j r) c -> j r n c", j=128, r=d)
                dst = v3[i][:, di, :, :].rearrange("p (r n) c -> p r n c", r=d)
                nb_ = 32 // d
                if d == 1:
                    for q4 in range(4):
                        tk.dma("sp", dst[:, :, q4 * 8:(q4 + 1) * 8, :], src[:, :, q4 * 8:(q4 + 1) * 8, :], [], [vB[i][di]])
                elif d == 4:
                    for r_ in range(4):
                        tk.dma("sp", dst[:, r_, :, :], src[:, r_, :, :], [], [vB[i][di]])
                else:
                    for n_ in range(2):
                        for hf in range(2):
                            tk.dma("sp", dst[:, hf * 8:(hf + 1) * 8, n_, :], src[:, hf * 8:(hf + 1) * 8, n_, :], [], [vB[i][di]])

        batches = []
        for h in range(8):
            for di, d in enumerate((1, 4, 16)):
                nb = 32 // d
                blocks = [(r, n, r * nb + n) for r in range(d) for n in range(nb)]
                for b0 in range(0, 32, 4):
                    batches.append((h, di, d, nb, blocks[b0:b0 + 4]))
        NB = len(batches)

        def emit_ST(gi):
            h, di, d, nb, blks = batches[gi]
            i = h % 2
            sbuf = gi % 2
            qv = qh[i][:, :].rearrange("p (m r) -> p r m", r=d)
            kv = kh[i][:, :].rearrange("p (m r) -> p r m", r=d)
            for j, (r, n, b) in enumerate(blks):
                qn = 256 if n < nb - 1 else 128
                o_ = ST[sbuf][:, j * 256:j * 256 + qn]
                tk.mm(o_, kv[:, r, n * 128:(n + 1) * 128], qv[:, r, n * 128:n * 128 + qn], True, True,
                      [inB[i]], [STB[sbuf]], tick=(j == 3))

        def emit_exp(gi):
            p3 = gi % 3
            tk.act(PT[p3][:], ST[gi % 2][:, :], AF.Exp, [STB[gi % 2]], PTB[p3], scale=SC_A)
            m01 = g.mask01[:].rearrange("p a b -> p (a b)")
            tk.tt("dve", PT[p3][:, :], PT[p3][:, :], m01[:, :], ALU.mult, PTB[p3] + [g.cstB], PTB[p3])

        def emit_PV(gi):
            h, di, d, nb, blks = batches[gi]
            i = h % 2
            ob = gi % 2
            cur = PT[gi % 3]
            prv = PT[(gi - 1) % 3]
            for j, (r, n, b) in enumerate(blks):
                o_ = Op[ob][0:65, j * 128:(j + 1) * 128]
                rd = [vB[i][di]] + PTB[gi % 3]
                if n > 0:
                    if j > 0:
                        pprev = cur[:, (j - 1) * 256 + 128:(j - 1) * 256 + 256]
                    else:
                        pprev = prv[:, 3 * 256 + 128:4 * 256]
                        rd = rd + PTB[(gi - 1) % 3]
                    tk.mm(o_, v3[i][:, di, b - 1, :], pprev, True, False, rd, [OpB[ob]])
                    tk.mm(o_, v3[i][:, di, b, :], cur[:, j * 256:j * 256 + 128], False, True, rd, [OpB[ob]], tick=(j == 3))
                else:
                    tk.mm(o_, v3[i][:, di, b, :], cur[:, j * 256:j * 256 + 128], True, True, rd, [OpB[ob]], tick=(j == 3))

        def emit_evac(gi):
            h, di, d, nb, blks = batches[gi]
            a, aB = acc[h % 2], accB[h % 2]
            ob = gi % 2
            o_ = Op[ob][0:65, :]
            r0, n0, b0 = blks[0]
            if d == 1:
                tk.copy("dve", a[:, n0 * 128:n0 * 128 + 512], o_, [OpB[ob]], [aB])
            elif d == 4:
                av = a[:, :].rearrange("p (m q) -> p q m", q=4)[:, r0, n0 * 128:n0 * 128 + 512]
                tk.tt("dve", av, av, o_, ALU.add, [OpB[ob], aB], [aB])
            else:
                av = a[:, :].rearrange("p (m q) -> p q m", q=16)[:, r0:r0 + 2, :]
                tk.tt("dve", av, av, o_.rearrange("p (a m) -> p a m", a=2), ALU.add, [OpB[ob], aB], [aB])

        def emit_post(h, t, stage):
            a, aB = acc[h % 2], accB[h % 2]
            B = tmp["B"]
            K, M, N = 65, 64, 512
            src = a[0:65, t * 512:(t + 1) * 512]
            sq, srcb, msq, var, dd = tmp["sq"], tmp["srcb"], tmp["msq"], tmp["var"], tmp["dd"]
            mean_ps, e2_ps = stat[0], stat[1]
            mB, eB = statB[0], statB[1]
            if stage == 1:
                tk.act(sq[0:K, 0:N], src, AF.Square, [aB], [B["sq"]])
                tk.act(srcb[0:K, 0:N], src, AF.Copy, [aB], [B["srcb"]])
                tk.mm(mean_ps[0:M, 0:N], A1, srcb[0:K, 0:N], True, True, [g.cstB, B["srcb"]], [mB], tick=True)
                tk.mm(e2_ps[0:M, 0:N], A2, sq[0:K, 0:N], True, True, [g.cstB, B["sq"]], [eB], tick=True)
            elif stage == 2:
                tk.act(msq[0:M, 0:N], mean_ps[0:M, 0:N], AF.Square, [mB], [B["msq"]])
                tk.tt("dve", var[0:M, 0:N], e2_ps[0:M, 0:N], msq[0:M, 0:N], ALU.subtract, [eB, B["msq"]], [B["var"]])
                tk.tt("dve", dd[0:64, :], a[0:64, t * 512:(t + 1) * 512], mean_ps[0:64, :], ALU.subtract, [aB, mB, B["msq"]], [B["dd"]])
            else:
                tk.act(var[0:64, :], var[0:64, :], AF.Ln, [B["var"]], [B["var"]])
                tk.act(var[0:64, :], var[0:64, :], AF.Exp, [B["var"]], [B["var"]], scale=-0.5)
                y, yB = yst[h % 2], ystB[h % 2]
                tk.stt(y[:, t * 512:(t + 1) * 512], dd[0:64, :], g.pl_sb[0:64, P_MSA + h:P_MSA + h + 1], var[0:64, :],
                       ALU.mult, ALU.mult, [B["dd"], B["var"], g.plB], [yB])
                if t == 7:
                    tk.dma("sp", YT[h * 64:(h + 1) * 64, :], y[:, :], [yB], [])

        for i in range(2):
            tk.memset("dve", qh[i][64:128, :], 0.0, [inB[i]])
            tk.memset("pool", kh[i][64:128, :], 0.0, [inB[i]])
        load_head(0)
        posts = []
        cur = [None, 0]
        emit_ST(0)
        emit_ST(1)
        emit_exp(0)
        for gi in range(NB):
            h = batches[gi][0]
            first_of_head = (gi % 24 == 0)
            if first_of_head and h + 1 < 8:
                load_head(h + 1)
            if gi + 2 < NB:
                emit_ST(gi + 2)
            if gi + 1 < NB:
                emit_exp(gi + 1)
            emit_PV(gi)
            emit_evac(gi)
            if cur[0] is None and posts:
                cur[0] = posts.pop(0)
                cur[1] = 1
            if cur[0] is not None:
                emit_post(cur[0][0], cur[0][1], cur[1])
                cur[1] += 1
                if cur[1] > 3:
                    cur[0] = None
            if gi % 24 == 23:
                posts += [(h, t) for t in range(8)]
        while cur[0] is not None or posts:
            if cur[0] is None:
                cur[0] = posts.pop(0)
                cur[1] = 1
            emit_post(cur[0][0], cur[0][1], cur[1])
            cur[1] += 1
            if cur[1] > 3:
                cur[0] = None


def phase_L(g, l, FMS, LA, VRG, YT):
    tk, nc = g.tk, g.nc
    cs = g.cst
    with ExitStack() as es:
        G = []
        for grp in range(2):
            d = Ctx()
            n = f"L{grp}_"
            d.qf = [sbt(nc, es, n + f"q{i}", [128, 512], BF16) for i in range(2)]
            d.kf = [sbt(nc, es, n + f"k{i}", [128, 512], BF16) for i in range(2)]
            d.vt = [sbt(nc, es, n + f"v{i}", [128, 4, 256], BF16) for i in range(2)]
            d.gt = [sbt(nc, es, n + f"g{i}", [128, 2, 512], BF16) for i in range(2)]
            d.lt = [sbt(nc, es, n + f"l{i}", [128, 512], F32) for i in range(2)] if grp == 1 else None
            d.inB = [Buf(n + f"in{i}") for i in range(2)]
            names = ("cum", "E", "Einv", "Kd", "dec", "Qbd", "kt", "ktil", "ktok", "og0", "og1", "sq", "srcb", "msq", "var", "dd")
            d.B = {k: Buf(n + k) for k in names}
            if grp == 1:
                d.cum = sbt(nc, es, n + "cum", [128, 512], F32)
                d.Eg = sbt(nc, es, n + "E", [128, 512], F32)
                d.Einvg = sbt(nc, es, n + "Einv", [128, 512], F32)
                d.Kdg = sbt(nc, es, n + "Kd", [128, 512], F32)
                d.decg = sbt(nc, es, n + "dec", [128, 4], F32)
            d.Qbd = sbt(nc, es, n + "Qbd", [128, 4, 512], BF16)
            d.kt = sbt(nc, es, n + "kt", [128, 512], BF16)
            d.ktil = sbt(nc, es, n + "ktil", [128, 512], BF16)
            d.ktok = sbt(nc, es, n + "ktok", [128, 4, 128], BF16)
            d.stf = sbt(nc, es, n + "stf", [128, 256], F32)
            d.stfB = Buf(n + "stf")
            d.stb = [sbt(nc, es, n + f"stb{i}", [128, 256], BF16) for i in range(2)]
            d.stbB = [Buf(n + f"stb{i}") for i in range(2)]
            d.nst = 0
            d.og = [sbt(nc, es, n + f"og{j}", [128, 512], F32) for j in range(2)]
            d.tmp = {"B": d.B}
            for k in ("msq", "var", "dd"):
                d.tmp[k] = sbt(nc, es, n + k, [128, 512], F32)
            for k in ("sq", "srcb"):
                d.tmp[k] = sbt(nc, es, n + k, [128, 512], BF16)
            d.yy = [sbt(nc, es, n + f"yy{i}", [128, 512], BF16) for i in range(2)]
            d.yyB = [Buf(n + f"yy{i}") for i in range(2)]
            d.nyy = 0
            d.hm = cs[:, C_HMR:C_HMR + 4] if grp == 0 else cs[:, C_HMG:C_HMG + 4]
            d.ch0 = 512 + grp * 256
            G.append(d)
        PT = [sbt(nc, es, f"l_PT{i}", [128, 4, 128], BF16) for i in range(2)]
        PTB = [Buf(f"l_PT{i}") for i in range(2)]
        STp = [pst(nc, es, f"l_ST{i}", [128, 512]) for i in range(2)]
        STpB = [Buf(f"l_ST{i}") for i in range(2)]
        kvp = pst(nc, es, "l_kv", [128, 512])
        kvB = Buf("l_kvp")
        Opp = [pst(nc, es, f"l_O{i}", [128, 512]) for i in range(2)]
        OppB = [Buf(f"l_O{i}") for i in range(2)]
        trp = pst(nc, es, "l_tr", [128, 1024], BF16)
        trB = Buf("l_tr")
        stat = [pst(nc, es, f"l_stat{i}", [128, 512]) for i in range(2)]
        statB = [Buf(f"l_stat{i}") for i in range(2)]
        B1 = g.B1b[:, :]
        lmask4 = cs[:, C_LMASK4:C_LMASK4 + 512]
        ones = cs[:, C_ONES:C_ONES + 128]
        cnt = {"pt": 0}

        def load(T, grp):
            d = G[grp]
            i = T % 2
            tsl = slice(T * 512, (T + 1) * 512)
            iB = d.inB[i]
            if grp == 0:
                tk.dma("sp", d.qf[i][0:64, :], FMS[8, 0:64, tsl], [], [iB])
                tk.dma("sp", d.qf[i][64:128, :], FMS[9, 0:64, tsl], [], [iB])
                tk.dma("sp", d.kf[i][0:64, :], FMS[8, 64:128, tsl], [], [iB])
                tk.dma("sp", d.kf[i][64:128, :], FMS[9, 64:128, tsl], [], [iB])
                g0 = 10
            else:
                tk.dma("sp", d.qf[i][:], FMS[12, :, tsl], [], [iB])
                tk.dma("sp", d.kf[i][:], FMS[13, :, tsl], [], [iB])
                tk.dma("sp", d.lt[i][:], LA[:, tsl], [], [iB])
                g0 = 14
            tk.dma("sp", d.gt[i][:, 0, :], FMS[g0, :, tsl], [], [iB])
            tk.dma("sp", d.gt[i][:, 1, :], FMS[g0 + 1, :, tsl], [], [iB])
            tk.dma("sp", d.vt[i][:], VRG[tsl, grp * 256:(grp + 1) * 256].rearrange("(c s) v -> s c v", s=128), [], [iB])

        def tables(d, grp):
            if grp == 0:
                return (cs[:, C_ER:C_ER + 512], cs[:, C_EINVR:C_EINVR + 512], cs[:, C_KDR:C_KDR + 512],
                        cs[:, C_DECR:C_DECR + 4], g.cstB, g.cstB, g.cstB, g.cstB)
            B = d.B
            return d.Eg[:], d.Einvg[:], d.Kdg[:], d.decg[:], B["E"], B["Einv"], B["Kd"], B["dec"]

        def prep(T, grp):
            d = G[grp]
            B = d.B
            i = T % 2
            iB = d.inB[i]
            q_, k_ = d.qf[i], d.kf[i]
            if T == 0:
                tk.memset("dve", d.stf[:], 0.0, [d.stfB])
                tk.memset("pool", d.stb[0][:], 0.0, [d.stbB[0]])
                d.nst = 0
            if grp == 1:
                l_ = d.lt[i]
                for c in range(4):
                    csl = slice(c * 128, (c + 1) * 128)
                    tk.op("dve", lambda csl=csl: nc.vector.tensor_tensor_scan(
                        out=d.cum[:, csl], data0=ones, data1=l_[:, csl], initial=0.0, op0=ALU.mult, op1=ALU.add),
                        [iB, g.cstB], [B["cum"]])
                tk.act(d.Eg[:], d.cum[:], AF.Exp, [B["cum"]], [B["E"]], scale=-1.0 / 16)
                tk.act(d.Einvg[:], d.cum[:], AF.Exp, [B["cum"]], [B["Einv"]], scale=1.0 / 16)
                tk.act(d.decg[:], d.cum[:].rearrange("p (c s) -> p c s", s=128)[:, :, 127], AF.Exp, [B["cum"]], [B["dec"]],
                       scale=-1.0 / 16)
                for c in range(4):
                    csl = slice(c * 128, (c + 1) * 128)
                    tk.ts("pool", d.Kdg[:, csl], d.Einvg[:, csl], d.decg[:, c:c + 1], None, ALU.mult, ALU.bypass,
                          [B["Einv"], B["dec"]], [B["Kd"]])
            E_, Einv_, Kd_, dec_, EB, EinvB, KdB, decB = tables(d, grp)
            for h in range(4):
                tk.stt(d.Qbd[:, h, :], q_[:], d.hm[:, h:h + 1], E_, ALU.mult, ALU.mult, [iB, g.cstB, EB], [B["Qbd"]])
            tk.tt("pool", d.kt[:], k_[:], Einv_, ALU.mult, [iB, EinvB], [B["kt"]])
            tk.tt("pool", d.ktil[:], k_[:], Kd_, ALU.mult, [iB, KdB], [B["ktil"]])

        def core(T, grp):
            d = G[grp]
            B = d.B
            i = T % 2
            iB = d.inB[i]
            v_ = d.vt[i]
            E_, Einv_, Kd_, dec_, EB, EinvB, KdB, decB = tables(d, grp)
            for c in range(4):
                tk.op("pe", lambda c=c: nc.tensor.transpose(out=trp[:, c * 128:(c + 1) * 128],
                                                            in_=d.ktil[:, c * 128:(c + 1) * 128], identity=g.ident_b[:]),
                      [B["ktil"], g.cstB], [trB], tick=(c == 3))
            tk.copy("act", d.ktok[:].rearrange("p c s -> p (c s)"), trp[:, 0:512], [trB], [B["ktok"]])
            sps = []

            def emit_ST(c):
                csl = slice(c * 128, (c + 1) * 128)
                sp_ = cnt["pt"] % 2
                cnt["pt"] += 1
                sps.append(sp_)
                tk.mm(STp[sp_][:, :], d.kt[:, csl], d.Qbd[:, :, csl], True, True, [B["kt"], B["Qbd"]], [STpB[sp_]], tick=True)
                tk.tt("dve", PT[sp_][:], STp[sp_][:, :].rearrange("p (h c) -> p h c", h=4),
                      lmask4.rearrange("p (h c) -> p h c", h=4), ALU.mult, [STpB[sp_], g.cstB], [PTB[sp_]])

            emit_ST(0)
            for c in range(4):
                csl = slice(c * 128, (c + 1) * 128)
                if c + 1 < 4:
                    emit_ST(c + 1)
                sp_ = sps[c]
                kvs = slice((c % 2) * 256, (c % 2) * 256 + 256)
                tk.mm(kvp[:, kvs], d.ktok[:, c, :], v_[:, c, :], True, True, [B["ktok"], iB], [kvB], tick=True)
                sbi = d.nst % 2
                for h in range(4):
                    j, half = h // 2, h % 2
                    o_ = Opp[j][half * 64:(half + 1) * 64, csl]
                    tk.mm(o_, v_[:, c, h * 64:(h + 1) * 64], PT[sp_][:, h, :], True, False, [iB, PTB[sp_]], [OppB[j]])
                    tk.mm(o_, d.stb[sbi][:, h * 64:(h + 1) * 64], d.Qbd[:, h, csl], False, True,
                          [d.stbB[sbi], B["Qbd"]], [OppB[j]], tick=(h % 2 == 1))
                tk.stt(d.stf[:], d.stf[:], dec_[:, c:c + 1], kvp[:, kvs], ALU.mult, ALU.add, [d.stfB, decB, kvB], [d.stfB])
                d.nst += 1
                sbn = d.nst % 2
                tk.copy("dve", d.stb[sbn][:], d.stf[:], [d.stfB], [d.stbB[sbn]])
            for j in range(2):
                tk.copy("act", d.og[j][:], Opp[j][:, :], [OppB[j]], [B[f"og{j}"]])

        def norm(T, grp):
            d = G[grp]
            B = d.B
            i = T % 2
            iB = d.inB[i]
            g_ = d.gt[i]
            for j in range(2):
                ogB = B[f"og{j}"]
                mean_ps, mB = head_norm(g, d.og[j][:], ogB, 128, B1, B1, 128, (stat[0], stat[1]), (statB[0], statB[1]), d.tmp)
                var, dd = d.tmp["var"], d.tmp["dd"]
                tk.act(var[:], var[:], AF.Ln, [B["var"]], [B["var"]], bias=g.eps_hn[:, 0:1])
                tk.act(var[:], var[:], AF.Exp, [B["var"]], [B["var"]], scale=-0.5)
                tk.tt("dve", dd[:], d.og[j][:], mean_ps[:, :], ALU.subtract, [ogB, mB], [B["dd"]])
                tk.stt(dd[:], dd[:], g.pl_sb[:, P_MSRG + grp * 2 + j:P_MSRG + grp * 2 + j + 1], var[:],
                       ALU.mult, ALU.mult, [B["dd"], B["var"], g.plB], [B["dd"]])
                yi = d.nyy % 2
                d.nyy += 1
                tk.tt("pool", d.yy[yi][:], dd[:], g_[:, j, :], ALU.mult, [B["dd"], iB], [d.yyB[yi]])
                tk.dma("sp", YT[d.ch0 + j * 128:d.ch0 + (j + 1) * 128, T * 512:(T + 1) * 512], d.yy[yi][:], [d.yyB[yi]], [])

        items = [(T, grp) for T in range(8) for grp in range(2)]
        NI = len(items)
        load(*items[0])
        load(*items[1])
        for s in range(NI + 2):
            if s + 2 < NI:
                pass
            if s < NI:
                prep(*items[s])
            if 0 <= s - 1 < NI:
                core(*items[s - 1])
            if 0 <= s - 2 < NI:
                norm(*items[s - 2])
                if s < NI:
                    pass
            if s + 2 < NI:
                load(*items[s + 2])


def ln_alloc(nc, es, pfx, N):
    tmp = {"B": {k: Buf(pfx + k) for k in ("zb", "zq", "msq", "var", "rstd", "nmr")}}
    tmp["zb"] = sbt(nc, es, pfx + "zb", [128, 8, N], BF16)
    tmp["zq"] = sbt(nc, es, pfx + "zq", [128, 8, N], BF16)
    for k in ("msq", "var", "rstd", "nmr"):
        tmp[k] = sbt(nc, es, pfx + k, [128, N], F32)
    return tmp


def ln_part1(g, z, zB, N, tmp):
    tk = g.tk
    B = tmp["B"]
    tk.copy("dve", tmp["zb"][:, :, 0:N], z[:, :, 0:N], list(zB), [B["zb"]])
    tk.act(tmp["zq"][:, :, 0:N], z[:, :, 0:N], AF.Square, list(zB), [B["zq"]])


def ln_part2(g, z, zB, N, gcol, bcol, dst_ap, stat_ps, stat_bufs, tmp, dst_bf=None):
    tk = g.tk
    zb, zq, msq, var, rstd, nmr = tmp["zb"], tmp["zq"], tmp["msq"], tmp["var"], tmp["rstd"], tmp["nmr"]
    B = tmp["B"]
    mean_ps, e2_ps = stat_ps
    mB, eB = stat_bufs
    for c in range(8):
        tk.mm(mean_ps[:, 0:N], g.ones_b[:], zb[:, c, 0:N], c == 0, c == 7, [g.cstB, B["zb"]], [mB], tick=(c == 7))
    for c in range(8):
        tk.mm(e2_ps[:, 0:N], g.ones_b[:], zq[:, c, 0:N], c == 0, c == 7, [g.cstB, B["zq"]], [eB], tick=(c == 7))
    tk.act(msq[:, 0:N], mean_ps[:, 0:N], AF.Square, [mB], [B["msq"]])
    tk.tt("dve", var[:, 0:N], e2_ps[:, 0:N], msq[:, 0:N], ALU.subtract, [eB, B["msq"]], [B["var"]])
    tk.act(var[:, 0:N], var[:, 0:N], AF.Ln, [B["var"]], [B["var"]], bias=g.eps_ln[:, 0:1])
    tk.act(rstd[:, 0:N], var[:, 0:N], AF.Exp, [B["var"]], [B["rstd"]], scale=-0.5)
    tk.stt(nmr[:, 0:N], mean_ps[:, 0:N], -1.0, rstd[:, 0:N], ALU.mult, ALU.mult, [mB, B["rstd"]], [B["nmr"]])
    for c in range(8):
        e2 = "pool" if c % 2 == 0 else "dve"
        tk.tt("dve", z[:, c, 0:N], z[:, c, 0:N], rstd[:, 0:N], ALU.mult, [zB[c], B["rstd"]], [zB[c]])
        tk.tt(e2, z[:, c, 0:N], z[:, c, 0:N], nmr[:, 0:N], ALU.add, [zB[c], B["nmr"]], [zB[c]])
        tk.act(z[:, c, 0:N], z[:, c, 0:N], AF.Identity, [zB[c], g.plB], [zB[c]], scale=gcol[:, c:c + 1], bias=bcol[:, c:c + 1])
    tk.dma("sp", dst_ap.rearrange("(c p) t -> p c t", p=128), z[:, :, 0:N], list(zB), [])
    if dst_bf is not None:
        tk.copy("act", zb[:, :, 0:N], z[:, :, 0:N], list(zB), [B["zb"]])
        tk.dma("sp", dst_bf.rearrange("(c p) t -> p c t", p=128), zb[:, :, 0:N], [B["zb"]], [])


def phase_O(g, l, xsrc, wout, YT, X1F, X1B):
    tk, nc = g.tk, g.nc
    with ExitStack() as es:
        wo = sbt(nc, es, "wo", [128, 8, D], BF16)
        wB = Buf("wo")
        yt = [sbt(nc, es, f"o_yt{i}", [128, 8, 512], BF16) for i in range(2)]
        xr = [sbt(nc, es, f"o_xr{i}", [128, 8, 512], F32) for i in range(2)]
        inB = [Buf(f"o_in{i}") for i in range(2)]
        z = [sbt(nc, es, f"o_z{i}", [128, 8, 512], F32) for i in range(2)]
        zB = [[Buf(f"o_z{i}_{c}") for c in range(8)] for i in range(2)]
        tmp = ln_alloc(nc, es, "o_", 512)
        pb = [pst(nc, es, f"o_pb{i}", [128, 512]) for i in range(6)]
        pbB = [Buf(f"o_pb{i}") for i in range(6)]
        stat = [pst(nc, es, f"o_stat{i}", [128, 512]) for i in range(2)]
        statB = [Buf(f"o_stat{i}") for i in range(2)]
        for kc in range(8):
            tk.dma("pool", wo[:, kc, :], wout[l, kc * 128:(kc + 1) * 128, :], [], [wB])
        yv = YT.rearrange("(c p) t -> p c t", p=128)
        xv = xsrc.rearrange("(c p) t -> p c t", p=128)
        gcol, bcol = g.pl_sb[:, P_LN1G:P_LN1G + 8], g.pl_sb[:, P_LN1B:P_LN1B + 8]

        def load(T):
            tk.dma("sp", yt[T % 2][:], yv[:, :, T * 512:(T + 1) * 512], [], [inB[T % 2]])
            tk.dma("sp", xr[T % 2][:], xv[:, :, T * 512:(T + 1) * 512], [], [inB[T % 2]])

        def fin(T):
            i = T % 2
            ln_part2(g, z[i], zB[i], 512, gcol, bcol, X1F[:, T * 512:(T + 1) * 512], (stat[0], stat[1]),
                     (statB[0], statB[1]), tmp, dst_bf=X1B[:, T * 512:(T + 1) * 512])

        load(0)
        nb = 0
        pend = None
        for T in range(8):
            if T + 1 < 8:
                load(T + 1)
            i = T % 2
            for oc in range(8):
                b = nb % 6
                nb += 1
                for kc in range(8):
                    tk.mm(pb[b][:, :], wo[:, kc, oc * 128:(oc + 1) * 128], yt[i][:, kc, :], kc == 0, kc == 7,
                          [wB, inB[i]], [pbB[b]], tick=(kc == 7))
                tk.stt(z[i][:, oc, :], xr[i][:, oc, :], ALPHA, pb[b][:, :], ALU.mult, ALU.add, [inB[i], pbB[b]], [zB[i][oc]])
                if oc == 3 and pend is not None:
                    fin(pend)
                    pend = None
            ln_part1(g, z[i], zB[i], 512, tmp)
            pend = T
        fin(pend)


def phase_F(g, l, wup, wdown, X1F, X1B, xdst, HT):
    tk, nc = g.tk, g.nc
    NT = 256
    NTI = S // NT
    xv = X1F.rearrange("(c p) t -> p c t", p=128)
    xbv = X1B.rearrange("(c p) t -> p c t", p=128)
    cw = g.pl_sb[:, P_CW:P_CW + 132]
    cb = g.pl_sb[:, P_CB:P_CB + 44]
    with ExitStack() as eso:
        wd = sbt(nc, eso, "wd", [128, 22, D], BF16)
        wdB = Buf("wd")
        with ExitStack() as es:
            x1b = sbt(nc, es, "f_x1b", [128, 8, S + 2], BF16)
            xB = [Buf(f"f_x1b{T}") for T in range(NTI)]
            haloB = Buf("f_halo")
            NW = 2
            wch = [sbt(nc, es, f"f_wch{i}", [128, 8, 256], BF16) for i in range(NW)]
            wchB = [Buf(f"f_wch{i}") for i in range(NW)]
            og = [sbt(nc, es, f"f_og{i}", [128, 2, NT], F32) for i in range(2)]
            a2 = [sbt(nc, es, f"f_a2{i}", [128, 2, NT], F32) for i in range(2)]
            ov = [sbt(nc, es, f"f_ov{i}", [128, 2, NT], F32) for i in range(2)]
            sg = [sbt(nc, es, f"f_sg{i}", [128, 2, NT], F32) for i in range(2)]
            ogB = [Buf(f"f_og{i}") for i in range(2)]
            a2B = [Buf(f"f_a2{i}") for i in range(2)]
            ovB = [Buf(f"f_ov{i}") for i in range(2)]
            sgB = [Buf(f"f_sg{i}") for i in range(2)]
            hst = [sbt(nc, es, f"f_hst{i}", [128, S], BF16) for i in range(2)]
            hstB = [Buf(f"f_hst{i}") for i in range(2)]
            Gp = [pst(nc, es, f"f_G{i}", [128, 2, 512]) for i in range(2)]
            Vp = [pst(nc, es, f"f_V{i}", [128, 2, 512]) for i in range(2)]
            GpB = [Buf(f"f_G{i}") for i in range(2)]
            VpB = [Buf(f"f_V{i}") for i in range(2)]
            wv = wup[l].rearrange("(kc p) n -> p kc n", p=128)

            def loadw(c):
                i = c % NW
                tk.dma("pool", wch[i][:, :, 0:128], wv[:, :, c * 128:(c + 1) * 128], [], [wchB[i]])
                tk.dma("pool", wch[i][:, :, 128:256], wv[:, :, (22 + c) * 128:(23 + c) * 128], [], [wchB[i]])

            tk.memset("dve", x1b[:, :, 0:2], 0.0, [haloB])
            loadw(0)
            for T in range(NTI):
                tk.dma("sp", x1b[:, :, 2 + T * NT:2 + (T + 1) * NT], xbv[:, :, T * NT:(T + 1) * NT], [], [xB[T]])
                if T == 1:
                    loadw(1)
            nb = 0
            ce = 0
            tail = [None]
            for c in range(22):
                if 1 <= c and c + 1 < 22:
                    loadw(c + 1)
                tk.dma("pool", wd[:, c, :], wdown[l, c * 128:(c + 1) * 128, :], [], [wdB])
                wi = c % NW
                hs, hsB = hst[c % 2], hstB[c % 2]
                for T2 in range(NTI // 2):
                    s = ce % 2
                    ce += 1
                    G_, V_ = Gp[s], Vp[s]
                    for tt in range(2):
                        T = T2 * 2 + tt
                        xrd = [wchB[wi], xB[T], xB[T - 1] if T > 0 else haloB]
                        for (dst, dB, off) in ((G_, GpB[s], 0), (V_, VpB[s], 128)):
                            for kc in range(8):
                                tk.mm(dst[:, tt, 0:NT + 2], wch[wi][:, kc, off:off + 128], x1b[:, kc, T * NT:T * NT + NT + 2],
                                      kc == 0, kc == 7, xrd, [dB], tick=(kc == 7))
                    cg, cv_ = c, 22 + c
                    wg = [cw[:, cg * 3 + j:cg * 3 + j + 1] for j in range(3)]
                    wv_ = [cw[:, cv_ * 3 + j:cv_ * 3 + j + 1] for j in range(3)]
                    tk.act(og[s][:], G_[:, :, 2:NT + 2], AF.Identity, [GpB[s], g.plB], [ogB[s]], scale=wg[2], bias=cb[:, cg:cg + 1])
                    tk.act(a2[s][:], G_[:, :, 1:NT + 1], AF.Identity, [GpB[s], g.plB], [a2B[s]], scale=wg[1])
                    tk.stt(og[s][:], G_[:, :, 0:NT], wg[0], og[s][:], ALU.mult, ALU.add, [GpB[s], g.plB, ogB[s], a2B[s]], [ogB[s]])
                    tk.act(ov[s][:], V_[:, :, 2:NT + 2], AF.Identity, [VpB[s], g.plB], [ovB[s]], scale=wv_[2], bias=cb[:, cv_:cv_ + 1])
                    tk.stt(ov[s][:], V_[:, :, 1:NT + 1], wv_[1], ov[s][:], ALU.mult, ALU.add, [VpB[s], g.plB, ovB[s]], [ovB[s]])
                    tk.stt(ov[s][:], V_[:, :, 0:NT], wv_[0], ov[s][:], ALU.mult, ALU.add, [VpB[s], g.plB, ovB[s]], [ovB[s]])
                    tk.tt("pool", og[s][:], og[s][:], a2[s][:], ALU.add, [ogB[s], a2B[s]], [ogB[s]])
                    if tail[0] is not None:
                        tail[0]()

                    def mk(s=s, hs=hs, hsB=hsB, T2=T2):
                        def f():
                            tk.act(sg[s][:], og[s][:], AF.Silu, [ogB[s]], [sgB[s]])
                            tk.tt("pool", hs[:, T2 * 2 * NT:(T2 + 1) * 2 * NT].rearrange("p (a b) -> p a b", a=2), sg[s][:], ov[s][:],
                                  ALU.mult, [sgB[s], ovB[s]], [hsB])
                        return f
                    tail[0] = mk()
                    if T2 == NTI // 2 - 1:
                        tail[0]()
                        tail[0] = None
                tk.dma("sp", HT[c * 128:(c + 1) * 128, :], hs[:, :], [hsB], [])
        tk.barrier()
        with ExitStack() as es:
            ht = [sbt(nc, es, f"d_ht{i}", [128, 22, 512], BF16) for i in range(2)]
            xr = [sbt(nc, es, f"d_xr{i}", [128, 512], F32) for i in range(4)]
            xrB = [Buf(f"d_xr{i}") for i in range(4)]
            inB = [Buf(f"d_in{i}") for i in range(2)]
            z = [sbt(nc, es, f"d_z{i}", [128, 8, 512], F32) for i in range(2)]
            zB = [[Buf(f"d_z{i}_{c}") for c in range(8)] for i in range(2)]
            tmp = ln_alloc(nc, es, "d_", 512)
            pb = [pst(nc, es, f"d_pb{i}", [128, 512]) for i in range(6)]
            pbB = [Buf(f"d_pb{i}") for i in range(6)]
            stat = [pst(nc, es, f"d_stat{i}", [128, 512]) for i in range(2)]
            statB = [Buf(f"d_stat{i}") for i in range(2)]
            hv = HT.rearrange("(c p) t -> p c t", p=128)
            gcol, bcol = g.pl_sb[:, P_LN2G:P_LN2G + 8], g.pl_sb[:, P_LN2B:P_LN2B + 8]

            def load(T):
                tk.dma("sp", ht[T % 2][:, 0:11, :], hv[:, 0:11, T * 512:(T + 1) * 512], [], [inB[T % 2]])
                tk.dma("sp", ht[T % 2][:, 11:22, :], hv[:, 11:22, T * 512:(T + 1) * 512], [], [inB[T % 2]])

            def loadx(k):
                T_, oc_ = k // 8, k % 8
                tk.dma("sp", xr[k % 4][:], X1F[oc_ * 128:(oc_ + 1) * 128, T_ * 512:(T_ + 1) * 512], [], [xrB[k % 4]])

            def fin(T):
                i = T % 2
                ln_part2(g, z[i], zB[i], 512, gcol, bcol, xdst[:, T * 512:(T + 1) * 512], (stat[0], stat[1]),
                         (statB[0], statB[1]), tmp)

            load(0)
            for k in range(3):
                loadx(k)
            nb = 0
            pend = None
            for T in range(8):
                if T + 1 < 8:
                    load(T + 1)
                i = T % 2
                for oc in range(8):
                    b = nb % 6
                    k = nb
                    nb += 1
                    if k + 3 < 64:
                        loadx(k + 3)
                    for c in range(22):
                        tk.mm(pb[b][:, :], wd[:, c, oc * 128:(oc + 1) * 128], ht[i][:, c, :], c == 0, c == 21,
                              [wdB, inB[i]], [pbB[b]], tick=(c == 21))
                    tk.stt(z[i][:, oc, :], xr[k % 4][:], ALPHA, pb[b][:, :], ALU.mult, ALU.add, [xrB[k % 4], pbB[b]], [zB[i][oc]])
                    if oc == 3 and pend is not None:
                        fin(pend)
                        pend = None
                ln_part1(g, z[i], zB[i], 512, tmp)
                pend = T
            fin(pend)


_CACHE = {}


def _prep_weights(w_in, w_alpha, b_alpha, mix_scale, w_out, ln1_g, ln1_b, w_up, conv_w, conv_b, w_down, ln2_g, ln2_b):
    f = lambda a: np.ascontiguousarray(np.asarray(a, dtype=np.float32))
    win_p = f(np.asarray(w_in)[:, :, win_perm()])
    plb = np.zeros((DEPTH, 128, NPL), np.float32)
    ms = np.asarray(mix_scale, np.float32)
    for l in range(DEPTH):
        plb[l, 0:64, P_MSA:P_MSA + 8] = ms[l, 0:512].reshape(8, 64).T
        plb[l, :, P_MSRG:P_MSRG + 4] = ms[l, 512:1024].reshape(4, 128).T
        plb[l, :, P_LN1G:P_LN1G + 8] = np.asarray(ln1_g)[l].reshape(8, 128).T
        plb[l, :, P_LN1B:P_LN1B + 8] = np.asarray(ln1_b)[l].reshape(8, 128).T
        plb[l, :, P_LN2G:P_LN2G + 8] = np.asarray(ln2_g)[l].reshape(8, 128).T
        plb[l, :, P_LN2B:P_LN2B + 8] = np.asarray(ln2_b)[l].reshape(8, 128).T
        cwl = np.asarray(conv_w)[l].reshape(3, 44, 128)
        plb[l, :, P_CW:P_CW + 132] = cwl.transpose(2, 1, 0).reshape(128, 132)
        plb[l, :, P_CB:P_CB + 44] = np.asarray(conv_b)[l].reshape(44, 128).T
        plb[l, :, P_BA] = np.asarray(b_alpha)[l]
    return dict(win=win_p, walpha=f(w_alpha), wout=f(w_out), wup=f(w_up), wdown=f(w_down), pl=plb)


def kernel(x, w_in, w_alpha, b_alpha, mix_scale, w_out, ln1_g, ln1_b, w_up, conv_w, conv_b, w_down, ln2_g, ln2_b):
    x = np.asarray(x, dtype=np.float32)
    if "nc" not in _CACHE:
        _CACHE["nc"] = build(DEPTH)[0]
        _CACHE["cst"] = make_consts()
    nc = _CACHE["nc"]
    cst, rope = _CACHE["cst"]
    wd = _prep_weights(w_in, w_alpha, b_alpha, mix_scale, w_out, ln1_g, ln1_b, w_up, conv_w, conv_b, w_down, ln2_g, ln2_b)
    in_maps = []
    for b in range(8):
        m = dict(wd)
        m["xin"] = np.ascontiguousarray(x[b].T)
        m["cst"] = cst
        m["rope"] = rope
        in_maps.append(m)
    res = run_bass_kernel_spmd(nc, in_maps, core_ids=list(range(8)))
    outp = np.stack([np.asarray(r["out"], dtype=np.float32).T for r in res.results], axis=0)
    return np.ascontiguousarray(outp)
```

```python
import numpy as np
from contextlib import ExitStack
import concourse.bass as bass
import concourse.mybir as mybir
from concourse.bass_utils import run_bass_kernel_spmd

F32 = mybir.dt.float32
BF16 = mybir.dt.bfloat16
AF = mybir.ActivationFunctionType
ALU = mybir.AluOpType

S = 4096
D = 1024
DEPTH = 4
DFF = 2816
PW = 3088
ALPHA = float((2 * DEPTH) ** 0.25)
LN_EPS = 1e-5
HN_EPS = 1e-6
NEG = -30000.0
NFM = 16
SC_A = 64 ** -0.5
SC_L = 32 ** -0.5

C_MASKB, C_IDENT, C_LMASK, C_A1, C_A2, C_B1, C_ONES1024, C_HMR, C_HMG, C_DECR, C_ER, C_EINVR, C_KDR, C_ONES = (
    0, 256, 384, 512, 576, 640, 768, 896, 900, 904, 908, 1420, 1932, 2444)
C_LMASK4 = 2572
C_MASK01 = 3084
NCST = 3340

P_MSA, P_MSRG, P_LN1G, P_LN1B, P_LN2G, P_LN2B, P_CW, P_CB, P_BA = 0, 8, 12, 20, 28, 36, 44, 176, 220
NPL = 221


def make_consts():
    c = np.zeros((128, NCST), np.float32)
    j = np.arange(128)[:, None]
    i = np.arange(128)[None, :]
    c[:, C_MASKB:C_MASKB + 128] = np.where(j <= i, 0.0, NEG)
    c[:, C_MASKB + 128:C_MASKB + 256] = np.where(j >= i, 0.0, NEG)
    c[:, C_IDENT:C_IDENT + 128] = np.eye(128, dtype=np.float32)
    c[:, C_MASK01:C_MASK01 + 128] = (j <= i).astype(np.float32)
    c[:, C_MASK01 + 128:C_MASK01 + 256] = (j >= i).astype(np.float32)
    c[:, C_LMASK:C_LMASK + 128] = (j <= i).astype(np.float32)
    for h in range(4):
        c[:, C_LMASK4 + h * 128:C_LMASK4 + (h + 1) * 128] = (j <= i).astype(np.float32)
    c[0:64, C_A1:C_A1 + 64] = 1.0 / 64
    c[0:64, C_A2:C_A2 + 64] = 1.0 / 64
    c[64, C_A2:C_A2 + 64] = HN_EPS
    for h in range(2):
        c[h * 64:(h + 1) * 64, C_B1 + h * 64:C_B1 + (h + 1) * 64] = 1.0 / 64
    c[:, C_ONES1024:C_ONES1024 + 128] = 1.0 / 1024
    p = np.arange(128)
    headR = (p % 64) // 16
    headG = p // 32
    for h in range(4):
        c[:, C_HMR + h] = (headR == h) * SC_L
        c[:, C_HMG + h] = (headG == h) * SC_L
    lg = np.log(1.0 - np.power(2.0, -5.0 - np.arange(4, dtype=np.float64)))
    lgp = lg[headR][:, None]
    idx = (np.arange(512) % 128)[None, :].astype(np.float64)
    c[:, C_DECR:C_DECR + 4] = np.exp(lgp * 128.0)
    c[:, C_ER:C_ER + 512] = np.exp(lgp * (idx + 1.0))
    c[:, C_EINVR:C_EINVR + 512] = np.exp(-lgp * (idx + 1.0))
    c[:, C_KDR:C_KDR + 512] = np.exp(lgp * (127.0 - idx))
    c[:, C_ONES:C_ONES + 128] = 1.0
    rope = np.zeros((4, 128, S), np.float32)
    pos = np.arange(S, dtype=np.float32)[None, :]
    invA = (1.0 / (10000.0 ** (np.arange(0, 64, 2, dtype=np.float32) / 64))).astype(np.float32)
    invR = (1.0 / (10000.0 ** (np.arange(0, 32, 2, dtype=np.float32) / 32))).astype(np.float32)
    angA = (pos * invA[p % 32][:, None]).astype(np.float32)
    angR = (pos * invR[p % 16][:, None]).astype(np.float32)
    rope[0], rope[1] = np.cos(angA), np.sin(angA)
    rope[2], rope[3] = np.cos(angR), np.sin(angR)
    return c, rope


def win_perm():
    qA, kA, vA, qR, kR, vR, gR, qG, kG, vG, rG, aG = 0, 512, 1024, 1536, 1664, 1792, 2048, 2304, 2432, 2560, 2816, 3072
    cols = []
    for base in (qA, kA):
        for g in range(2):
            cols += [base + h * 64 + i for h in range(4 * g, 4 * g + 4) for i in range(32)]
            cols += [base + h * 64 + 32 + i for h in range(4 * g, 4 * g + 4) for i in range(32)]
    for half in range(2):
        cols += [qR + h * 32 + half * 16 + i for h in range(4) for i in range(16)]
        cols += [kR + h * 32 + half * 16 + i for h in range(4) for i in range(16)]
    cols += list(range(gR, gR + 256))
    cols += list(range(qG, qG + 128))
    cols += list(range(kG, kG + 128))
    cols += list(range(rG, rG + 256))
    cols += list(range(aG, aG + 16))
    cols += list(range(vA, vA + 512))
    cols += list(range(vR, vR + 256))
    cols += list(range(vG, vG + 256))
    assert len(cols) == PW and len(set(cols)) == PW
    return np.array(cols)


class Buf:
    __slots__ = ("w", "r", "pw", "pr", "name")

    def __init__(self, name=""):
        self.w = {}
        self.r = {}
        self.pw = None
        self.pr = set()
        self.name = name


class TK:
    CE = ("pe", "act", "dve", "pool")

    def __init__(self, nc, es):
        self.nc = nc
        self.E = {"pe": nc.tensor, "act": nc.scalar, "dve": nc.vector, "pool": nc.gpsimd, "sp": nc.sync}
        self.sem = {}
        self.val = {}
        self.seen = {e: {} for e in self.E}
        for e in self.CE:
            self._mk(es, "c_" + e)
        self.dq = {}
        for q, n in (("sp", 8), ("pool", 8)):
            self.dq[q] = [self._mk(es, f"d_{q}{i}") for i in range(n)]
        self.dqi = {q: 0 for q in self.dq}
        self.pending = {e: [] for e in self.CE}
        self.nins = 0

    def _mk(self, es, name):
        self.sem[name] = es.enter_context(self.nc.semaphore(name))
        self.val[name] = 0
        return name

    def wait(self, e, s, v):
        if v <= self.seen[e].get(s, 0):
            return
        self.E[e].wait_ge(self.sem[s], v)
        self.seen[e][s] = v
        self.nins += 1

    def _deps(self, e, reads, writes, dma=False):
        own = "c_" + e
        deps = {}
        for b in reads:
            assert b.pw in (None, e), f"read of {b.name} with pending writer {b.pw}"
            for s, v in b.w.items():
                if s == own and (e == "pe" and not dma):
                    continue
                if v > deps.get(s, 0):
                    deps[s] = v
        for b in writes:
            assert b.pw in (None, e), f"write of {b.name} with pending writer {b.pw}"
            assert not (b.pr - {e}), f"write of {b.name} with pending readers {b.pr}"
            for dd in (b.w, b.r):
                for s, v in dd.items():
                    if s == own and not dma:
                        continue
                    if v > deps.get(s, 0):
                        deps[s] = v
        for s, v in deps.items():
            self.wait(e, s, v)

    def op(self, e, emit, reads=(), writes=(), tick=True):
        self._deps(e, reads, writes)
        ins = emit()
        self.nins += 1
        own = "c_" + e
        self.pending[e].append((reads, writes))
        if tick:
            self.val[own] += 1
            ins.then_inc(self.sem[own], 1)
            v = self.val[own]
            for rs, ws in self.pending[e]:
                for b in rs:
                    b.r[own] = v
                    b.pr.discard(e)
                for b in ws:
                    b.w[own] = v
                    b.pw = None
            self.pending[e] = []
        else:
            for b in reads:
                b.pr.add(e)
            for b in writes:
                b.pw = e
        return ins

    def dma(self, q, out, in_, reads=(), writes=()):
        self._deps(q, reads, writes, dma=True)
        names = self.dq[q]
        nm = names[self.dqi[q] % len(names)]
        self.dqi[q] += 1
        self.wait(q, nm, self.val[nm])
        ins = self.E[q].dma_start(out=out, in_=in_)
        self.nins += 1
        self.val[nm] += 16
        ins.then_inc(self.sem[nm], 16)
        v = self.val[nm]
        for b in reads:
            b.r[nm] = v
        for b in writes:
            b.w[nm] = v
        return ins

    def barrier(self):
        for e in self.CE:
            assert not self.pending[e], f"pending un-ticked ops on {e}"
        for e in self.E:
            for s, v in self.val.items():
                if s == "c_" + e:
                    continue
                self.wait(e, s, v)

    def mm(self, out, lhsT, rhs, start, stop, reads, writes, tick=False):
        return self.op("pe", lambda: self.nc.tensor.matmul(out, lhsT=lhsT, rhs=rhs, start=start, stop=stop,
                                                           skip_group_check=True), reads, writes, tick)

    def act(self, out, in_, func, reads, writes, **kw):
        return self.op("act", lambda: self.nc.scalar.activation(out=out, in_=in_, func=func, **kw), reads, writes)

    def tt(self, e, out, in0, in1, op, reads, writes):
        return self.op(e, lambda: self.E[e].tensor_tensor(out=out, in0=in0, in1=in1, op=op), reads, writes)

    def stt(self, out, in0, scalar, in1, op0, op1, reads, writes):
        return self.op("dve", lambda: self.nc.vector.scalar_tensor_tensor(out=out, in0=in0, scalar=scalar, in1=in1,
                                                                         op0=op0, op1=op1), reads, writes)

    def ts(self, e, out, in0, s1, s2, op0, op1, reads, writes):
        return self.op(e, lambda: self.E[e].tensor_scalar(out=out, in0=in0, scalar1=s1, scalar2=s2, op0=op0, op1=op1),
                       reads, writes)

    def copy(self, e, out, in_, reads, writes):
        if e == "act":
            return self.act(out, in_, AF.Copy, reads, writes)
        return self.op(e, lambda: self.E[e].tensor_copy(out=out, in_=in_), reads, writes)

    def memset(self, e, ap, val, writes):
        return self.op(e, lambda: self.E[e].memset(ap, val), (), writes)


class Ctx:
    pass


_UNIQ = [0]


def sbt(nc, es, name, shape, dt):
    _UNIQ[0] += 1
    return es.enter_context(nc.sbuf_tensor(f"{name}_{_UNIQ[0]}", shape, dt))


def pst(nc, es, name, shape, dt=F32):
    _UNIQ[0] += 1
    return es.enter_context(nc.psum_tensor(f"{name}_{_UNIQ[0]}", shape, dt))


def layer_norm(g, es_name, z, zB, N, gcol, bcol, dst_ap, stat_ps, stat_bufs, tmp):
    tk, nc = g.tk, g.nc
    zb, zq, msq, var, rstd, nmr = tmp["zb"], tmp["zq"], tmp["msq"], tmp["var"], tmp["rstd"], tmp["nmr"]
    B = tmp["B"]
    tk.act(zb[:, :, 0:N], z[:, :, 0:N], AF.Copy, [zB], [B["zb"]])
    tk.act(zq[:, :, 0:N], z[:, :, 0:N], AF.Square, [zB], [B["zq"]])
    mean_ps, e2_ps = stat_ps
    mB, eB = stat_bufs
    for c in range(8):
        tk.mm(mean_ps[:, 0:N], g.ones_b[:], zb[:, c, 0:N], c == 0, c == 7, [g.cstB, B["zb"]], [mB], tick=(c == 7))
    for c in range(8):
        tk.mm(e2_ps[:, 0:N], g.ones_b[:], zq[:, c, 0:N], c == 0, c == 7, [g.cstB, B["zq"]], [eB], tick=(c == 7))
    tk.act(msq[:, 0:N], mean_ps[:, 0:N], AF.Square, [mB], [B["msq"]])
    tk.tt("dve", var[:, 0:N], e2_ps[:, 0:N], msq[:, 0:N], ALU.subtract, [eB, B["msq"]], [B["var"]])
    tk.act(var[:, 0:N], var[:, 0:N], AF.Ln, [B["var"]], [B["var"]], bias=g.eps_ln[:, 0:1])
    tk.act(rstd[:, 0:N], var[:, 0:N], AF.Exp, [B["var"]], [B["rstd"]], scale=-0.5)
    tk.stt(nmr[:, 0:N], mean_ps[:, 0:N], -1.0, rstd[:, 0:N], ALU.mult, ALU.mult, [mB, B["rstd"]], [B["nmr"]])
    for c in range(8):
        e1 = "dve" if c % 2 == 0 else "pool"
        tk.tt(e1, z[:, c, 0:N], z[:, c, 0:N], rstd[:, 0:N], ALU.mult, [zB, B["rstd"]], [zB])
        tk.tt("pool", z[:, c, 0:N], z[:, c, 0:N], nmr[:, 0:N], ALU.add, [zB, B["nmr"]], [zB])
        tk.act(z[:, c, 0:N], z[:, c, 0:N], AF.Identity, [zB, g.plB], [zB], scale=gcol[:, c:c + 1], bias=bcol[:, c:c + 1])
    tk.dma("sp", dst_ap.rearrange("(c p) t -> p c t", p=128), z[:, :, 0:N], [zB], [])


def head_norm(g, src, srcB, K, lhs1, lhs2, M, stat_ps, stat_bufs, tmp, N=512):
    tk = g.tk
    B = tmp["B"]
    sq, msq, var, dd = tmp["sq"], tmp["msq"], tmp["var"], tmp["dd"]
    mean_ps, e2_ps = stat_ps
    mB, eB = stat_bufs
    srcb = tmp["srcb"]
    tk.act(sq[0:K, 0:N], src, AF.Square, [srcB], [B["sq"]])
    tk.act(srcb[0:K, 0:N], src, AF.Copy, [srcB], [B["srcb"]])
    tk.mm(mean_ps[0:M, 0:N], lhs1, srcb[0:K, 0:N], True, True, [g.cstB, B["srcb"]], [mB], tick=True)
    tk.mm(e2_ps[0:M, 0:N], lhs2, sq[0:K, 0:N], True, True, [g.cstB, B["sq"]], [eB], tick=True)
    tk.act(msq[0:M, 0:N], mean_ps[0:M, 0:N], AF.Square, [mB], [B["msq"]])
    tk.tt("dve", var[0:M, 0:N], e2_ps[0:M, 0:N], msq[0:M, 0:N], ALU.subtract, [eB, B["msq"]], [B["var"]])
    return mean_ps, mB


def build(depth=DEPTH, debug=False):
    nc = bass.Bass("TRN2", target_bir_lowering=False)
    g = Ctx()
    g.nc = nc
    dkind = "ExternalOutput" if debug else "Internal"
    xin = nc.dram_tensor("xin", [D, S], F32, kind="ExternalInput").ap()
    win = nc.dram_tensor("win", [DEPTH, D, PW], F32, kind="ExternalInput").ap()
    walpha = nc.dram_tensor("walpha", [DEPTH, 16, 128], F32, kind="ExternalInput").ap()
    wout = nc.dram_tensor("wout", [DEPTH, D, D], F32, kind="ExternalInput").ap()
    wup = nc.dram_tensor("wup", [DEPTH, D, 2 * DFF], F32, kind="ExternalInput").ap()
    wdown = nc.dram_tensor("wdown", [DEPTH, DFF, D], F32, kind="ExternalInput").ap()
    pl = nc.dram_tensor("pl", [DEPTH, 128, NPL], F32, kind="ExternalInput").ap()
    cst = nc.dram_tensor("cst", [128, NCST], F32, kind="ExternalInput").ap()
    rope = nc.dram_tensor("rope", [4, 128, S], F32, kind="ExternalInput").ap()
    out = nc.dram_tensor("out", [D, S], F32, kind="ExternalOutput").ap()
    XA = nc.dram_tensor("XA", [D, S], F32, kind="Internal").ap()
    X1F = nc.dram_tensor("X1F", [D, S], F32, kind=dkind).ap()
    YT = nc.dram_tensor("YT", [D, S], BF16, kind=dkind).ap()
    FMS = nc.dram_tensor("FMS", [NFM, 128, S], BF16, kind=dkind).ap()
    LA = nc.dram_tensor("LA", [128, S], F32, kind=dkind).ap()
    VA = nc.dram_tensor("VA", [S, 8, 65], BF16, kind=dkind).ap()
    VRG = nc.dram_tensor("VRG", [S, 512], BF16, kind=dkind).ap()
    HT = nc.dram_tensor("HT", [DFF, S], BF16, kind="Internal").ap()
    X1B = nc.dram_tensor("X1B", [D, S], BF16, kind="Internal").ap()

    with ExitStack() as es:
        tk = TK(nc, es)
        g.tk = tk
        block = es.enter_context(nc.Block())

        @block.sync
        def _(sync):
            cst_sb = sbt(nc, es, "cst_sb", [128, NCST], F32)
            g.cstB = Buf("cst")
            g.cst = cst_sb
            tk.dma("sp", cst_sb[:], cst[:, :], [], [g.cstB])
            g.maskb = sbt(nc, es, "maskb", [128, 256], BF16)
            g.ident_b = sbt(nc, es, "ident_b", [128, 128], BF16)
            g.ones_b = sbt(nc, es, "ones_b", [128, 128], BF16)
            g.eps_ln = sbt(nc, es, "eps_ln", [128, 1], F32)
            tk.copy("dve", g.maskb[:], cst_sb[:, C_MASKB:C_MASKB + 256], [g.cstB], [g.cstB])
            tk.copy("dve", g.ident_b[:], cst_sb[:, C_IDENT:C_IDENT + 128], [g.cstB], [g.cstB])
            tk.copy("dve", g.ones_b[:], cst_sb[:, C_ONES1024:C_ONES1024 + 128], [g.cstB], [g.cstB])
            tk.memset("dve", g.eps_ln[:], LN_EPS, [g.cstB])
            g.mask01 = sbt(nc, es, "mask01", [128, 4, 256], BF16)
            for jj in range(4):
                tk.copy("dve", g.mask01[:, jj, :], cst_sb[:, C_MASK01:C_MASK01 + 256], [g.cstB], [g.cstB])
            g.A1b = sbt(nc, es, "A1b", [128, 64], BF16)
            g.A2b = sbt(nc, es, "A2b", [128, 64], BF16)
            g.B1b = sbt(nc, es, "B1b", [128, 128], BF16)
            tk.copy("dve", g.A1b[:], cst_sb[:, C_A1:C_A1 + 64], [g.cstB], [g.cstB])
            tk.copy("dve", g.A2b[:], cst_sb[:, C_A2:C_A2 + 64], [g.cstB], [g.cstB])
            tk.copy("dve", g.B1b[:], cst_sb[:, C_B1:C_B1 + 128], [g.cstB], [g.cstB])
            g.eps_hn = sbt(nc, es, "eps_hn", [128, 1], F32)
            tk.memset("dve", g.eps_hn[:], HN_EPS, [g.cstB])
            g.one_col = cst_sb[:, C_ONES:C_ONES + 1]
            g.wfm = sbt(nc, es, "wfm", [128, 8, 2064], BF16)
            g.wal = sbt(nc, es, "wal", [16, 128], F32)
            g.wB = Buf("w")
            g.wtB = Buf("wt")

            def load_win(l):
                for kc in range(8):
                    tk.dma("pool", g.wfm[:, kc, :], win[l, kc * 128:(kc + 1) * 128, 0:2064], [], [g.wB])
                tk.dma("sp", g.wal[:], walpha[l], [], [g.wB])
            g.load_win = load_win
            g.pl_sb = sbt(nc, es, "pl_sb", [128, NPL], F32)
            g.negb = sbt(nc, es, "negb", [128, 1], F32)
            g.plB = Buf("pl")
            tk.barrier()
            g.load_win(0)

            for l in range(depth):
                xsrc = xin if l == 0 else XA
                xdst = out if l == depth - 1 else XA
                tk.dma("sp", g.pl_sb[:], pl[l], [], [g.plB])
                tk.ts("dve", g.negb[:], g.pl_sb[:, P_BA:P_BA + 1], -1.0, None, ALU.mult, ALU.bypass, [g.plB], [g.plB])
                phase_P(g, l, xsrc, win, walpha, rope, FMS, LA, VA, VRG)
                tk.barrier()
                if l + 1 < depth:
                    g.load_win(l + 1)
                phase_A(g, l, FMS, VA, YT)
                tk.barrier()
                phase_L(g, l, FMS, LA, VRG, YT)
                tk.barrier()
                phase_O(g, l, xsrc, wout, YT, X1F, X1B)
                tk.barrier()
                phase_F(g, l, wup, wdown, X1F, X1B, xdst, HT)
                tk.barrier()
    g.nins = tk.nins
    return nc, g


def phase_P(g, l, xsrc, win, walpha, rope, FMS, LA, VA, VRG):
    tk, nc = g.tk, g.nc
    with ExitStack() as es:
        wfm, wal, wB = g.wfm, g.wal, g.wB
        wtm = sbt(nc, es, "wtm", [128, 8, 1024], BF16)
        xt = [sbt(nc, es, f"xt{i}", [128, 8, 512], BF16) for i in range(2)]
        xtB = [Buf(f"xt{i}") for i in range(2)]
        rp = [sbt(nc, es, f"rp{i}", [128, 4, 512], F32) for i in range(2)]
        rpB = [Buf(f"rp{i}") for i in range(2)]
        NSO = 6
        so = [sbt(nc, es, f"so{i}", [128, 512], BF16) for i in range(NSO)]
        soB = [Buf(f"so{i}") for i in range(NSO)]
        tmpf = [[sbt(nc, es, f"rt{s}_{i}", [128, 512], F32) for i in range(4)] for s in range(2)]
        tmpB = [[Buf(f"rt{s}_{i}") for i in range(4)] for s in range(2)]
        vst = [sbt(nc, es, f"vst{i}", [128, 8, 65], BF16) for i in range(2)]
        vstB = [Buf(f"vst{i}") for i in range(2)]
        vrg = [sbt(nc, es, f"vrg{i}", [128, 512], BF16) for i in range(2)]
        vrgB = [Buf(f"vrg{i}") for i in range(2)]
        ag = sbt(nc, es, "ag", [16, 512], F32)
        agB = Buf("ag")
        ez = sbt(nc, es, "ez", [128, 512], F32)
        ezB = Buf("ez")
        lst = [sbt(nc, es, f"lst{i}", [128, 512], F32) for i in range(2)]
        lstB = [Buf(f"lst{i}") for i in range(2)]
        pb = [pst(nc, es, f"pb{i}", [128, 512]) for i in range(8)]
        pbB = [Buf(f"pb{i}") for i in range(8)]
        st = {"bank": 0, "so": 0, "ts": 0, "v": 0, "ev": 0}

        def nbank():
            i = st["bank"] % 8
            st["bank"] += 1
            return i

        def nso():
            i = st["so"] % NSO
            st["so"] += 1
            return i

        for i in range(2):
            tk.memset("pool", vst[i][:], 1.0, [vstB[i]])
        xv = xsrc.rearrange("(c p) t -> p c t", p=128)
        rv = rope.rearrange("f p t -> p f t")

        def load(T):
            tk.dma("pool", xt[T % 2][:], xv[:, :, T * 512:(T + 1) * 512], [], [xtB[T % 2]])
            tk.dma("sp", rp[T % 2][:], rv[:, :, T * 512:(T + 1) * 512], [], [rpB[T % 2]])

        load(0)
        for kc in range(8):
            tk.dma("pool", wtm[:, kc, :], win[l, kc * 128:(kc + 1) * 128, 2064:PW], [], [g.wtB])
        for T in range(8):
            if T + 1 < 8:
                load(T + 1)
            x_, xB_ = xt[T % 2], xtB[T % 2]
            r_, rB_ = rp[T % 2], rpB[T % 2]
            tsl = slice(T * 512, (T + 1) * 512)

            def fm_mm(tile, bank):
                for kc in range(8):
                    tk.mm(pb[bank][:, :], wfm[:, kc, tile * 128:(tile + 1) * 128], x_[:, kc, :], kc == 0, kc == 7,
                          [wB, xB_], [pbB[bank]], tick=(kc == 7))

            def store_fm(tile, si):
                tk.dma("sp", FMS[tile, :, tsl], so[si][:], [soB[si]], [])

            def tm_block(blk):
                for half in range(2):
                    b = nbank()
                    for kc in range(8):
                        tk.mm(pb[b][:, :], x_[:, kc, blk * 128:(blk + 1) * 128], wtm[:, kc, half * 512:(half + 1) * 512],
                              kc == 0, kc == 7, [g.wtB, xB_], [pbB[b]], tick=(kc == 7))
                    vi = st["v"] % 2
                    eng = "act" if st["ev"] % 2 == 0 else "dve"
                    st["ev"] += 1
                    rows = slice(T * 512 + blk * 128, T * 512 + (blk + 1) * 128)
                    if half == 0:
                        tk.copy(eng, vst[vi][:, :, 0:64], pb[b][:, :].rearrange("p (h d) -> p h d", d=64), [pbB[b]], [vstB[vi]])
                        tk.dma("sp", VA[rows, :, :], vst[vi][:], [vstB[vi]], [])
                    else:
                        tk.copy(eng, vrg[vi][:], pb[b][:, :], [pbB[b]], [vrgB[vi]])
                        tk.dma("sp", VRG[rows, :], vrg[vi][:], [vrgB[vi]], [])
                        st["v"] += 1

            pairs = [(0, 1, 0), (2, 3, 0), (4, 5, 0), (6, 7, 0), (8, 9, 2)]
            for pi, (ta, tb, ro) in enumerate(pairs):
                ba, bb = nbank(), nbank()
                fm_mm(ta, ba)
                fm_mm(tb, bb)
                C_ = r_[:, ro, :]
                S_ = r_[:, ro + 1, :]
                s = st["ts"] % 2
                st["ts"] += 1
                t1, t2, t3, t4 = tmpf[s]
                b1, b2, b3, b4 = tmpB[s]
                tk.tt("dve", t1[:], pb[ba][:, :], C_, ALU.mult, [pbB[ba], rB_], [b1])
                tk.tt("dve", t2[:], pb[bb][:, :], S_, ALU.mult, [pbB[bb], rB_], [b2])
                tk.tt("dve", t3[:], pb[ba][:, :], S_, ALU.mult, [pbB[ba], rB_], [b3])
                tk.tt("dve", t4[:], pb[bb][:, :], C_, ALU.mult, [pbB[bb], rB_], [b4])
                sa = nso()
                tk.tt("pool", so[sa][:], t1[:], t2[:], ALU.subtract, [b1, b2], [soB[sa]])
                store_fm(ta, sa)
                sb_ = nso()
                tk.tt("pool", so[sb_][:], t3[:], t4[:], ALU.add, [b3, b4], [soB[sb_]])
                store_fm(tb, sb_)
                if pi < 4 and T > 0:
                    tm_block(pi)
            for tile, kind in ((10, "silu"), (11, "silu"), (12, "copy"), (13, "copy"), (14, "silu"), (15, "silu")):
                b = nbank()
                fm_mm(tile, b)
                si = nso()
                tk.act(so[si][:], pb[b][:, :], AF.Silu if kind == "silu" else AF.Copy, [pbB[b]], [soB[si]])
                store_fm(tile, si)
            if T == 0:
                for pi in range(4):
                    tm_block(pi)
            b = nbank()
            for kc in range(8):
                tk.mm(pb[b][0:16, :], wfm[:, kc, 2048:2064], x_[:, kc, :], kc == 0, kc == 7, [wB, xB_], [pbB[b]], tick=(kc == 7))
            tk.act(ag[:], pb[b][0:16, :], AF.Copy, [pbB[b]], [agB])
            b2_ = nbank()
            tk.mm(pb[b2_][:, :], wal[:], ag[:], True, True, [wB, agB], [pbB[b2_]], tick=True)
            tk.act(ez[:], pb[b2_][:, :], AF.Exp, [pbB[b2_], g.plB], [ezB], scale=-1.0, bias=g.negb[:, 0:1])
            li = T % 2
            tk.act(lst[li][:], ez[:], AF.Ln, [ezB, g.cstB], [lstB[li]], bias=g.one_col)
            tk.dma("sp", LA[:, tsl], lst[li][:], [lstB[li]], [])


def phase_A(g, l, FMS, VA, YT):
    tk, nc = g.tk, g.nc
    with ExitStack() as es:
        qh = [sbt(nc, es, f"qh{i}", [128, S], BF16) for i in range(2)]
        kh = [sbt(nc, es, f"kh{i}", [128, S], BF16) for i in range(2)]
        v3 = [sbt(nc, es, f"v3{i}", [128, 3, 32, 65], BF16) for i in range(2)]
        inB = [Buf(f"ain{i}") for i in range(2)]
        vB = [[Buf(f"av{i}_{di}") for di in range(3)] for i in range(2)]
        acc = [sbt(nc, es, f"acc{i}", [65, S], F32) for i in range(2)]
        accB = [Buf(f"acc{i}") for i in range(2)]
        PT = [sbt(nc, es, f"PT{i}", [128, 1024], BF16) for i in range(3)]
        PTB = [[Buf(f"PT{i}a"), Buf(f"PT{i}b")] for i in range(3)]
        yst = [sbt(nc, es, f"yst{i}", [64, S], BF16) for i in range(2)]
        ystB = [Buf(f"yst{i}") for i in range(2)]
        tmp = {"B": {k: Buf("a_" + k) for k in ("sq", "srcb", "msq", "var", "dd")}}
        for k in ("msq", "var", "dd"):
            tmp[k] = sbt(nc, es, "a_" + k, [65, 512], F32)
        for k in ("sq", "srcb"):
            tmp[k] = sbt(nc, es, "a_" + k, [65, 512], BF16)
        ST = [pst(nc, es, f"ST{i}", [128, 1024]) for i in range(2)]
        STB = [Buf(f"ST{i}") for i in range(2)]
        Op = [pst(nc, es, f"Op{i}", [128, 512]) for i in range(2)]
        OpB = [Buf(f"Op{i}") for i in range(2)]
        stat = [pst(nc, es, f"astat{i}", [128, 512]) for i in range(2)]
        statB = [Buf(f"astat{i}") for i in range(2)]
        cs = g.cst
        A1 = g.A1b[0:65, :]
        A2 = g.A2b[0:65, :]

        def load_head(h):
            i = h % 2
            gI, hh = h // 4, h % 4
            rows = slice(hh * 32, hh * 32 + 32)
            tk.dma("sp", qh[i][0:32, :], FMS[2 * gI, rows, :], [], [inB[i]])
            tk.dma("sp", qh[i][32:64, :], FMS[2 * gI + 1, rows, :], [], [inB[i]])
            tk.dma("sp", kh[i][0:32, :], FMS[4 + 2 * gI, rows, :], [], [inB[i]])
            tk.dma("sp", kh[i][32:64, :], FMS[4 + 2 * gI + 1, rows, :], [], [inB[i]])
            for di, d in enumerate((1, 4, 16)):
                src = VA[:, h, :].rearrange("(n j r) c -> j r n c", j=128, r=d)
                dst = v3[i][:, di, :, :].rearrange("p (r n) c -> p r n c", r=d)
                nb_ = 32 // d
                if d == 1:
                    for q4 in range(4):
                        tk.dma("sp", dst[:, :, q4 * 8:(q4 + 1) * 8, :], src[:, :, q4 * 8:(q4 + 1) * 8, :], [], [vB[i][di]])
                elif d == 4:
                    for r_ in range(4):
                        tk.dma("sp", dst[:, r_, :, :], src[:, r_, :, :], [], [vB[i][di]])
                else:
                    for n_ in range(2):
                        for hf in range(2):
                            tk.dma("sp", dst[:, hf * 8:(hf + 1) * 8, n_, :], src[:, hf * 8:(hf + 1) * 8, n_, :], [], [vB[i][di]])

        batches = []
        for h in range(8):
            for di, d in enumerate((1, 4, 16)):
                nb = 32 // d
                blocks = [(r, n, r * nb + n) for r in range(d) for n in range(nb)]
                for b0 in range(0, 32, 4):
                    batches.append((h, di, d, nb, blocks[b0:b0 + 4]))
        NB = len(batches)

        def emit_ST(gi):
            h, di, d, nb, blks = batches[gi]
            i = h % 2
            sbuf = gi % 2
            qv = qh[i][:, :].rearrange("p (m r) -> p r m", r=d)
            kv = kh[i][:, :].rearrange("p (m r) -> p r m", r=d)
            for j, (r, n, b) in enumerate(blks):
                qn = 256 if n < nb - 1 else 128
                o_ = ST[sbuf][:, j * 256:j * 256 + qn]
                tk.mm(o_, kv[:, r, n * 128:(n + 1) * 128], qv[:, r, n * 128:n * 128 + qn], True, True,
                      [inB[i]], [STB[sbuf]], tick=(j == 3))

        def emit_exp(gi):
            p3 = gi % 3
            tk.act(PT[p3][:], ST[gi % 2][:, :], AF.Exp, [STB[gi % 2]], PTB[p3], scale=SC_A)
            m01 = g.mask01[:].rearrange("p a b -> p (a b)")
            tk.tt("dve", PT[p3][:, :], PT[p3][:, :], m01[:, :], ALU.mult, PTB[p3] + [g.cstB], PTB[p3])

        def emit_PV(gi):
            h, di, d, nb, blks = batches[gi]
            i = h % 2
            ob = gi % 2
            cur = PT[gi % 3]
            prv = PT[(gi - 1) % 3]
            for j, (r, n, b) in enumerate(blks):
                o_ = Op[ob][0:65, j * 128:(j + 1) * 128]
                rd = [vB[i][di]] + PTB[gi % 3]
                if n > 0:
                    if j > 0:
                        pprev = cur[:, (j - 1) * 256 + 128:(j - 1) * 256 + 256]
                    else:
                        pprev = prv[:, 3 * 256 + 128:4 * 256]
                        rd = rd + PTB[(gi - 1) % 3]
                    tk.mm(o_, v3[i][:, di, b - 1, :], pprev, True, False, rd, [OpB[ob]])
                    tk.mm(o_, v3[i][:, di, b, :], cur[:, j * 256:j * 256 + 128], False, True, rd, [OpB[ob]], tick=(j == 3))
                else:
                    tk.mm(o_, v3[i][:, di, b, :], cur[:, j * 256:j * 256 + 128], True, True, rd, [OpB[ob]], tick=(j == 3))

        def emit_evac(gi):
            h, di, d, nb, blks = batches[gi]
            a, aB = acc[h % 2], accB[h % 2]
            ob = gi % 2
            o_ = Op[ob][0:65, :]
            r0, n0, b0 = blks[0]
            if d == 1:
                tk.copy("dve", a[:, n0 * 128:n0 * 128 + 512], o_, [OpB[ob]], [aB])
            elif d == 4:
                av = a[:, :].rearrange("p (m q) -> p q m", q=4)[:, r0, n0 * 128:n0 * 128 + 512]
                tk.tt("dve", av, av, o_, ALU.add, [OpB[ob], aB], [aB])
            else:
                av = a[:, :].rearrange("p (m q) -> p q m", q=16)[:, r0:r0 + 2, :]
                tk.tt("dve", av, av, o_.rearrange("p (a m) -> p a m", a=2), ALU.add, [OpB[ob], aB], [aB])

        def emit_post(h, t, stage):
            a, aB = acc[h % 2], accB[h % 2]
            B = tmp["B"]
            K, M, N = 65, 64, 512
            src = a[0:65, t * 512:(t + 1) * 512]
            sq, srcb, msq, var, dd = tmp["sq"], tmp["srcb"], tmp["msq"], tmp["var"], tmp["dd"]
            mean_ps, e2_ps = stat[0], stat[1]
            mB, eB = statB[0], statB[1]
            if stage == 1:
                tk.act(sq[0:K, 0:N], src, AF.Square, [aB], [B["sq"]])
                tk.act(srcb[0:K, 0:N], src, AF.Copy, [aB], [B["srcb"]])
                tk.mm(mean_ps[0:M, 0:N], A1, srcb[0:K, 0:N], True, True, [g.cstB, B["srcb"]], [mB], tick=True)
                tk.mm(e2_ps[0:M, 0:N], A2, sq[0:K, 0:N], True, True, [g.cstB, B["sq"]], [eB], tick=True)
            elif stage == 2:
                tk.act(msq[0:M, 0:N], mean_ps[0:M, 0:N], AF.Square, [mB], [B["msq"]])
                tk.tt("dve", var[0:M, 0:N], e2_ps[0:M, 0:N], msq[0:M, 0:N], ALU.subtract, [eB, B["msq"]], [B["var"]])
                tk.tt("dve", dd[0:64, :], a[0:64, t * 512:(t + 1) * 512], mean_ps[0:64, :], ALU.subtract, [aB, mB, B["msq"]], [B["dd"]])
            else:
                tk.act(var[0:64, :], var[0:64, :], AF.Ln, [B["var"]], [B["var"]])
                tk.act(var[0:64, :], var[0:64, :], AF.Exp, [B["var"]], [B["var"]], scale=-0.5)
                y, yB = yst[h % 2], ystB[h % 2]
                tk.stt(y[:, t * 512:(t + 1) * 512], dd[0:64, :], g.pl_sb[0:64, P_MSA + h:P_MSA + h + 1], var[0:64, :],
                       ALU.mult, ALU.mult, [B["dd"], B["var"], g.plB], [yB])
                if t == 7:
                    tk.dma("sp", YT[h * 64:(h + 1) * 64, :], y[:, :], [yB], [])

        for i in range(2):
            tk.memset("dve", qh[i][64:128, :], 0.0, [inB[i]])
            tk.memset("pool", kh[i][64:128, :], 0.0, [inB[i]])
        load_head(0)
        posts = []
        cur = [None, 0]
        emit_ST(0)
        emit_ST(1)
        emit_exp(0)
        for gi in range(NB):
            h = batches[gi][0]
            first_of_head = (gi % 24 == 0)
            if first_of_head and h + 1 < 8:
                load_head(h + 1)
            if gi + 2 < NB:
                emit_ST(gi + 2)
            if gi + 1 < NB:
                emit_exp(gi + 1)
            emit_PV(gi)
            emit_evac(gi)
            if cur[0] is None and posts:
                cur[0] = posts.pop(0)
                cur[1] = 1
            if cur[0] is not None:
                emit_post(cur[0][0], cur[0][1], cur[1])
                cur[1] += 1
                if cur[1] > 3:
                    cur[0] = None
            if gi % 24 == 23:
                posts += [(h, t) for t in range(8)]
        while cur[0] is not None or posts:
            if cur[0] is None:
                cur[0] = posts.pop(0)
                cur[1] = 1
            emit_post(cur[0][0], cur[0][1], cur[1])
            cur[1] += 1
            if cur[1] > 3:
                cur[0] = None


def phase_L(g, l, FMS, LA, VRG, YT):
    tk, nc = g.tk, g.nc
    cs = g.cst
    with ExitStack() as es:
        G = []
        for grp in range(2):
            d = Ctx()
            n = f"L{grp}_"
            d.qf = [sbt(nc, es, n + f"q{i}", [128, 512], BF16) for i in range(2)]
            d.kf = [sbt(nc, es, n + f"k{i}", [128, 512], BF16) for i in range(2)]
            d.vt = [sbt(nc, es, n + f"v{i}", [128, 4, 256], BF16) for i in range(2)]
            d.gt = [sbt(nc, es, n + f"g{i}", [128, 2, 512], BF16) for i in range(2)]
            d.lt = [sbt(nc, es, n + f"l{i}", [128, 512], F32) for i in range(2)] if grp == 1 else None
            d.inB = [Buf(n + f"in{i}") for i in range(2)]
            names = ("cum", "E", "Einv", "Kd", "dec", "Qbd", "kt", "ktil", "ktok", "og0", "og1", "sq", "srcb", "msq", "var", "dd")
            d.B = {k: Buf(n + k) for k in names}
            if grp == 1:
                d.cum = sbt(nc, es, n + "cum", [128, 512], F32)
                d.Eg = sbt(nc, es, n + "E", [128, 512], F32)
                d.Einvg = sbt(nc, es, n + "Einv", [128, 512], F32)
                d.Kdg = sbt(nc, es, n + "Kd", [128, 512], F32)
                d.decg = sbt(nc, es, n + "dec", [128, 4], F32)
            d.Qbd = sbt(nc, es, n + "Qbd", [128, 4, 512], BF16)
            d.kt = sbt(nc, es, n + "kt", [128, 512], BF16)
            d.ktil = sbt(nc, es, n + "ktil", [128, 512], BF16)
            d.ktok = sbt(nc, es, n + "ktok", [128, 4, 128], BF16)
            d.stf = sbt(nc, es, n + "stf", [128, 256], F32)
            d.stfB = Buf(n + "stf")
            d.stb = [sbt(nc, es, n + f"stb{i}", [128, 256], BF16) for i in range(2)]
            d.stbB = [Buf(n + f"stb{i}") for i in range(2)]
            d.nst = 0
            d.og = [sbt(nc, es, n + f"og{j}", [128, 512], F32) for j in range(2)]
            d.tmp = {"B": d.B}
            for k in ("msq", "var", "dd"):
                d.tmp[k] = sbt(nc, es, n + k, [128, 512], F32)
            for k in ("sq", "srcb"):
                d.tmp[k] = sbt(nc, es, n + k, [128, 512], BF16)
            d.yy = [sbt(nc, es, n + f"yy{i}", [128, 512], BF16) for i in range(2)]
            d.yyB = [Buf(n + f"yy{i}") for i in range(2)]
            d.nyy = 0
            d.hm = cs[:, C_HMR:C_HMR + 4] if grp == 0 else cs[:, C_HMG:C_HMG + 4]
            d.ch0 = 512 + grp * 256
            G.append(d)
        PT = [sbt(nc, es, f"l_PT{i}", [128, 4, 128], BF16) for i in range(2)]
        PTB = [Buf(f"l_PT{i}") for i in range(2)]
        STp = [pst(nc, es, f"l_ST{i}", [128, 512]) for i in range(2)]
        STpB = [Buf(f"l_ST{i}") for i in range(2)]
        kvp = pst(nc, es, "l_kv", [128, 512])
        kvB = Buf("l_kvp")
        Opp = [pst(nc, es, f"l_O{i}", [128, 512]) for i in range(2)]
        OppB = [Buf(f"l_O{i}") for i in range(2)]
        trp = pst(nc, es, "l_tr", [128, 1024], BF16)
        trB = Buf("l_tr")
        stat = [pst(nc, es, f"l_stat{i}", [128, 512]) for i in range(2)]
        statB = [Buf(f"l_stat{i}") for i in range(2)]
        B1 = g.B1b[:, :]
        lmask4 = cs[:, C_LMASK4:C_LMASK4 + 512]
        ones = cs[:, C_ONES:C_ONES + 128]
        cnt = {"pt": 0}

        def load(T, grp):
            d = G[grp]
            i = T % 2
            tsl = slice(T * 512, (T + 1) * 512)
            iB = d.inB[i]
            if grp == 0:
                tk.dma("sp", d.qf[i][0:64, :], FMS[8, 0:64, tsl], [], [iB])
                tk.dma("sp", d.qf[i][64:128, :], FMS[9, 0:64, tsl], [], [iB])
                tk.dma("sp", d.kf[i][0:64, :], FMS[8, 64:128, tsl], [], [iB])
                tk.dma("sp", d.kf[i][64:128, :], FMS[9, 64:128, tsl], [], [iB])
                g0 = 10
            else:
                tk.dma("sp", d.qf[i][:], FMS[12, :, tsl], [], [iB])
                tk.dma("sp", d.kf[i][:], FMS[13, :, tsl], [], [iB])
                tk.dma("sp", d.lt[i][:], LA[:, tsl], [], [iB])
                g0 = 14
            tk.dma("sp", d.gt[i][:, 0, :], FMS[g0, :, tsl], [], [iB])
            tk.dma("sp", d.gt[i][:, 1, :], FMS[g0 + 1, :, tsl], [], [iB])
            tk.dma("sp", d.vt[i][:], VRG[tsl, grp * 256:(grp + 1) * 256].rearrange("(c s) v -> s c v", s=128), [], [iB])

        def tables(d, grp):
            if grp == 0:
                return (cs[:, C_ER:C_ER + 512], cs[:, C_EINVR:C_EINVR + 512], cs[:, C_KDR:C_KDR + 512],
                        cs[:, C_DECR:C_DECR + 4], g.cstB, g.cstB, g.cstB, g.cstB)
            B = d.B
            return d.Eg[:], d.Einvg[:], d.Kdg[:], d.decg[:], B["E"], B["Einv"], B["Kd"], B["dec"]

        def prep(T, grp):
            d = G[grp]
            B = d.B
            i = T % 2
            iB = d.inB[i]
            q_, k_ = d.qf[i], d.kf[i]
            if T == 0:
                tk.memset("dve", d.stf[:], 0.0, [d.stfB])
                tk.memset("pool", d.stb[0][:], 0.0, [d.stbB[0]])
                d.nst = 0
            if grp == 1:
                l_ = d.lt[i]
                for c in range(4):
                    csl = slice(c * 128, (c + 1) * 128)
                    tk.op("dve", lambda csl=csl: nc.vector.tensor_tensor_scan(
                        out=d.cum[:, csl], data0=ones, data1=l_[:, csl], initial=0.0, op0=ALU.mult, op1=ALU.add),
                        [iB, g.cstB], [B["cum"]])
                tk.act(d.Eg[:], d.cum[:], AF.Exp, [B["cum"]], [B["E"]], scale=-1.0 / 16)
                tk.act(d.Einvg[:], d.cum[:], AF.Exp, [B["cum"]], [B["Einv"]], scale=1.0 / 16)
                tk.act(d.decg[:], d.cum[:].rearrange("p (c s) -> p c s", s=128)[:, :, 127], AF.Exp, [B["cum"]], [B["dec"]],
                       scale=-1.0 / 16)
                for c in range(4):
                    csl = slice(c * 128, (c + 1) * 128)
                    tk.ts("pool", d.Kdg[:, csl], d.Einvg[:, csl], d.decg[:, c:c + 1], None, ALU.mult, ALU.bypass,
                          [B["Einv"], B["dec"]], [B["Kd"]])
            E_, Einv_, Kd_, dec_, EB, EinvB, KdB, decB = tables(d, grp)
            for h in range(4):
                tk.stt(d.Qbd[:, h, :], q_[:], d.hm[:, h:h + 1], E_, ALU.mult, ALU.mult, [iB, g.cstB, EB], [B["Qbd"]])
            tk.tt("pool", d.kt[:], k_[:], Einv_, ALU.mult, [iB, EinvB], [B["kt"]])
            tk.tt("pool", d.ktil[:], k_[:], Kd_, ALU.mult, [iB, KdB], [B["ktil"]])

        def core(T, grp):
            d = G[grp]
            B = d.B
            i = T % 2
            iB = d.inB[i]
            v_ = d.vt[i]
            E_, Einv_, Kd_, dec_, EB, EinvB, KdB, decB = tables(d, grp)
            for c in range(4):
                tk.op("pe", lambda c=c: nc.tensor.transpose(out=trp[:, c * 128:(c + 1) * 128],
                                                            in_=d.ktil[:, c * 128:(c + 1) * 128], identity=g.ident_b[:]),
                      [B["ktil"], g.cstB], [trB], tick=(c == 3))
            tk.copy("act", d.ktok[:].rearrange("p c s -> p (c s)"), trp[:, 0:512], [trB], [B["ktok"]])
            sps = []

            def emit_ST(c):
                csl = slice(c * 128, (c + 1) * 128)
                sp_ = cnt["pt"] % 2
                cnt["pt"] += 1
                sps.append(sp_)
                tk.mm(STp[sp_][:, :], d.kt[:, csl], d.Qbd[:, :, csl], True, True, [B["kt"], B["Qbd"]], [STpB[sp_]], tick=True)
                tk.tt("dve", PT[sp_][:], STp[sp_][:, :].rearrange("p (h c) -> p h c", h=4),
                      lmask4.rearrange("p (h c) -> p h c", h=4), ALU.mult, [STpB[sp_], g.cstB], [PTB[sp_]])

            emit_ST(0)
            for c in range(4):
                csl = slice(c * 128, (c + 1) * 128)
                if c + 1 < 4:
                    emit_ST(c + 1)
                sp_ = sps[c]
                kvs = slice((c % 2) * 256, (c % 2) * 256 + 256)
                tk.mm(kvp[:, kvs], d.ktok[:, c, :], v_[:, c, :], True, True, [B["ktok"], iB], [kvB], tick=True)
                sbi = d.nst % 2
                for h in range(4):
                    j, half = h // 2, h % 2
                    o_ = Opp[j][half * 64:(half + 1) * 64, csl]
                    tk.mm(o_, v_[:, c, h * 64:(h + 1) * 64], PT[sp_][:, h, :], True, False, [iB, PTB[sp_]], [OppB[j]])
                    tk.mm(o_, d.stb[sbi][:, h * 64:(h + 1) * 64], d.Qbd[:, h, csl], False, True,
                          [d.stbB[sbi], B["Qbd"]], [OppB[j]], tick=(h % 2 == 1))
                tk.stt(d.stf[:], d.stf[:], dec_[:, c:c + 1], kvp[:, kvs], ALU.mult, ALU.add, [d.stfB, decB, kvB], [d.stfB])
                d.nst += 1
                sbn = d.nst % 2
                tk.copy("dve", d.stb[sbn][:], d.stf[:], [d.stfB], [d.stbB[sbn]])
            for j in range(2):
                tk.copy("act", d.og[j][:], Opp[j][:, :], [OppB[j]], [B[f"og{j}"]])

        def norm(T, grp):
            d = G[grp]
            B = d.B
            i = T % 2
            iB = d.inB[i]
            g_ = d.gt[i]
            for j in range(2):
                ogB = B[f"og{j}"]
                mean_ps, mB = head_norm(g, d.og[j][:], ogB, 128, B1, B1, 128, (stat[0], stat[1]), (statB[0], statB[1]), d.tmp)
                var, dd = d.tmp["var"], d.tmp["dd"]
                tk.act(var[:], var[:], AF.Ln, [B["var"]], [B["var"]], bias=g.eps_hn[:, 0:1])
                tk.act(var[:], var[:], AF.Exp, [B["var"]], [B["var"]], scale=-0.5)
                tk.tt("dve", dd[:], d.og[j][:], mean_ps[:, :], ALU.subtract, [ogB, mB], [B["dd"]])
                tk.stt(dd[:], dd[:], g.pl_sb[:, P_MSRG + grp * 2 + j:P_MSRG + grp * 2 + j + 1], var[:],
                       ALU.mult, ALU.mult, [B["dd"], B["var"], g.plB], [B["dd"]])
                yi = d.nyy % 2
                d.nyy += 1
                tk.tt("pool", d.yy[yi][:], dd[:], g_[:, j, :], ALU.mult, [B["dd"], iB], [d.yyB[yi]])
                tk.dma("sp", YT[d.ch0 + j * 128:d.ch0 + (j + 1) * 128, T * 512:(T + 1) * 512], d.yy[yi][:], [d.yyB[yi]], [])

        items = [(T, grp) for T in range(8) for grp in range(2)]
        NI = len(items)
        load(*items[0])
        load(*items[1])
        for s in range(NI + 2):
            if s + 2 < NI:
                pass
            if s < NI:
                prep(*items[s])
            if 0 <= s - 1 < NI:
                core(*items[s - 1])
            if 0 <= s - 2 < NI:
                norm(*items[s - 2])
                if s < NI:
                    pass
            if s + 2 < NI:
                load(*items[s + 2])


def ln_alloc(nc, es, pfx, N):
    tmp = {"B": {k: Buf(pfx + k) for k in ("zb", "zq", "msq", "var", "rstd", "nmr")}}
    tmp["zb"] = sbt(nc, es, pfx + "zb", [128, 8, N], BF16)
    tmp["zq"] = sbt(nc, es, pfx + "zq", [128, 8, N], BF16)
    for k in ("msq", "var", "rstd", "nmr"):
        tmp[k] = sbt(nc, es, pfx + k, [128, N], F32)
    return tmp


def ln_part1(g, z, zB, N, tmp):
    tk = g.tk
    B = tmp["B"]
    tk.copy("dve", tmp["zb"][:, :, 0:N], z[:, :, 0:N], list(zB), [B["zb"]])
    tk.act(tmp["zq"][:, :, 0:N], z[:, :, 0:N], AF.Square, list(zB), [B["zq"]])


def ln_part2(g, z, zB, N, gcol, bcol, dst_ap, stat_ps, stat_bufs, tmp, dst_bf=None):
    tk = g.tk
    zb, zq, msq, var, rstd, nmr = tmp["zb"], tmp["zq"], tmp["msq"], tmp["var"], tmp["rstd"], tmp["nmr"]
    B = tmp["B"]
    mean_ps, e2_ps = stat_ps
    mB, eB = stat_bufs
    for c in range(8):
        tk.mm(mean_ps[:, 0:N], g.ones_b[:], zb[:, c, 0:N], c == 0, c == 7, [g.cstB, B["zb"]], [mB], tick=(c == 7))
    for c in range(8):
        tk.mm(e2_ps[:, 0:N], g.ones_b[:], zq[:, c, 0:N], c == 0, c == 7, [g.cstB, B["zq"]], [eB], tick=(c == 7))
    tk.act(msq[:, 0:N], mean_ps[:, 0:N], AF.Square, [mB], [B["msq"]])
    tk.tt("dve", var[:, 0:N], e2_ps[:, 0:N], msq[:, 0:N], ALU.subtract, [eB, B["msq"]], [B["var"]])
    tk.act(var[:, 0:N], var[:, 0:N], AF.Ln, [B["var"]], [B["var"]], bias=g.eps_ln[:, 0:1])
    tk.act(rstd[:, 0:N], var[:, 0:N], AF.Exp, [B["var"]], [B["rstd"]], scale=-0.5)
    tk.stt(nmr[:, 0:N], mean_ps[:, 0:N], -1.0, rstd[:, 0:N], ALU.mult, ALU.mult, [mB, B["rstd"]], [B["nmr"]])
    for c in range(8):
        e2 = "pool" if c % 2 == 0 else "dve"
        tk.tt("dve", z[:, c, 0:N], z[:, c, 0:N], rstd[:, 0:N], ALU.mult, [zB[c], B["rstd"]], [zB[c]])
        tk.tt(e2, z[:, c, 0:N], z[:, c, 0:N], nmr[:, 0:N], ALU.add, [zB[c], B["nmr"]], [zB[c]])
        tk.act(z[:, c, 0:N], z[:, c, 0:N], AF.Identity, [zB[c], g.plB], [zB[c]], scale=gcol[:, c:c + 1], bias=bcol[:, c:c + 1])
    tk.dma("sp", dst_ap.rearrange("(c p) t -> p c t", p=128), z[:, :, 0:N], list(zB), [])
    if dst_bf is not None:
        tk.copy("act", zb[:, :, 0:N], z[:, :, 0:N], list(zB), [B["zb"]])
        tk.dma("sp", dst_bf.rearrange("(c p) t -> p c t", p=128), zb[:, :, 0:N], [B["zb"]], [])


def phase_O(g, l, xsrc, wout, YT, X1F, X1B):
    tk, nc = g.tk, g.nc
    with ExitStack() as es:
        wo = sbt(nc, es, "wo", [128, 8, D], BF16)
        wB = Buf("wo")
        yt = [sbt(nc, es, f"o_yt{i}", [128, 8, 512], BF16) for i in range(2)]
        xr = [sbt(nc, es, f"o_xr{i}", [128, 8, 512], F32) for i in range(2)]
        inB = [Buf(f"o_in{i}") for i in range(2)]
        z = [sbt(nc, es, f"o_z{i}", [128, 8, 512], F32) for i in range(2)]
        zB = [[Buf(f"o_z{i}_{c}") for c in range(8)] for i in range(2)]
        tmp = ln_alloc(nc, es, "o_", 512)
        pb = [pst(nc, es, f"o_pb{i}", [128, 512]) for i in range(6)]
        pbB = [Buf(f"o_pb{i}") for i in range(6)]
        stat = [pst(nc, es, f"o_stat{i}", [128, 512]) for i in range(2)]
        statB = [Buf(f"o_stat{i}") for i in range(2)]
        for kc in range(8):
            tk.dma("pool", wo[:, kc, :], wout[l, kc * 128:(kc + 1) * 128, :], [], [wB])
        yv = YT.rearrange("(c p) t -> p c t", p=128)
        xv = xsrc.rearrange("(c p) t -> p c t", p=128)
        gcol, bcol = g.pl_sb[:, P_LN1G:P_LN1G + 8], g.pl_sb[:, P_LN1B:P_LN1B + 8]

        def load(T):
            tk.dma("sp", yt[T % 2][:], yv[:, :, T * 512:(T + 1) * 512], [], [inB[T % 2]])
            tk.dma("sp", xr[T % 2][:], xv[:, :, T * 512:(T + 1) * 512], [], [inB[T % 2]])

        def fin(T):
            i = T % 2
            ln_part2(g, z[i], zB[i], 512, gcol, bcol, X1F[:, T * 512:(T + 1) * 512], (stat[0], stat[1]),
                     (statB[0], statB[1]), tmp, dst_bf=X1B[:, T * 512:(T + 1) * 512])

        load(0)
        nb = 0
        pend = None
        for T in range(8):
            if T + 1 < 8:
                load(T + 1)
            i = T % 2
            for oc in range(8):
                b = nb % 6
                nb += 1
                for kc in range(8):
                    tk.mm(pb[b][:, :], wo[:, kc, oc * 128:(oc + 1) * 128], yt[i][:, kc, :], kc == 0, kc == 7,
                          [wB, inB[i]], [pbB[b]], tick=(kc == 7))
                tk.stt(z[i][:, oc, :], xr[i][:, oc, :], ALPHA, pb[b][:, :], ALU.mult, ALU.add, [inB[i], pbB[b]], [zB[i][oc]])
                if oc == 3 and pend is not None:
                    fin(pend)
                    pend = None
            ln_part1(g, z[i], zB[i], 512, tmp)
            pend = T
        fin(pend)


def phase_F(g, l, wup, wdown, X1F, X1B, xdst, HT):
    tk, nc = g.tk, g.nc
    NT = 256
    NTI = S // NT
    xv = X1F.rearrange("(c p) t -> p c t", p=128)
    xbv = X1B.rearrange("(c p) t -> p c t", p=128)
    cw = g.pl_sb[:, P_CW:P_CW + 132]
    cb = g.pl_sb[:, P_CB:P_CB + 44]
    with ExitStack() as eso:
        wd = sbt(nc, eso, "wd", [128, 22, D], BF16)
        wdB = Buf("wd")
        with ExitStack() as es:
            x1b = sbt(nc, es, "f_x1b", [128, 8, S + 2], BF16)
            xB = [Buf(f"f_x1b{T}") for T in range(NTI)]
            haloB = Buf("f_halo")
            NW = 2
            wch = [sbt(nc, es, f"f_wch{i}", [128, 8, 256], BF16) for i in range(NW)]
            wchB = [Buf(f"f_wch{i}") for i in range(NW)]
            og = [sbt(nc, es, f"f_og{i}", [128, 2, NT], F32) for i in range(2)]
            a2 = [sbt(nc, es, f"f_a2{i}", [128, 2, NT], F32) for i in range(2)]
            ov = [sbt(nc, es, f"f_ov{i}", [128, 2, NT], F32) for i in range(2)]
            sg = [sbt(nc, es, f"f_sg{i}", [128, 2, NT], F32) for i in range(2)]
            ogB = [Buf(f"f_og{i}") for i in range(2)]
            a2B = [Buf(f"f_a2{i}") for i in range(2)]
            ovB = [Buf(f"f_ov{i}") for i in range(2)]
            sgB = [Buf(f"f_sg{i}") for i in range(2)]
            hst = [sbt(nc, es, f"f_hst{i}", [128, S], BF16) for i in range(2)]
            hstB = [Buf(f"f_hst{i}") for i in range(2)]
            Gp = [pst(nc, es, f"f_G{i}", [128, 2, 512]) for i in range(2)]
            Vp = [pst(nc, es, f"f_V{i}", [128, 2, 512]) for i in range(2)]
            GpB = [Buf(f"f_G{i}") for i in range(2)]
            VpB = [Buf(f"f_V{i}") for i in range(2)]
            wv = wup[l].rearrange("(kc p) n -> p kc n", p=128)

            def loadw(c):
                i = c % NW
                tk.dma("pool", wch[i][:, :, 0:128], wv[:, :, c * 128:(c + 1) * 128], [], [wchB[i]])
                tk.dma("pool", wch[i][:, :, 128:256], wv[:, :, (22 + c) * 128:(23 + c) * 128], [], [wchB[i]])

            tk.memset("dve", x1b[:, :, 0:2], 0.0, [haloB])
            loadw(0)
            for T in range(NTI):
                tk.dma("sp", x1b[:, :, 2 + T * NT:2 + (T + 1) * NT], xbv[:, :, T * NT:(T + 1) * NT], [], [xB[T]])
                if T == 1:
                    loadw(1)
            nb = 0
            ce = 0
            tail = [None]
            for c in range(22):
                if 1 <= c and c + 1 < 22:
                    loadw(c + 1)
                tk.dma("pool", wd[:, c, :], wdown[l, c * 128:(c + 1) * 128, :], [], [wdB])
                wi = c % NW
                hs, hsB = hst[c % 2], hstB[c % 2]
                for T2 in range(NTI // 2):
                    s = ce % 2
                    ce += 1
                    G_, V_ = Gp[s], Vp[s]
                    for tt in range(2):
                        T = T2 * 2 + tt
                        xrd = [wchB[wi], xB[T], xB[T - 1] if T > 0 else haloB]
                        for (dst, dB, off) in ((G_, GpB[s], 0), (V_, VpB[s], 128)):
                            for kc in range(8):
                                tk.mm(dst[:, tt, 0:NT + 2], wch[wi][:, kc, off:off + 128], x1b[:, kc, T * NT:T * NT + NT + 2],
                                      kc == 0, kc == 7, xrd, [dB], tick=(kc == 7))
                    cg, cv_ = c, 22 + c
                    wg = [cw[:, cg * 3 + j:cg * 3 + j + 1] for j in range(3)]
                    wv_ = [cw[:, cv_ * 3 + j:cv_ * 3 + j + 1] for j in range(3)]
                    tk.act(og[s][:], G_[:, :, 2:NT + 2], AF.Identity, [GpB[s], g.plB], [ogB[s]], scale=wg[2], bias=cb[:, cg:cg + 1])
                    tk.act(a2[s][:], G_[:, :, 1:NT + 1], AF.Identity, [GpB[s], g.plB], [a2B[s]], scale=wg[1])
                    tk.stt(og[s][:], G_[:, :, 0:NT], wg[0], og[s][:], ALU.mult, ALU.add, [GpB[s], g.plB, ogB[s], a2B[s]], [ogB[s]])
                    tk.act(ov[s][:], V_[:, :, 2:NT + 2], AF.Identity, [VpB[s], g.plB], [ovB[s]], scale=wv_[2], bias=cb[:, cv_:cv_ + 1])
                    tk.stt(ov[s][:], V_[:, :, 1:NT + 1], wv_[1], ov[s][:], ALU.mult, ALU.add, [VpB[s], g.plB, ovB[s]], [ovB[s]])
                    tk.stt(ov[s][:], V_[:, :, 0:NT], wv_[0], ov[s][:], ALU.mult, ALU.add, [VpB[s], g.plB, ovB[s]], [ovB[s]])
                    tk.tt("pool", og[s][:], og[s][:], a2[s][:], ALU.add, [ogB[s], a2B[s]], [ogB[s]])
                    if tail[0] is not None:
                        tail[0]()

                    def mk(s=s, hs=hs, hsB=hsB, T2=T2):
                        def f():
                            tk.act(sg[s][:], og[s][:], AF.Silu, [ogB[s]], [sgB[s]])
                            tk.tt("pool", hs[:, T2 * 2 * NT:(T2 + 1) * 2 * NT].rearrange("p (a b) -> p a b", a=2), sg[s][:], ov[s][:],
                                  ALU.mult, [sgB[s], ovB[s]], [hsB])
                        return f
                    tail[0] = mk()
                    if T2 == NTI // 2 - 1:
                        tail[0]()
                        tail[0] = None
                tk.dma("sp", HT[c * 128:(c + 1) * 128, :], hs[:, :], [hsB], [])
        tk.barrier()
        with ExitStack() as es:
            ht = [sbt(nc, es, f"d_ht{i}", [128, 22, 512], BF16) for i in range(2)]
            xr = [sbt(nc, es, f"d_xr{i}", [128, 512], F32) for i in range(4)]
            xrB = [Buf(f"d_xr{i}") for i in range(4)]
            inB = [Buf(f"d_in{i}") for i in range(2)]
            z = [sbt(nc, es, f"d_z{i}", [128, 8, 512], F32) for i in range(2)]
            zB = [[Buf(f"d_z{i}_{c}") for c in range(8)] for i in range(2)]
            tmp = ln_alloc(nc, es, "d_", 512)
            pb = [pst(nc, es, f"d_pb{i}", [128, 512]) for i in range(6)]
            pbB = [Buf(f"d_pb{i}") for i in range(6)]
            stat = [pst(nc, es, f"d_stat{i}", [128, 512]) for i in range(2)]
            statB = [Buf(f"d_stat{i}") for i in range(2)]
            hv = HT.rearrange("(c p) t -> p c t", p=128)
            gcol, bcol = g.pl_sb[:, P_LN2G:P_LN2G + 8], g.pl_sb[:, P_LN2B:P_LN2B + 8]

            def load(T):
                tk.dma("sp", ht[T % 2][:, 0:11, :], hv[:, 0:11, T * 512:(T + 1) * 512], [], [inB[T % 2]])
                tk.dma("sp", ht[T % 2][:, 11:22, :], hv[:, 11:22, T * 512:(T + 1) * 512], [], [inB[T % 2]])

            def loadx(k):
                T_, oc_ = k // 8, k % 8
                tk.dma("sp", xr[k % 4][:], X1F[oc_ * 128:(oc_ + 1) * 128, T_ * 512:(T_ + 1) * 512], [], [xrB[k % 4]])

            def fin(T):
                i = T % 2
                ln_part2(g, z[i], zB[i], 512, gcol, bcol, xdst[:, T * 512:(T + 1) * 512], (stat[0], stat[1]),
                         (statB[0], statB[1]), tmp)

            load(0)
            for k in range(3):
                loadx(k)
            nb = 0
            pend = None
            for T in range(8):
                if T + 1 < 8:
                    load(T + 1)
                i = T % 2
                for oc in range(8):
                    b = nb % 6
                    k = nb
                    nb += 1
                    if k + 3 < 64:
                        loadx(k + 3)
                    for c in range(22):
                        tk.mm(pb[b][:, :], wd[:, c, oc * 128:(oc + 1) * 128], ht[i][:, c, :], c == 0, c == 21,
                              [wdB, inB[i]], [pbB[b]], tick=(c == 21))
                    tk.stt(z[i][:, oc, :], xr[k % 4][:], ALPHA, pb[b][:, :], ALU.mult, ALU.add, [xrB[k % 4], pbB[b]], [zB[i][oc]])
                    if oc == 3 and pend is not None:
                        fin(pend)
                        pend = None
                ln_part1(g, z[i], zB[i], 512, tmp)
                pend = T
            fin(pend)


_CACHE = {}


def _prep_weights(w_in, w_alpha, b_alpha, mix_scale, w_out, ln1_g, ln1_b, w_up, conv_w, conv_b, w_down, ln2_g, ln2_b):
    f = lambda a: np.ascontiguousarray(np.asarray(a, dtype=np.float32))
    win_p = f(np.asarray(w_in)[:, :, win_perm()])
    plb = np.zeros((DEPTH, 128, NPL), np.float32)
    ms = np.asarray(mix_scale, np.float32)
    for l in range(DEPTH):
        plb[l, 0:64, P_MSA:P_MSA + 8] = ms[l, 0:512].reshape(8, 64).T
        plb[l, :, P_MSRG:P_MSRG + 4] = ms[l, 512:1024].reshape(4, 128).T
        plb[l, :, P_LN1G:P_LN1G + 8] = np.asarray(ln1_g)[l].reshape(8, 128).T
        plb[l, :, P_LN1B:P_LN1B + 8] = np.asarray(ln1_b)[l].reshape(8, 128).T
        plb[l, :, P_LN2G:P_LN2G + 8] = np.asarray(ln2_g)[l].reshape(8, 128).T
        plb[l, :, P_LN2B:P_LN2B + 8] = np.asarray(ln2_b)[l].reshape(8, 128).T
        cwl = np.asarray(conv_w)[l].reshape(3, 44, 128)
        plb[l, :, P_CW:P_CW + 132] = cwl.transpose(2, 1, 0).reshape(128, 132)
        plb[l, :, P_CB:P_CB + 44] = np.asarray(conv_b)[l].reshape(44, 128).T
        plb[l, :, P_BA] = np.asarray(b_alpha)[l]
    return dict(win=win_p, walpha=f(w_alpha), wout=f(w_out), wup=f(w_up), wdown=f(w_down), pl=plb)


def kernel(x, w_in, w_alpha, b_alpha, mix_scale, w_out, ln1_g, ln1_b, w_up, conv_w, conv_b, w_down, ln2_g, ln2_b):
    x = np.asarray(x, dtype=np.float32)
    if "nc" not in _CACHE:
        _CACHE["nc"] = build(DEPTH)[0]
        _CACHE["cst"] = make_consts()
    nc = _CACHE["nc"]
    cst, rope = _CACHE["cst"]
    wd = _prep_weights(w_in, w_alpha, b_alpha, mix_scale, w_out, ln1_g, ln1_b, w_up, conv_w, conv_b, w_down, ln2_g, ln2_b)
    in_maps = []
    for b in range(8):
        m = dict(wd)
        m["xin"] = np.ascontiguousarray(x[b].T)
        m["cst"] = cst
        m["rope"] = rope
        in_maps.append(m)
    res = run_bass_kernel_spmd(nc, in_maps, core_ids=list(range(8)))
    outp = np.stack([np.asarray(r["out"], dtype=np.float32).T for r in res.results], axis=0)
    return np.ascontiguousarray(outp)
```
